# Optimizing a Trainium2 kernel written in Bass

```python
import jax, jax.numpy as jnp
from jax import lax
import numpy as np

D_MODEL = 1024
BATCH = 2
SEQ = 8192
DEPTH = 1

N_META = 16
D_RNN = 1280
RNN_BLOCK = 128
N_RNN_BLOCKS = D_RNN // RNN_BLOCK
RNN_CONV = 4
LRU_C = 8.0
RET_HEADS = 4
RET_DK = D_MODEL // RET_HEADS
RET_DV = 2 * RET_DK
RET_CHUNK = 128
ROPE_BASE = 10000.0
D_FF = 3 * D_MODEL
FFN_CONV = 3
EPS = 1e-6

IN_WIDTHS = (D_RNN, D_RNN, RET_HEADS * RET_DK, RET_HEADS * RET_DK,
             RET_HEADS * RET_DV, RET_HEADS * RET_DV, D_MODEL, D_MODEL)
D_IN = sum(IN_WIDTHS)
IN_SPLITS = tuple(sum(IN_WIDTHS[:i + 1]) for i in range(len(IN_WIDTHS) - 1))

kernel_name = 'hybrid_rglru_retention_convffn'


def rms_norm(x, g):
    x32 = x.astype(jnp.float32)
    y = x32 * lax.rsqrt(jnp.mean(x32 * x32, axis=-1, keepdims=True) + EPS)
    return (y * g.astype(jnp.float32)).astype(x.dtype)


def causal_dwconv(x, w, b):
    k_width = w.shape[0]
    t = x.shape[1]
    xp = jnp.pad(x, ((0, 0), (k_width - 1, 0), (0, 0)))
    y = b + xp[:, 0:t] * w[0]
    for j in range(1, k_width):
        y = y + xp[:, j:j + t] * w[j]
    return y


def rg_lru(x, w_a, b_a, w_x, b_x, lam):
    bsz, t, _ = x.shape
    xb = x.reshape(bsz, t, N_RNN_BLOCKS, RNN_BLOCK)
    r = jax.nn.sigmoid(jnp.einsum('btni,nij->btnj', xb, w_a).reshape(bsz, t, D_RNN) + b_a)
    i = jax.nn.sigmoid(jnp.einsum('btni,nij->btnj', xb, w_x).reshape(bsz, t, D_RNN) + b_x)
    log_a = -LRU_C * r.astype(jnp.float32) * jax.nn.softplus(-lam.astype(jnp.float32))
    a = jnp.exp(log_a)
    b = jnp.sqrt(-jnp.expm1(2.0 * log_a)) * (i * x).astype(jnp.float32)

    def combine(left, right):
        a1, b1 = left
        a2, b2 = right
        return a1 * a2, a2 * b1 + b2

    _, h = lax.associative_scan(combine, (a, b), axis=1)
    return h.astype(x.dtype)


def rotary(x, pos):
    half = x.shape[-1] // 2
    inv_freq = 1.0 / (ROPE_BASE ** jnp.linspace(0.0, 1.0, half, dtype=jnp.float32))
    ang = pos.astype(jnp.float32)[:, None] * inv_freq[None, :]
    cos = jnp.cos(ang)[:, None, :]
    sin = jnp.sin(ang)[:, None, :]
    x1 = x[..., :half]
    x2 = x[..., half:]
    out = jnp.concatenate([x1 * cos - x2 * sin, x2 * cos + x1 * sin], axis=-1)
    return out.astype(x.dtype)


def retention(q, k, v):
    bsz, t = q.shape[0], q.shape[1]
    n_pad = (-t) % RET_CHUNK
    pad = ((0, 0), (n_pad, 0), (0, 0), (0, 0))
    q, k, v = jnp.pad(q, pad), jnp.pad(k, pad), jnp.pad(v, pad)
    tp = t + n_pad
    n_ch = tp // RET_CHUNK
    qc = q.reshape(bsz, n_ch, RET_CHUNK, RET_HEADS, RET_DK)
    kc = k.reshape(bsz, n_ch, RET_CHUNK, RET_HEADS, RET_DK)
    vc = v.reshape(bsz, n_ch, RET_CHUNK, RET_HEADS, RET_DV)

    log_g = jnp.log1p(-(2.0 ** (-5.0 - jnp.arange(RET_HEADS, dtype=jnp.float32))))
    idx = jnp.arange(RET_CHUNK, dtype=jnp.float32)
    diff = idx[:, None] - idx[None, :]
    decay = jnp.where(diff[None] >= 0,
                      jnp.exp(log_g[:, None, None] * jnp.maximum(diff, 0.0)[None]), 0.0)

    scores = jnp.einsum('bnihd,bnjhd->bnhij', qc, kc) * decay
    o_intra = jnp.einsum('bnhij,bnjhe->bnihe', scores, vc)

    q_dec = jnp.exp(log_g[None, :] * (idx + 1.0)[:, None])[..., None]
    k_dec = jnp.exp(log_g[None, :] * (RET_CHUNK - 1.0 - idx)[:, None])[..., None]
    chunk_dec = jnp.exp(log_g * RET_CHUNK)[:, None, None]

    def step(state, inp):
        qn, kn, vn = inp
        cross = jnp.einsum('bihd,bhde->bihe', qn * q_dec, state)
        state = state * chunk_dec + jnp.einsum('bjhd,bjhe->bhde', kn * k_dec, vn)
        return state, cross

    s0 = jnp.zeros((bsz, RET_HEADS, RET_DK, RET_DV), jnp.float32)
    _, cross = lax.scan(step, s0, (jnp.moveaxis(qc, 1, 0), jnp.moveaxis(kc, 1, 0),
                                   jnp.moveaxis(vc, 1, 0)))
    o = o_intra + jnp.moveaxis(cross, 0, 1).astype(o_intra.dtype)
    return o.reshape(bsz, tp, RET_HEADS, RET_DV)[:, n_pad:]


def head_rms(o):
    o32 = o.astype(jnp.float32)
    return (o32 * lax.rsqrt(jnp.mean(o32 * o32, axis=-1, keepdims=True) + EPS)).astype(o.dtype)


def setup_inputs(seed: int = 0) -> dict:
    key = jax.random.key(seed)
    ks = jax.random.split(key, 24)
    f32 = jnp.float32

    def nrm(k, shape, scale):
        return jax.random.normal(k, shape, f32) * scale

    u = jax.random.uniform(ks[10], (DEPTH, D_RNN), f32, 0.9, 0.999)
    a_base = u ** (1.0 / LRU_C)
    lru_lambda = jnp.log(a_base) - jnp.log1p(-a_base)
    return {
        'x': nrm(ks[0], (BATCH, SEQ, D_MODEL), 1.0),
        'meta_tokens': nrm(ks[1], (N_META, D_MODEL), 1.0),
        'norm_mix_g': 1.0 + nrm(ks[2], (DEPTH, D_MODEL), 0.01),
        'w_in': nrm(ks[3], (DEPTH, D_MODEL, D_IN), D_MODEL ** -0.5),
        'rnn_conv_w': nrm(ks[4], (DEPTH, RNN_CONV, D_RNN), RNN_CONV ** -0.5),
        'rnn_conv_b': nrm(ks[5], (DEPTH, D_RNN), 0.01),
        'rg_a_w': nrm(ks[6], (DEPTH, N_RNN_BLOCKS, RNN_BLOCK, RNN_BLOCK), RNN_BLOCK ** -0.5),
        'rg_a_b': nrm(ks[7], (DEPTH, D_RNN), 0.01),
        'rg_x_w': nrm(ks[8], (DEPTH, N_RNN_BLOCKS, RNN_BLOCK, RNN_BLOCK), RNN_BLOCK ** -0.5),
        'rg_x_b': nrm(ks[9], (DEPTH, D_RNN), 0.01),
        'lru_lambda': lru_lambda,
        'w_branch_rnn': nrm(ks[11], (DEPTH, D_RNN, D_MODEL), D_RNN ** -0.5),
        'w_branch_ret': nrm(ks[12], (DEPTH, RET_HEADS * RET_DV, D_MODEL), (RET_HEADS * RET_DV) ** -0.5),
        'w_out': nrm(ks[13], (DEPTH, D_MODEL, D_MODEL), D_MODEL ** -0.5),
        'norm_ffn_g': 1.0 + nrm(ks[14], (DEPTH, D_MODEL), 0.01),
        'w_up': nrm(ks[15], (DEPTH, D_MODEL, 2 * D_FF), D_MODEL ** -0.5),
        'ffn_conv_w': nrm(ks[16], (DEPTH, FFN_CONV, 2 * D_FF), FFN_CONV ** -0.5),
        'ffn_conv_b': nrm(ks[17], (DEPTH, 2 * D_FF), 0.01),
        'w_down': nrm(ks[18], (DEPTH, D_FF, D_MODEL), D_FF ** -0.5),
        'norm_final_g': 1.0 + nrm(ks[19], (D_MODEL,), 0.01),
    }


def reference(x, meta_tokens, norm_mix_g, w_in, rnn_conv_w, rnn_conv_b, rg_a_w, rg_a_b,
              rg_x_w, rg_x_b, lru_lambda, w_branch_rnn, w_branch_ret, w_out, norm_ffn_g,
              w_up, ffn_conv_w, ffn_conv_b, w_down, norm_final_g):
    bsz = x.shape[0]
    meta = jnp.broadcast_to(meta_tokens.astype(x.dtype)[None], (bsz, N_META, D_MODEL))
    h = jnp.concatenate([meta, x], axis=1)
    t = h.shape[1]
    pos = jnp.arange(t, dtype=jnp.int32)

    for l in range(DEPTH):
        xn = rms_norm(h, norm_mix_g[l])
        z = xn @ w_in[l]
        u_rnn, u_gate, q, k, v, g_ret, ga, gb = jnp.split(z, IN_SPLITS, axis=-1)

        xr = causal_dwconv(u_rnn, rnn_conv_w[l], rnn_conv_b[l])
        hr = rg_lru(xr, rg_a_w[l], rg_a_b[l], rg_x_w[l], rg_x_b[l], lru_lambda[l])
        o_a = hr * jax.nn.gelu(u_gate)

        q = rotary(q.reshape(bsz, t, RET_HEADS, RET_DK), pos)
        k = rotary(k.reshape(bsz, t, RET_HEADS, RET_DK), pos) * (RET_DK ** -0.5)
        o = retention(q, k, v.reshape(bsz, t, RET_HEADS, RET_DV))
        o_b = head_rms(o).reshape(bsz, t, RET_HEADS * RET_DV) * jax.nn.silu(g_ret)

        mixed = (jax.nn.sigmoid(ga) * (o_a @ w_branch_rnn[l])
                 + jax.nn.sigmoid(gb) * (o_b @ w_branch_ret[l]))
        h = h + mixed @ w_out[l]

        xn = rms_norm(h, norm_ffn_g[l])
        up = causal_dwconv(xn @ w_up[l], ffn_conv_w[l], ffn_conv_b[l])
        gate, val = jnp.split(up, 2, axis=-1)
        h = h + (jax.nn.gelu(gate) * val) @ w_down[l]

    out = rms_norm(h, norm_final_g)[:, N_META:]
    return out
```

```python
from contextlib import ExitStack
import numpy as np
import concourse.bass as bass
import concourse.mybir as mybir
from concourse.bass_utils import run_bass_kernel_spmd

F32 = mybir.dt.float32
BF16 = mybir.dt.bfloat16
AF = mybir.ActivationFunctionType
ALU = mybir.AluOpType
AX = mybir.AxisListType

D = 1024
KD = 8
SEQ = 8192
NMETA = 16
DRNN = 1280
NB = 10
H = 4
DK = 256
DV = 512
DFF = 3072
DIN = 10752
EPS = 1e-6
NPRE = 6144
NPG = NPRE // 512
HALO = 16
NOWN = 2048
NM = HALO + NOWN
L = NPRE + NM
NT_MAIN = 17
O_U, O_UG, O_Q, O_K, O_V, O_G, O_GA, O_GB = 0, 1280, 2560, 3584, 4608, 6656, 8704, 9728

MG = [(0, 16)] + [(16 + 512 * i, 512) for i in range(4)]


def tile_rows(t):
    if t == 0:
        return 0, 16
    return 16 + 128 * (t - 1), 128


class Sched:
    ENG = ("pe", "act", "dve", "pool", "sp")

    def __init__(self, nc, es):
        self.nc = nc
        self.q = {e: [] for e in self.ENG}
        self.sems = {}
        self.cnt = {}
        for e in self.ENG:
            self.sems[e] = es.enter_context(nc.semaphore("s_" + e))
            self.cnt[e] = 0
        self.ndma = 12
        self.dma_rr = 0
        for i in range(self.ndma):
            k = "d%d" % i
            self.sems[k] = es.enter_context(nc.semaphore("s_" + k))
            self.cnt[k] = 0
        self.waited = {e: {} for e in self.ENG}
        self.res = {}
        self.nops = 0

    def op(self, eng, fn, reads=(), writes=(), dma=False):
        deps = {}

        def add(ev):
            if ev is None:
                return
            s, v = ev
            if deps.get(s, 0) < v:
                deps[s] = v

        for r in reads:
            st = self.res.get(r)
            if st:
                add(st["w"])
        for w in writes:
            st = self.res.get(w)
            if st:
                add(st["w"])
                for s, v in st["r"].items():
                    add((s, v))
        if dma:
            k = "d%d" % self.dma_rr
            self.dma_rr = (self.dma_rr + 1) % self.ndma
            if self.cnt[k] > 0:
                add((k, 16 * self.cnt[k]))
            self.cnt[k] += 1
            ev = (k, 16 * self.cnt[k])
            inc = 16
        else:
            self.cnt[eng] += 1
            ev = (eng, self.cnt[eng])
            inc = 1
        waits = []
        for s, v in deps.items():
            if s == eng and eng == "pe":
                continue
            if self.waited[eng].get(s, 0) >= v:
                continue
            self.waited[eng][s] = v
            waits.append((s, v))
        sem = self.sems[ev[0]]
        sems = self.sems

        def emit(e, fn=fn, waits=waits, sem=sem, inc=inc):
            for s, v in waits:
                e.wait_ge(sems[s], v)
            fn(e).then_inc(sem, inc)

        self.q[eng].append(emit)
        for r in reads:
            st = self.res.setdefault(r, {"w": None, "r": {}})
            if st["r"].get(ev[0], 0) < ev[1]:
                st["r"][ev[0]] = ev[1]
        for w in writes:
            self.res[w] = {"w": ev, "r": {}}
        self.nops += 1

    def finish(self, eng="sp"):
        waits = [(k, 16 * self.cnt[k]) for k in self.sems if k[1:].isdigit() and self.cnt[k] > 0]
        sems = self.sems

        def emit(e):
            for s, v in waits:
                e.wait_ge(sems[s], v)

        self.q[eng].append(emit)


class Stage:
    _n = [0]

    def __init__(self, nc, S):
        self.nc, self.S = nc, S
        self.es = ExitStack()
        Stage._n[0] += 1
        self.pfx = "g%d_" % Stage._n[0]

    def __enter__(self):
        self.es.__enter__()
        return self

    def sb(self, name, shape, dt=F32):
        return self.es.enter_context(self.nc.sbuf_tensor(self.pfx + name, list(shape), dt))

    def ps(self, name, shape, dt=F32):
        return self.es.enter_context(self.nc.psum_tensor(self.pfx + name, list(shape), dt))

    def __exit__(self, *a):
        S = self.S
        S.barrier()
        q = S.q
        with self.nc.Block() as block:
            @block.tensor
            def _(e):
                for f in q["pe"]:
                    f(e)

            @block.scalar
            def _(e):
                for f in q["act"]:
                    f(e)

            @block.vector
            def _(e):
                for f in q["dve"]:
                    f(e)

            @block.gpsimd
            def _(e):
                for f in q["pool"]:
                    f(e)

            @block.sync
            def _(e):
                for f in q["sp"]:
                    f(e)
        S.q = {e: [] for e in S.ENG}
        S.res = {}
        return self.es.__exit__(*a)


def _barrier(self):
    targets = []
    for k in self.sems:
        v = self.cnt[k] * (16 if k[1:].isdigit() else 1)
        if v > 0:
            targets.append((k, v))
    sems = self.sems
    for eng in self.ENG:
        waits = []
        for s, v in targets:
            if self.waited[eng].get(s, 0) >= v:
                continue
            self.waited[eng][s] = v
            waits.append((s, v))

        def emit(e, waits=waits):
            for s, v in waits:
                e.wait_ge(sems[s], v)

        self.q[eng].append(emit)


Sched.barrier = _barrier


def build(stop=99, dbg=False):
    nc = bass.Bass("TRN2", target_bir_lowering=False)
    skind = "ExternalOutput" if dbg else "Internal"

    def din(name, shape, dt=F32):
        return nc.dram_tensor(name, list(shape), dt, kind="ExternalInput").ap()

    xs = din("xs", [L, D])
    w_in = din("w_in", [D, DIN])
    w_brnn = din("w_brnn", [DRNN, D])
    w_bret = din("w_bret", [H * DV, D])
    w_out = din("w_out", [D, D])
    w_up = din("w_up", [D, 2 * DFF])
    w_down = din("w_down", [DFF, D])
    rgw = din("rgw", [128, 2, NB, 128])
    pvec = din("pvec", [128, 80])
    fvec = din("fvec", [128, 4, 48])
    g1b = din("g1b", [128, KD, 128])
    g2b = din("g2b", [128, KD, 128])
    gfb = din("gfb", [128, D])
    cosT = din("cosT", [L, 128])
    sinT = din("sinT", [L, 128])
    decpre = din("decpre", [128, NPRE // 128, H])
    ctab = din("ctab", [128, 16])
    maskT = din("maskT", [128, H, 128])
    mgrp = din("mgrp", [128, NPG])
    iden_in = din("iden", [128, 128])
    out = nc.dram_tensor("out", [NOWN, D], F32, kind="ExternalOutput").ap()
    XNT_d = nc.dram_tensor("xnt_d", [128, KD, L], BF16, kind=skind).ap()
    OA_d = nc.dram_tensor("oa_d", [128, NB, NM], BF16, kind=skind).ap()
    OB_d = nc.dram_tensor("ob_d", [128, 16, NM], BF16, kind=skind).ap()
    MIX_d = nc.dram_tensor("mix_d", [128, KD, NM], BF16, kind=skind).ap()
    HM_d = nc.dram_tensor("hm_d", [NT_MAIN * 128, D], F32, kind=skind).ap()
    XN2_d = nc.dram_tensor("xn2_d", [128, KD, NM], BF16, kind=skind).ap()
    ACT_d = nc.dram_tensor("act_d", [128, 24, NOWN], BF16, kind=skind).ap()
    SF_d = nc.dram_tensor("sf_d", [128, 2 * H, DV], F32, kind=skind).ap()

    ges = ExitStack()
    with ges:
        S = Sched(nc, ges)

        def dma(eng, out_ap, in_ap, reads=(), writes=()):
            S.op(eng, lambda e: e.dma_start(out=out_ap, in_=in_ap), reads, writes, dma=True)

        def act(out_ap, in_ap, func, reads, writes, bias=None, scale=None):
            kw = {}
            if bias is not None:
                kw["bias"] = bias
            if scale is not None:
                kw["scale"] = scale
            S.op("act", lambda e: e.activation(out=out_ap, in_=in_ap, func=func, **kw), reads, writes)

        def tt(out_ap, a, b, op, reads, writes, eng="dve"):
            S.op(eng, lambda e: e.tensor_tensor(out=out_ap, in0=a, in1=b, op=op), reads, writes)

        def ts(out_ap, a, s1, s2, op0, op1, reads, writes, eng="dve"):
            if op1 is None:
                S.op(eng, lambda e: e.tensor_scalar(out=out_ap, in0=a, scalar1=s1, scalar2=None, op0=op0),
                     reads, writes)
            else:
                S.op(eng, lambda e: e.tensor_scalar(out=out_ap, in0=a, scalar1=s1, scalar2=s2, op0=op0, op1=op1),
                     reads, writes)

        def stt(out_ap, a, s, b, op0, op1, reads, writes):
            S.op("dve", lambda e: e.scalar_tensor_tensor(out=out_ap, in0=a, scalar=s, in1=b, op0=op0, op1=op1),
                 reads, writes)

        def cp(eng, out_ap, in_ap, reads, writes):
            if eng == "act":
                S.op(eng, lambda e: e.activation(out=out_ap, in_=in_ap, func=AF.Copy), reads, writes)
            else:
                S.op(eng, lambda e: e.tensor_copy(out=out_ap, in_=in_ap), reads, writes)

        def mm(out_ap, lhsT, rhs, start, stop, reads, writes):
            S.op("pe", lambda e: e.matmul(out_ap, lhsT, rhs, start=start, stop=stop), reads, writes)

        STG = [ges.enter_context(nc.sbuf_tensor("STG%d" % i, [128, 640], F32)) for i in range(4)]
        stg_i = [0]

        def wcast(dst_view, src_ap, key):
            kc = dst_view.shape[1]
            ncols = dst_view.shape[2]
            for k in range(kc):
                i = stg_i[0] % 4
                stg_i[0] += 1
                dma("sp", STG[i][:, 0:ncols], src_ap[k * 128:(k + 1) * 128, :], writes=("STG%d" % i,))
                cp("pool", dst_view[:, k, :], STG[i][:, 0:ncols], ("STG%d" % i,), (key,))

        def norm_rows(st, x_ap, xk, rows, gb, gbkey, dst_fn, dst_key, T):
            act(T["SQJ"][0:rows, :], x_ap, AF.Square, (xk,), ("SQJ",))
            S.op("dve", lambda e: e.reduce_sum(out=T["ST"][0:rows, 0:1], in_=T["SQJ"][0:rows, :], axis=AX.X),
                 ("SQJ",), ("ST0",))
            ts(T["ST"][0:rows, 1:2], T["ST"][0:rows, 0:1], 1.0 / D, EPS, ALU.mult, ALU.add, ("ST0",), ("ST1",))
            act(T["ST"][0:rows, 2:3], T["ST"][0:rows, 1:2], AF.Sqrt, ("ST1",), ("ST2",))
            S.op("dve", lambda e: e.reciprocal(out=T["ST"][0:rows, 3:4], in_=T["ST"][0:rows, 2:3]),
                 ("ST2",), ("ST3",))
            ts(T["XB"][0:rows, :], x_ap, T["ST"][0:rows, 3:4], None, ALU.mult, None, (xk, "ST3"), ("XB",))
            for kc in range(KD):
                S.op("pe", lambda e, kc=kc: e.transpose(T["PT"][:, kc, 0:rows],
                                                        T["XB"][0:rows, kc * 128:(kc + 1) * 128],
                                                        T["IDB"][0:rows, 0:rows]),
                     ("XB", "IDB"), ("PT",))
            for kc in range(KD):
                tt(dst_fn(kc), T["PT"][:, kc, 0:rows], gb[:, kc, 0:rows], ALU.mult, ("PT", gbkey), (dst_key,))

        def load_iden(st, T):
            T["IDF"] = st.sb("IDF", [128, 128])
            T["IDB"] = st.sb("IDB", [128, 128], BF16)
            dma("sp", T["IDF"][:], iden_in[:, :], writes=("IDF",))
            cp("dve", T["IDB"][:], T["IDF"][:], ("IDF",), ("IDB",))

        stream_tiles = [(i * 128, 128) for i in range(NPRE // 128)] + [(NPRE, 16)] + \
                       [(NPRE + 16 + i * 128, 128) for i in range(16)]
        main_tiles = stream_tiles[NPRE // 128:]
        pre_groups = [(i * 512, 512) for i in range(NPG)]
        main_groups = [(NPRE, 16)] + [(NPRE + 16 + i * 512, 512) for i in range(4)]

        with Stage(nc, S) as st:
            T = {}
            load_iden(st, T)
            G1B = st.sb("G1B", [128, KD, 128])
            dma("sp", G1B[:], g1b[:, :, :], writes=("G1B",))
            XS = [st.sb("XS%d" % i, [128, D]) for i in range(3)]
            T["SQJ"] = st.sb("SQJ", [128, D])
            T["ST"] = st.sb("ST", [128, 8])
            T["XB"] = st.sb("XB", [128, D], BF16)
            T["PT"] = st.ps("PT", [128, KD, 128], BF16)
            XO = [st.sb("XO%d" % i, [128, KD, 128], BF16) for i in range(2)]
            for ti, (r0, rows) in enumerate(stream_tiles):
                i = ti % 3
                o = ti % 2
                dma("sp", XS[i][0:rows, :], xs[r0:r0 + rows, :], writes=("XS%d" % i,))
                norm_rows(st, XS[i][0:rows, :], "XS%d" % i, rows, G1B, "G1B",
                          lambda kc, o=o, rows=rows: XO[o][:, kc, 0:rows], "XO%d" % o, T)
                dma("sp", XNT_d[:, :, r0:r0 + rows], XO[o][:, :, 0:rows], reads=("XO%d" % o,))

        if stop <= 1:
            return nc
        with Stage(nc, S) as st:
            WU = st.sb("WU", [128, KD, DRNN], BF16)
            WG = st.sb("WG", [128, KD, DRNN], BF16)
            RGW = st.sb("RGW", [128, 2, NB, 128], BF16)
            PV = st.sb("PV", [128, 80])
            MGR = st.sb("MGR", [128, NPG])
            SC = st.sb("SC", [128, 3, NB])
            for c0 in range(0, DRNN, 640):
                wcast(WU[:, :, c0:c0 + 640], w_in[:, O_U + c0:O_U + c0 + 640], "WU")
                wcast(WG[:, :, c0:c0 + 640], w_in[:, O_UG + c0:O_UG + c0 + 640], "WG")
            for ax in range(2):
                for n0 in range(0, NB, 5):
                    i = stg_i[0] % 4
                    stg_i[0] += 1
                    dma("sp", STG[i][:, 0:640].rearrange("p (n j) -> p n j", n=5), rgw[:, ax, n0:n0 + 5, :],
                        writes=("STG%d" % i,))
                    cp("pool", RGW[:, ax, n0:n0 + 5, :], STG[i][:, 0:640].rearrange("p (n j) -> p n j", n=5),
                       ("STG%d" % i,), ("RGW",))
            dma("sp", PV[:], pvec[:, :], writes=("PV",))
            dma("sp", MGR[:], mgrp[:, :], writes=("MGR",))
            act(SC[:, 0, :], PV[:, 70:80], AF.Exp, ("PV",), ("SC0",), scale=-1.0)
            act(SC[:, 0, :], SC[:, 0, :], AF.Ln, ("SC0",), ("SC0",), bias=1.0)
            ts(SC[:, 1, :], SC[:, 0, :], -8.0, None, ALU.mult, None, ("SC0",), ("SC",))
            ts(SC[:, 2, :], SC[:, 0, :], -16.0, None, ALU.mult, None, ("SC0",), ("SC",))
            XG = [st.sb("XG%d" % i, [128, KD, 512], BF16) for i in range(2)]
            U = st.sb("U", [128, NB, 516])
            HST = st.sb("HST", [128, NB])
            S.op("dve", lambda e: e.memset(U[:], 0.0), (), tuple("U%d" % n for n in range(NB)))
            S.op("dve", lambda e: e.memset(HST[:], 0.0), (), ("HST",))
            names = ["ACC", "RR", "II", "AA", "A2", "T1", "BB", "HH", "GG"]
            B = {k: st.sb(k, [128, 512]) for k in names}
            XR = st.sb("XR", [128, 512], BF16)
            OAT = [st.sb("OAT%d" % i, [128, NB, 512], BF16) for i in range(2)]
            PB = [st.ps("PB%d" % i, [128, 512]) for i in range(4)]
            allg = [(g, True) for g in pre_groups] + [(g, False) for g in main_groups]
            for gi, ((c0, ncols), is_pre) in enumerate(allg):
                xi = gi % 2
                xk = "XG%d" % xi
                dma("sp", XG[xi][:, :, 0:ncols], XNT_d[:, :, c0:c0 + ncols], writes=(xk,))
                oi = gi % 2
                for n in range(NB):
                    uk = "U%d" % n
                    u_ps = PB[0][:, 0:ncols]
                    for kc in range(KD):
                        mm(u_ps, WU[:, kc, n * 128:(n + 1) * 128], XG[xi][:, kc, 0:ncols], kc == 0, kc == KD - 1,
                           ("WU", xk), ("PB0",))
                    cp("act", U[:, n, 3:3 + ncols], u_ps, ("PB0",), (uk,))
                    ts(B["ACC"][:, 0:ncols], U[:, n, 0:ncols], PV[:, n:n + 1], PV[:, 40 + n:41 + n],
                       ALU.mult, ALU.add, (uk, "PV"), ("ACC",))
                    stt(B["ACC"][:, 0:ncols], U[:, n, 1:1 + ncols], PV[:, 10 + n:11 + n], B["ACC"][:, 0:ncols],
                        ALU.mult, ALU.add, (uk, "PV", "ACC"), ("ACC",))
                    stt(B["ACC"][:, 0:ncols], U[:, n, 2:2 + ncols], PV[:, 20 + n:21 + n], B["ACC"][:, 0:ncols],
                        ALU.mult, ALU.add, (uk, "PV", "ACC"), ("ACC",))
                    stt(XR[:, 0:ncols], U[:, n, 3:3 + ncols], PV[:, 30 + n:31 + n], B["ACC"][:, 0:ncols],
                        ALU.mult, ALU.add, (uk, "PV", "ACC"), ("XR",))
                    cp("act", B["T1"][:, 0:3], U[:, n, ncols:ncols + 3], (uk,), ("T1",))
                    cp("act", U[:, n, 0:3], B["T1"][:, 0:3], ("T1",), (uk,))
                    mm(PB[1][:, 0:ncols], RGW[:, 0, n, :], XR[:, 0:ncols], True, True, ("RGW", "XR"), ("PB1",))
                    mm(PB[2][:, 0:ncols], RGW[:, 1, n, :], XR[:, 0:ncols], True, True, ("RGW", "XR"), ("PB2",))
                    act(B["RR"][:, 0:ncols], PB[1][:, 0:ncols], AF.Sigmoid, ("PB1", "PV"), ("RR",),
                        bias=PV[:, 50 + n:51 + n])
                    act(B["II"][:, 0:ncols], PB[2][:, 0:ncols], AF.Sigmoid, ("PB2", "PV"), ("II",),
                        bias=PV[:, 60 + n:61 + n])
                    act(B["AA"][:, 0:ncols], B["RR"][:, 0:ncols], AF.Exp, ("RR", "SC"), ("AA",),
                        scale=SC[:, 1, n:n + 1])
                    act(B["A2"][:, 0:ncols], B["RR"][:, 0:ncols], AF.Exp, ("RR", "SC"), ("A2",),
                        scale=SC[:, 2, n:n + 1])
                    act(B["A2"][:, 0:ncols], B["A2"][:, 0:ncols], AF.Sqrt, ("A2",), ("A2",), scale=-1.0, bias=1.0)
                    tt(B["T1"][:, 0:ncols], B["II"][:, 0:ncols], XR[:, 0:ncols], ALU.mult, ("II", "XR"), ("T1",))
                    if is_pre:
                        g = c0 // 512
                        stt(B["BB"][:, 0:ncols], B["T1"][:, 0:ncols], MGR[:, g:g + 1], B["A2"][:, 0:ncols],
                            ALU.mult, ALU.mult, ("T1", "A2", "MGR"), ("BB",))
                    else:
                        tt(B["BB"][:, 0:ncols], B["T1"][:, 0:ncols], B["A2"][:, 0:ncols], ALU.mult,
                           ("T1", "A2"), ("BB",))
                    S.op("dve", lambda e, n=n, ncols=ncols: e.tensor_tensor_scan(
                        out=B["HH"][:, 0:ncols], data0=B["AA"][:, 0:ncols], data1=B["BB"][:, 0:ncols],
                        initial=HST[:, n:n + 1], op0=ALU.mult, op1=ALU.add), ("AA", "BB", "HST"), ("HH",))
                    cp("act", HST[:, n:n + 1], B["HH"][:, ncols - 1:ncols], ("HH",), ("HST",))
                    if not is_pre:
                        g_ps = PB[3][:, 0:ncols]
                        for kc in range(KD):
                            mm(g_ps, WG[:, kc, n * 128:(n + 1) * 128], XG[xi][:, kc, 0:ncols], kc == 0,
                               kc == KD - 1, ("WG", xk), ("PB3",))
                        act(B["GG"][:, 0:ncols], g_ps, AF.Gelu_apprx_tanh, ("PB3",), ("GG",))
                        tt(OAT[oi][:, n, 0:ncols], B["HH"][:, 0:ncols], B["GG"][:, 0:ncols], ALU.mult,
                           ("HH", "GG"), ("OAT%d" % oi,))
                if not is_pre:
                    m0 = c0 - NPRE
                    dma("sp", OA_d[:, :, m0:m0 + ncols], OAT[oi][:, :, 0:ncols], reads=("OAT%d" % oi,))

        def rotary(T, src, skey, dst, dkey, rows, nh):
            c = T["COS"][0:rows, :]
            s = T["SIN"][0:rows, :]
            for h in range(nh):
                x1 = src[0:rows, h * DK:h * DK + 128]
                x2 = src[0:rows, h * DK + 128:(h + 1) * DK]
                R = T["RT"]
                tt(R[0][0:rows, :], x1, c, ALU.mult, (skey, "COS"), ("RT0",))
                tt(R[1][0:rows, :], x2, s, ALU.mult, (skey, "SIN"), ("RT1",))
                tt(dst[0:rows, h * DK:h * DK + 128], R[0][0:rows, :], R[1][0:rows, :], ALU.subtract,
                   ("RT0", "RT1"), (dkey,))
                tt(R[2][0:rows, :], x2, c, ALU.mult, (skey, "COS"), ("RT2",))
                tt(R[3][0:rows, :], x1, s, ALU.mult, (skey, "SIN"), ("RT3",))
                tt(dst[0:rows, h * DK + 128:(h + 1) * DK], R[2][0:rows, :], R[3][0:rows, :], ALU.add,
                   ("RT2", "RT3"), (dkey,))

        if stop <= 2:
            return nc
        with Stage(nc, S) as st:
            T = {}
            WK = st.sb("WK", [128, KD, H * DK], BF16)
            WV = st.sb("WV", [128, KD, H * DV], BF16)
            for c0 in range(0, H * DK, 512):
                wcast(WK[:, :, c0:c0 + 512], w_in[:, O_K + c0:O_K + c0 + 512], "WK")
            for c0 in range(0, H * DV, 512):
                wcast(WV[:, :, c0:c0 + 512], w_in[:, O_V + c0:O_V + c0 + 512], "WV")
            DPRE = st.sb("DPRE", [128, NPRE // 128, H])
            dma("sp", DPRE[:], decpre[:, :, :], writes=("DPRE",))
            XG = [st.sb("XG%d" % i, [128, KD, 512], BF16) for i in range(2)]
            T["COS"] = st.sb("COS", [128, 128])
            T["SIN"] = st.sb("SIN", [128, 128])
            T["RT"] = [st.sb("RT%d" % i, [128, 128]) for i in range(4)]
            KF = st.sb("KF", [128, H * DK])
            KR = st.sb("KR", [128, 4, H * DK], BF16)
            VV = st.sb("VV", [128, 4, H * DV], BF16)
            SF = st.sb("SF", [128, 2 * H, DV])
            S.op("dve", lambda e: e.memset(SF[:], 0.0), (), ("SF",))
            PK = [st.ps("PK%d" % i, [128, 512]) for i in range(2)]
            PVv = [st.ps("PV%d" % i, [128, 512]) for i in range(4)]
            PS_ = [st.ps("PS%d" % i, [128, 512]) for i in range(2)]
            for gi, (c0, ncols) in enumerate(pre_groups):
                xi = gi % 2
                xk = "XG%d" % xi
                dma("sp", XG[xi][:, :, :], XNT_d[:, :, c0:c0 + 512], writes=(xk,))
                for t4 in range(4):
                    tile = c0 // 128 + t4
                    r0 = c0 + t4 * 128
                    dma("sp", T["COS"][:], cosT[r0:r0 + 128, :], writes=("COS",))
                    dma("sp", T["SIN"][:], sinT[r0:r0 + 128, :], writes=("SIN",))
                    xt = lambda kc: XG[xi][:, kc, t4 * 128:(t4 + 1) * 128]
                    for hf in range(2):
                        for kc in range(KD):
                            mm(PK[hf][:, :], xt(kc), WK[:, kc, hf * 512:(hf + 1) * 512], kc == 0, kc == KD - 1,
                               (xk, "WK"), ("PK%d" % hf,))
                    for q4 in range(4):
                        for kc in range(KD):
                            mm(PVv[q4][:, :], xt(kc), WV[:, kc, q4 * 512:(q4 + 1) * 512], kc == 0, kc == KD - 1,
                               (xk, "WV"), ("PV%d" % q4,))
                    for h in range(H):
                        hf, o = divmod(h, 2)
                        act(KF[:, h * DK:(h + 1) * DK], PK[hf][:, o * DK:(o + 1) * DK], AF.Identity,
                            ("PK%d" % hf, "DPRE"), ("KF",), scale=DPRE[:, tile, h:h + 1])
                    rotary(T, KF, "KF", KR[:, t4, :], "KR", 128, H)
                    for q4 in range(4):
                        cp("act", VV[:, t4, q4 * 512:(q4 + 1) * 512], PVv[q4][:, :], ("PV%d" % q4,), ("VV",))
                for h in range(H):
                    for dc in range(2):
                        j = (h * 2 + dc) % 2
                        for t4 in range(4):
                            mm(PS_[j][:, :], KR[:, t4, h * DK + dc * 128:h * DK + (dc + 1) * 128],
                               VV[:, t4, h * DV:(h + 1) * DV], t4 == 0, t4 == 3, ("KR", "VV"), ("PS%d" % j,))
                        tt(SF[:, h * 2 + dc, :], PS_[j][:, :], SF[:, h * 2 + dc, :], ALU.add,
                           ("PS%d" % j, "SF"), ("SF",))
            dma("sp", SF_d[:, :, :], SF[:], reads=("SF",))

        if stop <= 3:
            return nc
        GAM = [1.0 - 2.0 ** (-5.0 - h) for h in range(H)]
        with Stage(nc, S) as st:
            T = {}
            load_iden(st, T)
            XM = st.sb("XM", [128, KD, NM], BF16)
            for kc in range(KD):
                dma("sp", XM[:, kc, :], XNT_d[:, kc, NPRE:NPRE + NM], writes=("XM",))
            CT = st.sb("CT", [128, 16])
            MT = st.sb("MT", [128, H, 128])
            dma("sp", CT[:], ctab[:, :], writes=("CT",))
            dma("sp", MT[:], maskT[:, :, :], writes=("MT",))
            WH = [st.sb("WH%d" % i, [128, KD, 1536], BF16) for i in range(2)]
            T["COS"] = st.sb("COS", [128, 128])
            T["SIN"] = st.sb("SIN", [128, 128])
            T["RT"] = [st.sb("RT%d" % i, [128, 128]) for i in range(4)]
            QKF = st.sb("QKF", [128, 2 * DK])
            QKR = st.sb("QKR", [128, 2 * DK], BF16)
            KDc = st.sb("KDc", [128, DK], BF16)
            VV = st.sb("VV", [128, DV], BF16)
            SG = st.sb("SG", [128, DV])
            QKT = st.sb("QKT", [128, 4, 128], BF16)
            SCT = st.sb("SCT", [128, 128], BF16)
            SF = st.sb("SF", [128, 2 * H, DV])
            SBF = st.sb("SBF", [128, 2, DV], BF16)
            OBT = st.sb("OBT", [128, DV], BF16)
            OBO = [st.sb("OBO%d" % i, [128, 4, 128], BF16) for i in range(2)]
            SQJ = st.sb("SQJ", [128, DV])
            STT_ = st.sb("STs", [128, 8])
            dma("sp", SF[:], SF_d[:, :, :], writes=("SF",))
            PQ = st.ps("PQ", [128, 512])
            PV_ = st.ps("PVm", [128, 512])
            PG = st.ps("PG", [128, 512])
            PSC = st.ps("PSC", [128, 512])
            PO = st.ps("PO", [128, 512])
            PSt = st.ps("PSt", [128, 512])
            PT = st.ps("PT", [128, KD, 128], BF16)
            for h in range(H):
                wi = h % 2
                wk = "WH%d" % wi
                W = WH[wi]
                wcast(W[:, :, 0:256], w_in[:, O_Q + h * DK:O_Q + (h + 1) * DK], wk)
                wcast(W[:, :, 256:512], w_in[:, O_K + h * DK:O_K + (h + 1) * DK], wk)
                wcast(W[:, :, 512:1024], w_in[:, O_V + h * DV:O_V + (h + 1) * DV], wk)
                wcast(W[:, :, 1024:1536], w_in[:, O_G + h * DV:O_G + (h + 1) * DV], wk)
                for dc in range(2):
                    cp("act", SBF[:, dc, :], SF[:, h * 2 + dc, :], ("SF",), ("SBF",))
                for ti, (r0, rows) in enumerate(main_tiles):
                    m0 = r0 - NPRE
                    dma("sp", T["COS"][0:rows, :], cosT[r0:r0 + rows, :], writes=("COS",))
                    dma("sp", T["SIN"][0:rows, :], sinT[r0:r0 + rows, :], writes=("SIN",))
                    xt = lambda kc: XM[:, kc, m0:m0 + rows]
                    for kc in range(KD):
                        mm(PQ[0:rows, :], xt(kc), W[:, kc, 0:512], kc == 0, kc == KD - 1, ("XM", wk), ("PQ",))
                    for kc in range(KD):
                        mm(PV_[0:rows, :], xt(kc), W[:, kc, 512:1024], kc == 0, kc == KD - 1, ("XM", wk), ("PVm",))
                    for kc in range(KD):
                        mm(PG[0:rows, :], xt(kc), W[:, kc, 1024:1536], kc == 0, kc == KD - 1, ("XM", wk), ("PG",))
                    act(QKF[0:rows, 0:DK], PQ[0:rows, 0:DK], AF.Identity, ("PQ", "CT"), ("QKF",),
                        scale=CT[0:rows, h:h + 1])
                    cp("act", QKF[0:rows, DK:2 * DK], PQ[0:rows, DK:2 * DK], ("PQ",), ("QKF",))
                    rotary(T, QKF, "QKF", QKR, "QKR", rows, 2)
                    kcol = (4 + h) if rows == 128 else (8 + h)
                    ts(KDc[0:rows, :], QKR[0:rows, DK:2 * DK], CT[0:rows, kcol:kcol + 1], None, ALU.mult, None,
                       ("QKR", "CT"), ("KDc",))
                    cp("act", VV[0:rows, :], PV_[0:rows, :], ("PVm",), ("VV",))
                    act(SG[0:rows, :], PG[0:rows, :], AF.Silu, ("PG",), ("SG",))
                    for j in range(4):
                        S.op("pe", lambda e, j=j, rows=rows: e.transpose(
                            PT[:, j, 0:rows], QKR[0:rows, j * 128:(j + 1) * 128], T["IDB"][0:rows, 0:rows]),
                            ("QKR", "IDB"), ("PT",))
                    cp("dve", QKT[:, :, 0:rows], PT[:, 0:4, 0:rows], ("PT",), ("QKT",))
                    for dc in range(2):
                        mm(PSC[0:rows, 0:rows], QKT[:, 2 + dc, 0:rows], QKT[:, dc, 0:rows], dc == 0, dc == 1,
                           ("QKT",), ("PSC",))
                    tt(SCT[0:rows, 0:rows], PSC[0:rows, 0:rows], MT[0:rows, h, 0:rows], ALU.mult,
                       ("PSC", "MT"), ("SCT",))
                    mm(PO[0:rows, :], SCT[0:rows, 0:rows], VV[0:rows, :], True, False, ("SCT", "VV"), ("PO",))
                    for dc in range(2):
                        mm(PO[0:rows, :], QKT[:, dc, 0:rows], SBF[:, dc, :], False, dc == 1,
                           ("QKT", "SBF"), ("PO",))
                    act(SQJ[0:rows, :], PO[0:rows, :], AF.Square, ("PO",), ("SQJ",))
                    S.op("dve", lambda e, rows=rows: e.reduce_sum(out=STT_[0:rows, 0:1], in_=SQJ[0:rows, :],
                                                                  axis=AX.X), ("SQJ",), ("ST0",))
                    ts(STT_[0:rows, 1:2], STT_[0:rows, 0:1], 1.0 / DV, EPS, ALU.mult, ALU.add, ("ST0",), ("ST1",))
                    act(STT_[0:rows, 2:3], STT_[0:rows, 1:2], AF.Sqrt, ("ST1",), ("ST2",))
                    S.op("dve", lambda e, rows=rows: e.reciprocal(out=STT_[0:rows, 3:4], in_=STT_[0:rows, 2:3]),
                         ("ST2",), ("ST3",))
                    stt(OBT[0:rows, :], PO[0:rows, :], STT_[0:rows, 3:4], SG[0:rows, :], ALU.mult, ALU.mult,
                        ("PO", "ST3", "SG"), ("OBT",))
                    for ec in range(4):
                        S.op("pe", lambda e, ec=ec, rows=rows: e.transpose(
                            PT[:, 4 + ec, 0:rows], OBT[0:rows, ec * 128:(ec + 1) * 128], T["IDB"][0:rows, 0:rows]),
                            ("OBT", "IDB"), ("PT",))
                    oi = ti % 2
                    cp("act", OBO[oi][:, :, 0:rows], PT[:, 4:8, 0:rows], ("PT",), ("OBO%d" % oi,))
                    dma("sp", OB_d[:, h * 4:(h + 1) * 4, m0:m0 + rows], OBO[oi][:, :, 0:rows],
                        reads=("OBO%d" % oi,))
                    gC = GAM[h] ** rows
                    for dc in range(2):
                        mm(PSt[:, :], KDc[0:rows, dc * 128:(dc + 1) * 128], VV[0:rows, :], True, True,
                           ("KDc", "VV"), ("PSt",))
                        stt(SF[:, h * 2 + dc, :], SF[:, h * 2 + dc, :], gC, PSt[:, :], ALU.mult, ALU.add,
                            ("SF", "PSt"), ("SF",))
                        cp("act", SBF[:, dc, :], SF[:, h * 2 + dc, :], ("SF",), ("SBF",))

        if stop <= 4:
            return nc
        with Stage(nc, S) as st:
            XM = st.sb("XM", [128, KD, NM], BF16)
            OA = st.sb("OA", [128, NB, NM], BF16)
            OB = st.sb("OB", [128, 16, NM], BF16)
            for kc in range(KD):
                dma("sp", XM[:, kc, :], XNT_d[:, kc, NPRE:NPRE + NM], writes=("XM",))
            for kc in range(NB):
                dma("sp", OA[:, kc, :], OA_d[:, kc, :], writes=("OA",))
            for kc in range(16):
                dma("sp", OB[:, kc, :], OB_d[:, kc, :], writes=("OB",))
            WA = [st.sb("WA%d" % i, [128, NB, 128], BF16) for i in range(2)]
            WR = [st.sb("WR%d" % i, [128, 16, 128], BF16) for i in range(2)]
            WGa = [st.sb("WGa%d" % i, [128, KD, 128], BF16) for i in range(2)]
            WGb = [st.sb("WGb%d" % i, [128, KD, 128], BF16) for i in range(2)]
            GA = st.sb("GA", [128, 512])
            GB = st.sb("GB", [128, 512])
            M1 = st.sb("M1", [128, 512])
            M2 = st.sb("M2", [128, 512])
            MO = [st.sb("MO%d" % i, [128, NM], BF16) for i in range(2)]
            P = [st.ps("PE%d" % i, [128, 512]) for i in range(4)]
            for c in range(8):
                wi = c % 2
                cs = slice(c * 128, (c + 1) * 128)
                wcast(WA[wi][:], w_brnn[:, cs], "WA%d" % wi)
                wcast(WR[wi][:], w_bret[:, cs], "WR%d" % wi)
                wcast(WGa[wi][:], w_in[:, O_GA + c * 128:O_GA + (c + 1) * 128], "WGa%d" % wi)
                wcast(WGb[wi][:], w_in[:, O_GB + c * 128:O_GB + (c + 1) * 128], "WGb%d" % wi)
                for (m0, ncols) in MG:
                    for k in range(NB):
                        mm(P[0][:, 0:ncols], WA[wi][:, k, :], OA[:, k, m0:m0 + ncols], k == 0, k == NB - 1,
                           ("WA%d" % wi, "OA"), ("PE0",))
                    for k in range(16):
                        mm(P[1][:, 0:ncols], WR[wi][:, k, :], OB[:, k, m0:m0 + ncols], k == 0, k == 15,
                           ("WR%d" % wi, "OB"), ("PE1",))
                    for k in range(KD):
                        mm(P[2][:, 0:ncols], WGa[wi][:, k, :], XM[:, k, m0:m0 + ncols], k == 0, k == KD - 1,
                           ("WGa%d" % wi, "XM"), ("PE2",))
                    for k in range(KD):
                        mm(P[3][:, 0:ncols], WGb[wi][:, k, :], XM[:, k, m0:m0 + ncols], k == 0, k == KD - 1,
                           ("WGb%d" % wi, "XM"), ("PE3",))
                    act(GA[:, 0:ncols], P[2][:, 0:ncols], AF.Sigmoid, ("PE2",), ("GA",))
                    act(GB[:, 0:ncols], P[3][:, 0:ncols], AF.Sigmoid, ("PE3",), ("GB",))
                    tt(M1[:, 0:ncols], P[0][:, 0:ncols], GA[:, 0:ncols], ALU.mult, ("PE0", "GA"), ("M1",))
                    tt(M2[:, 0:ncols], P[1][:, 0:ncols], GB[:, 0:ncols], ALU.mult, ("PE1", "GB"), ("M2",))
                    tt(MO[wi][:, m0:m0 + ncols], M1[:, 0:ncols], M2[:, 0:ncols], ALU.add, ("M1", "M2"),
                       ("MO%d" % wi,))
                dma("sp", MIX_d[:, c, :], MO[wi][:, :], reads=("MO%d" % wi,))

        if stop <= 5:
            return nc
        with Stage(nc, S) as st:
            T = {}
            load_iden(st, T)
            MX = st.sb("MX", [128, KD, NM], BF16)
            for kc in range(KD):
                dma("sp", MX[:, kc, :], MIX_d[:, kc, :], writes=("MX",))
            WO = st.sb("WO", [128, KD, D], BF16)
            for c0 in range(0, D, 512):
                wcast(WO[:, :, c0:c0 + 512], w_out[:, c0:c0 + 512], "WO")
            G2B = st.sb("G2B", [128, KD, 128])
            dma("sp", G2B[:], g2b[:, :, :], writes=("G2B",))
            XS = [st.sb("XS%d" % i, [128, D]) for i in range(2)]
            HMs = [st.sb("HM%d" % i, [128, D]) for i in range(2)]
            T["SQJ"] = st.sb("SQJ", [128, D])
            T["ST"] = st.sb("ST", [128, 8])
            T["XB"] = st.sb("XB", [128, D], BF16)
            T["PT"] = st.ps("PT", [128, KD, 128], BF16)
            XO = [st.sb("XO%d" % i, [128, KD, 128], BF16) for i in range(2)]
            P = [st.ps("PF%d" % i, [128, 512]) for i in range(2)]
            for ti, (r0, rows) in enumerate(main_tiles):
                m0 = r0 - NPRE
                i = ti % 2
                dma("sp", XS[i][0:rows, :], xs[r0:r0 + rows, :], writes=("XS%d" % i,))
                for hf in range(2):
                    for kc in range(KD):
                        mm(P[hf][0:rows, :], MX[:, kc, m0:m0 + rows], WO[:, kc, hf * 512:(hf + 1) * 512],
                           kc == 0, kc == KD - 1, ("MX", "WO"), ("PF%d" % hf,))
                    tt(HMs[i][0:rows, hf * 512:(hf + 1) * 512], P[hf][0:rows, :],
                       XS[i][0:rows, hf * 512:(hf + 1) * 512], ALU.add, ("PF%d" % hf, "XS%d" % i), ("HM%d" % i,))
                dma("sp", HM_d[ti * 128:ti * 128 + rows, :], HMs[i][0:rows, :], reads=("HM%d" % i,))
                norm_rows(st, HMs[i][0:rows, :], "HM%d" % i, rows, G2B, "G2B",
                          lambda kc, i=i, rows=rows: XO[i][:, kc, 0:rows], "XO%d" % i, T)
                dma("sp", XN2_d[:, :, m0:m0 + rows], XO[i][:, :, 0:rows], reads=("XO%d" % i,))

        if stop <= 6:
            return nc
        with Stage(nc, S) as st:
            X2 = st.sb("X2", [128, KD, NM], BF16)
            for kc in range(KD):
                dma("sp", X2[:, kc, :], XN2_d[:, kc, :], writes=("X2",))
            FV = st.sb("FV", [128, 4, 48])
            dma("sp", FV[:], fvec[:, :, :], writes=("FV",))
            WUg = [st.sb("WUg%d" % i, [128, KD, 128], BF16) for i in range(2)]
            WUv = [st.sb("WUv%d" % i, [128, KD, 128], BF16) for i in range(2)]
            FU = [st.sb("FU%d" % i, [128, 2, NM]) for i in range(2)]
            FA = st.sb("FA", [128, 2, NOWN])
            AO = [st.sb("AO%d" % i, [128, NOWN], BF16) for i in range(2)]
            P = [st.ps("PG%d" % i, [128, 512]) for i in range(4)]
            for c in range(24):
                wi = c % 2
                wcast(WUg[wi][:], w_up[:, c * 128:(c + 1) * 128], "WUg%d" % wi)
                wcast(WUv[wi][:], w_up[:, DFF + c * 128:DFF + (c + 1) * 128], "WUv%d" % wi)
                fk = "FU%d" % wi
                for gi, (m0, ncols) in enumerate(MG):
                    pg, pv = P[(gi % 2) * 2], P[(gi % 2) * 2 + 1]
                    kg, kv = "PG%d" % ((gi % 2) * 2), "PG%d" % ((gi % 2) * 2 + 1)
                    for k in range(KD):
                        mm(pg[:, 0:ncols], WUg[wi][:, k, :], X2[:, k, m0:m0 + ncols], k == 0, k == KD - 1,
                           ("WUg%d" % wi, "X2"), (kg,))
                    for k in range(KD):
                        mm(pv[:, 0:ncols], WUv[wi][:, k, :], X2[:, k, m0:m0 + ncols], k == 0, k == KD - 1,
                           ("WUv%d" % wi, "X2"), (kv,))
                    cp("act", FU[wi][:, 0, m0:m0 + ncols], pg[:, 0:ncols], (kg,), (fk,))
                    cp("act", FU[wi][:, 1, m0:m0 + ncols], pv[:, 0:ncols], (kv,), (fk,))
                for j, cc in ((0, c), (1, 24 + c)):
                    ts(FA[:, j, :], FU[wi][:, j, 14:14 + NOWN], FV[:, 0, cc:cc + 1], FV[:, 3, cc:cc + 1],
                       ALU.mult, ALU.add, (fk, "FV"), ("FA%d" % j,))
                    stt(FA[:, j, :], FU[wi][:, j, 15:15 + NOWN], FV[:, 1, cc:cc + 1], FA[:, j, :],
                        ALU.mult, ALU.add, (fk, "FV", "FA%d" % j), ("FA%d" % j,))
                    stt(FA[:, j, :], FU[wi][:, j, 16:16 + NOWN], FV[:, 2, cc:cc + 1], FA[:, j, :],
                        ALU.mult, ALU.add, (fk, "FV", "FA%d" % j), ("FA%d" % j,))
                act(FA[:, 0, :], FA[:, 0, :], AF.Gelu_apprx_tanh, ("FA0",), ("FA0",))
                tt(AO[wi][:, :], FA[:, 0, :], FA[:, 1, :], ALU.mult, ("FA0", "FA1"), ("AO%d" % wi,))
                dma("sp", ACT_d[:, c, :], AO[wi][:, :], reads=("AO%d" % wi,))

        if stop <= 7:
            return nc
        with Stage(nc, S) as st:
            AC = st.sb("AC", [128, 24, NOWN], BF16)
            for c in range(24):
                dma("sp", AC[:, c, :], ACT_d[:, c, :], writes=("AC",))
            WD = st.sb("WD", [128, 24, 512], BF16)
            GFB = st.sb("GFB", [128, D])
            dma("sp", GFB[:], gfb[:, :], writes=("GFB",))
            HMs = [st.sb("HM%d" % i, [128, D]) for i in range(2)]
            YO = [st.sb("YO%d" % i, [128, D]) for i in range(2)]
            SQJ = st.sb("SQJ", [128, D])
            STs = st.sb("STs", [128, 8])
            P = [st.ps("PH%d" % i, [128, 512]) for i in range(2)]
            for hf in range(2):
                for c0 in range(0, 24, 8):
                    wcast(WD[:, c0:c0 + 8, :], w_down[c0 * 128:(c0 + 8) * 128, hf * 512:(hf + 1) * 512], "WD")
                for t in range(16):
                    i = t % 2
                    rr = (t + 1) * 128
                    if hf == 0:
                        dma("sp", HMs[i][:, 0:512], HM_d[rr:rr + 128, 0:512], writes=("HM%d" % i,))
                    else:
                        dma("sp", HMs[i][:, :], HM_d[rr:rr + 128, :], writes=("HM%d" % i,))
                    for c in range(24):
                        mm(P[i][:, :], AC[:, c, t * 128:(t + 1) * 128], WD[:, c, :], c == 0, c == 23,
                           ("AC", "WD"), ("PH%d" % i,))
                    tt(HMs[i][:, hf * 512:(hf + 1) * 512], P[i][:, :], HMs[i][:, hf * 512:(hf + 1) * 512],
                       ALU.add, ("PH%d" % i, "HM%d" % i), ("HM%d" % i,))
                    if hf == 0:
                        dma("sp", HM_d[rr:rr + 128, 0:512], HMs[i][:, 0:512], reads=("HM%d" % i,))
                    else:
                        act(SQJ[:, :], HMs[i][:, :], AF.Square, ("HM%d" % i,), ("SQJ",))
                        S.op("dve", lambda e: e.reduce_sum(out=STs[:, 0:1], in_=SQJ[:, :], axis=AX.X),
                             ("SQJ",), ("ST0",))
                        ts(STs[:, 1:2], STs[:, 0:1], 1.0 / D, EPS, ALU.mult, ALU.add, ("ST0",), ("ST1",))
                        act(STs[:, 2:3], STs[:, 1:2], AF.Sqrt, ("ST1",), ("ST2",))
                        S.op("dve", lambda e: e.reciprocal(out=STs[:, 3:4], in_=STs[:, 2:3]), ("ST2",), ("ST3",))
                        stt(YO[i][:, :], HMs[i][:, :], STs[:, 3:4], GFB[:, :], ALU.mult, ALU.mult,
                            ("HM%d" % i, "ST3", "GFB"), ("YO%d" % i,))
                        dma("sp", out[t * 128:(t + 1) * 128, :], YO[i][:, :], reads=("YO%d" % i,))
    return nc


def _tables(core):
    b, p = divmod(core, 4)
    n_real_pre = 2048 * p
    pad = NPRE - n_real_pre
    pos = np.arange(L, dtype=np.int64) - pad
    pos = np.maximum(pos, 0).astype(np.float32)
    half = 128
    inv_freq = (1.0 / (10000.0 ** np.linspace(0.0, 1.0, half, dtype=np.float32))).astype(np.float32)
    ang = (pos[:, None] * inv_freq[None, :]).astype(np.float32)
    cos = np.cos(ang).astype(np.float32)
    sin = np.sin(ang).astype(np.float32)
    g = (1.0 - 2.0 ** (-5.0 - np.arange(H, dtype=np.float64)))
    t = np.arange(NPRE, dtype=np.float64)
    dec = g[None, :] ** (NPRE - 1 - t)[:, None]
    dec = dec * (DK ** -0.5)
    decpre = dec.reshape(NPRE // 128, 128, H).transpose(1, 0, 2).astype(np.float32)
    i = np.arange(128, dtype=np.float64)
    ctab = np.zeros((128, 16), np.float32)
    ctab[:, 0:4] = g[None, :] ** (i + 1.0)[:, None]
    ctab[:, 4:8] = (g[None, :] ** (127.0 - i)[:, None]) * (DK ** -0.5)
    ctab[:, 8:12] = (g[None, :] ** np.maximum(15.0 - i, 0.0)[:, None]) * (DK ** -0.5)
    jj = i[:, None]
    ii = i[None, :]
    maskT = np.zeros((128, H, 128), np.float32)
    for h in range(H):
        maskT[:, h, :] = (g[h] ** (-(jj + 1.0))) * (ii >= jj) * (DK ** -0.5)
    mgrp = np.zeros((128, NPG), np.float32)
    for gi in range(NPG):
        mgrp[:, gi] = 1.0 if gi * 512 >= pad else 0.0
    return cos, sin, decpre, ctab, maskT, mgrp


def kernel(x, meta_tokens, norm_mix_g, w_in, rnn_conv_w, rnn_conv_b, rg_a_w, rg_a_b,
           rg_x_w, rg_x_b, lru_lambda, w_branch_rnn, w_branch_ret, w_out, norm_ffn_g,
           w_up, ffn_conv_w, ffn_conv_b, w_down, norm_final_g, _ret_maps=False):
    f = lambda a: np.ascontiguousarray(np.asarray(a, dtype=np.float32))
    x = f(x)
    meta = f(meta_tokens)
    pvec = np.zeros((128, 80), np.float32)
    cw = f(rnn_conv_w)[0].reshape(4, NB, 128)
    for j in range(4):
        pvec[:, j * 10:(j + 1) * 10] = cw[j].T
    pvec[:, 40:50] = f(rnn_conv_b)[0].reshape(NB, 128).T
    pvec[:, 50:60] = f(rg_a_b)[0].reshape(NB, 128).T
    pvec[:, 60:70] = f(rg_x_b)[0].reshape(NB, 128).T
    pvec[:, 70:80] = f(lru_lambda)[0].reshape(NB, 128).T
    fvec = np.zeros((128, 4, 48), np.float32)
    fw = f(ffn_conv_w)[0].reshape(3, 48, 128)
    for j in range(3):
        fvec[:, j, :] = fw[j].T
    fvec[:, 3, :] = f(ffn_conv_b)[0].reshape(48, 128).T
    rgw = np.ascontiguousarray(np.stack([f(rg_a_w)[0], f(rg_x_w)[0]], 0).transpose(2, 0, 1, 3))
    g1b = np.ascontiguousarray(np.broadcast_to(f(norm_mix_g)[0].reshape(KD, 128).T[:, :, None], (128, KD, 128)))
    g2b = np.ascontiguousarray(np.broadcast_to(f(norm_ffn_g)[0].reshape(KD, 128).T[:, :, None], (128, KD, 128)))
    gfb = np.ascontiguousarray(np.broadcast_to(f(norm_final_g)[None, :], (128, D)))
    iden = np.eye(128, dtype=np.float32)
    shared = {
        "w_in": f(w_in)[0], "w_brnn": f(w_branch_rnn)[0], "w_bret": f(w_branch_ret)[0], "w_out": f(w_out)[0],
        "w_up": f(w_up)[0], "w_down": f(w_down)[0], "rgw": rgw, "pvec": pvec, "fvec": fvec,
        "g1b": g1b, "g2b": g2b, "gfb": gfb, "iden": iden,
    }
    in_maps = []
    for core in range(8):
        b, p = divmod(core, 4)
        seq = np.concatenate([meta, x[b]], 0)
        end = NMETA + 2048 * (p + 1)
        stream = np.zeros((L, D), np.float32)
        stream[L - end:] = seq[:end]
        cos, sin, decpre, ctab, maskT, mgrp = _tables(core)
        m = dict(shared)
        m.update({"xs": stream, "cosT": cos, "sinT": sin, "decpre": decpre, "ctab": ctab,
                  "maskT": maskT, "mgrp": mgrp})
        in_maps.append(m)
    if _ret_maps:
        return in_maps
    nc = build()
    res = run_bass_kernel_spmd(nc, in_maps, core_ids=list(range(8)))
    outp = np.zeros((2, SEQ, D), np.float32)
    for core in range(8):
        b, p = divmod(core, 4)
        outp[b, p * 2048:(p + 1) * 2048] = res.results[core]["out"]
    return outp
```

```python
from contextlib import ExitStack
import numpy as np
import concourse.bass as bass
import concourse.mybir as mybir
from concourse.bass_utils import run_bass_kernel_spmd

F32 = mybir.dt.float32
BF16 = mybir.dt.bfloat16
AF = mybir.ActivationFunctionType
ALU = mybir.AluOpType
AX = mybir.AxisListType

D = 1024
KD = 8
SEQ = 8192
NMETA = 16
DRNN = 1280
NB = 10
H = 4
DK = 256
DV = 512
DFF = 3072
DIN = 10752
EPS = 1e-6
NPRE = 6144
NPG = NPRE // 512
HALO = 16
NOWN = 2048
NM = HALO + NOWN
L = NPRE + NM
NT_MAIN = 17
O_U, O_UG, O_Q, O_K, O_V, O_G, O_GA, O_GB = 0, 1280, 2560, 3584, 4608, 6656, 8704, 9728

MG = [(0, 16)] + [(16 + 512 * i, 512) for i in range(4)]


def tile_rows(t):
    if t == 0:
        return 0, 16
    return 16 + 128 * (t - 1), 128


class Sched:
    ENG = ("pe", "act", "dve", "pool", "sp")

    def __init__(self, nc, es):
        self.nc = nc
        self.q = {e: [] for e in self.ENG}
        self.sems = {}
        self.cnt = {}
        for e in self.ENG:
            self.sems[e] = es.enter_context(nc.semaphore("s_" + e))
            self.cnt[e] = 0
        self.ndma = 12
        self.dma_rr = 0
        for i in range(self.ndma):
            k = "d%d" % i
            self.sems[k] = es.enter_context(nc.semaphore("s_" + k))
            self.cnt[k] = 0
        self.waited = {e: {} for e in self.ENG}
        self.res = {}
        self.nops = 0

    def op(self, eng, fn, reads=(), writes=(), dma=False):
        deps = {}

        def add(ev):
            if ev is None:
                return
            s, v = ev
            if deps.get(s, 0) < v:
                deps[s] = v

        for r in reads:
            st = self.res.get(r)
            if st:
                add(st["w"])
        for w in writes:
            st = self.res.get(w)
            if st:
                add(st["w"])
                for s, v in st["r"].items():
                    add((s, v))
        if dma:
            k = "d%d" % self.dma_rr
            self.dma_rr = (self.dma_rr + 1) % self.ndma
            if self.cnt[k] > 0:
                add((k, 16 * self.cnt[k]))
            self.cnt[k] += 1
            ev = (k, 16 * self.cnt[k])
            inc = 16
        else:
            self.cnt[eng] += 1
            ev = (eng, self.cnt[eng])
            inc = 1
        waits = []
        for s, v in deps.items():
            if s == eng and eng == "pe":
                continue
            if self.waited[eng].get(s, 0) >= v:
                continue
            self.waited[eng][s] = v
            waits.append((s, v))
        sem = self.sems[ev[0]]
        sems = self.sems

        def emit(e, fn=fn, waits=waits, sem=sem, inc=inc):
            for s, v in waits:
                e.wait_ge(sems[s], v)
            fn(e).then_inc(sem, inc)

        self.q[eng].append(emit)
        for r in reads:
            st = self.res.setdefault(r, {"w": None, "r": {}})
            if st["r"].get(ev[0], 0) < ev[1]:
                st["r"][ev[0]] = ev[1]
        for w in writes:
            self.res[w] = {"w": ev, "r": {}}
        self.nops += 1

    def finish(self, eng="sp"):
        waits = [(k, 16 * self.cnt[k]) for k in self.sems if k[1:].isdigit() and self.cnt[k] > 0]
        sems = self.sems

        def emit(e):
            for s, v in waits:
                e.wait_ge(sems[s], v)

        self.q[eng].append(emit)


class Stage:
    _n = [0]

    def __init__(self, nc, S):
        self.nc, self.S = nc, S
        self.es = ExitStack()
        Stage._n[0] += 1
        self.pfx = "g%d_" % Stage._n[0]

    def __enter__(self):
        self.es.__enter__()
        return self

    def sb(self, name, shape, dt=F32):
        return self.es.enter_context(self.nc.sbuf_tensor(self.pfx + name, list(shape), dt))

    def ps(self, name, shape, dt=F32):
        return self.es.enter_context(self.nc.psum_tensor(self.pfx + name, list(shape), dt))

    def __exit__(self, *a):
        S = self.S
        S.barrier()
        q = S.q
        with self.nc.Block() as block:
            @block.tensor
            def _(e):
                for f in q["pe"]:
                    f(e)

            @block.scalar
            def _(e):
                for f in q["act"]:
                    f(e)

            @block.vector
            def _(e):
                for f in q["dve"]:
                    f(e)

            @block.gpsimd
            def _(e):
                for f in q["pool"]:
                    f(e)

            @block.sync
            def _(e):
                for f in q["sp"]:
                    f(e)
        S.q = {e: [] for e in S.ENG}
        S.res = {}
        return self.es.__exit__(*a)


def _barrier(self):
    targets = []
    for k in self.sems:
        v = self.cnt[k] * (16 if k[1:].isdigit() else 1)
        if v > 0:
            targets.append((k, v))
    sems = self.sems
    for eng in self.ENG:
        waits = []
        for s, v in targets:
            if self.waited[eng].get(s, 0) >= v:
                continue
            self.waited[eng][s] = v
            waits.append((s, v))

        def emit(e, waits=waits):
            for s, v in waits:
                e.wait_ge(sems[s], v)

        self.q[eng].append(emit)


Sched.barrier = _barrier


def build(stop=99, dbg=False):
    nc = bass.Bass("TRN2", target_bir_lowering=False)
    skind = "ExternalOutput" if dbg else "Internal"

    def din(name, shape, dt=F32):
        return nc.dram_tensor(name, list(shape), dt, kind="ExternalInput").ap()

    xs = din("xs", [L, D])
    w_in = din("w_in", [D, DIN])
    w_brnn = din("w_brnn", [DRNN, D])
    w_bret = din("w_bret", [H * DV, D])
    w_out = din("w_out", [D, D])
    w_up = din("w_up", [D, 2 * DFF])
    w_down = din("w_down", [DFF, D])
    rgw = din("rgw", [128, 2, NB, 128])
    pvec = din("pvec", [128, 80])
    fvec = din("fvec", [128, 4, 48])
    g1b = din("g1b", [128, KD, 128])
    g2b = din("g2b", [128, KD, 128])
    gfb = din("gfb", [128, D])
    cosT = din("cosT", [L, 128])
    sinT = din("sinT", [L, 128])
    decpre = din("decpre", [128, NPRE // 128, H])
    ctab = din("ctab", [128, 16])
    maskT = din("maskT", [128, H, 128])
    mgrp = din("mgrp", [128, NPG])
    iden_in = din("iden", [128, 128])
    out = nc.dram_tensor("out", [NOWN, D], F32, kind="ExternalOutput").ap()
    XNT_d = nc.dram_tensor("xnt_d", [128, KD, L], BF16, kind=skind).ap()
    OA_d = nc.dram_tensor("oa_d", [128, NB, NM], BF16, kind=skind).ap()
    OB_d = nc.dram_tensor("ob_d", [128, 16, NM], BF16, kind=skind).ap()
    MIX_d = nc.dram_tensor("mix_d", [128, KD, NM], BF16, kind=skind).ap()
    HM_d = nc.dram_tensor("hm_d", [NT_MAIN * 128, D], F32, kind=skind).ap()
    XN2_d = nc.dram_tensor("xn2_d", [128, KD, NM], BF16, kind=skind).ap()
    ACT_d = nc.dram_tensor("act_d", [128, 24, NOWN], BF16, kind=skind).ap()
    SF_d = nc.dram_tensor("sf_d", [128, 2 * H, DV], F32, kind=skind).ap()

    ges = ExitStack()
    with ges:
        S = Sched(nc, ges)

        def dma(eng, out_ap, in_ap, reads=(), writes=()):
            S.op(eng, lambda e: e.dma_start(out=out_ap, in_=in_ap), reads, writes, dma=True)

        def act(out_ap, in_ap, func, reads, writes, bias=None, scale=None):
            kw = {}
            if bias is not None:
                kw["bias"] = bias
            if scale is not None:
                kw["scale"] = scale
            S.op("act", lambda e: e.activation(out=out_ap, in_=in_ap, func=func, **kw), reads, writes)

        def tt(out_ap, a, b, op, reads, writes, eng="dve"):
            S.op(eng, lambda e: e.tensor_tensor(out=out_ap, in0=a, in1=b, op=op), reads, writes)

        def ts(out_ap, a, s1, s2, op0, op1, reads, writes, eng="dve"):
            if op1 is None:
                S.op(eng, lambda e: e.tensor_scalar(out=out_ap, in0=a, scalar1=s1, scalar2=None, op0=op0),
                     reads, writes)
            else:
                S.op(eng, lambda e: e.tensor_scalar(out=out_ap, in0=a, scalar1=s1, scalar2=s2, op0=op0, op1=op1),
                     reads, writes)

        def stt(out_ap, a, s, b, op0, op1, reads, writes):
            S.op("dve", lambda e: e.scalar_tensor_tensor(out=out_ap, in0=a, scalar=s, in1=b, op0=op0, op1=op1),
                 reads, writes)

        def cp(eng, out_ap, in_ap, reads, writes):
            if eng == "act":
                S.op(eng, lambda e: e.activation(out=out_ap, in_=in_ap, func=AF.Copy), reads, writes)
            else:
                S.op(eng, lambda e: e.tensor_copy(out=out_ap, in_=in_ap), reads, writes)

        def mm(out_ap, lhsT, rhs, start, stop, reads, writes):
            S.op("pe", lambda e: e.matmul(out_ap, lhsT, rhs, start=start, stop=stop), reads, writes)

        STG = [ges.enter_context(nc.sbuf_tensor("STG%d" % i, [128, 640], F32)) for i in range(4)]
        stg_i = [0]

        def wcast(dst_view, src_ap, key):
            kc = dst_view.shape[1]
            ncols = dst_view.shape[2]
            for k in range(kc):
                i = stg_i[0] % 4
                stg_i[0] += 1
                dma("sp", STG[i][:, 0:ncols], src_ap[k * 128:(k + 1) * 128, :], writes=("STG%d" % i,))
                cp("pool", dst_view[:, k, :], STG[i][:, 0:ncols], ("STG%d" % i,), (key,))

        class Rot:
            def __init__(self, st, name, shape, dt, n, psum=False):
                self.items = []
                for i in range(n):
                    t = (st.ps if psum else st.sb)("%s%d" % (name, i), shape, dt)
                    self.items.append((t, "%s%d" % (name, i)))
                self.i = 0

            def nxt(self):
                it = self.items[self.i % len(self.items)]
                self.i += 1
                return it

        def pipeline(n, phases):
            for step in range(n + len(phases) - 1):
                for pi in reversed(range(len(phases))):
                    t = step - pi
                    if 0 <= t < n:
                        phases[pi](t)

        def norm_bufs(st, T, n=3):
            T["SQJ"] = Rot(st, "SQJ", [128, D], F32, 3)
            T["ST"] = Rot(st, "ST", [128, 4], F32, n + 2)
            T["XB"] = Rot(st, "XB", [128, D], BF16, n)
            T["PT"] = Rot(st, "PT", [128, KD, 128], BF16, 2, psum=True)
            T["CM"] = st.sb("CM", [128, 1])
            S.op("pool", lambda e: e.memset(T["CM"][:], -0.5), (), ("CM",))

        def norm_phases(T, items, gb, gbkey):
            ctx = {}

            def p0(t):
                ctx[t] = dict(zip(("x_ap", "xk"), items[t]["x_fn"]()))

            def p1(t):
                c = ctx[t]
                rows = items[t]["rows"]
                SQJ, sqk = T["SQJ"].nxt()
                act(SQJ[0:rows, :], c["x_ap"], AF.Square, (c["xk"],), (sqk,))
                c.update(SQJ=SQJ, sqk=sqk)

            def p2(t):
                c = ctx[t]
                rows = items[t]["rows"]
                STt, stk = T["ST"].nxt()
                SQJ = c["SQJ"]
                S.op("dve", lambda e: e.reduce_sum(out=STt[0:rows, 0:1], in_=SQJ[0:rows, :], axis=AX.X),
                     (c["sqk"],), (stk,))
                ts(STt[0:rows, 1:2], STt[0:rows, 0:1], 1.0 / D, EPS, ALU.mult, ALU.add, (stk,), (stk,))
                tt(STt[0:rows, 2:3], STt[0:rows, 1:2], T["CM"][0:rows, :], ALU.pow, (stk, "CM"), (stk,), eng="pool")
                c.update(STt=STt, stk=stk)

            def p3(t):
                c = ctx[t]
                rows = items[t]["rows"]
                XBt, xbk = T["XB"].nxt()
                act(XBt[0:rows, :], c["x_ap"], AF.Identity, (c["xk"], c["stk"]), (xbk,), scale=c["STt"][0:rows, 2:3])
                c.update(XBt=XBt, xbk=xbk)

            def p4(t):
                c = ctx[t]
                rows = items[t]["rows"]
                PTt, ptk = T["PT"].nxt()
                XBt = c["XBt"]
                for kc in range(KD):
                    S.op("pe", lambda e, kc=kc: e.transpose(PTt[:, kc, 0:rows],
                                                            XBt[0:rows, kc * 128:(kc + 1) * 128],
                                                            T["IDB"][0:rows, 0:rows]),
                         (c["xbk"], "IDB"), (ptk,))
                c.update(PTt=PTt, ptk=ptk)

            def p5(t):
                c = ctx.pop(t)
                it = items[t]
                rows = it["rows"]
                tt(it["dst"], c["PTt"][:, :, 0:rows], gb[:, :, 0:rows], ALU.mult, (c["ptk"], gbkey), (it["dkey"],))
                if it.get("post"):
                    it["post"]()

            return [p0, p1, p2, p3, p4, p5]

        def load_iden(st, T):
            T["IDF"] = st.sb("IDF", [128, 128])
            T["IDB"] = st.sb("IDB", [128, 128], BF16)
            dma("sp", T["IDF"][:], iden_in[:, :], writes=("IDF",))
            cp("dve", T["IDB"][:], T["IDF"][:], ("IDF",), ("IDB",))

        stream_tiles = [(i * 128, 128) for i in range(NPRE // 128)] + [(NPRE, 16)] + \
                       [(NPRE + 16 + i * 128, 128) for i in range(16)]
        main_tiles = stream_tiles[NPRE // 128:]
        pre_groups = [(i * 512, 512) for i in range(NPG)]
        main_groups = [(NPRE, 16)] + [(NPRE + 16 + i * 512, 512) for i in range(4)]

        with Stage(nc, S) as st:
            T = {}
            load_iden(st, T)
            G1B = st.sb("G1B", [128, KD, 128])
            dma("sp", G1B[:], g1b[:, :, :], writes=("G1B",))
            XSr = Rot(st, "XS", [128, D], F32, 6)
            norm_bufs(st, T, 3)
            XOr = Rot(st, "XO", [128, KD, 512], BF16, 2)
            items = []
            for (c0, ncols) in pre_groups + main_groups:
                XOt, ok = XOr.nxt()
                nt = max(1, ncols // 128)
                for j in range(nt):
                    rows = min(128, ncols)
                    r0 = c0 + j * 128

                    def x_fn(r0=r0, rows=rows):
                        XSt, xk = XSr.nxt()
                        dma("sp", XSt[0:rows, :], xs[r0:r0 + rows, :], writes=(xk,))
                        return XSt[0:rows, :], xk

                    post = None
                    if j == nt - 1:
                        def post(XOt=XOt, ok=ok, c0=c0, ncols=ncols):
                            dma("sp", XNT_d[:, :, c0:c0 + ncols], XOt[:, :, 0:ncols], reads=(ok,))
                    items.append(dict(x_fn=x_fn, rows=rows, dst=XOt[:, :, j * 128:j * 128 + rows], dkey=ok, post=post))
            pipeline(len(items), norm_phases(T, items, G1B, "G1B"))

        if stop <= 1:
            return nc
        with Stage(nc, S) as st:
            WU = st.sb("WU", [128, KD, DRNN], BF16)
            WG = st.sb("WG", [128, KD, DRNN], BF16)
            RGW = st.sb("RGW", [128, 2, NB, 128], BF16)
            PV = st.sb("PV", [128, 80])
            MGR = st.sb("MGR", [128, NPG])
            BM = st.sb("BM", [128, NPG, NB])
            SC = st.sb("SC", [128, 3, NB])
            HB = st.sb("HB", [128, 2 * NB])
            dma("sp", PV[:], pvec[:, :], writes=("PV",))
            dma("sp", MGR[:], mgrp[:, :], writes=("MGR",))
            for c0 in range(0, DRNN, 640):
                wcast(WU[:, :, c0:c0 + 640], w_in[:, O_U + c0:O_U + c0 + 640], "WU")
            for ax in range(2):
                for n0 in range(0, NB, 5):
                    i = stg_i[0] % 4
                    stg_i[0] += 1
                    dma("sp", STG[i][:, 0:640].rearrange("p (n j) -> p n j", n=5), rgw[:, ax, n0:n0 + 5, :],
                        writes=("STG%d" % i,))
                    cp("pool", RGW[:, ax, n0:n0 + 5, :], STG[i][:, 0:640].rearrange("p (n j) -> p n j", n=5),
                       ("STG%d" % i,), ("RGW",))
            for c0 in range(0, DRNN, 640):
                wcast(WG[:, :, c0:c0 + 640], w_in[:, O_UG + c0:O_UG + c0 + 640], "WG")
            act(SC[:, 0, :], PV[:, 70:80], AF.Exp, ("PV",), ("SC0",), scale=-1.0)
            act(SC[:, 0, :], SC[:, 0, :], AF.Ln, ("SC0",), ("SC0",), bias=1.0)
            ts(SC[:, 1, :], SC[:, 0, :], -8.0, None, ALU.mult, None, ("SC0",), ("SC",))
            ts(SC[:, 2, :], SC[:, 0, :], -4.0, None, ALU.mult, None, ("SC0",), ("SC",))
            ts(HB[:, :], PV[:, 50:70], 0.5, None, ALU.mult, None, ("PV",), ("HB",))
            for g in range(NPG):
                ts(BM[:, g, :], PV[:, 40:50], MGR[:, g:g + 1], None, ALU.mult, None, ("PV", "MGR"), ("BM",))
            XG = [st.sb("XG%d" % i, [128, KD, 512], BF16) for i in range(2)]
            U = st.sb("U", [128, NB, 516])
            HST = st.sb("HST", [128, NB])
            S.op("dve", lambda e: e.memset(U[:], 0.0), (), tuple("U%d" % n for n in range(NB)))
            S.op("dve", lambda e: e.memset(HST[:], 0.0), (), tuple("HST%d" % n for n in range(NB)))
            BS = 3
            QB = st.sb("QB", [128, 1])
            S.op("pool", lambda e: e.memset(QB[:], 0.25), (), ("QB",))
            ACCr = Rot(st, "ACC", [128, 512], F32, 3)
            XRr = Rot(st, "XR", [128, 512], BF16, 6)
            TRr = Rot(st, "TR", [128, 512], F32, 3)
            TIr = Rot(st, "TI", [128, 512], F32, 3)
            BBr = Rot(st, "BB", [128, 512], F32, 3)
            HHr = Rot(st, "HH", [128, 512], F32, 3)
            OATr = Rot(st, "OAT", [128, 512], BF16, 3)
            GGb = [st.sb("GG%d" % i, [128, NB, 512], BF16) for i in range(2)]
            AAg = [st.sb("AAg%d" % i, [128, BS, 512]) for i in range(2)]
            OMg = [st.sb("OMg%d" % i, [128, BS, 512]) for i in range(2)]
            T1g = [st.sb("T1g%d" % i, [128, BS, 512]) for i in range(2)]
            PU = Rot(st, "PU", [128, 512], F32, 3, psum=True)
            PRA = Rot(st, "PRA", [128, 512], F32, 2, psum=True)
            PRX = Rot(st, "PRX", [128, 512], F32, 2, psum=True)
            allg = [(g, True) for g in pre_groups] + [(g, False) for g in main_groups]
            chains = []
            for gi, ((c0, ncols), is_pre) in enumerate(allg):
                for n in range(NB):
                    chains.append((gi, c0, ncols, is_pre, n))
            NCH = len(chains)
            ctx = {}

            def load_xg(gi):
                (c0, ncols), _ = allg[gi]
                dma("sp", XG[gi % 2][:, :, 0:ncols], XNT_d[:, :, c0:c0 + ncols], writes=("XG%d" % (gi % 2),))

            load_xg(0)

            def b1(ci):
                gi, c0, ncols, is_pre, n = chains[ci]
                XGt, xk = XG[gi % 2], "XG%d" % (gi % 2)
                if n == 0:
                    if gi + 1 < len(allg):
                        load_xg(gi + 1)
                    if not is_pre:
                        mgi = gi - NPG
                        GG, ggk = GGb[mgi % 2], "GG%d" % (mgi % 2)
                        for n2 in range(NB):
                            g_ps, gk = PU.nxt()
                            for kc in range(KD):
                                mm(g_ps[:, 0:ncols], WG[:, kc, n2 * 128:(n2 + 1) * 128], XGt[:, kc, 0:ncols],
                                   kc == 0, kc == KD - 1, ("WG", xk), (gk,))
                            act(GG[:, n2, 0:ncols], g_ps[:, 0:ncols], AF.Gelu_apprx_tanh, (gk,), (ggk,))
                u_ps, puk = PU.nxt()
                for kc in range(KD):
                    mm(u_ps[:, 0:ncols], WU[:, kc, n * 128:(n + 1) * 128], XGt[:, kc, 0:ncols], kc == 0,
                       kc == KD - 1, ("WU", xk), (puk,))
                ctx[ci] = dict(u_ps=u_ps, puk=puk)

            def b2(ci):
                gi, c0, ncols, is_pre, n = chains[ci]
                c = ctx[ci]
                cp("act", U[:, n, 3:3 + ncols], c["u_ps"][:, 0:ncols], (c["puk"],), ("U%d" % n,))

            def b3(ci):
                gi, c0, ncols, is_pre, n = chains[ci]
                c = ctx[ci]
                uk = "U%d" % n
                ACC, ak = ACCr.nxt()
                if is_pre:
                    g = c0 // 512
                    ts(ACC[:, 0:ncols], U[:, n, 0:ncols], PV[:, n:n + 1], BM[:, g, n:n + 1],
                       ALU.mult, ALU.add, (uk, "PV", "BM"), (ak,), eng="pool")
                else:
                    ts(ACC[:, 0:ncols], U[:, n, 0:ncols], PV[:, n:n + 1], PV[:, 40 + n:41 + n],
                       ALU.mult, ALU.add, (uk, "PV"), (ak,), eng="pool")
                c.update(ACC=ACC, ak=ak)

            def b4(ci):
                gi, c0, ncols, is_pre, n = chains[ci]
                c = ctx[ci]
                uk = "U%d" % n
                ACC, ak = c["ACC"], c["ak"]
                XR, xrk = XRr.nxt()
                stt(ACC[:, 0:ncols], U[:, n, 1:1 + ncols], PV[:, 10 + n:11 + n], ACC[:, 0:ncols],
                    ALU.mult, ALU.add, (uk, "PV", ak), (ak,))
                stt(ACC[:, 0:ncols], U[:, n, 2:2 + ncols], PV[:, 20 + n:21 + n], ACC[:, 0:ncols],
                    ALU.mult, ALU.add, (uk, "PV", ak), (ak,))
                stt(XR[:, 0:ncols], U[:, n, 3:3 + ncols], PV[:, 30 + n:31 + n], ACC[:, 0:ncols],
                    ALU.mult, ALU.add, (uk, "PV", ak), (xrk,))
                cp("pool", U[:, n, 0:3], U[:, n, ncols:ncols + 3], (uk,), (uk,))
                c.update(XR=XR, xrk=xrk)

            def b5(ci):
                gi, c0, ncols, is_pre, n = chains[ci]
                c = ctx[ci]
                XR, xrk = c["XR"], c["xrk"]
                ra, rak = PRA.nxt()
                rx, rxk = PRX.nxt()
                mm(ra[:, 0:ncols], RGW[:, 0, n, :], XR[:, 0:ncols], True, True, ("RGW", xrk), (rak,))
                mm(rx[:, 0:ncols], RGW[:, 1, n, :], XR[:, 0:ncols], True, True, ("RGW", xrk), (rxk,))
                c.update(ra=ra, rak=rak, rx=rx, rxk=rxk)

            def b6(ci):
                gi, c0, ncols, is_pre, n = chains[ci]
                c = ctx[ci]
                TR, trk = TRr.nxt()
                TI, tik = TIr.nxt()
                act(TR[:, 0:ncols], c["ra"][:, 0:ncols], AF.Tanh, (c["rak"], "HB"), (trk,), bias=HB[:, n:n + 1],
                    scale=0.5)
                act(TI[:, 0:ncols], c["rx"][:, 0:ncols], AF.Tanh, (c["rxk"], "HB"), (tik,),
                    bias=HB[:, NB + n:NB + n + 1], scale=0.5)
                c.update(TR=TR, trk=trk, TI=TI, tik=tik)

            def b7(ci):
                gi, c0, ncols, is_pre, n = chains[ci]
                c = ctx[ci]
                bid, slot = divmod(ci, BS)
                bb = bid % 2
                TR, trk, TI, tik, XR, xrk = c["TR"], c["trk"], c["TI"], c["tik"], c["XR"], c["xrk"]
                act(AAg[bb][:, slot, 0:ncols], TR[:, 0:ncols], AF.Exp, (trk, "SC"), ("AAg%d" % bb,),
                    scale=SC[:, 2, n:n + 1], bias=SC[:, 2, n:n + 1])
                act(OMg[bb][:, slot, 0:ncols], TR[:, 0:ncols], AF.Exp, (trk, "SC"), ("OMg%d" % bb,),
                    scale=SC[:, 1, n:n + 1], bias=SC[:, 1, n:n + 1])
                if ncols < 512:
                    S.op("pool", lambda e, bb=bb, slot=slot, ncols=ncols: e.memset(OMg[bb][:, slot, ncols:512], 0.0),
                         ("OMg%d" % bb,), ("OMg%d" % bb,))
                stt(T1g[bb][:, slot, 0:ncols], TI[:, 0:ncols], 1.0, XR[:, 0:ncols], ALU.add, ALU.mult,
                    (tik, xrk), ("T1g%d" % bb,))

            def batch_of(ci):
                if ci % BS == BS - 1 or ci == NCH - 1:
                    bid = ci // BS
                    return bid % 2, list(range(bid * BS, ci + 1))
                return None

            def b8(ci):
                b = batch_of(ci)
                if b is None:
                    return
                bb, ids = b
                nb = len(ids)
                act(OMg[bb][:, 0:nb, :], OMg[bb][:, 0:nb, :], AF.Sqrt, ("OMg%d" % bb, "QB"), ("OMg%d" % bb,),
                    scale=-0.25, bias=QB[:, 0:1])

            def b9(ci):
                b = batch_of(ci)
                if b is None:
                    return
                bb, ids = b
                for slot, cj in enumerate(ids):
                    gi, c0, ncols, is_pre, n = chains[cj]
                    BB, bbk = BBr.nxt()
                    tt(BB[:, 0:ncols], T1g[bb][:, slot, 0:ncols], OMg[bb][:, slot, 0:ncols], ALU.mult,
                       ("T1g%d" % bb, "OMg%d" % bb), (bbk,), eng="pool")
                    ctx[cj].update(BB=BB, bbk=bbk)

            def b10(ci):
                b = batch_of(ci)
                if b is None:
                    return
                bb, ids = b
                for slot, cj in enumerate(ids):
                    gi, c0, ncols, is_pre, n = chains[cj]
                    c = ctx.pop(cj)
                    BB, bbk = c["BB"], c["bbk"]
                    HH, hhk = HHr.nxt()
                    hk = "HST%d" % n
                    S.op("dve", lambda e, n=n, ncols=ncols, HH=HH, BB=BB, bb=bb, slot=slot: e.tensor_tensor_scan(
                        out=HH[:, 0:ncols], data0=AAg[bb][:, slot, 0:ncols], data1=BB[:, 0:ncols],
                        initial=HST[:, n:n + 1], op0=ALU.mult, op1=ALU.add), ("AAg%d" % bb, bbk, hk), (hhk,))
                    cp("dve", HST[:, n:n + 1], HH[:, ncols - 1:ncols], (hhk,), (hk,))
                    if not is_pre:
                        mgi = gi - NPG
                        GG, ggk = GGb[mgi % 2], "GG%d" % (mgi % 2)
                        OAT, ok = OATr.nxt()
                        tt(OAT[:, 0:ncols], HH[:, 0:ncols], GG[:, n, 0:ncols], ALU.mult, (hhk, ggk), (ok,))
                        m0 = c0 - NPRE
                        dma("sp", OA_d[:, n, m0:m0 + ncols], OAT[:, 0:ncols], reads=(ok,))

            pipeline(NCH, [b1, b2, b3, b4, b5, b6, b7, b8, b9, b10])
        if stop <= 2:
            return nc

        def rotary(T, src, skey, dst, dkey, rows, nh, c, s, ckey, sgn_eng="dve"):
            for h in range(nh):
                x1 = src[0:rows, h * DK:h * DK + 128]
                x2 = src[0:rows, h * DK + 128:(h + 1) * DK]
                R0, k0 = T["RT"].nxt()
                R1, k1 = T["RT"].nxt()
                R2, k2 = T["RT"].nxt()
                R3, k3 = T["RT"].nxt()
                tt(R0[0:rows, :], x1, c, ALU.mult, (skey, ckey), (k0,))
                tt(R1[0:rows, :], x2, s, ALU.mult, (skey, ckey), (k1,), eng="pool")
                tt(R2[0:rows, :], x2, c, ALU.mult, (skey, ckey), (k2,))
                tt(R3[0:rows, :], x1, s, ALU.mult, (skey, ckey), (k3,), eng="pool")
                tt(dst[0:rows, h * DK:h * DK + 128], R0[0:rows, :], R1[0:rows, :], ALU.subtract,
                   (k0, k1), (dkey,))
                tt(dst[0:rows, h * DK + 128:(h + 1) * DK], R2[0:rows, :], R3[0:rows, :], ALU.add,
                   (k2, k3), (dkey,))

        with Stage(nc, S) as st:
            T = {}
            WK = st.sb("WK", [128, KD, H * DK], BF16)
            WV = st.sb("WV", [128, KD, H * DV], BF16)
            for c0 in range(0, H * DK, 512):
                wcast(WK[:, :, c0:c0 + 512], w_in[:, O_K + c0:O_K + c0 + 512], "WK")
            for c0 in range(0, H * DV, 512):
                wcast(WV[:, :, c0:c0 + 512], w_in[:, O_V + c0:O_V + c0 + 512], "WV")
            DPRE = st.sb("DPRE", [128, NPRE // 128, H])
            dma("sp", DPRE[:], decpre[:, :, :], writes=("DPRE",))
            XGr = Rot(st, "XG", [128, KD, 512], BF16, 2)
            CSr = Rot(st, "CS", [128, 2, 128], F32, 3)
            T["RT"] = Rot(st, "RT", [128, 128], F32, 8)
            KFr = Rot(st, "KF", [128, H * DK], F32, 2)
            KRr = Rot(st, "KR", [128, 4, H * DK], BF16, 2)
            VVr = Rot(st, "VV", [128, 4, H * DV], BF16, 2)
            SF = st.sb("SF", [128, 2 * H, DV])
            S.op("dve", lambda e: e.memset(SF[:], 0.0), (), ("SF",))
            PK = [st.ps("PK%d" % i, [128, 512]) for i in range(2)]
            PVv = [st.ps("PV%d" % i, [128, 512]) for i in range(4)]
            PSr = Rot(st, "PS", [128, 512], F32, 2, psum=True)
            tiles = [(g, t4) for g in range(NPG) for t4 in range(4)]
            gst = {}
            ctx = {}

            def c1(i):
                g, t4 = tiles[i]
                c0 = g * 512
                if t4 == 0:
                    XGt, xk = XGr.nxt()
                    dma("sp", XGt[:, :, :], XNT_d[:, :, c0:c0 + 512], writes=(xk,))
                    gst[g] = (XGt, xk, KRr.nxt(), VVr.nxt())
                XGt, xk, (KR, krk), (VV, vvk) = gst[g]
                tile = g * 4 + t4
                r0 = c0 + t4 * 128
                CS, csk = CSr.nxt()
                dma("sp", CS[:, 0, :], cosT[r0:r0 + 128, :], writes=(csk,))
                dma("sp", CS[:, 1, :], sinT[r0:r0 + 128, :], writes=(csk,))
                KF, kfk = KFr.nxt()
                xt = lambda kc: XGt[:, kc, t4 * 128:(t4 + 1) * 128]
                for hf in range(2):
                    for kc in range(KD):
                        mm(PK[hf][:, :], xt(kc), WK[:, kc, hf * 512:(hf + 1) * 512], kc == 0, kc == KD - 1,
                           (xk, "WK"), ("PK%d" % hf,))
                for h in range(H):
                    hf, o = divmod(h, 2)
                    act(KF[:, h * DK:(h + 1) * DK], PK[hf][:, o * DK:(o + 1) * DK], AF.Identity,
                        ("PK%d" % hf, "DPRE"), (kfk,), scale=DPRE[:, tile, h:h + 1])
                for q4 in range(4):
                    for kc in range(KD):
                        mm(PVv[q4][:, :], xt(kc), WV[:, kc, q4 * 512:(q4 + 1) * 512], kc == 0, kc == KD - 1,
                           (xk, "WV"), ("PV%d" % q4,))
                    cp("act", VV[:, t4, q4 * 512:(q4 + 1) * 512], PVv[q4][:, :], ("PV%d" % q4,), (vvk,))
                ctx[i] = (KF, kfk, CS, csk)

            def c2(i):
                g, t4 = tiles[i]
                XGt, xk, (KR, krk), (VV, vvk) = gst[g]
                KF, kfk, CS, csk = ctx.pop(i)
                rotary(T, KF, kfk, KR[:, t4, :], krk, 128, H, CS[:, 0, :], CS[:, 1, :], csk)
                if t4 == 3:
                    for h in range(H):
                        for dc in range(2):
                            Pt, pk = PSr.nxt()
                            for t in range(4):
                                mm(Pt[:, :], KR[:, t, h * DK + dc * 128:h * DK + (dc + 1) * 128],
                                   VV[:, t, h * DV:(h + 1) * DV], t == 0, t == 3, (krk, vvk), (pk,))
                            tt(SF[:, h * 2 + dc, :], Pt[:, :], SF[:, h * 2 + dc, :], ALU.add,
                               (pk, "SF%d" % (h * 2 + dc)), ("SF%d" % (h * 2 + dc),))

            pipeline(len(tiles), [c1, c2])
            dma("sp", SF_d[:, :, :], SF[:], reads=tuple("SF%d" % i for i in range(2 * H)))

        if stop <= 3:
            return nc
        GAM = [1.0 - 2.0 ** (-5.0 - h) for h in range(H)]
        with Stage(nc, S) as st:
            T = {}
            load_iden(st, T)
            XM = st.sb("XM", [128, KD, NM], BF16)
            for kc in range(KD):
                dma("sp", XM[:, kc, :], XNT_d[:, kc, NPRE:NPRE + NM], writes=("XM",))
            CT = st.sb("CT", [128, 16])
            MT = st.sb("MT", [128, H, 128])
            CM = st.sb("CM", [128, 1])
            S.op("pool", lambda e: e.memset(CM[:], -0.5), (), ("CM",))
            dma("sp", CT[:], ctab[:, :], writes=("CT",))
            dma("sp", MT[:], maskT[:, :, :], writes=("MT",))
            WH = [st.sb("WH%d" % i, [128, KD, 1536], BF16) for i in range(2)]
            CSr = Rot(st, "CS", [128, 2, 128], F32, 4)
            T["RT"] = Rot(st, "RT", [128, 128], F32, 8)
            QKFr = Rot(st, "QKF", [128, 2 * DK], F32, 3)
            QKRr = Rot(st, "QKR", [128, 2 * DK], BF16, 3)
            KDr = Rot(st, "KDc", [128, DK], BF16, 5)
            VVr = Rot(st, "VV", [128, DV], BF16, 8)
            SGr = Rot(st, "SG", [128, DV], F32, 11)
            QKTr = Rot(st, "QKT", [128, 4, 128], BF16, 5)
            SCTr = Rot(st, "SCT", [128, 128], BF16, 3)
            SF = st.sb("SF", [128, 2 * H, DV])
            SBFring = [st.sb("SBFr%d" % j, [128, 2, DV], BF16) for j in range(5)]
            OFr = Rot(st, "OF", [128, DV], F32, 4)
            OBTr = Rot(st, "OBT", [128, DV], BF16, 3)
            OBOr = Rot(st, "OBO", [128, 4, 128], BF16, 2)
            SQJr = Rot(st, "SQJ", [128, DV], F32, 3)
            STr = Rot(st, "STs", [128, 4], F32, 4)
            dma("sp", SF[:], SF_d[:, :, :], writes=tuple("SF%d" % i for i in range(2 * H)))
            PQ = st.ps("PQ", [128, 512])
            PV_ = st.ps("PVm", [128, 512])
            PG = st.ps("PG", [128, 512])
            PSC = st.ps("PSC", [128, 512])
            PO = st.ps("PO", [128, 512])
            PSt = st.ps("PSt", [128, 512])
            PT = st.ps("PT", [128, 4, 128], BF16)
            PT2 = st.ps("PT2", [128, 4, 128], BF16)
            its = [(h, ti) for h in range(H) for ti in range(len(main_tiles))]
            NT = len(main_tiles)
            ctx = {}
            sbf = {}

            def load_w(h):
                W = WH[h % 2]
                wk = "WH%d" % (h % 2)
                wcast(W[:, :, 0:256], w_in[:, O_Q + h * DK:O_Q + (h + 1) * DK], wk)
                wcast(W[:, :, 256:512], w_in[:, O_K + h * DK:O_K + (h + 1) * DK], wk)
                wcast(W[:, :, 512:1024], w_in[:, O_V + h * DV:O_V + (h + 1) * DV], wk)
                wcast(W[:, :, 1024:1536], w_in[:, O_G + h * DV:O_G + (h + 1) * DV], wk)

            load_w(0)

            def geo(i):
                h, ti = its[i]
                r0, rows = main_tiles[ti]
                return h, ti, r0, rows, r0 - NPRE

            def q1(i):
                h, ti, r0, rows, m0 = geo(i)
                if ti == 0 and h + 1 < H:
                    load_w(h + 1)
                W = WH[h % 2]
                wk = "WH%d" % (h % 2)
                CS, csk = CSr.nxt()
                dma("sp", CS[0:rows, 0, :], cosT[r0:r0 + rows, :], writes=(csk,))
                dma("sp", CS[0:rows, 1, :], sinT[r0:r0 + rows, :], writes=(csk,))
                xt = lambda kc: XM[:, kc, m0:m0 + rows]
                for kc in range(KD):
                    mm(PQ[0:rows, :], xt(kc), W[:, kc, 0:512], kc == 0, kc == KD - 1, ("XM", wk), ("PQ",))
                for kc in range(KD):
                    mm(PV_[0:rows, :], xt(kc), W[:, kc, 512:1024], kc == 0, kc == KD - 1, ("XM", wk), ("PVm",))
                for kc in range(KD):
                    mm(PG[0:rows, :], xt(kc), W[:, kc, 1024:1536], kc == 0, kc == KD - 1, ("XM", wk), ("PG",))
                ctx[i] = dict(CS=CS, csk=csk)

            def q2(i):
                h, ti, r0, rows, m0 = geo(i)
                c = ctx[i]
                QKF, qfk = QKFr.nxt()
                VV, vvk = VVr.nxt()
                SG, sgk = SGr.nxt()
                act(QKF[0:rows, 0:DK], PQ[0:rows, 0:DK], AF.Identity, ("PQ", "CT"), (qfk,),
                    scale=CT[0:rows, h:h + 1])
                cp("act", QKF[0:rows, DK:2 * DK], PQ[0:rows, DK:2 * DK], ("PQ",), (qfk,))
                cp("act", VV[0:rows, :], PV_[0:rows, :], ("PVm",), (vvk,))
                act(SG[0:rows, :], PG[0:rows, :], AF.Silu, ("PG",), (sgk,))
                c.update(QKF=QKF, qfk=qfk, VV=VV, vvk=vvk, SG=SG, sgk=sgk)

            def q3(i):
                h, ti, r0, rows, m0 = geo(i)
                c = ctx[i]
                QKR, qrk = QKRr.nxt()
                rotary(T, c["QKF"], c["qfk"], QKR, qrk, rows, 2, c["CS"][0:rows, 0, :], c["CS"][0:rows, 1, :],
                       c["csk"])
                KDc, kdk = KDr.nxt()
                kcol = (4 + h) if rows == 128 else (8 + h)
                ts(KDc[0:rows, :], QKR[0:rows, DK:2 * DK], CT[0:rows, kcol:kcol + 1], None, ALU.mult, None,
                   (qrk, "CT"), (kdk,))
                c.update(QKR=QKR, qrk=qrk, KDc=KDc, kdk=kdk)

            def state_mm(i, dc):
                h, ti, r0, rows, m0 = geo(i)
                c = ctx[i]
                mm(PSt[:, :], c["KDc"][0:rows, dc * 128:(dc + 1) * 128], c["VV"][0:rows, :], True, True,
                   (c["kdk"], c["vvk"]), ("PSt",))

            def state_stt(i, dc):
                h, ti, r0, rows, m0 = geo(i)
                sfk = "SF%d" % (h * 2 + dc)
                gC = GAM[h] ** rows
                stt(SF[:, h * 2 + dc, :], SF[:, h * 2 + dc, :], gC, PSt[:, :], ALU.mult, ALU.add,
                    (sfk, "PSt"), (sfk,))

            def q4(i):
                h, ti, r0, rows, m0 = geo(i)
                c = ctx[i]
                QKR, qrk = c["QKR"], c["qrk"]
                for j in range(4):
                    S.op("pe", lambda e, j=j, rows=rows, QKR=QKR: e.transpose(
                        PT[:, j, 0:rows], QKR[0:rows, j * 128:(j + 1) * 128], T["IDB"][0:rows, 0:rows]),
                        (qrk, "IDB"), ("PTa",))
                if ti + 1 < NT:
                    state_mm(i, 0)

            def q5(i):
                h, ti, r0, rows, m0 = geo(i)
                c = ctx[i]
                QKT, qtk = QKTr.nxt()
                cp("dve", QKT[:, :, 0:rows], PT[:, 0:4, 0:rows], ("PTa",), (qtk,))
                if ti + 1 < NT:
                    state_stt(i, 0)
                    state_mm(i, 1)
                    state_stt(i, 1)
                c.update(QKT=QKT, qtk=qtk)

            def q6(i):
                h, ti, r0, rows, m0 = geo(i)
                c = ctx[i]
                QKT, qtk = c["QKT"], c["qtk"]
                for dc in range(2):
                    mm(PSC[0:rows, 0:rows], QKT[:, 2 + dc, 0:rows], QKT[:, dc, 0:rows], dc == 0, dc == 1,
                       (qtk,), ("PSC",))
                if ti + 1 < NT:
                    j = (i + 1) % 5
                    for dc in range(2):
                        cp("act", SBFring[j][:, dc, :], SF[:, h * 2 + dc, :], ("SF%d" % (h * 2 + dc),),
                           ("SBFr%d_%d" % (j, dc),))

            def q7(i):
                h, ti, r0, rows, m0 = geo(i)
                c = ctx[i]
                SCT, sck = SCTr.nxt()
                tt(SCT[0:rows, 0:rows], PSC[0:rows, 0:rows], MT[0:rows, h, 0:rows], ALU.mult,
                   ("PSC", "MT"), (sck,))
                c.update(SCT=SCT, sck=sck)

            def q8(i):
                h, ti, r0, rows, m0 = geo(i)
                c = ctx[i]
                j = i % 5
                VV, vvk, QKT, qtk, SCT, sck = c["VV"], c["vvk"], c["QKT"], c["qtk"], c["SCT"], c["sck"]
                mm(PO[0:rows, :], SCT[0:rows, 0:rows], VV[0:rows, :], True, False, (sck, vvk), ("PO",))
                for dc in range(2):
                    if ti == 0:
                        mm(PO[0:rows, :], QKT[:, dc, 0:rows], SB0[:, h * 2 + dc, :], False, dc == 1,
                           (qtk, "SB0"), ("PO",))
                    else:
                        mm(PO[0:rows, :], QKT[:, dc, 0:rows], SBFring[j][:, dc, :], False, dc == 1,
                           (qtk, "SBFr%d_%d" % (j, dc)), ("PO",))

            def q9(i):
                h, ti, r0, rows, m0 = geo(i)
                c = ctx[i]
                OF, ofk = OFr.nxt()
                SQJ, sqk = SQJr.nxt()
                cp("act", OF[0:rows, :], PO[0:rows, :], ("PO",), (ofk,))
                act(SQJ[0:rows, :], PO[0:rows, :], AF.Square, ("PO",), (sqk,))
                c.update(OF=OF, ofk=ofk, SQJ=SQJ, sqk=sqk)

            def q10(i):
                h, ti, r0, rows, m0 = geo(i)
                c = ctx[i]
                STs, stk = STr.nxt()
                SQJ, sqk = c["SQJ"], c["sqk"]
                S.op("dve", lambda e, rows=rows, STs=STs, SQJ=SQJ: e.reduce_sum(
                    out=STs[0:rows, 0:1], in_=SQJ[0:rows, :], axis=AX.X), (sqk,), (stk,))
                ts(STs[0:rows, 1:2], STs[0:rows, 0:1], 1.0 / DV, EPS, ALU.mult, ALU.add, (stk,), (stk,))
                tt(STs[0:rows, 2:3], STs[0:rows, 1:2], CM[0:rows, :], ALU.pow, (stk, "CM"), (stk,), eng="pool")
                c.update(STs=STs, stk=stk)

            def q11(i):
                h, ti, r0, rows, m0 = geo(i)
                c = ctx[i]
                OBT, obk = OBTr.nxt()
                stt(OBT[0:rows, :], c["OF"][0:rows, :], c["STs"][0:rows, 2:3], c["SG"][0:rows, :], ALU.mult,
                    ALU.mult, (c["ofk"], c["stk"], c["sgk"]), (obk,))
                c.update(OBT=OBT, obk=obk)

            def q12(i):
                h, ti, r0, rows, m0 = geo(i)
                c = ctx[i]
                OBT, obk = c["OBT"], c["obk"]
                for ec in range(4):
                    S.op("pe", lambda e, ec=ec, rows=rows, OBT=OBT: e.transpose(
                        PT2[:, ec, 0:rows], OBT[0:rows, ec * 128:(ec + 1) * 128], T["IDB"][0:rows, 0:rows]),
                        (obk, "IDB"), ("PTb",))

            def q13(i):
                h, ti, r0, rows, m0 = geo(i)
                ctx.pop(i)
                OBO, ook = OBOr.nxt()
                cp("act", OBO[:, :, 0:rows], PT2[:, :, 0:rows], ("PTb",), (ook,))
                dma("sp", OB_d[:, h * 4:(h + 1) * 4, m0:m0 + rows], OBO[:, :, 0:rows], reads=(ook,))

            SB0 = st.sb("SB0", [128, 2 * H, DV], BF16)
            cp("act", SB0[:, :, :], SF[:, :, :], tuple("SF%d" % i for i in range(2 * H)), ("SB0",))
            pipeline(len(its), [q1, q2, q3, q4, q5, q6, q7, q8, q9, q10, q11, q12, q13])

        if stop <= 4:
            return nc
        with Stage(nc, S) as st:
            XM = st.sb("XM", [128, KD, NM], BF16)
            OA = st.sb("OA", [128, NB, NM], BF16)
            OB = st.sb("OB", [128, 16, NM], BF16)
            WA = [st.sb("WA%d" % i, [128, NB, 128], BF16) for i in range(2)]
            WR = [st.sb("WR%d" % i, [128, 16, 128], BF16) for i in range(2)]
            WGa = [st.sb("WGa%d" % i, [128, KD, 128], BF16) for i in range(2)]
            WGb = [st.sb("WGb%d" % i, [128, KD, 128], BF16) for i in range(2)]
            Rt = {k: Rot(st, k, [128, 512], F32, 2) for k in ["GA", "GB", "M1", "M2"]}
            MO = [st.sb("MO%d" % i, [128, NM], BF16) for i in range(2)]
            PEr = [Rot(st, "PE%d_" % i, [128, 512], F32, 2, psum=True) for i in range(4)]

            def load_e(c):
                wi = c % 2
                cs = slice(c * 128, (c + 1) * 128)
                wcast(WA[wi][:], w_brnn[:, cs], "WA%d" % wi)
                wcast(WGa[wi][:], w_in[:, O_GA + c * 128:O_GA + (c + 1) * 128], "WGa%d" % wi)
                wcast(WR[wi][:], w_bret[:, cs], "WR%d" % wi)
                wcast(WGb[wi][:], w_in[:, O_GB + c * 128:O_GB + (c + 1) * 128], "WGb%d" % wi)

            load_e(0)
            for kc in range(NB):
                dma("sp", OA[:, kc, :], OA_d[:, kc, :], writes=("OA",))
            for kc in range(KD):
                dma("sp", XM[:, kc, :], XNT_d[:, kc, NPRE:NPRE + NM], writes=("XM",))
            for kc in range(16):
                dma("sp", OB[:, kc, :], OB_d[:, kc, :], writes=("OB",))
            eits = [(c, gi) for c in range(8) for gi in range(len(MG))]
            ectx = {}

            def e1(i):
                c, gi = eits[i]
                m0, ncols = MG[gi]
                wi = c % 2
                if gi == 0 and c + 1 < 8:
                    load_e(c + 1)
                ps_ = [r.nxt() for r in PEr]
                for k in range(NB):
                    mm(ps_[0][0][:, 0:ncols], WA[wi][:, k, :], OA[:, k, m0:m0 + ncols], k == 0, k == NB - 1,
                       ("WA%d" % wi, "OA"), (ps_[0][1],))
                for k in range(KD):
                    mm(ps_[2][0][:, 0:ncols], WGa[wi][:, k, :], XM[:, k, m0:m0 + ncols], k == 0, k == KD - 1,
                       ("WGa%d" % wi, "XM"), (ps_[2][1],))
                for k in range(16):
                    mm(ps_[1][0][:, 0:ncols], WR[wi][:, k, :], OB[:, k, m0:m0 + ncols], k == 0, k == 15,
                       ("WR%d" % wi, "OB"), (ps_[1][1],))
                for k in range(KD):
                    mm(ps_[3][0][:, 0:ncols], WGb[wi][:, k, :], XM[:, k, m0:m0 + ncols], k == 0, k == KD - 1,
                       ("WGb%d" % wi, "XM"), (ps_[3][1],))
                ectx[i] = ps_

            def e2(i):
                c, gi = eits[i]
                m0, ncols = MG[gi]
                wi = c % 2
                ps_ = ectx.pop(i)
                GA, gak = Rt["GA"].nxt()
                GB, gbk = Rt["GB"].nxt()
                M1, m1k = Rt["M1"].nxt()
                M2, m2k = Rt["M2"].nxt()
                act(GA[:, 0:ncols], ps_[2][0][:, 0:ncols], AF.Sigmoid, (ps_[2][1],), (gak,))
                act(GB[:, 0:ncols], ps_[3][0][:, 0:ncols], AF.Sigmoid, (ps_[3][1],), (gbk,))
                tt(M1[:, 0:ncols], ps_[0][0][:, 0:ncols], GA[:, 0:ncols], ALU.mult, (ps_[0][1], gak), (m1k,))
                tt(M2[:, 0:ncols], ps_[1][0][:, 0:ncols], GB[:, 0:ncols], ALU.mult, (ps_[1][1], gbk), (m2k,))
                tt(MO[wi][:, m0:m0 + ncols], M1[:, 0:ncols], M2[:, 0:ncols], ALU.add, (m1k, m2k),
                   ("MO%d" % wi,), eng="pool")
                if gi == len(MG) - 1:
                    dma("sp", MIX_d[:, c, :], MO[wi][:, :], reads=("MO%d" % wi,))

            pipeline(len(eits), [e1, e2])

        if stop <= 5:
            return nc
        with Stage(nc, S) as st:
            T = {}
            load_iden(st, T)
            MX = st.sb("MX", [128, KD, NM], BF16)
            for kc in range(KD):
                dma("sp", MX[:, kc, :], MIX_d[:, kc, :], writes=("MX",))
            WO = st.sb("WO", [128, KD, D], BF16)
            for c0 in range(0, D, 512):
                wcast(WO[:, :, c0:c0 + 512], w_out[:, c0:c0 + 512], "WO")
            G2B = st.sb("G2B", [128, KD, 128])
            dma("sp", G2B[:], g2b[:, :, :], writes=("G2B",))
            XSr = Rot(st, "XS", [128, D], F32, 3)
            HMr = Rot(st, "HM", [128, D], F32, 6)
            norm_bufs(st, T, 3)
            XOr = Rot(st, "XO", [128, KD, 128], BF16, 8)
            PF = Rot(st, "PF", [128, 512], F32, 4, psum=True)
            items = []
            for ti, (r0, rows) in enumerate(main_tiles):
                m0 = r0 - NPRE
                XOt, ok = XOr.nxt()

                def x_fn(ti=ti, r0=r0, rows=rows, m0=m0):
                    XSt, xk = XSr.nxt()
                    HMt, hk = HMr.nxt()
                    dma("sp", XSt[0:rows, :], xs[r0:r0 + rows, :], writes=(xk,))
                    for hf in range(2):
                        Pt, pk = PF.nxt()
                        for kc in range(KD):
                            mm(Pt[0:rows, :], MX[:, kc, m0:m0 + rows], WO[:, kc, hf * 512:(hf + 1) * 512],
                               kc == 0, kc == KD - 1, ("MX", "WO"), (pk,))
                        tt(HMt[0:rows, hf * 512:(hf + 1) * 512], Pt[0:rows, :],
                           XSt[0:rows, hf * 512:(hf + 1) * 512], ALU.add, (pk, xk), (hk,))
                    dma("sp", HM_d[ti * 128:ti * 128 + rows, :], HMt[0:rows, :], reads=(hk,))
                    return HMt[0:rows, :], hk

                def post(XOt=XOt, ok=ok, m0=m0, rows=rows):
                    dma("sp", XN2_d[:, :, m0:m0 + rows], XOt[:, :, 0:rows], reads=(ok,))
                items.append(dict(x_fn=x_fn, rows=rows, dst=XOt[:, :, 0:rows], dkey=ok, post=post))
            pipeline(len(items), norm_phases(T, items, G2B, "G2B"))

        if stop <= 6:
            return nc
        with Stage(nc, S) as st:
            X2 = st.sb("X2", [128, KD, NM], BF16)
            for kc in range(KD):
                dma("sp", X2[:, kc, :], XN2_d[:, kc, :], writes=("X2",))
            FV = st.sb("FV", [128, 4, 48])
            dma("sp", FV[:], fvec[:, :, :], writes=("FV",))
            WUg = [st.sb("WUg%d" % i, [128, KD, 128], BF16) for i in range(3)]
            WUv = [st.sb("WUv%d" % i, [128, KD, 128], BF16) for i in range(3)]
            FU = [st.sb("FU%d" % i, [128, 2, NM]) for i in range(2)]
            FAr = Rot(st, "FA", [128, 2, NOWN], F32, 3)
            AO = [st.sb("AO%d" % i, [128, NOWN], BF16) for i in range(2)]
            P = [st.ps("PG%d" % i, [128, 512]) for i in range(8)]

            def load_g(c):
                w3 = c % 3
                wcast(WUg[w3][:], w_up[:, c * 128:(c + 1) * 128], "WUg%d" % w3)
                wcast(WUv[w3][:], w_up[:, DFF + c * 128:DFF + (c + 1) * 128], "WUv%d" % w3)

            load_g(0)
            load_g(1)
            pcount = [0]
            fa_of = {}

            def g_mm(c):
                wi = c % 2
                if c + 2 < 24:
                    load_g(c + 2)
                w3 = c % 3
                fk = "FU%d" % wi
                for gi, (m0, ncols) in enumerate(MG):
                    b0 = (pcount[0] % 4) * 2
                    pcount[0] += 1
                    pg, pv = P[b0], P[b0 + 1]
                    kg, kv = "PG%d" % b0, "PG%d" % (b0 + 1)
                    for k in range(KD):
                        mm(pg[:, 0:ncols], WUg[w3][:, k, :], X2[:, k, m0:m0 + ncols], k == 0, k == KD - 1,
                           ("WUg%d" % w3, "X2"), (kg,))
                    for k in range(KD):
                        mm(pv[:, 0:ncols], WUv[w3][:, k, :], X2[:, k, m0:m0 + ncols], k == 0, k == KD - 1,
                           ("WUv%d" % w3, "X2"), (kv,))
                    cp("act", FU[wi][:, 0, m0:m0 + ncols], pg[:, 0:ncols], (kg,), (fk,))
                    cp("act", FU[wi][:, 1, m0:m0 + ncols], pv[:, 0:ncols], (kv,), (fk,))

            def g_conv(c):
                wi = c % 2
                fk = "FU%d" % wi
                FA, fak = FAr.nxt()
                fa_of[c] = (FA, fak)
                for j, cc in ((0, c), (1, 24 + c)):
                    fj = fak + "_%d" % j
                    ts(FA[:, j, :], FU[wi][:, j, 14:14 + NOWN], FV[:, 0, cc:cc + 1], FV[:, 3, cc:cc + 1],
                       ALU.mult, ALU.add, (fk, "FV"), (fj,))
                    stt(FA[:, j, :], FU[wi][:, j, 15:15 + NOWN], FV[:, 1, cc:cc + 1], FA[:, j, :],
                        ALU.mult, ALU.add, (fk, "FV", fj), (fj,))
                    stt(FA[:, j, :], FU[wi][:, j, 16:16 + NOWN], FV[:, 2, cc:cc + 1], FA[:, j, :],
                        ALU.mult, ALU.add, (fk, "FV", fj), (fj,))

            def g_gelu(c):
                wi = c % 2
                FA, fak = fa_of.pop(c)
                act(FA[:, 0, :], FA[:, 0, :], AF.Gelu_apprx_tanh, (fak + "_0",), (fak + "_0",))
                tt(AO[wi][:, :], FA[:, 0, :], FA[:, 1, :], ALU.mult, (fak + "_0", fak + "_1"), ("AO%d" % wi,),
                   eng="pool")
                dma("sp", ACT_d[:, c, :], AO[wi][:, :], reads=("AO%d" % wi,))

            pipeline(24, [g_mm, g_conv, g_gelu])

        if stop <= 7:
            return nc
        with Stage(nc, S) as st:
            AC = st.sb("AC", [128, 24, NOWN], BF16)
            WD = [st.sb("WD%d" % i, [128, 24, 512], BF16) for i in range(2)]
            for hf in range(2):
                for c0 in range(0, 24, 8):
                    wcast(WD[hf][:, c0:c0 + 8, :], w_down[c0 * 128:(c0 + 8) * 128, hf * 512:(hf + 1) * 512],
                          "WD%d" % hf)
                if hf == 0:
                    for c in range(24):
                        dma("sp", AC[:, c, :], ACT_d[:, c, :], writes=("AC",))
            GFB = st.sb("GFB", [128, D])
            dma("sp", GFB[:], gfb[:, :], writes=("GFB",))
            CM = st.sb("CM", [128, 1])
            S.op("pool", lambda e: e.memset(CM[:], -0.5), (), ("CM",))
            HMr = Rot(st, "HM", [128, D], F32, 3)
            YOr = Rot(st, "YO", [128, D], F32, 2)
            SQr = Rot(st, "SQJ", [128, D], F32, 2)
            STr = Rot(st, "STs", [128, 4], F32, 3)
            Pr = Rot(st, "PH", [128, 512], F32, 4, psum=True)
            for hf in range(2):
                for t in range(16):
                    rr = (t + 1) * 128
                    HMt, hk = HMr.nxt()
                    Pt, pk = Pr.nxt()
                    if hf == 0:
                        dma("sp", HMt[:, 0:512], HM_d[rr:rr + 128, 0:512], writes=(hk,))
                    else:
                        dma("sp", HMt[:, :], HM_d[rr:rr + 128, :], writes=(hk,))
                    for c in range(24):
                        mm(Pt[:, :], AC[:, c, t * 128:(t + 1) * 128], WD[hf][:, c, :], c == 0, c == 23,
                           ("AC", "WD%d" % hf), (pk,))
                    tt(HMt[:, hf * 512:(hf + 1) * 512], Pt[:, :], HMt[:, hf * 512:(hf + 1) * 512],
                       ALU.add, (pk, hk), (hk,))
                    if hf == 0:
                        dma("sp", HM_d[rr:rr + 128, 0:512], HMt[:, 0:512], reads=(hk,))
                    else:
                        SQJ, sqk = SQr.nxt()
                        STs, stk = STr.nxt()
                        YO, yk = YOr.nxt()
                        act(SQJ[:, :], HMt[:, :], AF.Square, (hk,), (sqk,))
                        S.op("dve", lambda e, STs=STs, SQJ=SQJ: e.reduce_sum(out=STs[:, 0:1], in_=SQJ[:, :],
                                                                              axis=AX.X), (sqk,), (stk,))
                        ts(STs[:, 1:2], STs[:, 0:1], 1.0 / D, EPS, ALU.mult, ALU.add, (stk,), (stk,))
                        tt(STs[:, 2:3], STs[:, 1:2], CM[:, :], ALU.pow, (stk, "CM"), (stk,), eng="pool")
                        stt(YO[:, :], HMt[:, :], STs[:, 2:3], GFB[:, :], ALU.mult, ALU.mult,
                            (hk, stk, "GFB"), (yk,))
                        dma("sp", out[t * 128:(t + 1) * 128, :], YO[:, :], reads=(yk,))
    return nc


def _tables(core):
    b, p = divmod(core, 4)
    n_real_pre = 2048 * p
    pad = NPRE - n_real_pre
    pos = np.arange(L, dtype=np.int64) - pad
    pos = np.maximum(pos, 0).astype(np.float32)
    half = 128
    inv_freq = (1.0 / (10000.0 ** np.linspace(0.0, 1.0, half, dtype=np.float32))).astype(np.float32)
    ang = (pos[:, None] * inv_freq[None, :]).astype(np.float32)
    cos = np.cos(ang).astype(np.float32)
    sin = np.sin(ang).astype(np.float32)
    g = (1.0 - 2.0 ** (-5.0 - np.arange(H, dtype=np.float64)))
    t = np.arange(NPRE, dtype=np.float64)
    dec = g[None, :] ** (NPRE - 1 - t)[:, None]
    dec = dec * (DK ** -0.5)
    decpre = dec.reshape(NPRE // 128, 128, H).transpose(1, 0, 2).astype(np.float32)
    i = np.arange(128, dtype=np.float64)
    ctab = np.zeros((128, 16), np.float32)
    ctab[:, 0:4] = g[None, :] ** (i + 1.0)[:, None]
    ctab[:, 4:8] = (g[None, :] ** (127.0 - i)[:, None]) * (DK ** -0.5)
    ctab[:, 8:12] = (g[None, :] ** np.maximum(15.0 - i, 0.0)[:, None]) * (DK ** -0.5)
    jj = i[:, None]
    ii = i[None, :]
    maskT = np.zeros((128, H, 128), np.float32)
    for h in range(H):
        maskT[:, h, :] = (g[h] ** (-(jj + 1.0))) * (ii >= jj) * (DK ** -0.5)
    mgrp = np.zeros((128, NPG), np.float32)
    for gi in range(NPG):
        mgrp[:, gi] = 1.0 if gi * 512 >= pad else 0.0
    return cos, sin, decpre, ctab, maskT, mgrp


def kernel(x, meta_tokens, norm_mix_g, w_in, rnn_conv_w, rnn_conv_b, rg_a_w, rg_a_b,
           rg_x_w, rg_x_b, lru_lambda, w_branch_rnn, w_branch_ret, w_out, norm_ffn_g,
           w_up, ffn_conv_w, ffn_conv_b, w_down, norm_final_g, _ret_maps=False):
    f = lambda a: np.ascontiguousarray(np.asarray(a, dtype=np.float32))
    x = f(x)
    meta = f(meta_tokens)
    pvec = np.zeros((128, 80), np.float32)
    cw = f(rnn_conv_w)[0].reshape(4, NB, 128)
    for j in range(4):
        pvec[:, j * 10:(j + 1) * 10] = cw[j].T
    pvec[:, 40:50] = f(rnn_conv_b)[0].reshape(NB, 128).T
    pvec[:, 50:60] = f(rg_a_b)[0].reshape(NB, 128).T
    pvec[:, 60:70] = f(rg_x_b)[0].reshape(NB, 128).T
    pvec[:, 70:80] = f(lru_lambda)[0].reshape(NB, 128).T
    fvec = np.zeros((128, 4, 48), np.float32)
    fw = f(ffn_conv_w)[0].reshape(3, 48, 128)
    for j in range(3):
        fvec[:, j, :] = fw[j].T
    fvec[:, 3, :] = f(ffn_conv_b)[0].reshape(48, 128).T
    rgw = np.ascontiguousarray(np.stack([f(rg_a_w)[0], f(rg_x_w)[0]], 0).transpose(2, 0, 1, 3))
    g1b = np.ascontiguousarray(np.broadcast_to(f(norm_mix_g)[0].reshape(KD, 128).T[:, :, None], (128, KD, 128)))
    g2b = np.ascontiguousarray(np.broadcast_to(f(norm_ffn_g)[0].reshape(KD, 128).T[:, :, None], (128, KD, 128)))
    gfb = np.ascontiguousarray(np.broadcast_to(f(norm_final_g)[None, :], (128, D)))
    iden = np.eye(128, dtype=np.float32)
    shared = {
        "w_in": f(w_in)[0], "w_brnn": f(w_branch_rnn)[0], "w_bret": f(w_branch_ret)[0], "w_out": f(w_out)[0],
        "w_up": f(w_up)[0], "w_down": f(w_down)[0], "rgw": rgw, "pvec": pvec, "fvec": fvec,
        "g1b": g1b, "g2b": g2b, "gfb": gfb, "iden": iden,
    }
    in_maps = []
    for core in range(8):
        b, p = divmod(core, 4)
        seq = np.concatenate([meta, x[b]], 0)
        end = NMETA + 2048 * (p + 1)
        stream = np.zeros((L, D), np.float32)
        stream[L - end:] = seq[:end]
        cos, sin, decpre, ctab, maskT, mgrp = _tables(core)
        m = dict(shared)
        m.update({"xs": stream, "cosT": cos, "sinT": sin, "decpre": decpre, "ctab": ctab,
                  "maskT": maskT, "mgrp": mgrp})
        in_maps.append(m)
    if _ret_maps:
        return in_maps
    nc = build()
    res = run_bass_kernel_spmd(nc, in_maps, core_ids=list(range(8)))
    outp = np.zeros((2, SEQ, D), np.float32)
    for core in range(8):
        b, p = divmod(core, 4)
        outp[b, p * 2048:(p + 1) * 2048] = res.results[core]["out"]
    return outp
```

```python
from contextlib import ExitStack
import numpy as np
import concourse.bass as bass
import concourse.mybir as mybir
from concourse.bass_utils import run_bass_kernel_spmd

F32 = mybir.dt.float32
BF16 = mybir.dt.bfloat16
AF = mybir.ActivationFunctionType
ALU = mybir.AluOpType
AX = mybir.AxisListType

D = 1024
KD = 8
SEQ = 8192
NMETA = 16
DRNN = 1280
NB = 10
H = 4
DK = 256
DV = 512
DFF = 3072
DIN = 10752
EPS = 1e-6
NPRE = 6144
NPG = NPRE // 512
HALO = 16
NOWN = 2048
NM = HALO + NOWN
L = NPRE + NM
NT_MAIN = 17
O_U, O_UG, O_Q, O_K, O_V, O_G, O_GA, O_GB = 0, 1280, 2560, 3584, 4608, 6656, 8704, 9728

MG = [(0, 16)] + [(16 + 512 * i, 512) for i in range(4)]


def tile_rows(t):
    if t == 0:
        return 0, 16
    return 16 + 128 * (t - 1), 128


class Sched:
    ENG = ("pe", "act", "dve", "pool", "sp")

    def __init__(self, nc, es):
        self.nc = nc
        self.q = {e: [] for e in self.ENG}
        self.sems = {}
        self.cnt = {}
        for e in self.ENG:
            self.sems[e] = es.enter_context(nc.semaphore("s_" + e))
            self.cnt[e] = 0
        self.ndma = 12
        self.dma_rr = 0
        for i in range(self.ndma):
            k = "d%d" % i
            self.sems[k] = es.enter_context(nc.semaphore("s_" + k))
            self.cnt[k] = 0
        self.waited = {e: {} for e in self.ENG}
        self.res = {}
        self.nops = 0

    def op(self, eng, fn, reads=(), writes=(), dma=False):
        deps = {}

        def add(ev):
            if ev is None:
                return
            s, v = ev
            if deps.get(s, 0) < v:
                deps[s] = v

        for r in reads:
            st = self.res.get(r)
            if st:
                add(st["w"])
        for w in writes:
            st = self.res.get(w)
            if st:
                add(st["w"])
                for s, v in st["r"].items():
                    add((s, v))
        if dma:
            k = "d%d" % self.dma_rr
            self.dma_rr = (self.dma_rr + 1) % self.ndma
            if self.cnt[k] > 0:
                add((k, 16 * self.cnt[k]))
            self.cnt[k] += 1
            ev = (k, 16 * self.cnt[k])
            inc = 16
        else:
            self.cnt[eng] += 1
            ev = (eng, self.cnt[eng])
            inc = 1
        waits = []
        for s, v in deps.items():
            if s == eng and eng == "pe":
                continue
            if self.waited[eng].get(s, 0) >= v:
                continue
            self.waited[eng][s] = v
            waits.append((s, v))
        sem = self.sems[ev[0]]
        sems = self.sems

        def emit(e, fn=fn, waits=waits, sem=sem, inc=inc):
            for s, v in waits:
                e.wait_ge(sems[s], v)
            fn(e).then_inc(sem, inc)

        self.q[eng].append(emit)
        for r in reads:
            st = self.res.setdefault(r, {"w": None, "r": {}})
            if st["r"].get(ev[0], 0) < ev[1]:
                st["r"][ev[0]] = ev[1]
        for w in writes:
            self.res[w] = {"w": ev, "r": {}}
        self.nops += 1

    def finish(self, eng="sp"):
        waits = [(k, 16 * self.cnt[k]) for k in self.sems if k[1:].isdigit() and self.cnt[k] > 0]
        sems = self.sems

        def emit(e):
            for s, v in waits:
                e.wait_ge(sems[s], v)

        self.q[eng].append(emit)


class Stage:
    _n = [0]

    def __init__(self, nc, S):
        self.nc, self.S = nc, S
        self.es = ExitStack()
        Stage._n[0] += 1
        self.pfx = "g%d_" % Stage._n[0]

    def __enter__(self):
        self.es.__enter__()
        return self

    def sb(self, name, shape, dt=F32):
        return self.es.enter_context(self.nc.sbuf_tensor(self.pfx + name, list(shape), dt))

    def ps(self, name, shape, dt=F32):
        return self.es.enter_context(self.nc.psum_tensor(self.pfx + name, list(shape), dt))

    def __exit__(self, *a):
        S = self.S
        S.barrier()
        q = S.q
        with self.nc.Block() as block:
            @block.tensor
            def _(e):
                for f in q["pe"]:
                    f(e)

            @block.scalar
            def _(e):
                for f in q["act"]:
                    f(e)

            @block.vector
            def _(e):
                for f in q["dve"]:
                    f(e)

            @block.gpsimd
            def _(e):
                for f in q["pool"]:
                    f(e)

            @block.sync
            def _(e):
                for f in q["sp"]:
                    f(e)
        S.q = {e: [] for e in S.ENG}
        S.res = {}
        return self.es.__exit__(*a)


def _barrier(self):
    targets = []
    for k in self.sems:
        v = self.cnt[k] * (16 if k[1:].isdigit() else 1)
        if v > 0:
            targets.append((k, v))
    sems = self.sems
    for eng in self.ENG:
        waits = []
        for s, v in targets:
            if self.waited[eng].get(s, 0) >= v:
                continue
            self.waited[eng][s] = v
            waits.append((s, v))

        def emit(e, waits=waits):
            for s, v in waits:
                e.wait_ge(sems[s], v)

        self.q[eng].append(emit)


Sched.barrier = _barrier


def build(stop=99, dbg=False):
    nc = bass.Bass("TRN2", target_bir_lowering=False)
    skind = "ExternalOutput" if dbg else "Internal"

    def din(name, shape, dt=F32):
        return nc.dram_tensor(name, list(shape), dt, kind="ExternalInput").ap()

    xs = din("xs", [L, D])
    w_in = din("w_in", [D, DIN])
    w_brnn = din("w_brnn", [DRNN, D])
    w_bret = din("w_bret", [H * DV, D])
    w_out = din("w_out", [D, D])
    w_up = din("w_up", [D, 2 * DFF])
    w_down = din("w_down", [DFF, D])
    rgw = din("rgw", [128, 2, NB, 128])
    pvec = din("pvec", [128, 80])
    fvec = din("fvec", [128, 4, 48])
    g1b = din("g1b", [128, KD, 128])
    g2b = din("g2b", [128, KD, 128])
    gfb = din("gfb", [128, D])
    cosT = din("cosT", [L, 128])
    sinT = din("sinT", [L, 128])
    decpre = din("decpre", [128, NPRE // 128, H])
    ctab = din("ctab", [128, 16])
    maskT = din("maskT", [128, H, 128])
    mgrp = din("mgrp", [128, NPG])
    iden_in = din("iden", [128, 128])
    out = nc.dram_tensor("out", [NOWN, D], F32, kind="ExternalOutput").ap()
    XNT_d = nc.dram_tensor("xnt_d", [128, KD, L], BF16, kind=skind).ap()
    OA_d = nc.dram_tensor("oa_d", [128, NB, NM], BF16, kind=skind).ap()
    OB_d = nc.dram_tensor("ob_d", [128, 16, NM], BF16, kind=skind).ap()
    MIX_d = nc.dram_tensor("mix_d", [128, KD, NM], BF16, kind=skind).ap()
    HM_d = nc.dram_tensor("hm_d", [NT_MAIN * 128, D], F32, kind=skind).ap()
    XN2_d = nc.dram_tensor("xn2_d", [128, KD, NM], BF16, kind=skind).ap()
    ACT_d = nc.dram_tensor("act_d", [128, 24, NOWN], BF16, kind=skind).ap()
    SF_d = nc.dram_tensor("sf_d", [128, 2 * H, DV], F32, kind=skind).ap()

    ges = ExitStack()
    with ges:
        S = Sched(nc, ges)

        def dma(eng, out_ap, in_ap, reads=(), writes=()):
            S.op(eng, lambda e: e.dma_start(out=out_ap, in_=in_ap), reads, writes, dma=True)

        def act(out_ap, in_ap, func, reads, writes, bias=None, scale=None):
            kw = {}
            if bias is not None:
                kw["bias"] = bias
            if scale is not None:
                kw["scale"] = scale
            S.op("act", lambda e: e.activation(out=out_ap, in_=in_ap, func=func, **kw), reads, writes)

        def tt(out_ap, a, b, op, reads, writes, eng="dve"):
            S.op(eng, lambda e: e.tensor_tensor(out=out_ap, in0=a, in1=b, op=op), reads, writes)

        def ts(out_ap, a, s1, s2, op0, op1, reads, writes, eng="dve"):
            if op1 is None:
                S.op(eng, lambda e: e.tensor_scalar(out=out_ap, in0=a, scalar1=s1, scalar2=None, op0=op0),
                     reads, writes)
            else:
                S.op(eng, lambda e: e.tensor_scalar(out=out_ap, in0=a, scalar1=s1, scalar2=s2, op0=op0, op1=op1),
                     reads, writes)

        def stt(out_ap, a, s, b, op0, op1, reads, writes):
            S.op("dve", lambda e: e.scalar_tensor_tensor(out=out_ap, in0=a, scalar=s, in1=b, op0=op0, op1=op1),
                 reads, writes)

        def cp(eng, out_ap, in_ap, reads, writes):
            if eng == "act":
                S.op(eng, lambda e: e.activation(out=out_ap, in_=in_ap, func=AF.Copy), reads, writes)
            else:
                S.op(eng, lambda e: e.tensor_copy(out=out_ap, in_=in_ap), reads, writes)

        def mm(out_ap, lhsT, rhs, start, stop, reads, writes):
            S.op("pe", lambda e: e.matmul(out_ap, lhsT, rhs, start=start, stop=stop), reads, writes)

        STG = [ges.enter_context(nc.sbuf_tensor("STG%d" % i, [128, 640], F32)) for i in range(4)]
        stg_i = [0]

        def wcast(dst_view, src_ap, key):
            kc = dst_view.shape[1]
            ncols = dst_view.shape[2]
            for k in range(kc):
                i = stg_i[0] % 4
                stg_i[0] += 1
                dma("sp", STG[i][:, 0:ncols], src_ap[k * 128:(k + 1) * 128, :], writes=("STG%d" % i,))
                cp("pool", dst_view[:, k, :], STG[i][:, 0:ncols], ("STG%d" % i,), (key,))

        class Rot:
            def __init__(self, st, name, shape, dt, n, psum=False):
                self.items = []
                for i in range(n):
                    t = (st.ps if psum else st.sb)("%s%d" % (name, i), shape, dt)
                    self.items.append((t, "%s%d" % (name, i)))
                self.i = 0

            def nxt(self):
                it = self.items[self.i % len(self.items)]
                self.i += 1
                return it

        def pipeline(n, phases):
            for step in range(n + len(phases) - 1):
                for pi in reversed(range(len(phases))):
                    t = step - pi
                    if 0 <= t < n:
                        phases[pi](t)

        def norm_bufs(st, T, n=3):
            T["SQJ"] = Rot(st, "SQJ", [128, D], F32, 3)
            T["ST"] = Rot(st, "ST", [128, 4], F32, n + 2)
            T["XB"] = Rot(st, "XB", [128, D], BF16, n)
            T["PT"] = Rot(st, "PT", [128, KD, 128], BF16, 2, psum=True)
            T["CM"] = st.sb("CM", [128, 1])
            S.op("pool", lambda e: e.memset(T["CM"][:], -0.5), (), ("CM",))

        def norm_phases(T, items, gb, gbkey):
            ctx = {}

            def p0(t):
                ctx[t] = dict(zip(("x_ap", "xk"), items[t]["x_fn"]()))

            def p1(t):
                c = ctx[t]
                rows = items[t]["rows"]
                SQJ, sqk = T["SQJ"].nxt()
                act(SQJ[0:rows, :], c["x_ap"], AF.Square, (c["xk"],), (sqk,))
                c.update(SQJ=SQJ, sqk=sqk)

            def p2(t):
                c = ctx[t]
                rows = items[t]["rows"]
                STt, stk = T["ST"].nxt()
                SQJ = c["SQJ"]
                S.op("dve", lambda e: e.reduce_sum(out=STt[0:rows, 0:1], in_=SQJ[0:rows, :], axis=AX.X),
                     (c["sqk"],), (stk,))
                ts(STt[0:rows, 1:2], STt[0:rows, 0:1], 1.0 / D, EPS, ALU.mult, ALU.add, (stk,), (stk,))
                tt(STt[0:rows, 2:3], STt[0:rows, 1:2], T["CM"][0:rows, :], ALU.pow, (stk, "CM"), (stk,), eng="pool")
                c.update(STt=STt, stk=stk)

            def p3(t):
                c = ctx[t]
                rows = items[t]["rows"]
                XBt, xbk = T["XB"].nxt()
                act(XBt[0:rows, :], c["x_ap"], AF.Identity, (c["xk"], c["stk"]), (xbk,), scale=c["STt"][0:rows, 2:3])
                c.update(XBt=XBt, xbk=xbk)

            def p4(t):
                c = ctx[t]
                rows = items[t]["rows"]
                PTt, ptk = T["PT"].nxt()
                XBt = c["XBt"]
                for kc in range(KD):
                    S.op("pe", lambda e, kc=kc: e.transpose(PTt[:, kc, 0:rows],
                                                            XBt[0:rows, kc * 128:(kc + 1) * 128],
                                                            T["IDB"][0:rows, 0:rows]),
                         (c["xbk"], "IDB"), (ptk,))
                c.update(PTt=PTt, ptk=ptk)

            def p5(t):
                c = ctx.pop(t)
                it = items[t]
                rows = it["rows"]
                tt(it["dst"], c["PTt"][:, :, 0:rows], gb[:, :, 0:rows], ALU.mult, (c["ptk"], gbkey), (it["dkey"],))
                if it.get("post"):
                    it["post"]()

            return [p0, p1, p2, p3, p4, p5]

        def load_iden(st, T):
            T["IDF"] = st.sb("IDF", [128, 128])
            T["IDB"] = st.sb("IDB", [128, 128], BF16)
            dma("sp", T["IDF"][:], iden_in[:, :], writes=("IDF",))
            cp("dve", T["IDB"][:], T["IDF"][:], ("IDF",), ("IDB",))

        stream_tiles = [(i * 128, 128) for i in range(NPRE // 128)] + [(NPRE, 16)] + \
                       [(NPRE + 16 + i * 128, 128) for i in range(16)]
        main_tiles = stream_tiles[NPRE // 128:]
        pre_groups = [(i * 512, 512) for i in range(NPG)]
        main_groups = [(NPRE, 16)] + [(NPRE + 16 + i * 512, 512) for i in range(4)]

        with Stage(nc, S) as st:
            T = {}
            load_iden(st, T)
            G1B = st.sb("G1B", [128, KD, 128])
            dma("sp", G1B[:], g1b[:, :, :], writes=("G1B",))
            XSr = Rot(st, "XS", [128, D], F32, 6)
            norm_bufs(st, T, 3)
            XOr = Rot(st, "XO", [128, KD, 512], BF16, 2)
            items = []
            for (c0, ncols) in pre_groups + main_groups:
                XOt, ok = XOr.nxt()
                nt = max(1, ncols // 128)
                for j in range(nt):
                    rows = min(128, ncols)
                    r0 = c0 + j * 128

                    def x_fn(r0=r0, rows=rows):
                        XSt, xk = XSr.nxt()
                        dma("sp", XSt[0:rows, :], xs[r0:r0 + rows, :], writes=(xk,))
                        return XSt[0:rows, :], xk

                    post = None
                    if j == nt - 1:
                        def post(XOt=XOt, ok=ok, c0=c0, ncols=ncols):
                            dma("sp", XNT_d[:, :, c0:c0 + ncols], XOt[:, :, 0:ncols], reads=(ok,))
                    items.append(dict(x_fn=x_fn, rows=rows, dst=XOt[:, :, j * 128:j * 128 + rows], dkey=ok, post=post))
            pipeline(len(items), norm_phases(T, items, G1B, "G1B"))

        if stop <= 1:
            return nc
        with Stage(nc, S) as st:
            WU = st.sb("WU", [128, KD, DRNN], BF16)
            WG = st.sb("WG", [128, KD, DRNN], BF16)
            RGW = st.sb("RGW", [128, 2, NB, 128], BF16)
            PV = st.sb("PV", [128, 80])
            MGR = st.sb("MGR", [128, NPG])
            BM = st.sb("BM", [128, NPG, NB])
            SC = st.sb("SC", [128, 3, NB])
            HB = st.sb("HB", [128, 2 * NB])
            dma("sp", PV[:], pvec[:, :], writes=("PV",))
            dma("sp", MGR[:], mgrp[:, :], writes=("MGR",))
            for c0 in range(0, DRNN, 640):
                wcast(WU[:, :, c0:c0 + 640], w_in[:, O_U + c0:O_U + c0 + 640], "WU")
            for ax in range(2):
                for n0 in range(0, NB, 5):
                    i = stg_i[0] % 4
                    stg_i[0] += 1
                    dma("sp", STG[i][:, 0:640].rearrange("p (n j) -> p n j", n=5), rgw[:, ax, n0:n0 + 5, :],
                        writes=("STG%d" % i,))
                    cp("pool", RGW[:, ax, n0:n0 + 5, :], STG[i][:, 0:640].rearrange("p (n j) -> p n j", n=5),
                       ("STG%d" % i,), ("RGW",))
            for c0 in range(0, DRNN, 640):
                wcast(WG[:, :, c0:c0 + 640], w_in[:, O_UG + c0:O_UG + c0 + 640], "WG")
            act(SC[:, 0, :], PV[:, 70:80], AF.Exp, ("PV",), ("SC0",), scale=-1.0)
            act(SC[:, 0, :], SC[:, 0, :], AF.Ln, ("SC0",), ("SC0",), bias=1.0)
            ts(SC[:, 1, :], SC[:, 0, :], -8.0, None, ALU.mult, None, ("SC0",), ("SC",))
            ts(SC[:, 2, :], SC[:, 0, :], -4.0, None, ALU.mult, None, ("SC0",), ("SC",))
            ts(HB[:, :], PV[:, 50:70], 0.5, None, ALU.mult, None, ("PV",), ("HB",))
            for g in range(NPG):
                ts(BM[:, g, :], PV[:, 40:50], MGR[:, g:g + 1], None, ALU.mult, None, ("PV", "MGR"), ("BM",))
            XG = [st.sb("XG%d" % i, [128, KD, 512], BF16) for i in range(2)]
            IDFb = st.sb("IDFb", [128, 128])
            dma("sp", IDFb[:], iden_in[:, :], writes=("IDFb",))
            DG = st.sb("DG", [128, 4, NB, 128], BF16)
            for j in range(4):
                for n in range(NB):
                    ts(DG[:, j, n, :], IDFb[:, :], PV[:, j * 10 + n:j * 10 + n + 1], None, ALU.mult, None,
                       ("IDFb", "PV"), ("DG",))
            UB = st.sb("UB", [128, NB, 516], BF16)
            HST = st.sb("HST", [128, NB])
            S.op("dve", lambda e: e.memset(UB[:], 0.0), (), tuple("U%d" % n for n in range(NB)))
            S.op("dve", lambda e: e.memset(HST[:], 0.0), (), tuple("HST%d" % n for n in range(NB)))
            BS = 4
            QB = st.sb("QB", [128, 1])
            S.op("pool", lambda e: e.memset(QB[:], 0.25), (), ("QB",))
            XRr = Rot(st, "XR", [128, 512], BF16, 5)
            TRr = Rot(st, "TR", [128, 512], F32, 3)
            TIr = Rot(st, "TI", [128, 512], F32, 3)
            BBr = Rot(st, "BB", [128, 512], F32, 5)
            HHr = Rot(st, "HH", [128, 512], F32, 3)
            OATr = Rot(st, "OAT", [128, 512], BF16, 3)
            GGb = [st.sb("GG%d" % i, [128, NB, 512], BF16) for i in range(3)]
            AAg = [st.sb("AAg%d" % i, [128, BS, 512]) for i in range(2)]
            OMg = [st.sb("OMg%d" % i, [128, BS, 512]) for i in range(2)]
            T1g = [st.sb("T1g%d" % i, [128, BS, 512]) for i in range(2)]
            PU = Rot(st, "PU", [128, 512], F32, 2, psum=True)
            PC = Rot(st, "PC", [128, 512], F32, 2, psum=True)
            PRA = Rot(st, "PRA", [128, 512], F32, 2, psum=True)
            PRX = Rot(st, "PRX", [128, 512], F32, 2, psum=True)
            allg = [(g, True) for g in pre_groups] + [(g, False) for g in main_groups]
            chains = []
            for gi, ((c0, ncols), is_pre) in enumerate(allg):
                for n in range(NB):
                    chains.append((gi, c0, ncols, is_pre, n))
            NCH = len(chains)
            ctx = {}

            def load_xg(gi):
                (c0, ncols), _ = allg[gi]
                dma("sp", XG[gi % 2][:, :, 0:ncols], XNT_d[:, :, c0:c0 + ncols], writes=("XG%d" % (gi % 2),))

            load_xg(0)

            def b1(ci):
                gi, c0, ncols, is_pre, n = chains[ci]
                XGt, xk = XG[gi % 2], "XG%d" % (gi % 2)
                if n == 0:
                    if gi + 1 < len(allg):
                        load_xg(gi + 1)
                    if not is_pre:
                        mgi = gi - NPG
                        GG, ggk = GGb[mgi % 3], "GG%d" % (mgi % 3)
                        for n2 in range(NB):
                            g_ps, gk = PU.nxt()
                            for kc in range(KD):
                                mm(g_ps[:, 0:ncols], WG[:, kc, n2 * 128:(n2 + 1) * 128], XGt[:, kc, 0:ncols],
                                   kc == 0, kc == KD - 1, ("WG", xk), (gk,))
                            act(GG[:, n2, 0:ncols], g_ps[:, 0:ncols], AF.Gelu_apprx_tanh, (gk,), (ggk,))
                u_ps, puk = PU.nxt()
                for kc in range(KD):
                    mm(u_ps[:, 0:ncols], WU[:, kc, n * 128:(n + 1) * 128], XGt[:, kc, 0:ncols], kc == 0,
                       kc == KD - 1, ("WU", xk), (puk,))
                ctx[ci] = dict(u_ps=u_ps, puk=puk)

            def b2(ci):
                gi, c0, ncols, is_pre, n = chains[ci]
                c = ctx[ci]
                cp("dve", UB[:, n, 3:3 + ncols], c["u_ps"][:, 0:ncols], (c["puk"],), ("U%d" % n,))

            def b3(ci):
                gi, c0, ncols, is_pre, n = chains[ci]
                c = ctx[ci]
                uk = "U%d" % n
                pc, pck = PC.nxt()
                for j in range(4):
                    mm(pc[:, 0:ncols], DG[:, j, n, :], UB[:, n, j:j + ncols], j == 0, j == 3, ("DG", uk), (pck,))
                c.update(pc=pc, pck=pck)

            def b4(ci):
                gi, c0, ncols, is_pre, n = chains[ci]
                c = ctx[ci]
                uk = "U%d" % n
                XR, xrk = XRr.nxt()
                if is_pre:
                    g = c0 // 512
                    act(XR[:, 0:ncols], c["pc"][:, 0:ncols], AF.Identity, (c["pck"], "BM"), (xrk,),
                        bias=BM[:, g, n:n + 1])
                else:
                    act(XR[:, 0:ncols], c["pc"][:, 0:ncols], AF.Identity, (c["pck"], "PV"), (xrk,),
                        bias=PV[:, 40 + n:41 + n])
                cp("dve", UB[:, n, 0:3], UB[:, n, ncols:ncols + 3], (uk,), (uk,))
                import os as _os
                if _os.environ.get("DBGXR") and not is_pre:
                    dma("sp", OA_d[:, n, c0 - NPRE:c0 - NPRE + ncols], XR[:, 0:ncols], reads=(xrk,))
                c.update(XR=XR, xrk=xrk)

            def b5(ci):
                gi, c0, ncols, is_pre, n = chains[ci]
                c = ctx[ci]
                XR, xrk = c["XR"], c["xrk"]
                ra, rak = PRA.nxt()
                rx, rxk = PRX.nxt()
                mm(ra[:, 0:ncols], RGW[:, 0, n, :], XR[:, 0:ncols], True, True, ("RGW", xrk), (rak,))
                mm(rx[:, 0:ncols], RGW[:, 1, n, :], XR[:, 0:ncols], True, True, ("RGW", xrk), (rxk,))
                c.update(ra=ra, rak=rak, rx=rx, rxk=rxk)

            def b6(ci):
                gi, c0, ncols, is_pre, n = chains[ci]
                c = ctx[ci]
                TR, trk = TRr.nxt()
                TI, tik = TIr.nxt()
                act(TR[:, 0:ncols], c["ra"][:, 0:ncols], AF.Tanh, (c["rak"], "HB"), (trk,), bias=HB[:, n:n + 1],
                    scale=0.5)
                act(TI[:, 0:ncols], c["rx"][:, 0:ncols], AF.Tanh, (c["rxk"], "HB"), (tik,),
                    bias=HB[:, NB + n:NB + n + 1], scale=0.5)
                c.update(TR=TR, trk=trk, TI=TI, tik=tik)

            def b7(ci):
                gi, c0, ncols, is_pre, n = chains[ci]
                c = ctx[ci]
                bid, slot = divmod(ci, BS)
                bb = bid % 2
                TR, trk, TI, tik, XR, xrk = c["TR"], c["trk"], c["TI"], c["tik"], c["XR"], c["xrk"]
                act(AAg[bb][:, slot, 0:ncols], TR[:, 0:ncols], AF.Exp, (trk, "SC"), ("AAg%d" % bb,),
                    scale=SC[:, 2, n:n + 1], bias=SC[:, 2, n:n + 1])
                stt(T1g[bb][:, slot, 0:ncols], TI[:, 0:ncols], 1.0, XR[:, 0:ncols], ALU.add, ALU.mult,
                    (tik, xrk), ("T1g%d" % bb,))

            def b7b(ci):
                gi, c0, ncols, is_pre, n = chains[ci]
                bid, slot = divmod(ci, BS)
                bb = bid % 2
                if ncols < 512:
                    S.op("pool", lambda e, bb=bb, slot=slot: e.memset(OMg[bb][:, slot, :], 0.0),
                         ("OMg%d" % bb,), ("OMg%d" % bb,))
                tt(OMg[bb][:, slot, 0:ncols], AAg[bb][:, slot, 0:ncols], AAg[bb][:, slot, 0:ncols], ALU.mult,
                   ("AAg%d" % bb,), ("OMg%d" % bb,), eng="pool")

            def batch_of(ci):
                if ci % BS == BS - 1 or ci == NCH - 1:
                    bid = ci // BS
                    return bid % 2, list(range(bid * BS, ci + 1))
                return None

            def b8(ci):
                b = batch_of(ci)
                if b is None:
                    return
                bb, ids = b
                nb = len(ids)
                act(OMg[bb][:, 0:nb, :], OMg[bb][:, 0:nb, :], AF.Sqrt, ("OMg%d" % bb, "QB"), ("OMg%d" % bb,),
                    scale=-0.25, bias=QB[:, 0:1])

            def b9(ci):
                b = batch_of(ci)
                if b is None:
                    return
                bb, ids = b
                for slot, cj in enumerate(ids):
                    gi, c0, ncols, is_pre, n = chains[cj]
                    BB, bbk = BBr.nxt()
                    tt(BB[:, 0:ncols], T1g[bb][:, slot, 0:ncols], OMg[bb][:, slot, 0:ncols], ALU.mult,
                       ("T1g%d" % bb, "OMg%d" % bb), (bbk,), eng="pool")
                    ctx[cj].update(BB=BB, bbk=bbk)

            def b10(ci):
                b = batch_of(ci)
                if b is None:
                    return
                bb, ids = b
                for slot, cj in enumerate(ids):
                    gi, c0, ncols, is_pre, n = chains[cj]
                    c = ctx.pop(cj)
                    BB, bbk = c["BB"], c["bbk"]
                    HH, hhk = HHr.nxt()
                    hk = "HST%d" % n
                    S.op("dve", lambda e, n=n, ncols=ncols, HH=HH, BB=BB, bb=bb, slot=slot: e.tensor_tensor_scan(
                        out=HH[:, 0:ncols], data0=AAg[bb][:, slot, 0:ncols], data1=BB[:, 0:ncols],
                        initial=HST[:, n:n + 1], op0=ALU.mult, op1=ALU.add), ("AAg%d" % bb, bbk, hk), (hhk,))
                    cp("dve", HST[:, n:n + 1], HH[:, ncols - 1:ncols], (hhk,), (hk,))
                    if not is_pre:
                        mgi = gi - NPG
                        GG, ggk = GGb[mgi % 3], "GG%d" % (mgi % 3)
                        OAT, ok = OATr.nxt()
                        tt(OAT[:, 0:ncols], HH[:, 0:ncols], GG[:, n, 0:ncols], ALU.mult, (hhk, ggk), (ok,))
                        m0 = c0 - NPRE
                        import os as _os
                        if not _os.environ.get("DBGXR"):
                            dma("sp", OA_d[:, n, m0:m0 + ncols], OAT[:, 0:ncols], reads=(ok,))

            pipeline(NCH, [b1, b2, b3, b4, b5, b6, b7, b7b, b8, b9, b10])
        if stop <= 2:
            return nc

        def rotary(T, src, skey, dst, dkey, rows, nh, c, s, ckey, sgn_eng="dve"):
            for h in range(nh):
                x1 = src[0:rows, h * DK:h * DK + 128]
                x2 = src[0:rows, h * DK + 128:(h + 1) * DK]
                R0, k0 = T["RT"].nxt()
                R1, k1 = T["RT"].nxt()
                R2, k2 = T["RT"].nxt()
                R3, k3 = T["RT"].nxt()
                tt(R0[0:rows, :], x1, c, ALU.mult, (skey, ckey), (k0,))
                tt(R1[0:rows, :], x2, s, ALU.mult, (skey, ckey), (k1,), eng="pool")
                tt(R2[0:rows, :], x2, c, ALU.mult, (skey, ckey), (k2,))
                tt(R3[0:rows, :], x1, s, ALU.mult, (skey, ckey), (k3,), eng="pool")
                tt(dst[0:rows, h * DK:h * DK + 128], R0[0:rows, :], R1[0:rows, :], ALU.subtract,
                   (k0, k1), (dkey,))
                tt(dst[0:rows, h * DK + 128:(h + 1) * DK], R2[0:rows, :], R3[0:rows, :], ALU.add,
                   (k2, k3), (dkey,))

        with Stage(nc, S) as st:
            T = {}
            WK = st.sb("WK", [128, KD, H * DK], BF16)
            WV = st.sb("WV", [128, KD, H * DV], BF16)
            for c0 in range(0, H * DK, 512):
                wcast(WK[:, :, c0:c0 + 512], w_in[:, O_K + c0:O_K + c0 + 512], "WK")
            for c0 in range(0, H * DV, 512):
                wcast(WV[:, :, c0:c0 + 512], w_in[:, O_V + c0:O_V + c0 + 512], "WV")
            DPRE = st.sb("DPRE", [128, NPRE // 128, H])
            dma("sp", DPRE[:], decpre[:, :, :], writes=("DPRE",))
            XGr = Rot(st, "XG", [128, KD, 512], BF16, 2)
            CSr = Rot(st, "CS", [128, 2, 128], F32, 3)
            T["RT"] = Rot(st, "RT", [128, 128], F32, 8)
            KFr = Rot(st, "KF", [128, H * DK], F32, 2)
            KRr = Rot(st, "KR", [128, 4, H * DK], BF16, 2)
            VVr = Rot(st, "VV", [128, 4, H * DV], BF16, 2)
            SF = st.sb("SF", [128, 2 * H, DV])
            S.op("dve", lambda e: e.memset(SF[:], 0.0), (), ("SF",))
            PK = [st.ps("PK%d" % i, [128, 512]) for i in range(2)]
            PVv = [st.ps("PV%d" % i, [128, 512]) for i in range(4)]
            PSr = Rot(st, "PS", [128, 512], F32, 2, psum=True)
            tiles = [(g, t4) for g in range(NPG) for t4 in range(4)]
            gst = {}
            ctx = {}

            def c1(i):
                g, t4 = tiles[i]
                c0 = g * 512
                if t4 == 0:
                    XGt, xk = XGr.nxt()
                    dma("sp", XGt[:, :, :], XNT_d[:, :, c0:c0 + 512], writes=(xk,))
                    gst[g] = (XGt, xk, KRr.nxt(), VVr.nxt())
                XGt, xk, (KR, krk), (VV, vvk) = gst[g]
                tile = g * 4 + t4
                r0 = c0 + t4 * 128
                CS, csk = CSr.nxt()
                dma("sp", CS[:, 0, :], cosT[r0:r0 + 128, :], writes=(csk,))
                dma("sp", CS[:, 1, :], sinT[r0:r0 + 128, :], writes=(csk,))
                KF, kfk = KFr.nxt()
                xt = lambda kc: XGt[:, kc, t4 * 128:(t4 + 1) * 128]
                for hf in range(2):
                    for kc in range(KD):
                        mm(PK[hf][:, :], xt(kc), WK[:, kc, hf * 512:(hf + 1) * 512], kc == 0, kc == KD - 1,
                           (xk, "WK"), ("PK%d" % hf,))
                for h in range(H):
                    hf, o = divmod(h, 2)
                    act(KF[:, h * DK:(h + 1) * DK], PK[hf][:, o * DK:(o + 1) * DK], AF.Identity,
                        ("PK%d" % hf, "DPRE"), (kfk,), scale=DPRE[:, tile, h:h + 1])
                for q4 in range(4):
                    for kc in range(KD):
                        mm(PVv[q4][:, :], xt(kc), WV[:, kc, q4 * 512:(q4 + 1) * 512], kc == 0, kc == KD - 1,
                           (xk, "WV"), ("PV%d" % q4,))
                    cp("act", VV[:, t4, q4 * 512:(q4 + 1) * 512], PVv[q4][:, :], ("PV%d" % q4,), (vvk,))
                ctx[i] = (KF, kfk, CS, csk)

            def c2(i):
                g, t4 = tiles[i]
                XGt, xk, (KR, krk), (VV, vvk) = gst[g]
                KF, kfk, CS, csk = ctx.pop(i)
                rotary(T, KF, kfk, KR[:, t4, :], krk, 128, H, CS[:, 0, :], CS[:, 1, :], csk)
                if t4 == 3:
                    for h in range(H):
                        for dc in range(2):
                            Pt, pk = PSr.nxt()
                            for t in range(4):
                                mm(Pt[:, :], KR[:, t, h * DK + dc * 128:h * DK + (dc + 1) * 128],
                                   VV[:, t, h * DV:(h + 1) * DV], t == 0, t == 3, (krk, vvk), (pk,))
                            tt(SF[:, h * 2 + dc, :], Pt[:, :], SF[:, h * 2 + dc, :], ALU.add,
                               (pk, "SF%d" % (h * 2 + dc)), ("SF%d" % (h * 2 + dc),))

            pipeline(len(tiles), [c1, c2])
            dma("sp", SF_d[:, :, :], SF[:], reads=tuple("SF%d" % i for i in range(2 * H)))

        if stop <= 3:
            return nc
        GAM = [1.0 - 2.0 ** (-5.0 - h) for h in range(H)]
        with Stage(nc, S) as st:
            T = {}
            load_iden(st, T)
            XM = st.sb("XM", [128, KD, NM], BF16)
            for kc in range(KD):
                dma("sp", XM[:, kc, :], XNT_d[:, kc, NPRE:NPRE + NM], writes=("XM",))
            CT = st.sb("CT", [128, 16])
            MT = st.sb("MT", [128, H, 128])
            CM = st.sb("CM", [128, 1])
            S.op("pool", lambda e: e.memset(CM[:], -0.5), (), ("CM",))
            dma("sp", CT[:], ctab[:, :], writes=("CT",))
            dma("sp", MT[:], maskT[:, :, :], writes=("MT",))
            WH = [st.sb("WH%d" % i, [128, KD, 1536], BF16) for i in range(2)]
            CSr = Rot(st, "CS", [128, 2, 128], F32, 4)
            T["RT"] = Rot(st, "RT", [128, 128], F32, 8)
            QKFr = Rot(st, "QKF", [128, 2 * DK], F32, 3)
            QKRr = Rot(st, "QKR", [128, 2 * DK], BF16, 3)
            KDr = Rot(st, "KDc", [128, DK], BF16, 5)
            VVr = Rot(st, "VV", [128, DV], BF16, 8)
            SGr = Rot(st, "SG", [128, DV], F32, 11)
            QKTr = Rot(st, "QKT", [128, 4, 128], BF16, 5)
            SCTr = Rot(st, "SCT", [128, 128], BF16, 3)
            SF = st.sb("SF", [128, 2 * H, DV])
            SBFring = [st.sb("SBFr%d" % j, [128, 2, DV], BF16) for j in range(5)]
            OFr = Rot(st, "OF", [128, DV], F32, 4)
            OBTr = Rot(st, "OBT", [128, DV], BF16, 3)
            OBOr = Rot(st, "OBO", [128, 4, 128], BF16, 2)
            SQJr = Rot(st, "SQJ", [128, DV], F32, 3)
            STr = Rot(st, "STs", [128, 4], F32, 4)
            dma("sp", SF[:], SF_d[:, :, :], writes=tuple("SF%d" % i for i in range(2 * H)))
            PQ = st.ps("PQ", [128, 512])
            PV_ = st.ps("PVm", [128, 512])
            PG = st.ps("PG", [128, 512])
            PSC = st.ps("PSC", [128, 512])
            PO = st.ps("PO", [128, 512])
            PStb = [st.ps("PSt%d" % i, [128, 512]) for i in range(2)]
            PT8 = st.ps("PT8", [128, 8, 128], BF16)
            PT = PT8[:, 0:4, :]
            PT2 = PT8[:, 4:8, :]
            its = [(h, ti) for h in range(H) for ti in range(len(main_tiles))]
            NT = len(main_tiles)
            ctx = {}
            sbf = {}

            def load_w(h):
                W = WH[h % 2]
                wk = "WH%d" % (h % 2)
                wcast(W[:, :, 0:256], w_in[:, O_Q + h * DK:O_Q + (h + 1) * DK], wk)
                wcast(W[:, :, 256:512], w_in[:, O_K + h * DK:O_K + (h + 1) * DK], wk)
                wcast(W[:, :, 512:1024], w_in[:, O_V + h * DV:O_V + (h + 1) * DV], wk)
                wcast(W[:, :, 1024:1536], w_in[:, O_G + h * DV:O_G + (h + 1) * DV], wk)

            load_w(0)

            def geo(i):
                h, ti = its[i]
                r0, rows = main_tiles[ti]
                return h, ti, r0, rows, r0 - NPRE

            def q1(i):
                h, ti, r0, rows, m0 = geo(i)
                if ti == 0 and h + 1 < H:
                    load_w(h + 1)
                W = WH[h % 2]
                wk = "WH%d" % (h % 2)
                CS, csk = CSr.nxt()
                dma("sp", CS[0:rows, 0, :], cosT[r0:r0 + rows, :], writes=(csk,))
                dma("sp", CS[0:rows, 1, :], sinT[r0:r0 + rows, :], writes=(csk,))
                xt = lambda kc: XM[:, kc, m0:m0 + rows]
                for kc in range(KD):
                    mm(PQ[0:rows, :], xt(kc), W[:, kc, 0:512], kc == 0, kc == KD - 1, ("XM", wk), ("PQ",))
                for kc in range(KD):
                    mm(PV_[0:rows, :], xt(kc), W[:, kc, 512:1024], kc == 0, kc == KD - 1, ("XM", wk), ("PVm",))
                for kc in range(KD):
                    mm(PG[0:rows, :], xt(kc), W[:, kc, 1024:1536], kc == 0, kc == KD - 1, ("XM", wk), ("PG",))
                ctx[i] = dict(CS=CS, csk=csk)

            def q2(i):
                h, ti, r0, rows, m0 = geo(i)
                c = ctx[i]
                QKF, qfk = QKFr.nxt()
                VV, vvk = VVr.nxt()
                SG, sgk = SGr.nxt()
                act(QKF[0:rows, 0:DK], PQ[0:rows, 0:DK], AF.Identity, ("PQ", "CT"), (qfk,),
                    scale=CT[0:rows, h:h + 1])
                cp("act", QKF[0:rows, DK:2 * DK], PQ[0:rows, DK:2 * DK], ("PQ",), (qfk,))
                cp("act", VV[0:rows, :], PV_[0:rows, :], ("PVm",), (vvk,))
                act(SG[0:rows, :], PG[0:rows, :], AF.Silu, ("PG",), (sgk,))
                c.update(QKF=QKF, qfk=qfk, VV=VV, vvk=vvk, SG=SG, sgk=sgk)

            def q3(i):
                h, ti, r0, rows, m0 = geo(i)
                c = ctx[i]
                QKR, qrk = QKRr.nxt()
                rotary(T, c["QKF"], c["qfk"], QKR, qrk, rows, 2, c["CS"][0:rows, 0, :], c["CS"][0:rows, 1, :],
                       c["csk"])
                KDc, kdk = KDr.nxt()
                kcol = (4 + h) if rows == 128 else (8 + h)
                ts(KDc[0:rows, :], QKR[0:rows, DK:2 * DK], CT[0:rows, kcol:kcol + 1], None, ALU.mult, None,
                   (qrk, "CT"), (kdk,))
                c.update(QKR=QKR, qrk=qrk, KDc=KDc, kdk=kdk)

            def state_mm(i, dc):
                h, ti, r0, rows, m0 = geo(i)
                c = ctx[i]
                mm(PStb[dc][:, :], c["KDc"][0:rows, dc * 128:(dc + 1) * 128], c["VV"][0:rows, :], True, True,
                   (c["kdk"], c["vvk"]), ("PSt%d" % dc,))

            def state_stt(i, dc):
                h, ti, r0, rows, m0 = geo(i)
                sfk = "SF%d" % (h * 2 + dc)
                gC = GAM[h] ** rows
                stt(SF[:, h * 2 + dc, :], SF[:, h * 2 + dc, :], gC, PStb[dc][:, :], ALU.mult, ALU.add,
                    (sfk, "PSt%d" % dc), (sfk,))

            def q4(i):
                h, ti, r0, rows, m0 = geo(i)
                c = ctx[i]
                QKR, qrk = c["QKR"], c["qrk"]
                for j in range(4):
                    S.op("pe", lambda e, j=j, rows=rows, QKR=QKR: e.transpose(
                        PT[:, j, 0:rows], QKR[0:rows, j * 128:(j + 1) * 128], T["IDB"][0:rows, 0:rows]),
                        (qrk, "IDB"), ("PT8",))
                if ti + 1 < NT:
                    state_mm(i, 0)
                    state_mm(i, 1)

            def q5(i):
                h, ti, r0, rows, m0 = geo(i)
                c = ctx[i]
                QKT, qtk = QKTr.nxt()
                cp("dve", QKT[:, :, 0:rows], PT[:, 0:4, 0:rows], ("PT8",), (qtk,))
                if ti + 1 < NT:
                    state_stt(i, 0)
                    state_stt(i, 1)
                c.update(QKT=QKT, qtk=qtk)

            def q6(i):
                h, ti, r0, rows, m0 = geo(i)
                c = ctx[i]
                QKT, qtk = c["QKT"], c["qtk"]
                for dc in range(2):
                    mm(PSC[0:rows, 0:rows], QKT[:, 2 + dc, 0:rows], QKT[:, dc, 0:rows], dc == 0, dc == 1,
                       (qtk,), ("PSC",))
                if ti + 1 < NT:
                    j = (i + 1) % 5
                    for dc in range(2):
                        cp("act", SBFring[j][:, dc, :], SF[:, h * 2 + dc, :], ("SF%d" % (h * 2 + dc),),
                           ("SBFr%d_%d" % (j, dc),))

            def q7(i):
                h, ti, r0, rows, m0 = geo(i)
                c = ctx[i]
                SCT, sck = SCTr.nxt()
                tt(SCT[0:rows, 0:rows], PSC[0:rows, 0:rows], MT[0:rows, h, 0:rows], ALU.mult,
                   ("PSC", "MT"), (sck,))
                c.update(SCT=SCT, sck=sck)

            def q8(i):
                h, ti, r0, rows, m0 = geo(i)
                c = ctx[i]
                j = i % 5
                VV, vvk, QKT, qtk, SCT, sck = c["VV"], c["vvk"], c["QKT"], c["qtk"], c["SCT"], c["sck"]
                mm(PO[0:rows, :], SCT[0:rows, 0:rows], VV[0:rows, :], True, False, (sck, vvk), ("PO",))
                for dc in range(2):
                    if ti == 0:
                        mm(PO[0:rows, :], QKT[:, dc, 0:rows], SB0[:, h * 2 + dc, :], False, dc == 1,
                           (qtk, "SB0"), ("PO",))
                    else:
                        mm(PO[0:rows, :], QKT[:, dc, 0:rows], SBFring[j][:, dc, :], False, dc == 1,
                           (qtk, "SBFr%d_%d" % (j, dc)), ("PO",))

            def q9(i):
                h, ti, r0, rows, m0 = geo(i)
                c = ctx[i]
                OF, ofk = OFr.nxt()
                SQJ, sqk = SQJr.nxt()
                cp("act", OF[0:rows, :], PO[0:rows, :], ("PO",), (ofk,))
                act(SQJ[0:rows, :], PO[0:rows, :], AF.Square, ("PO",), (sqk,))
                c.update(OF=OF, ofk=ofk, SQJ=SQJ, sqk=sqk)

            def q10(i):
                h, ti, r0, rows, m0 = geo(i)
                c = ctx[i]
                STs, stk = STr.nxt()
                SQJ, sqk = c["SQJ"], c["sqk"]
                S.op("dve", lambda e, rows=rows, STs=STs, SQJ=SQJ: e.reduce_sum(
                    out=STs[0:rows, 0:1], in_=SQJ[0:rows, :], axis=AX.X), (sqk,), (stk,))
                ts(STs[0:rows, 1:2], STs[0:rows, 0:1], 1.0 / DV, EPS, ALU.mult, ALU.add, (stk,), (stk,))
                tt(STs[0:rows, 2:3], STs[0:rows, 1:2], CM[0:rows, :], ALU.pow, (stk, "CM"), (stk,), eng="pool")
                c.update(STs=STs, stk=stk)

            def q11(i):
                h, ti, r0, rows, m0 = geo(i)
                c = ctx[i]
                OBT, obk = OBTr.nxt()
                stt(OBT[0:rows, :], c["OF"][0:rows, :], c["STs"][0:rows, 2:3], c["SG"][0:rows, :], ALU.mult,
                    ALU.mult, (c["ofk"], c["stk"], c["sgk"]), (obk,))
                c.update(OBT=OBT, obk=obk)

            def q12(i):
                h, ti, r0, rows, m0 = geo(i)
                c = ctx[i]
                OBT, obk = c["OBT"], c["obk"]
                for ec in range(4):
                    S.op("pe", lambda e, ec=ec, rows=rows, OBT=OBT: e.transpose(
                        PT2[:, ec, 0:rows], OBT[0:rows, ec * 128:(ec + 1) * 128], T["IDB"][0:rows, 0:rows]),
                        (obk, "IDB"), ("PT8",))

            def q13(i):
                h, ti, r0, rows, m0 = geo(i)
                ctx.pop(i)
                OBO, ook = OBOr.nxt()
                cp("act", OBO[:, :, 0:rows], PT2[:, :, 0:rows], ("PT8",), (ook,))
                dma("sp", OB_d[:, h * 4:(h + 1) * 4, m0:m0 + rows], OBO[:, :, 0:rows], reads=(ook,))

            SB0 = st.sb("SB0", [128, 2 * H, DV], BF16)
            cp("act", SB0[:, :, :], SF[:, :, :], tuple("SF%d" % i for i in range(2 * H)), ("SB0",))
            pipeline(len(its), [q1, q2, q3, q4, q5, q6, q7, q8, q9, q10, q11, q12, q13])

        if stop <= 4:
            return nc
        with Stage(nc, S) as st:
            XM = st.sb("XM", [128, KD, NM], BF16)
            OA = st.sb("OA", [128, NB, NM], BF16)
            OB = st.sb("OB", [128, 16, NM], BF16)
            WA = [st.sb("WA%d" % i, [128, NB, 128], BF16) for i in range(2)]
            WR = [st.sb("WR%d" % i, [128, 16, 128], BF16) for i in range(2)]
            WGa = [st.sb("WGa%d" % i, [128, KD, 128], BF16) for i in range(2)]
            WGb = [st.sb("WGb%d" % i, [128, KD, 128], BF16) for i in range(2)]
            Rt = {k: Rot(st, k, [128, 512], F32, 2) for k in ["GA", "GB", "M1", "M2"]}
            MO = [st.sb("MO%d" % i, [128, NM], BF16) for i in range(2)]
            PEr = [Rot(st, "PE%d_" % i, [128, 512], F32, 2, psum=True) for i in range(4)]

            def load_e(c):
                wi = c % 2
                cs = slice(c * 128, (c + 1) * 128)
                wcast(WA[wi][:], w_brnn[:, cs], "WA%d" % wi)
                wcast(WGa[wi][:], w_in[:, O_GA + c * 128:O_GA + (c + 1) * 128], "WGa%d" % wi)
                wcast(WR[wi][:], w_bret[:, cs], "WR%d" % wi)
                wcast(WGb[wi][:], w_in[:, O_GB + c * 128:O_GB + (c + 1) * 128], "WGb%d" % wi)

            load_e(0)
            for kc in range(NB):
                dma("sp", OA[:, kc, :], OA_d[:, kc, :], writes=("OA",))
            for kc in range(KD):
                dma("sp", XM[:, kc, :], XNT_d[:, kc, NPRE:NPRE + NM], writes=("XM",))
            for kc in range(16):
                dma("sp", OB[:, kc, :], OB_d[:, kc, :], writes=("OB",))
            eits = [(c, gi) for c in range(8) for gi in range(len(MG))]
            ectx = {}

            def e1(i):
                c, gi = eits[i]
                m0, ncols = MG[gi]
                wi = c % 2
                if gi == 0 and c + 1 < 8:
                    load_e(c + 1)
                ps_ = [r.nxt() for r in PEr]
                for k in range(NB):
                    mm(ps_[0][0][:, 0:ncols], WA[wi][:, k, :], OA[:, k, m0:m0 + ncols], k == 0, k == NB - 1,
                       ("WA%d" % wi, "OA"), (ps_[0][1],))
                for k in range(KD):
                    mm(ps_[2][0][:, 0:ncols], WGa[wi][:, k, :], XM[:, k, m0:m0 + ncols], k == 0, k == KD - 1,
                       ("WGa%d" % wi, "XM"), (ps_[2][1],))
                for k in range(16):
                    mm(ps_[1][0][:, 0:ncols], WR[wi][:, k, :], OB[:, k, m0:m0 + ncols], k == 0, k == 15,
                       ("WR%d" % wi, "OB"), (ps_[1][1],))
                for k in range(KD):
                    mm(ps_[3][0][:, 0:ncols], WGb[wi][:, k, :], XM[:, k, m0:m0 + ncols], k == 0, k == KD - 1,
                       ("WGb%d" % wi, "XM"), (ps_[3][1],))
                ectx[i] = ps_

            def e2(i):
                c, gi = eits[i]
                m0, ncols = MG[gi]
                wi = c % 2
                ps_ = ectx.pop(i)
                GA, gak = Rt["GA"].nxt()
                GB, gbk = Rt["GB"].nxt()
                M1, m1k = Rt["M1"].nxt()
                M2, m2k = Rt["M2"].nxt()
                act(GA[:, 0:ncols], ps_[2][0][:, 0:ncols], AF.Sigmoid, (ps_[2][1],), (gak,))
                act(GB[:, 0:ncols], ps_[3][0][:, 0:ncols], AF.Sigmoid, (ps_[3][1],), (gbk,))
                tt(M1[:, 0:ncols], ps_[0][0][:, 0:ncols], GA[:, 0:ncols], ALU.mult, (ps_[0][1], gak), (m1k,))
                tt(M2[:, 0:ncols], ps_[1][0][:, 0:ncols], GB[:, 0:ncols], ALU.mult, (ps_[1][1], gbk), (m2k,))
                tt(MO[wi][:, m0:m0 + ncols], M1[:, 0:ncols], M2[:, 0:ncols], ALU.add, (m1k, m2k),
                   ("MO%d" % wi,), eng="pool")
                if gi == len(MG) - 1:
                    dma("sp", MIX_d[:, c, :], MO[wi][:, :], reads=("MO%d" % wi,))

            pipeline(len(eits), [e1, e2])

        if stop <= 5:
            return nc
        with Stage(nc, S) as st:
            T = {}
            load_iden(st, T)
            MX = st.sb("MX", [128, KD, NM], BF16)
            for kc in range(KD):
                dma("sp", MX[:, kc, :], MIX_d[:, kc, :], writes=("MX",))
            WO = st.sb("WO", [128, KD, D], BF16)
            for c0 in range(0, D, 512):
                wcast(WO[:, :, c0:c0 + 512], w_out[:, c0:c0 + 512], "WO")
            G2B = st.sb("G2B", [128, KD, 128])
            dma("sp", G2B[:], g2b[:, :, :], writes=("G2B",))
            XSr = Rot(st, "XS", [128, D], F32, 3)
            HMr = Rot(st, "HM", [128, D], F32, 6)
            norm_bufs(st, T, 3)
            XOr = Rot(st, "XO", [128, KD, 128], BF16, 8)
            PF = Rot(st, "PF", [128, 512], F32, 4, psum=True)
            items = []
            for ti, (r0, rows) in enumerate(main_tiles):
                m0 = r0 - NPRE
                XOt, ok = XOr.nxt()

                def x_fn(ti=ti, r0=r0, rows=rows, m0=m0):
                    XSt, xk = XSr.nxt()
                    HMt, hk = HMr.nxt()
                    dma("sp", XSt[0:rows, :], xs[r0:r0 + rows, :], writes=(xk,))
                    for hf in range(2):
                        Pt, pk = PF.nxt()
                        for kc in range(KD):
                            mm(Pt[0:rows, :], MX[:, kc, m0:m0 + rows], WO[:, kc, hf * 512:(hf + 1) * 512],
                               kc == 0, kc == KD - 1, ("MX", "WO"), (pk,))
                        tt(HMt[0:rows, hf * 512:(hf + 1) * 512], Pt[0:rows, :],
                           XSt[0:rows, hf * 512:(hf + 1) * 512], ALU.add, (pk, xk), (hk,))
                    dma("sp", HM_d[ti * 128:ti * 128 + rows, :], HMt[0:rows, :], reads=(hk,))
                    return HMt[0:rows, :], hk

                def post(XOt=XOt, ok=ok, m0=m0, rows=rows):
                    dma("sp", XN2_d[:, :, m0:m0 + rows], XOt[:, :, 0:rows], reads=(ok,))
                items.append(dict(x_fn=x_fn, rows=rows, dst=XOt[:, :, 0:rows], dkey=ok, post=post))
            pipeline(len(items), norm_phases(T, items, G2B, "G2B"))

        if stop <= 6:
            return nc
        gh = ExitStack()
        gh.__enter__()
        AC = gh.enter_context(nc.sbuf_tensor("ACgh", [128, 24, NOWN], BF16))
        with Stage(nc, S) as st:
            X2 = st.sb("X2", [128, KD, NM], BF16)
            for kc in range(KD):
                dma("sp", X2[:, kc, :], XN2_d[:, kc, :], writes=("X2",))
            FV = st.sb("FV", [128, 4, 48])
            dma("sp", FV[:], fvec[:, :, :], writes=("FV",))
            WUg = [st.sb("WUg%d" % i, [128, KD, 128], BF16) for i in range(3)]
            WUv = [st.sb("WUv%d" % i, [128, KD, 128], BF16) for i in range(3)]
            FU = [st.sb("FU%d" % i, [128, 2, NM]) for i in range(2)]
            FAgr = Rot(st, "FAg", [128, NOWN], F32, 2)
            FAv = st.sb("FAv", [128, NOWN])
            P = [st.ps("PG%d" % i, [128, 512]) for i in range(8)]

            def load_g(c):
                w3 = c % 3
                wcast(WUg[w3][:], w_up[:, c * 128:(c + 1) * 128], "WUg%d" % w3)
                wcast(WUv[w3][:], w_up[:, DFF + c * 128:DFF + (c + 1) * 128], "WUv%d" % w3)

            load_g(0)
            load_g(1)
            pcount = [0]
            for c in range(24):
                wi = c % 2
                if c + 2 < 24:
                    load_g(c + 2)
                w3 = c % 3
                fk = "FU%d" % wi
                for gi, (m0, ncols) in enumerate(MG):
                    b0 = (pcount[0] % 4) * 2
                    pcount[0] += 1
                    pg, pv = P[b0], P[b0 + 1]
                    kg, kv = "PG%d" % b0, "PG%d" % (b0 + 1)
                    for k in range(KD):
                        mm(pg[:, 0:ncols], WUg[w3][:, k, :], X2[:, k, m0:m0 + ncols], k == 0, k == KD - 1,
                           ("WUg%d" % w3, "X2"), (kg,))
                    for k in range(KD):
                        mm(pv[:, 0:ncols], WUv[w3][:, k, :], X2[:, k, m0:m0 + ncols], k == 0, k == KD - 1,
                           ("WUv%d" % w3, "X2"), (kv,))
                    cp("act", FU[wi][:, 0, m0:m0 + ncols], pg[:, 0:ncols], (kg,), (fk,))
                    cp("act", FU[wi][:, 1, m0:m0 + ncols], pv[:, 0:ncols], (kv,), (fk,))
                FG, fgk = FAgr.nxt()
                for j, cc in ((0, c), (1, 24 + c)):
                    FAj, fj = (FG, fgk) if j == 0 else (FAv, "FAv")
                    ts(FAj[:, :], FU[wi][:, j, 14:14 + NOWN], FV[:, 0, cc:cc + 1], FV[:, 3, cc:cc + 1],
                       ALU.mult, ALU.add, (fk, "FV"), (fj,))
                    stt(FAj[:, :], FU[wi][:, j, 15:15 + NOWN], FV[:, 1, cc:cc + 1], FAj[:, :],
                        ALU.mult, ALU.add, (fk, "FV", fj), (fj,))
                    stt(FAj[:, :], FU[wi][:, j, 16:16 + NOWN], FV[:, 2, cc:cc + 1], FAj[:, :],
                        ALU.mult, ALU.add, (fk, "FV", fj), (fj,))
                act(FG[:, :], FG[:, :], AF.Gelu_apprx_tanh, (fgk,), (fgk,))
                tt(AC[:, c, :], FG[:, :], FAv[:, :], ALU.mult, (fgk, "FAv"), ("AC%d" % c,))
                if dbg:
                    dma("sp", ACT_d[:, c, :], AC[:, c, :], reads=("AC%d" % c,))

        if stop <= 7:
            gh.close()
            return nc
        with Stage(nc, S) as st:
            WD = [st.sb("WD%d" % i, [128, 24, 512], BF16) for i in range(2)]
            for hf in range(2):
                for c0 in range(0, 24, 8):
                    wcast(WD[hf][:, c0:c0 + 8, :], w_down[c0 * 128:(c0 + 8) * 128, hf * 512:(hf + 1) * 512],
                          "WD%d" % hf)
            GFB = st.sb("GFB", [128, D])
            dma("sp", GFB[:], gfb[:, :], writes=("GFB",))
            CM = st.sb("CM", [128, 1])
            S.op("pool", lambda e: e.memset(CM[:], -0.5), (), ("CM",))
            HMr = Rot(st, "HM", [128, D], F32, 3)
            YOr = Rot(st, "YO", [128, D], F32, 2)
            SQr = Rot(st, "SQJ", [128, D], F32, 2)
            STr = Rot(st, "STs", [128, 4], F32, 3)
            Pr = Rot(st, "PH", [128, 512], F32, 4, psum=True)
            for hf in range(2):
                for t in range(16):
                    rr = (t + 1) * 128
                    HMt, hk = HMr.nxt()
                    Pt, pk = Pr.nxt()
                    if hf == 0:
                        dma("sp", HMt[:, 0:512], HM_d[rr:rr + 128, 0:512], writes=(hk,))
                    else:
                        dma("sp", HMt[:, :], HM_d[rr:rr + 128, :], writes=(hk,))
                    for c in range(24):
                        mm(Pt[:, :], AC[:, c, t * 128:(t + 1) * 128], WD[hf][:, c, :], c == 0, c == 23,
                           ("WD%d" % hf,), (pk,))
                    tt(HMt[:, hf * 512:(hf + 1) * 512], Pt[:, :], HMt[:, hf * 512:(hf + 1) * 512],
                       ALU.add, (pk, hk), (hk,))
                    if hf == 0:
                        dma("sp", HM_d[rr:rr + 128, 0:512], HMt[:, 0:512], reads=(hk,))
                    else:
                        SQJ, sqk = SQr.nxt()
                        STs, stk = STr.nxt()
                        YO, yk = YOr.nxt()
                        act(SQJ[:, :], HMt[:, :], AF.Square, (hk,), (sqk,))
                        S.op("dve", lambda e, STs=STs, SQJ=SQJ: e.reduce_sum(out=STs[:, 0:1], in_=SQJ[:, :],
                                                                              axis=AX.X), (sqk,), (stk,))
                        ts(STs[:, 1:2], STs[:, 0:1], 1.0 / D, EPS, ALU.mult, ALU.add, (stk,), (stk,))
                        tt(STs[:, 2:3], STs[:, 1:2], CM[:, :], ALU.pow, (stk, "CM"), (stk,), eng="pool")
                        stt(YO[:, :], HMt[:, :], STs[:, 2:3], GFB[:, :], ALU.mult, ALU.mult,
                            (hk, stk, "GFB"), (yk,))
                        dma("sp", out[t * 128:(t + 1) * 128, :], YO[:, :], reads=(yk,))
        gh.close()
    return nc


def _tables(core):
    b, p = divmod(core, 4)
    n_real_pre = 2048 * p
    pad = NPRE - n_real_pre
    pos = np.arange(L, dtype=np.int64) - pad
    pos = np.maximum(pos, 0).astype(np.float32)
    half = 128
    inv_freq = (1.0 / (10000.0 ** np.linspace(0.0, 1.0, half, dtype=np.float32))).astype(np.float32)
    ang = (pos[:, None] * inv_freq[None, :]).astype(np.float32)
    cos = np.cos(ang).astype(np.float32)
    sin = np.sin(ang).astype(np.float32)
    g = (1.0 - 2.0 ** (-5.0 - np.arange(H, dtype=np.float64)))
    t = np.arange(NPRE, dtype=np.float64)
    dec = g[None, :] ** (NPRE - 1 - t)[:, None]
    dec = dec * (DK ** -0.5)
    decpre = dec.reshape(NPRE // 128, 128, H).transpose(1, 0, 2).astype(np.float32)
    i = np.arange(128, dtype=np.float64)
    ctab = np.zeros((128, 16), np.float32)
    ctab[:, 0:4] = g[None, :] ** (i + 1.0)[:, None]
    ctab[:, 4:8] = (g[None, :] ** (127.0 - i)[:, None]) * (DK ** -0.5)
    ctab[:, 8:12] = (g[None, :] ** np.maximum(15.0 - i, 0.0)[:, None]) * (DK ** -0.5)
    jj = i[:, None]
    ii = i[None, :]
    maskT = np.zeros((128, H, 128), np.float32)
    for h in range(H):
        maskT[:, h, :] = (g[h] ** (-(jj + 1.0))) * (ii >= jj) * (DK ** -0.5)
    mgrp = np.zeros((128, NPG), np.float32)
    for gi in range(NPG):
        mgrp[:, gi] = 1.0 if gi * 512 >= pad else 0.0
    return cos, sin, decpre, ctab, maskT, mgrp


def kernel(x, meta_tokens, norm_mix_g, w_in, rnn_conv_w, rnn_conv_b, rg_a_w, rg_a_b,
           rg_x_w, rg_x_b, lru_lambda, w_branch_rnn, w_branch_ret, w_out, norm_ffn_g,
           w_up, ffn_conv_w, ffn_conv_b, w_down, norm_final_g, _ret_maps=False):
    f = lambda a: np.ascontiguousarray(np.asarray(a, dtype=np.float32))
    x = f(x)
    meta = f(meta_tokens)
    pvec = np.zeros((128, 80), np.float32)
    cw = f(rnn_conv_w)[0].reshape(4, NB, 128)
    for j in range(4):
        pvec[:, j * 10:(j + 1) * 10] = cw[j].T
    pvec[:, 40:50] = f(rnn_conv_b)[0].reshape(NB, 128).T
    pvec[:, 50:60] = f(rg_a_b)[0].reshape(NB, 128).T
    pvec[:, 60:70] = f(rg_x_b)[0].reshape(NB, 128).T
    pvec[:, 70:80] = f(lru_lambda)[0].reshape(NB, 128).T
    fvec = np.zeros((128, 4, 48), np.float32)
    fw = f(ffn_conv_w)[0].reshape(3, 48, 128)
    for j in range(3):
        fvec[:, j, :] = fw[j].T
    fvec[:, 3, :] = f(ffn_conv_b)[0].reshape(48, 128).T
    rgw = np.ascontiguousarray(np.stack([f(rg_a_w)[0], f(rg_x_w)[0]], 0).transpose(2, 0, 1, 3))
    g1b = np.ascontiguousarray(np.broadcast_to(f(norm_mix_g)[0].reshape(KD, 128).T[:, :, None], (128, KD, 128)))
    g2b = np.ascontiguousarray(np.broadcast_to(f(norm_ffn_g)[0].reshape(KD, 128).T[:, :, None], (128, KD, 128)))
    gfb = np.ascontiguousarray(np.broadcast_to(f(norm_final_g)[None, :], (128, D)))
    iden = np.eye(128, dtype=np.float32)
    shared = {
        "w_in": f(w_in)[0], "w_brnn": f(w_branch_rnn)[0], "w_bret": f(w_branch_ret)[0], "w_out": f(w_out)[0],
        "w_up": f(w_up)[0], "w_down": f(w_down)[0], "rgw": rgw, "pvec": pvec, "fvec": fvec,
        "g1b": g1b, "g2b": g2b, "gfb": gfb, "iden": iden,
    }
    in_maps = []
    for core in range(8):
        b, p = divmod(core, 4)
        seq = np.concatenate([meta, x[b]], 0)
        end = NMETA + 2048 * (p + 1)
        stream = np.zeros((L, D), np.float32)
        stream[L - end:] = seq[:end]
        cos, sin, decpre, ctab, maskT, mgrp = _tables(core)
        m = dict(shared)
        m.update({"xs": stream, "cosT": cos, "sinT": sin, "decpre": decpre, "ctab": ctab,
                  "maskT": maskT, "mgrp": mgrp})
        in_maps.append(m)
    if _ret_maps:
        return in_maps
    nc = build()
    res = run_bass_kernel_spmd(nc, in_maps, core_ids=list(range(8)))
    outp = np.zeros((2, SEQ, D), np.float32)
    for core in range(8):
        b, p = divmod(core, 4)
        outp[b, p * 2048:(p + 1) * 2048] = res.results[core]["out"]
    return outp
```

```python
from contextlib import ExitStack
import numpy as np
import concourse.bass as bass
import concourse.mybir as mybir
from concourse.bass_utils import run_bass_kernel_spmd

F32 = mybir.dt.float32
BF16 = mybir.dt.bfloat16
AF = mybir.ActivationFunctionType
ALU = mybir.AluOpType
AX = mybir.AxisListType

D = 1024
KD = 8
SEQ = 8192
NMETA = 16
DRNN = 1280
NB = 10
H = 4
DK = 256
DV = 512
DFF = 3072
DIN = 10752
EPS = 1e-6
NPRE = 6144
NPG = NPRE // 512
HALO = 16
NOWN = 2048
NM = HALO + NOWN
L = NPRE + NM
NT_MAIN = 17
O_U, O_UG, O_Q, O_K, O_V, O_G, O_GA, O_GB = 0, 1280, 2560, 3584, 4608, 6656, 8704, 9728

MG = [(0, 16)] + [(16 + 512 * i, 512) for i in range(4)]


def tile_rows(t):
    if t == 0:
        return 0, 16
    return 16 + 128 * (t - 1), 128


class Sched:
    ENG = ("pe", "act", "dve", "pool", "sp")

    def __init__(self, nc, es):
        self.nc = nc
        self.q = {e: [] for e in self.ENG}
        self.sems = {}
        self.cnt = {}
        for e in self.ENG:
            self.sems[e] = es.enter_context(nc.semaphore("s_" + e))
            self.cnt[e] = 0
        self.ndma = 12
        self.dma_rr = 0
        for i in range(self.ndma):
            k = "d%d" % i
            self.sems[k] = es.enter_context(nc.semaphore("s_" + k))
            self.cnt[k] = 0
        self.waited = {e: {} for e in self.ENG}
        self.res = {}
        self.nops = 0

    def op(self, eng, fn, reads=(), writes=(), dma=False):
        deps = {}

        def add(ev):
            if ev is None:
                return
            s, v = ev
            if deps.get(s, 0) < v:
                deps[s] = v

        for r in reads:
            st = self.res.get(r)
            if st:
                add(st["w"])
        for w in writes:
            st = self.res.get(w)
            if st:
                add(st["w"])
                for s, v in st["r"].items():
                    add((s, v))
        if dma:
            k = "d%d" % self.dma_rr
            self.dma_rr = (self.dma_rr + 1) % self.ndma
            if self.cnt[k] > 0:
                add((k, 16 * self.cnt[k]))
            self.cnt[k] += 1
            ev = (k, 16 * self.cnt[k])
            inc = 16
        else:
            self.cnt[eng] += 1
            ev = (eng, self.cnt[eng])
            inc = 1
        waits = []
        for s, v in deps.items():
            if s == eng and eng == "pe":
                continue
            if self.waited[eng].get(s, 0) >= v:
                continue
            self.waited[eng][s] = v
            waits.append((s, v))
        sem = self.sems[ev[0]]
        sems = self.sems

        def emit(e, fn=fn, waits=waits, sem=sem, inc=inc):
            for s, v in waits:
                e.wait_ge(sems[s], v)
            fn(e).then_inc(sem, inc)

        self.q[eng].append(emit)
        for r in reads:
            st = self.res.setdefault(r, {"w": None, "r": {}})
            if st["r"].get(ev[0], 0) < ev[1]:
                st["r"][ev[0]] = ev[1]
        for w in writes:
            self.res[w] = {"w": ev, "r": {}}
        self.nops += 1

    def finish(self, eng="sp"):
        waits = [(k, 16 * self.cnt[k]) for k in self.sems if k[1:].isdigit() and self.cnt[k] > 0]
        sems = self.sems

        def emit(e):
            for s, v in waits:
                e.wait_ge(sems[s], v)

        self.q[eng].append(emit)


class Stage:
    _n = [0]

    def __init__(self, nc, S):
        self.nc, self.S = nc, S
        self.es = ExitStack()
        Stage._n[0] += 1
        self.pfx = "g%d_" % Stage._n[0]

    def __enter__(self):
        self.es.__enter__()
        return self

    def sb(self, name, shape, dt=F32):
        return self.es.enter_context(self.nc.sbuf_tensor(self.pfx + name, list(shape), dt))

    def ps(self, name, shape, dt=F32):
        return self.es.enter_context(self.nc.psum_tensor(self.pfx + name, list(shape), dt))

    def __exit__(self, *a):
        S = self.S
        S.barrier()
        q = S.q
        with self.nc.Block() as block:
            @block.tensor
            def _(e):
                for f in q["pe"]:
                    f(e)

            @block.scalar
            def _(e):
                for f in q["act"]:
                    f(e)

            @block.vector
            def _(e):
                for f in q["dve"]:
                    f(e)

            @block.gpsimd
            def _(e):
                for f in q["pool"]:
                    f(e)

            @block.sync
            def _(e):
                for f in q["sp"]:
                    f(e)
        S.q = {e: [] for e in S.ENG}
        S.res = {}
        return self.es.__exit__(*a)


def _barrier(self):
    targets = []
    for k in self.sems:
        v = self.cnt[k] * (16 if k[1:].isdigit() else 1)
        if v > 0:
            targets.append((k, v))
    sems = self.sems
    for eng in self.ENG:
        waits = []
        for s, v in targets:
            if self.waited[eng].get(s, 0) >= v:
                continue
            self.waited[eng][s] = v
            waits.append((s, v))

        def emit(e, waits=waits):
            for s, v in waits:
                e.wait_ge(sems[s], v)

        self.q[eng].append(emit)


Sched.barrier = _barrier


def build(stop=99, dbg=False):
    nc = bass.Bass("TRN2", target_bir_lowering=False)
    skind = "ExternalOutput" if dbg else "Internal"

    def din(name, shape, dt=F32):
        return nc.dram_tensor(name, list(shape), dt, kind="ExternalInput").ap()

    xs = din("xs", [L, D])
    w_in = din("w_in", [D, DIN])
    w_brnn = din("w_brnn", [DRNN, D])
    w_bret = din("w_bret", [H * DV, D])
    w_out = din("w_out", [D, D])
    w_up = din("w_up", [D, 2 * DFF])
    w_down = din("w_down", [DFF, D])
    rgw = din("rgw", [128, 2, NB, 128])
    pvec = din("pvec", [128, 80])
    fvec = din("fvec", [128, 4, 48])
    g1b = din("g1b", [128, KD, 128])
    g2b = din("g2b", [128, KD, 128])
    gfb = din("gfb", [128, D])
    cosT = din("cosT", [L, 128])
    sinT = din("sinT", [L, 128])
    decpre = din("decpre", [128, NPRE // 128, H])
    ctab = din("ctab", [128, 16])
    maskT = din("maskT", [128, H, 128])
    mgrp = din("mgrp", [128, NPG])
    iden_in = din("iden", [128, 128])
    out = nc.dram_tensor("out", [NOWN, D], F32, kind="ExternalOutput").ap()
    XNT_d = nc.dram_tensor("xnt_d", [128, KD, L], BF16, kind=skind).ap()
    OA_d = nc.dram_tensor("oa_d", [128, NB, NM], BF16, kind=skind).ap()
    OB_d = nc.dram_tensor("ob_d", [128, 16, NM], BF16, kind=skind).ap()
    MIX_d = nc.dram_tensor("mix_d", [128, KD, NM], BF16, kind=skind).ap()
    HM_d = nc.dram_tensor("hm_d", [NT_MAIN * 128, D], F32, kind=skind).ap()
    XN2_d = nc.dram_tensor("xn2_d", [128, KD, NM], BF16, kind=skind).ap()
    ACT_d = nc.dram_tensor("act_d", [128, 24, NOWN], BF16, kind=skind).ap()
    SF_d = nc.dram_tensor("sf_d", [128, 2 * H, DV], F32, kind=skind).ap()

    ges = ExitStack()
    with ges:
        S = Sched(nc, ges)

        def dma(eng, out_ap, in_ap, reads=(), writes=()):
            S.op(eng, lambda e: e.dma_start(out=out_ap, in_=in_ap), reads, writes, dma=True)

        def act(out_ap, in_ap, func, reads, writes, bias=None, scale=None):
            kw = {}
            if bias is not None:
                kw["bias"] = bias
            if scale is not None:
                kw["scale"] = scale
            S.op("act", lambda e: e.activation(out=out_ap, in_=in_ap, func=func, **kw), reads, writes)

        def tt(out_ap, a, b, op, reads, writes, eng="dve"):
            S.op(eng, lambda e: e.tensor_tensor(out=out_ap, in0=a, in1=b, op=op), reads, writes)

        def ts(out_ap, a, s1, s2, op0, op1, reads, writes, eng="dve"):
            if op1 is None:
                S.op(eng, lambda e: e.tensor_scalar(out=out_ap, in0=a, scalar1=s1, scalar2=None, op0=op0),
                     reads, writes)
            else:
                S.op(eng, lambda e: e.tensor_scalar(out=out_ap, in0=a, scalar1=s1, scalar2=s2, op0=op0, op1=op1),
                     reads, writes)

        def stt(out_ap, a, s, b, op0, op1, reads, writes):
            S.op("dve", lambda e: e.scalar_tensor_tensor(out=out_ap, in0=a, scalar=s, in1=b, op0=op0, op1=op1),
                 reads, writes)

        def cp(eng, out_ap, in_ap, reads, writes):
            if eng == "act":
                S.op(eng, lambda e: e.activation(out=out_ap, in_=in_ap, func=AF.Copy), reads, writes)
            else:
                S.op(eng, lambda e: e.tensor_copy(out=out_ap, in_=in_ap), reads, writes)

        def mm(out_ap, lhsT, rhs, start, stop, reads, writes):
            S.op("pe", lambda e: e.matmul(out_ap, lhsT, rhs, start=start, stop=stop), reads, writes)

        STG = [ges.enter_context(nc.sbuf_tensor("STG%d" % i, [128, 640], F32)) for i in range(4)]
        stg_i = [0]

        def wcast(dst_view, src_ap, key):
            kc = dst_view.shape[1]
            ncols = dst_view.shape[2]
            for k in range(kc):
                i = stg_i[0] % 4
                stg_i[0] += 1
                dma("sp", STG[i][:, 0:ncols], src_ap[k * 128:(k + 1) * 128, :], writes=("STG%d" % i,))
                cp("pool", dst_view[:, k, :], STG[i][:, 0:ncols], ("STG%d" % i,), (key,))

        class Rot:
            def __init__(self, st, name, shape, dt, n, psum=False):
                self.items = []
                for i in range(n):
                    t = (st.ps if psum else st.sb)("%s%d" % (name, i), shape, dt)
                    self.items.append((t, "%s%d" % (name, i)))
                self.i = 0

            def nxt(self):
                it = self.items[self.i % len(self.items)]
                self.i += 1
                return it

        def pipeline(n, phases):
            for step in range(n + len(phases) - 1):
                for pi in reversed(range(len(phases))):
                    t = step - pi
                    if 0 <= t < n:
                        phases[pi](t)

        def norm_bufs(st, T, n=3, npt=2):
            T["SQJ"] = Rot(st, "SQJ", [128, D], F32, 3)
            T["ST"] = Rot(st, "ST", [128, 4], F32, n + 2)
            T["XB"] = Rot(st, "XB", [128, D], BF16, n)
            T["PT"] = Rot(st, "PT", [128, KD, 128], BF16, npt, psum=True)
            T["CM"] = st.sb("CM", [128, 1])
            S.op("pool", lambda e: e.memset(T["CM"][:], -0.5), (), ("CM",))

        def norm_phases(T, items, gb, gbkey):
            ctx = {}

            def p0(t):
                ctx[t] = dict(zip(("x_ap", "xk"), items[t]["x_fn"]()))

            def p1(t):
                c = ctx[t]
                rows = items[t]["rows"]
                SQJ, sqk = T["SQJ"].nxt()
                act(SQJ[0:rows, :], c["x_ap"], AF.Square, (c["xk"],), (sqk,))
                c.update(SQJ=SQJ, sqk=sqk)

            def p2(t):
                c = ctx[t]
                rows = items[t]["rows"]
                STt, stk = T["ST"].nxt()
                SQJ = c["SQJ"]
                S.op("dve", lambda e: e.reduce_sum(out=STt[0:rows, 0:1], in_=SQJ[0:rows, :], axis=AX.X),
                     (c["sqk"],), (stk,))
                ts(STt[0:rows, 1:2], STt[0:rows, 0:1], 1.0 / D, EPS, ALU.mult, ALU.add, (stk,), (stk,))
                tt(STt[0:rows, 2:3], STt[0:rows, 1:2], T["CM"][0:rows, :], ALU.pow, (stk, "CM"), (stk,), eng="pool")
                c.update(STt=STt, stk=stk)

            def p3(t):
                c = ctx[t]
                rows = items[t]["rows"]
                XBt, xbk = T["XB"].nxt()
                act(XBt[0:rows, :], c["x_ap"], AF.Identity, (c["xk"], c["stk"]), (xbk,), scale=c["STt"][0:rows, 2:3])
                c.update(XBt=XBt, xbk=xbk)

            def p4(t):
                c = ctx[t]
                rows = items[t]["rows"]
                PTt, ptk = T["PT"].nxt()
                XBt = c["XBt"]
                for kc in range(KD):
                    S.op("pe", lambda e, kc=kc: e.transpose(PTt[:, kc, 0:rows],
                                                            XBt[0:rows, kc * 128:(kc + 1) * 128],
                                                            T["IDB"][0:rows, 0:rows]),
                         (c["xbk"], "IDB"), (ptk,))
                c.update(PTt=PTt, ptk=ptk)

            def p5(t):
                c = ctx.pop(t)
                it = items[t]
                rows = it["rows"]
                tt(it["dst"], c["PTt"][:, :, 0:rows], gb[:, :, 0:rows], ALU.mult, (c["ptk"], gbkey), (it["dkey"],))
                if it.get("post"):
                    it["post"]()

            return [p0, p1, p2, p3, p4, p5]

        def load_iden(st, T):
            T["IDF"] = st.sb("IDF", [128, 128])
            T["IDB"] = st.sb("IDB", [128, 128], BF16)
            dma("sp", T["IDF"][:], iden_in[:, :], writes=("IDF",))
            cp("dve", T["IDB"][:], T["IDF"][:], ("IDF",), ("IDB",))

        stream_tiles = [(i * 128, 128) for i in range(NPRE // 128)] + [(NPRE, 16)] + \
                       [(NPRE + 16 + i * 128, 128) for i in range(16)]
        main_tiles = stream_tiles[NPRE // 128:]
        pre_groups = [(i * 512, 512) for i in range(NPG)]
        main_groups = [(NPRE, 16)] + [(NPRE + 16 + i * 512, 512) for i in range(4)]

        def rotary(T, src, skey, dst, dkey, rows, nh, c, s, ckey, sgn_eng="dve"):
            for h in range(nh):
                x1 = src[0:rows, h * DK:h * DK + 128]
                x2 = src[0:rows, h * DK + 128:(h + 1) * DK]
                R0, k0 = T["RT"].nxt()
                R1, k1 = T["RT"].nxt()
                R2, k2 = T["RT"].nxt()
                R3, k3 = T["RT"].nxt()
                tt(R0[0:rows, :], x1, c, ALU.mult, (skey, ckey), (k0,))
                tt(R1[0:rows, :], x2, s, ALU.mult, (skey, ckey), (k1,), eng="pool")
                tt(R2[0:rows, :], x2, c, ALU.mult, (skey, ckey), (k2,))
                tt(R3[0:rows, :], x1, s, ALU.mult, (skey, ckey), (k3,), eng="pool")
                tt(dst[0:rows, h * DK:h * DK + 128], R0[0:rows, :], R1[0:rows, :], ALU.subtract,
                   (k0, k1), (dkey,))
                tt(dst[0:rows, h * DK + 128:(h + 1) * DK], R2[0:rows, :], R3[0:rows, :], ALU.add,
                   (k2, k3), (dkey,))

        with Stage(nc, S) as st:
            T = {}
            load_iden(st, T)
            G1B = st.sb("G1B", [128, KD, 128])
            dma("sp", G1B[:], g1b[:, :, :], writes=("G1B",))
            WK = st.sb("WK", [128, KD, H * DK], BF16)
            WV = st.sb("WV", [128, KD, H * DV], BF16)
            for c0 in range(0, H * DK, 512):
                wcast(WK[:, :, c0:c0 + 512], w_in[:, O_K + c0:O_K + c0 + 512], "WK")
            for c0 in range(0, H * DV, 512):
                wcast(WV[:, :, c0:c0 + 512], w_in[:, O_V + c0:O_V + c0 + 512], "WV")
            DPRE = st.sb("DPRE", [128, NPRE // 128, H])
            dma("sp", DPRE[:], decpre[:, :, :], writes=("DPRE",))
            XSr = Rot(st, "XS", [128, D], F32, 6)
            norm_bufs(st, T, 3, npt=1)
            XOr = Rot(st, "XO", [128, KD, 512], BF16, 2)
            CSr = Rot(st, "CS", [128, 2, 128], F32, 3)
            T["RT"] = Rot(st, "RT", [128, 128], F32, 8)
            KFr = Rot(st, "KF", [128, H * DK], F32, 2)
            KRr = Rot(st, "KR", [128, 4, H * DK], BF16, 2)
            VVr = Rot(st, "VV", [128, 4, H * DV], BF16, 2)
            SF = st.sb("SF", [128, 2 * H, DV])
            S.op("dve", lambda e: e.memset(SF[:], 0.0), (), tuple("SF%d" % i for i in range(2 * H)))
            PK = [st.ps("PK%d" % i, [128, 512]) for i in range(2)]
            PVv = [st.ps("PV%d" % i, [128, 512]) for i in range(2)]
            PSr = Rot(st, "PS", [128, 512], F32, 2, psum=True)
            items = []
            tinfo = []
            for gidx, (c0, ncols) in enumerate(pre_groups + main_groups):
                XOt, ok = XOr.nxt()
                nt = max(1, ncols // 128)
                is_pre = gidx < NPG
                for j in range(nt):
                    rows = min(128, ncols)
                    r0 = c0 + j * 128

                    def x_fn(r0=r0, rows=rows):
                        XSt, xk = XSr.nxt()
                        dma("sp", XSt[0:rows, :], xs[r0:r0 + rows, :], writes=(xk,))
                        return XSt[0:rows, :], xk

                    post = None
                    if j == nt - 1:
                        def post(XOt=XOt, ok=ok, c0=c0, ncols=ncols, nt=nt):
                            dma("sp", XNT_d[:, :, c0:c0 + ncols], XOt[:, :, 0:ncols],
                                reads=tuple("%s_%d" % (ok, jj) for jj in range(nt)))
                    items.append(dict(x_fn=x_fn, rows=rows, dst=XOt[:, :, j * 128:j * 128 + rows],
                                      dkey="%s_%d" % (ok, j), post=post))
                    tinfo.append((gidx, j, is_pre, XOt, "%s_%d" % (ok, j), r0))
            gst = {}
            ctx = {}

            def c1(i):
                g, t4, is_pre, XGt, xk, r0 = tinfo[i]
                if not is_pre:
                    return
                if t4 == 0:
                    gst[g] = (KRr.nxt(), VVr.nxt())
                (KR, krk), (VV, vvk) = gst[g]
                tile = g * 4 + t4
                CS, csk = CSr.nxt()
                dma("sp", CS[:, 0, :], cosT[r0:r0 + 128, :], writes=(csk,))
                dma("sp", CS[:, 1, :], sinT[r0:r0 + 128, :], writes=(csk,))
                KF, kfk = KFr.nxt()
                xt = lambda kc: XGt[:, kc, t4 * 128:(t4 + 1) * 128]
                for hf in range(2):
                    for kc in range(KD):
                        mm(PK[hf][:, :], xt(kc), WK[:, kc, hf * 512:(hf + 1) * 512], kc == 0, kc == KD - 1,
                           (xk, "WK"), ("PK%d" % hf,))
                for h in range(H):
                    hf, o = divmod(h, 2)
                    act(KF[:, h * DK:(h + 1) * DK], PK[hf][:, o * DK:(o + 1) * DK], AF.Identity,
                        ("PK%d" % hf, "DPRE"), (kfk,), scale=DPRE[:, tile, h:h + 1])
                for q4 in range(4):
                    pv, pvk = PVv[q4 % 2], "PV%d" % (q4 % 2)
                    for kc in range(KD):
                        mm(pv[:, :], xt(kc), WV[:, kc, q4 * 512:(q4 + 1) * 512], kc == 0, kc == KD - 1,
                           (xk, "WV"), (pvk,))
                    cp("act", VV[:, t4, q4 * 512:(q4 + 1) * 512], pv[:, :], (pvk,), (vvk,))
                ctx[i] = (KF, kfk, CS, csk)

            def c2(i):
                g, t4, is_pre, XGt, xk, r0 = tinfo[i]
                if not is_pre:
                    return
                (KR, krk), (VV, vvk) = gst[g]
                KF, kfk, CS, csk = ctx.pop(i)
                rotary(T, KF, kfk, KR[:, t4, :], krk, 128, H, CS[:, 0, :], CS[:, 1, :], csk)
                if t4 == 3:
                    for h in range(H):
                        for dc in range(2):
                            Pt, pk = PSr.nxt()
                            for t in range(4):
                                mm(Pt[:, :], KR[:, t, h * DK + dc * 128:h * DK + (dc + 1) * 128],
                                   VV[:, t, h * DV:(h + 1) * DV], t == 0, t == 3, (krk, vvk), (pk,))
                            tt(SF[:, h * 2 + dc, :], Pt[:, :], SF[:, h * 2 + dc, :], ALU.add,
                               (pk, "SF%d" % (h * 2 + dc)), ("SF%d" % (h * 2 + dc),))

            pipeline(len(items), norm_phases(T, items, G1B, "G1B") + [c1, c2])
            dma("sp", SF_d[:, :, :], SF[:], reads=tuple("SF%d" % i for i in range(2 * H)))

        if stop <= 1:
            return nc
        with Stage(nc, S) as st:
            WU = st.sb("WU", [128, KD, DRNN], BF16)
            WG = st.sb("WG", [128, KD, DRNN], BF16)
            RGW = st.sb("RGW", [128, 2, NB, 128], BF16)
            PV = st.sb("PV", [128, 80])
            MGR = st.sb("MGR", [128, NPG])
            BM = st.sb("BM", [128, NPG, NB])
            SC = st.sb("SC", [128, 3, NB])
            HB = st.sb("HB", [128, 2 * NB])
            dma("sp", PV[:], pvec[:, :], writes=("PV",))
            dma("sp", MGR[:], mgrp[:, :], writes=("MGR",))
            for c0 in range(0, DRNN, 640):
                wcast(WU[:, :, c0:c0 + 640], w_in[:, O_U + c0:O_U + c0 + 640], "WU")
            for ax in range(2):
                for n0 in range(0, NB, 5):
                    i = stg_i[0] % 4
                    stg_i[0] += 1
                    dma("sp", STG[i][:, 0:640].rearrange("p (n j) -> p n j", n=5), rgw[:, ax, n0:n0 + 5, :],
                        writes=("STG%d" % i,))
                    cp("pool", RGW[:, ax, n0:n0 + 5, :], STG[i][:, 0:640].rearrange("p (n j) -> p n j", n=5),
                       ("STG%d" % i,), ("RGW",))
            for c0 in range(0, DRNN, 640):
                wcast(WG[:, :, c0:c0 + 640], w_in[:, O_UG + c0:O_UG + c0 + 640], "WG")
            act(SC[:, 0, :], PV[:, 70:80], AF.Exp, ("PV",), ("SC0",), scale=-1.0)
            act(SC[:, 0, :], SC[:, 0, :], AF.Ln, ("SC0",), ("SC0",), bias=1.0)
            ts(SC[:, 1, :], SC[:, 0, :], -8.0, None, ALU.mult, None, ("SC0",), ("SC",))
            ts(SC[:, 2, :], SC[:, 0, :], -4.0, None, ALU.mult, None, ("SC0",), ("SC",))
            ts(HB[:, :], PV[:, 50:70], 0.5, None, ALU.mult, None, ("PV",), ("HB",))
            for g in range(NPG):
                ts(BM[:, g, :], PV[:, 40:50], MGR[:, g:g + 1], None, ALU.mult, None, ("PV", "MGR"), ("BM",))
            XG = [st.sb("XG%d" % i, [128, KD, 512], BF16) for i in range(2)]
            IDFb = st.sb("IDFb", [128, 128])
            dma("sp", IDFb[:], iden_in[:, :], writes=("IDFb",))
            DG = st.sb("DG", [128, 4, NB, 128], BF16)
            for j in range(4):
                for n in range(NB):
                    ts(DG[:, j, n, :], IDFb[:, :], PV[:, j * 10 + n:j * 10 + n + 1], None, ALU.mult, None,
                       ("IDFb", "PV"), ("DG",))
            UB = st.sb("UB", [128, NB, 516], BF16)
            HST = st.sb("HST", [128, NB])
            S.op("dve", lambda e: e.memset(UB[:], 0.0), (), tuple("U%d" % n for n in range(NB)))
            S.op("dve", lambda e: e.memset(HST[:], 0.0), (), tuple("HST%d" % n for n in range(NB)))
            BS = 4
            QB = st.sb("QB", [128, 1])
            S.op("pool", lambda e: e.memset(QB[:], 0.25), (), ("QB",))
            XRr = Rot(st, "XR", [128, 512], BF16, 5)
            TRr = Rot(st, "TR", [128, 512], F32, 3)
            TIr = Rot(st, "TI", [128, 512], F32, 3)
            BBr = Rot(st, "BB", [128, 512], F32, 5)
            HHr = Rot(st, "HH", [128, 512], F32, 3)
            OATr = Rot(st, "OAT", [128, 512], BF16, 3)
            GGb = [st.sb("GG%d" % i, [128, NB, 512], BF16) for i in range(3)]
            AAg = [st.sb("AAg%d" % i, [128, BS, 512]) for i in range(2)]
            OMg = [st.sb("OMg%d" % i, [128, BS, 512]) for i in range(2)]
            T1g = [st.sb("T1g%d" % i, [128, BS, 512]) for i in range(2)]
            PU = Rot(st, "PU", [128, 512], F32, 2, psum=True)
            PC = Rot(st, "PC", [128, 512], F32, 2, psum=True)
            PRA = Rot(st, "PRA", [128, 512], F32, 2, psum=True)
            PRX = Rot(st, "PRX", [128, 512], F32, 2, psum=True)
            allg = [(g, True) for g in pre_groups] + [(g, False) for g in main_groups]
            chains = []
            for gi, ((c0, ncols), is_pre) in enumerate(allg):
                for n in range(NB):
                    chains.append((gi, c0, ncols, is_pre, n))
            NCH = len(chains)
            ctx = {}

            def load_xg(gi):
                (c0, ncols), _ = allg[gi]
                dma("sp", XG[gi % 2][:, :, 0:ncols], XNT_d[:, :, c0:c0 + ncols], writes=("XG%d" % (gi % 2),))

            load_xg(0)

            def b1(ci):
                gi, c0, ncols, is_pre, n = chains[ci]
                XGt, xk = XG[gi % 2], "XG%d" % (gi % 2)
                if n == 0:
                    if gi + 1 < len(allg):
                        load_xg(gi + 1)
                    if not is_pre:
                        mgi = gi - NPG
                        GG, ggk = GGb[mgi % 3], "GG%d" % (mgi % 3)
                        for n2 in range(NB):
                            g_ps, gk = PU.nxt()
                            for kc in range(KD):
                                mm(g_ps[:, 0:ncols], WG[:, kc, n2 * 128:(n2 + 1) * 128], XGt[:, kc, 0:ncols],
                                   kc == 0, kc == KD - 1, ("WG", xk), (gk,))
                            act(GG[:, n2, 0:ncols], g_ps[:, 0:ncols], AF.Gelu_apprx_tanh, (gk,), (ggk,))
                u_ps, puk = PU.nxt()
                for kc in range(KD):
                    mm(u_ps[:, 0:ncols], WU[:, kc, n * 128:(n + 1) * 128], XGt[:, kc, 0:ncols], kc == 0,
                       kc == KD - 1, ("WU", xk), (puk,))
                ctx[ci] = dict(u_ps=u_ps, puk=puk)

            def b2(ci):
                gi, c0, ncols, is_pre, n = chains[ci]
                c = ctx[ci]
                cp("dve", UB[:, n, 3:3 + ncols], c["u_ps"][:, 0:ncols], (c["puk"],), ("U%d" % n,))

            def b3(ci):
                gi, c0, ncols, is_pre, n = chains[ci]
                c = ctx[ci]
                uk = "U%d" % n
                pc, pck = PC.nxt()
                for j in range(4):
                    mm(pc[:, 0:ncols], DG[:, j, n, :], UB[:, n, j:j + ncols], j == 0, j == 3, ("DG", uk), (pck,))
                c.update(pc=pc, pck=pck)

            def b4(ci):
                gi, c0, ncols, is_pre, n = chains[ci]
                c = ctx[ci]
                uk = "U%d" % n
                XR, xrk = XRr.nxt()
                if is_pre:
                    g = c0 // 512
                    act(XR[:, 0:ncols], c["pc"][:, 0:ncols], AF.Identity, (c["pck"], "BM"), (xrk,),
                        bias=BM[:, g, n:n + 1])
                else:
                    act(XR[:, 0:ncols], c["pc"][:, 0:ncols], AF.Identity, (c["pck"], "PV"), (xrk,),
                        bias=PV[:, 40 + n:41 + n])
                cp("dve", UB[:, n, 0:3], UB[:, n, ncols:ncols + 3], (uk,), (uk,))
                import os as _os
                if _os.environ.get("DBGXR") and not is_pre:
                    dma("sp", OA_d[:, n, c0 - NPRE:c0 - NPRE + ncols], XR[:, 0:ncols], reads=(xrk,))
                c.update(XR=XR, xrk=xrk)

            def b5(ci):
                gi, c0, ncols, is_pre, n = chains[ci]
                c = ctx[ci]
                XR, xrk = c["XR"], c["xrk"]
                ra, rak = PRA.nxt()
                rx, rxk = PRX.nxt()
                mm(ra[:, 0:ncols], RGW[:, 0, n, :], XR[:, 0:ncols], True, True, ("RGW", xrk), (rak,))
                mm(rx[:, 0:ncols], RGW[:, 1, n, :], XR[:, 0:ncols], True, True, ("RGW", xrk), (rxk,))
                c.update(ra=ra, rak=rak, rx=rx, rxk=rxk)

            def b6(ci):
                gi, c0, ncols, is_pre, n = chains[ci]
                c = ctx[ci]
                TR, trk = TRr.nxt()
                TI, tik = TIr.nxt()
                act(TR[:, 0:ncols], c["ra"][:, 0:ncols], AF.Tanh, (c["rak"], "HB"), (trk,), bias=HB[:, n:n + 1],
                    scale=0.5)
                act(TI[:, 0:ncols], c["rx"][:, 0:ncols], AF.Tanh, (c["rxk"], "HB"), (tik,),
                    bias=HB[:, NB + n:NB + n + 1], scale=0.5)
                c.update(TR=TR, trk=trk, TI=TI, tik=tik)

            def b7(ci):
                gi, c0, ncols, is_pre, n = chains[ci]
                c = ctx[ci]
                bid, slot = divmod(ci, BS)
                bb = bid % 2
                TR, trk, TI, tik, XR, xrk = c["TR"], c["trk"], c["TI"], c["tik"], c["XR"], c["xrk"]
                act(AAg[bb][:, slot, 0:ncols], TR[:, 0:ncols], AF.Exp, (trk, "SC"), ("AAg%d" % bb,),
                    scale=SC[:, 2, n:n + 1], bias=SC[:, 2, n:n + 1])
                stt(T1g[bb][:, slot, 0:ncols], TI[:, 0:ncols], 1.0, XR[:, 0:ncols], ALU.add, ALU.mult,
                    (tik, xrk), ("T1g%d" % bb,))

            def b7b(ci):
                gi, c0, ncols, is_pre, n = chains[ci]
                bid, slot = divmod(ci, BS)
                bb = bid % 2
                if ncols < 512:
                    S.op("pool", lambda e, bb=bb, slot=slot: e.memset(OMg[bb][:, slot, :], 0.0),
                         ("OMg%d" % bb,), ("OMg%d" % bb,))
                tt(OMg[bb][:, slot, 0:ncols], AAg[bb][:, slot, 0:ncols], AAg[bb][:, slot, 0:ncols], ALU.mult,
                   ("AAg%d" % bb,), ("OMg%d" % bb,), eng="pool")

            def batch_of(ci):
                if ci % BS == BS - 1 or ci == NCH - 1:
                    bid = ci // BS
                    return bid % 2, list(range(bid * BS, ci + 1))
                return None

            def b8(ci):
                b = batch_of(ci)
                if b is None:
                    return
                bb, ids = b
                nb = len(ids)
                act(OMg[bb][:, 0:nb, :], OMg[bb][:, 0:nb, :], AF.Sqrt, ("OMg%d" % bb, "QB"), ("OMg%d" % bb,),
                    scale=-0.25, bias=QB[:, 0:1])

            def b9(ci):
                b = batch_of(ci)
                if b is None:
                    return
                bb, ids = b
                for slot, cj in enumerate(ids):
                    gi, c0, ncols, is_pre, n = chains[cj]
                    BB, bbk = BBr.nxt()
                    tt(BB[:, 0:ncols], T1g[bb][:, slot, 0:ncols], OMg[bb][:, slot, 0:ncols], ALU.mult,
                       ("T1g%d" % bb, "OMg%d" % bb), (bbk,), eng="pool")
                    ctx[cj].update(BB=BB, bbk=bbk)

            def b10(ci):
                b = batch_of(ci)
                if b is None:
                    return
                bb, ids = b
                for slot, cj in enumerate(ids):
                    gi, c0, ncols, is_pre, n = chains[cj]
                    c = ctx.pop(cj)
                    BB, bbk = c["BB"], c["bbk"]
                    HH, hhk = HHr.nxt()
                    hk = "HST%d" % n
                    S.op("dve", lambda e, n=n, ncols=ncols, HH=HH, BB=BB, bb=bb, slot=slot: e.tensor_tensor_scan(
                        out=HH[:, 0:ncols], data0=AAg[bb][:, slot, 0:ncols], data1=BB[:, 0:ncols],
                        initial=HST[:, n:n + 1], op0=ALU.mult, op1=ALU.add), ("AAg%d" % bb, bbk, hk), (hhk,))
                    cp("dve", HST[:, n:n + 1], HH[:, ncols - 1:ncols], (hhk,), (hk,))
                    if not is_pre:
                        mgi = gi - NPG
                        GG, ggk = GGb[mgi % 3], "GG%d" % (mgi % 3)
                        OAT, ok = OATr.nxt()
                        tt(OAT[:, 0:ncols], HH[:, 0:ncols], GG[:, n, 0:ncols], ALU.mult, (hhk, ggk), (ok,))
                        m0 = c0 - NPRE
                        import os as _os
                        if not _os.environ.get("DBGXR"):
                            dma("sp", OA_d[:, n, m0:m0 + ncols], OAT[:, 0:ncols], reads=(ok,))

            pipeline(NCH, [b1, b2, b3, b4, b5, b6, b7, b7b, b8, b9, b10])
        if stop <= 2:
            return nc

        GAM = [1.0 - 2.0 ** (-5.0 - h) for h in range(H)]
        with Stage(nc, S) as st:
            T = {}
            load_iden(st, T)
            XM = st.sb("XM", [128, KD, NM], BF16)
            for kc in range(KD):
                dma("sp", XM[:, kc, :], XNT_d[:, kc, NPRE:NPRE + NM], writes=("XM",))
            CT = st.sb("CT", [128, 16])
            MT = st.sb("MT", [128, H, 128])
            CM = st.sb("CM", [128, 1])
            S.op("pool", lambda e: e.memset(CM[:], -0.5), (), ("CM",))
            dma("sp", CT[:], ctab[:, :], writes=("CT",))
            dma("sp", MT[:], maskT[:, :, :], writes=("MT",))
            WH = [st.sb("WH%d" % i, [128, KD, 1536], BF16) for i in range(2)]
            CSr = Rot(st, "CS", [128, 2, 128], F32, 4)
            T["RT"] = Rot(st, "RT", [128, 128], F32, 8)
            QKFr = Rot(st, "QKF", [128, 2 * DK], F32, 3)
            QKRr = Rot(st, "QKR", [128, 2 * DK], BF16, 3)
            KDr = Rot(st, "KDc", [128, DK], BF16, 5)
            VVr = Rot(st, "VV", [128, DV], BF16, 8)
            SGr = Rot(st, "SG", [128, DV], F32, 11)
            QKTr = Rot(st, "QKT", [128, 4, 128], BF16, 5)
            SCTr = Rot(st, "SCT", [128, 128], BF16, 3)
            SF = st.sb("SF", [128, 2 * H, DV])
            SBFring = [st.sb("SBFr%d" % j, [128, 2, DV], BF16) for j in range(5)]
            OFr = Rot(st, "OF", [128, DV], F32, 4)
            OBTr = Rot(st, "OBT", [128, DV], BF16, 3)
            OBOr = Rot(st, "OBO", [128, 4, 128], BF16, 2)
            SQJr = Rot(st, "SQJ", [128, DV], F32, 3)
            STr = Rot(st, "STs", [128, 4], F32, 4)
            dma("sp", SF[:], SF_d[:, :, :], writes=tuple("SF%d" % i for i in range(2 * H)))
            PQ = st.ps("PQ", [128, 512])
            PV_ = st.ps("PVm", [128, 512])
            PG = st.ps("PG", [128, 512])
            PSC = st.ps("PSC", [128, 512])
            PO = st.ps("PO", [128, 512])
            PStb = [st.ps("PSt%d" % i, [128, 512]) for i in range(2)]
            PT8 = st.ps("PT8", [128, 8, 128], BF16)
            PT = PT8[:, 0:4, :]
            PT2 = PT8[:, 4:8, :]
            its = [(h, ti) for h in range(H) for ti in range(len(main_tiles))]
            NT = len(main_tiles)
            ctx = {}
            sbf = {}

            def load_w(h):
                W = WH[h % 2]
                wk = "WH%d" % (h % 2)
                wcast(W[:, :, 0:256], w_in[:, O_Q + h * DK:O_Q + (h + 1) * DK], wk)
                wcast(W[:, :, 256:512], w_in[:, O_K + h * DK:O_K + (h + 1) * DK], wk)
                wcast(W[:, :, 512:1024], w_in[:, O_V + h * DV:O_V + (h + 1) * DV], wk)
                wcast(W[:, :, 1024:1536], w_in[:, O_G + h * DV:O_G + (h + 1) * DV], wk)

            load_w(0)

            def geo(i):
                h, ti = its[i]
                r0, rows = main_tiles[ti]
                return h, ti, r0, rows, r0 - NPRE

            def q1(i):
                h, ti, r0, rows, m0 = geo(i)
                if ti == 0 and h + 1 < H:
                    load_w(h + 1)
                W = WH[h % 2]
                wk = "WH%d" % (h % 2)
                CS, csk = CSr.nxt()
                dma("sp", CS[0:rows, 0, :], cosT[r0:r0 + rows, :], writes=(csk,))
                dma("sp", CS[0:rows, 1, :], sinT[r0:r0 + rows, :], writes=(csk,))
                xt = lambda kc: XM[:, kc, m0:m0 + rows]
                for kc in range(KD):
                    mm(PQ[0:rows, :], xt(kc), W[:, kc, 0:512], kc == 0, kc == KD - 1, ("XM", wk), ("PQ",))
                for kc in range(KD):
                    mm(PV_[0:rows, :], xt(kc), W[:, kc, 512:1024], kc == 0, kc == KD - 1, ("XM", wk), ("PVm",))
                for kc in range(KD):
                    mm(PG[0:rows, :], xt(kc), W[:, kc, 1024:1536], kc == 0, kc == KD - 1, ("XM", wk), ("PG",))
                ctx[i] = dict(CS=CS, csk=csk)

            def q2(i):
                h, ti, r0, rows, m0 = geo(i)
                c = ctx[i]
                QKF, qfk = QKFr.nxt()
                VV, vvk = VVr.nxt()
                SG, sgk = SGr.nxt()
                act(QKF[0:rows, 0:DK], PQ[0:rows, 0:DK], AF.Identity, ("PQ", "CT"), (qfk,),
                    scale=CT[0:rows, h:h + 1])
                cp("act", QKF[0:rows, DK:2 * DK], PQ[0:rows, DK:2 * DK], ("PQ",), (qfk,))
                cp("act", VV[0:rows, :], PV_[0:rows, :], ("PVm",), (vvk,))
                act(SG[0:rows, :], PG[0:rows, :], AF.Silu, ("PG",), (sgk,))
                c.update(QKF=QKF, qfk=qfk, VV=VV, vvk=vvk, SG=SG, sgk=sgk)

            def q3(i):
                h, ti, r0, rows, m0 = geo(i)
                c = ctx[i]
                QKR, qrk = QKRr.nxt()
                rotary(T, c["QKF"], c["qfk"], QKR, qrk, rows, 2, c["CS"][0:rows, 0, :], c["CS"][0:rows, 1, :],
                       c["csk"])
                KDc, kdk = KDr.nxt()
                kcol = (4 + h) if rows == 128 else (8 + h)
                ts(KDc[0:rows, :], QKR[0:rows, DK:2 * DK], CT[0:rows, kcol:kcol + 1], None, ALU.mult, None,
                   (qrk, "CT"), (kdk,))
                c.update(QKR=QKR, qrk=qrk, KDc=KDc, kdk=kdk)

            def state_mm(i, dc):
                h, ti, r0, rows, m0 = geo(i)
                c = ctx[i]
                mm(PStb[dc][:, :], c["KDc"][0:rows, dc * 128:(dc + 1) * 128], c["VV"][0:rows, :], True, True,
                   (c["kdk"], c["vvk"]), ("PSt%d" % dc,))

            def state_stt(i, dc):
                h, ti, r0, rows, m0 = geo(i)
                sfk = "SF%d" % (h * 2 + dc)
                gC = GAM[h] ** rows
                stt(SF[:, h * 2 + dc, :], SF[:, h * 2 + dc, :], gC, PStb[dc][:, :], ALU.mult, ALU.add,
                    (sfk, "PSt%d" % dc), (sfk,))

            def q4(i):
                h, ti, r0, rows, m0 = geo(i)
                c = ctx[i]
                QKR, qrk = c["QKR"], c["qrk"]
                for j in range(4):
                    S.op("pe", lambda e, j=j, rows=rows, QKR=QKR: e.transpose(
                        PT[:, j, 0:rows], QKR[0:rows, j * 128:(j + 1) * 128], T["IDB"][0:rows, 0:rows]),
                        (qrk, "IDB"), ("PT8",))
                if ti + 1 < NT:
                    state_mm(i, 0)
                    state_mm(i, 1)

            def q5(i):
                h, ti, r0, rows, m0 = geo(i)
                c = ctx[i]
                QKT, qtk = QKTr.nxt()
                cp("dve", QKT[:, :, 0:rows], PT[:, 0:4, 0:rows], ("PT8",), (qtk,))
                if ti + 1 < NT:
                    state_stt(i, 0)
                    state_stt(i, 1)
                c.update(QKT=QKT, qtk=qtk)

            def q6(i):
                h, ti, r0, rows, m0 = geo(i)
                c = ctx[i]
                QKT, qtk = c["QKT"], c["qtk"]
                for dc in range(2):
                    mm(PSC[0:rows, 0:rows], QKT[:, 2 + dc, 0:rows], QKT[:, dc, 0:rows], dc == 0, dc == 1,
                       (qtk,), ("PSC",))
                if ti + 1 < NT:
                    j = (i + 1) % 5
                    for dc in range(2):
                        cp("act", SBFring[j][:, dc, :], SF[:, h * 2 + dc, :], ("SF%d" % (h * 2 + dc),),
                           ("SBFr%d_%d" % (j, dc),))

            def q7(i):
                h, ti, r0, rows, m0 = geo(i)
                c = ctx[i]
                SCT, sck = SCTr.nxt()
                tt(SCT[0:rows, 0:rows], PSC[0:rows, 0:rows], MT[0:rows, h, 0:rows], ALU.mult,
                   ("PSC", "MT"), (sck,))
                c.update(SCT=SCT, sck=sck)

            def q8(i):
                h, ti, r0, rows, m0 = geo(i)
                c = ctx[i]
                j = i % 5
                VV, vvk, QKT, qtk, SCT, sck = c["VV"], c["vvk"], c["QKT"], c["qtk"], c["SCT"], c["sck"]
                mm(PO[0:rows, :], SCT[0:rows, 0:rows], VV[0:rows, :], True, False, (sck, vvk), ("PO",))
                for dc in range(2):
                    if ti == 0:
                        mm(PO[0:rows, :], QKT[:, dc, 0:rows], SB0[:, h * 2 + dc, :], False, dc == 1,
                           (qtk, "SB0"), ("PO",))
                    else:
                        mm(PO[0:rows, :], QKT[:, dc, 0:rows], SBFring[j][:, dc, :], False, dc == 1,
                           (qtk, "SBFr%d_%d" % (j, dc)), ("PO",))

            def q9(i):
                h, ti, r0, rows, m0 = geo(i)
                c = ctx[i]
                OF, ofk = OFr.nxt()
                SQJ, sqk = SQJr.nxt()
                cp("act", OF[0:rows, :], PO[0:rows, :], ("PO",), (ofk,))
                act(SQJ[0:rows, :], PO[0:rows, :], AF.Square, ("PO",), (sqk,))
                c.update(OF=OF, ofk=ofk, SQJ=SQJ, sqk=sqk)

            def q10(i):
                h, ti, r0, rows, m0 = geo(i)
                c = ctx[i]
                STs, stk = STr.nxt()
                SQJ, sqk = c["SQJ"], c["sqk"]
                S.op("dve", lambda e, rows=rows, STs=STs, SQJ=SQJ: e.reduce_sum(
                    out=STs[0:rows, 0:1], in_=SQJ[0:rows, :], axis=AX.X), (sqk,), (stk,))
                ts(STs[0:rows, 1:2], STs[0:rows, 0:1], 1.0 / DV, EPS, ALU.mult, ALU.add, (stk,), (stk,))
                tt(STs[0:rows, 2:3], STs[0:rows, 1:2], CM[0:rows, :], ALU.pow, (stk, "CM"), (stk,), eng="pool")
                c.update(STs=STs, stk=stk)

            def q11(i):
                h, ti, r0, rows, m0 = geo(i)
                c = ctx[i]
                OBT, obk = OBTr.nxt()
                stt(OBT[0:rows, :], c["OF"][0:rows, :], c["STs"][0:rows, 2:3], c["SG"][0:rows, :], ALU.mult,
                    ALU.mult, (c["ofk"], c["stk"], c["sgk"]), (obk,))
                c.update(OBT=OBT, obk=obk)

            def q12(i):
                h, ti, r0, rows, m0 = geo(i)
                c = ctx[i]
                OBT, obk = c["OBT"], c["obk"]
                for ec in range(4):
                    S.op("pe", lambda e, ec=ec, rows=rows, OBT=OBT: e.transpose(
                        PT2[:, ec, 0:rows], OBT[0:rows, ec * 128:(ec + 1) * 128], T["IDB"][0:rows, 0:rows]),
                        (obk, "IDB"), ("PT8",))

            def q13(i):
                h, ti, r0, rows, m0 = geo(i)
                ctx.pop(i)
                OBO, ook = OBOr.nxt()
                cp("act", OBO[:, :, 0:rows], PT2[:, :, 0:rows], ("PT8",), (ook,))
                dma("sp", OB_d[:, h * 4:(h + 1) * 4, m0:m0 + rows], OBO[:, :, 0:rows], reads=(ook,))

            SB0 = st.sb("SB0", [128, 2 * H, DV], BF16)
            cp("act", SB0[:, :, :], SF[:, :, :], tuple("SF%d" % i for i in range(2 * H)), ("SB0",))
            pipeline(len(its), [q1, q2, q3, q4, q5, q6, q7, q8, q9, q10, q11, q12, q13])

        if stop <= 4:
            return nc
        with Stage(nc, S) as st:
            XM = st.sb("XM", [128, KD, NM], BF16)
            OA = st.sb("OA", [128, NB, NM], BF16)
            OB = st.sb("OB", [128, 16, NM], BF16)
            WA = [st.sb("WA%d" % i, [128, NB, 128], BF16) for i in range(2)]
            WR = [st.sb("WR%d" % i, [128, 16, 128], BF16) for i in range(2)]
            WGa = [st.sb("WGa%d" % i, [128, KD, 128], BF16) for i in range(2)]
            WGb = [st.sb("WGb%d" % i, [128, KD, 128], BF16) for i in range(2)]
            Rt = {k: Rot(st, k, [128, 512], F32, 2) for k in ["GA", "GB", "M1", "M2"]}
            MO = [st.sb("MO%d" % i, [128, NM], BF16) for i in range(2)]
            PEr = [Rot(st, "PE%d_" % i, [128, 512], F32, 2, psum=True) for i in range(4)]

            def load_e(c):
                wi = c % 2
                cs = slice(c * 128, (c + 1) * 128)
                wcast(WA[wi][:], w_brnn[:, cs], "WA%d" % wi)
                wcast(WGa[wi][:], w_in[:, O_GA + c * 128:O_GA + (c + 1) * 128], "WGa%d" % wi)
                wcast(WR[wi][:], w_bret[:, cs], "WR%d" % wi)
                wcast(WGb[wi][:], w_in[:, O_GB + c * 128:O_GB + (c + 1) * 128], "WGb%d" % wi)

            load_e(0)
            for kc in range(NB):
                dma("sp", OA[:, kc, :], OA_d[:, kc, :], writes=("OA",))
            for kc in range(KD):
                dma("sp", XM[:, kc, :], XNT_d[:, kc, NPRE:NPRE + NM], writes=("XM",))
            for kc in range(16):
                dma("sp", OB[:, kc, :], OB_d[:, kc, :], writes=("OB",))
            eits = [(c, gi) for c in range(8) for gi in range(len(MG))]
            ectx = {}

            def e1(i):
                c, gi = eits[i]
                m0, ncols = MG[gi]
                wi = c % 2
                if gi == 0 and c + 1 < 8:
                    load_e(c + 1)
                ps_ = [r.nxt() for r in PEr]
                for k in range(NB):
                    mm(ps_[0][0][:, 0:ncols], WA[wi][:, k, :], OA[:, k, m0:m0 + ncols], k == 0, k == NB - 1,
                       ("WA%d" % wi, "OA"), (ps_[0][1],))
                for k in range(KD):
                    mm(ps_[2][0][:, 0:ncols], WGa[wi][:, k, :], XM[:, k, m0:m0 + ncols], k == 0, k == KD - 1,
                       ("WGa%d" % wi, "XM"), (ps_[2][1],))
                for k in range(16):
                    mm(ps_[1][0][:, 0:ncols], WR[wi][:, k, :], OB[:, k, m0:m0 + ncols], k == 0, k == 15,
                       ("WR%d" % wi, "OB"), (ps_[1][1],))
                for k in range(KD):
                    mm(ps_[3][0][:, 0:ncols], WGb[wi][:, k, :], XM[:, k, m0:m0 + ncols], k == 0, k == KD - 1,
                       ("WGb%d" % wi, "XM"), (ps_[3][1],))
                ectx[i] = ps_

            def e2(i):
                c, gi = eits[i]
                m0, ncols = MG[gi]
                wi = c % 2
                ps_ = ectx.pop(i)
                GA, gak = Rt["GA"].nxt()
                GB, gbk = Rt["GB"].nxt()
                M1, m1k = Rt["M1"].nxt()
                M2, m2k = Rt["M2"].nxt()
                act(GA[:, 0:ncols], ps_[2][0][:, 0:ncols], AF.Sigmoid, (ps_[2][1],), (gak,))
                act(GB[:, 0:ncols], ps_[3][0][:, 0:ncols], AF.Sigmoid, (ps_[3][1],), (gbk,))
                tt(M1[:, 0:ncols], ps_[0][0][:, 0:ncols], GA[:, 0:ncols], ALU.mult, (ps_[0][1], gak), (m1k,))
                tt(M2[:, 0:ncols], ps_[1][0][:, 0:ncols], GB[:, 0:ncols], ALU.mult, (ps_[1][1], gbk), (m2k,))
                tt(MO[wi][:, m0:m0 + ncols], M1[:, 0:ncols], M2[:, 0:ncols], ALU.add, (m1k, m2k),
                   ("MO%d" % wi,), eng="pool")
                if gi == len(MG) - 1:
                    dma("sp", MIX_d[:, c, :], MO[wi][:, :], reads=("MO%d" % wi,))

            pipeline(len(eits), [e1, e2])

        if stop <= 5:
            return nc
        with Stage(nc, S) as st:
            T = {}
            load_iden(st, T)
            MX = st.sb("MX", [128, KD, NM], BF16)
            for kc in range(KD):
                dma("sp", MX[:, kc, :], MIX_d[:, kc, :], writes=("MX",))
            WO = st.sb("WO", [128, KD, D], BF16)
            for c0 in range(0, D, 512):
                wcast(WO[:, :, c0:c0 + 512], w_out[:, c0:c0 + 512], "WO")
            G2B = st.sb("G2B", [128, KD, 128])
            dma("sp", G2B[:], g2b[:, :, :], writes=("G2B",))
            XSr = Rot(st, "XS", [128, D], F32, 3)
            HMr = Rot(st, "HM", [128, D], F32, 6)
            norm_bufs(st, T, 3)
            XOr = Rot(st, "XO", [128, KD, 128], BF16, 8)
            PF = Rot(st, "PF", [128, 512], F32, 4, psum=True)
            items = []
            for ti, (r0, rows) in enumerate(main_tiles):
                m0 = r0 - NPRE
                XOt, ok = XOr.nxt()

                def x_fn(ti=ti, r0=r0, rows=rows, m0=m0):
                    XSt, xk = XSr.nxt()
                    HMt, hk = HMr.nxt()
                    dma("sp", XSt[0:rows, :], xs[r0:r0 + rows, :], writes=(xk,))
                    for hf in range(2):
                        Pt, pk = PF.nxt()
                        for kc in range(KD):
                            mm(Pt[0:rows, :], MX[:, kc, m0:m0 + rows], WO[:, kc, hf * 512:(hf + 1) * 512],
                               kc == 0, kc == KD - 1, ("MX", "WO"), (pk,))
                        tt(HMt[0:rows, hf * 512:(hf + 1) * 512], Pt[0:rows, :],
                           XSt[0:rows, hf * 512:(hf + 1) * 512], ALU.add, (pk, xk), (hk,))
                    dma("sp", HM_d[ti * 128:ti * 128 + rows, :], HMt[0:rows, :], reads=(hk,))
                    return HMt[0:rows, :], hk

                def post(XOt=XOt, ok=ok, m0=m0, rows=rows):
                    dma("sp", XN2_d[:, :, m0:m0 + rows], XOt[:, :, 0:rows], reads=(ok,))
                items.append(dict(x_fn=x_fn, rows=rows, dst=XOt[:, :, 0:rows], dkey=ok, post=post))
            pipeline(len(items), norm_phases(T, items, G2B, "G2B"))

        if stop <= 6:
            return nc
        gh = ExitStack()
        gh.__enter__()
        AC = gh.enter_context(nc.sbuf_tensor("ACgh", [128, 24, NOWN], BF16))
        with Stage(nc, S) as st:
            X2 = st.sb("X2", [128, KD, NM], BF16)
            for kc in range(KD):
                dma("sp", X2[:, kc, :], XN2_d[:, kc, :], writes=("X2",))
            FV = st.sb("FV", [128, 4, 48])
            dma("sp", FV[:], fvec[:, :, :], writes=("FV",))
            WUg = [st.sb("WUg%d" % i, [128, KD, 128], BF16) for i in range(3)]
            WUv = [st.sb("WUv%d" % i, [128, KD, 128], BF16) for i in range(3)]
            FU = [st.sb("FU%d" % i, [128, 2, NM]) for i in range(2)]
            FAgr = Rot(st, "FAg", [128, NOWN], F32, 2)
            FAv = st.sb("FAv", [128, NOWN])
            P = [st.ps("PG%d" % i, [128, 512]) for i in range(8)]

            def load_g(c):
                w3 = c % 3
                wcast(WUg[w3][:], w_up[:, c * 128:(c + 1) * 128], "WUg%d" % w3)
                wcast(WUv[w3][:], w_up[:, DFF + c * 128:DFF + (c + 1) * 128], "WUv%d" % w3)

            load_g(0)
            load_g(1)
            pcount = [0]
            for c in range(24):
                wi = c % 2
                if c + 2 < 24:
                    load_g(c + 2)
                w3 = c % 3
                fk = "FU%d" % wi
                for gi, (m0, ncols) in enumerate(MG):
                    b0 = (pcount[0] % 4) * 2
                    pcount[0] += 1
                    pg, pv = P[b0], P[b0 + 1]
                    kg, kv = "PG%d" % b0, "PG%d" % (b0 + 1)
                    for k in range(KD):
                        mm(pg[:, 0:ncols], WUg[w3][:, k, :], X2[:, k, m0:m0 + ncols], k == 0, k == KD - 1,
                           ("WUg%d" % w3, "X2"), (kg,))
                    for k in range(KD):
                        mm(pv[:, 0:ncols], WUv[w3][:, k, :], X2[:, k, m0:m0 + ncols], k == 0, k == KD - 1,
                           ("WUv%d" % w3, "X2"), (kv,))
                    cp("act", FU[wi][:, 0, m0:m0 + ncols], pg[:, 0:ncols], (kg,), (fk,))
                    cp("act", FU[wi][:, 1, m0:m0 + ncols], pv[:, 0:ncols], (kv,), (fk,))
                FG, fgk = FAgr.nxt()
                for j, cc in ((0, c), (1, 24 + c)):
                    FAj, fj = (FG, fgk) if j == 0 else (FAv, "FAv")
                    ts(FAj[:, :], FU[wi][:, j, 14:14 + NOWN], FV[:, 0, cc:cc + 1], FV[:, 3, cc:cc + 1],
                       ALU.mult, ALU.add, (fk, "FV"), (fj,))
                    stt(FAj[:, :], FU[wi][:, j, 15:15 + NOWN], FV[:, 1, cc:cc + 1], FAj[:, :],
                        ALU.mult, ALU.add, (fk, "FV", fj), (fj,))
                    stt(FAj[:, :], FU[wi][:, j, 16:16 + NOWN], FV[:, 2, cc:cc + 1], FAj[:, :],
                        ALU.mult, ALU.add, (fk, "FV", fj), (fj,))
                act(FG[:, :], FG[:, :], AF.Gelu_apprx_tanh, (fgk,), (fgk,))
                tt(AC[:, c, :], FG[:, :], FAv[:, :], ALU.mult, (fgk, "FAv"), ("AC%d" % c,))
                if dbg:
                    dma("sp", ACT_d[:, c, :], AC[:, c, :], reads=("AC%d" % c,))

        if stop <= 7:
            gh.close()
            return nc
        with Stage(nc, S) as st:
            WD = [st.sb("WD%d" % i, [128, 24, 512], BF16) for i in range(2)]
            for hf in range(2):
                for c0 in range(0, 24, 8):
                    wcast(WD[hf][:, c0:c0 + 8, :], w_down[c0 * 128:(c0 + 8) * 128, hf * 512:(hf + 1) * 512],
                          "WD%d" % hf)
            GFB = st.sb("GFB", [128, D])
            dma("sp", GFB[:], gfb[:, :], writes=("GFB",))
            CM = st.sb("CM", [128, 1])
            S.op("pool", lambda e: e.memset(CM[:], -0.5), (), ("CM",))
            HMr = Rot(st, "HM", [128, D], F32, 3)
            YOr = Rot(st, "YO", [128, D], F32, 2)
            SQr = Rot(st, "SQJ", [128, D], F32, 2)
            STr = Rot(st, "STs", [128, 4], F32, 3)
            Pr = Rot(st, "PH", [128, 512], F32, 4, psum=True)
            for hf in range(2):
                for t in range(16):
                    rr = (t + 1) * 128
                    HMt, hk = HMr.nxt()
                    Pt, pk = Pr.nxt()
                    if hf == 0:
                        dma("sp", HMt[:, 0:512], HM_d[rr:rr + 128, 0:512], writes=(hk,))
                    else:
                        dma("sp", HMt[:, :], HM_d[rr:rr + 128, :], writes=(hk,))
                    for c in range(24):
                        mm(Pt[:, :], AC[:, c, t * 128:(t + 1) * 128], WD[hf][:, c, :], c == 0, c == 23,
                           ("WD%d" % hf,), (pk,))
                    tt(HMt[:, hf * 512:(hf + 1) * 512], Pt[:, :], HMt[:, hf * 512:(hf + 1) * 512],
                       ALU.add, (pk, hk), (hk,))
                    if hf == 0:
                        dma("sp", HM_d[rr:rr + 128, 0:512], HMt[:, 0:512], reads=(hk,))
                    else:
                        SQJ, sqk = SQr.nxt()
                        STs, stk = STr.nxt()
                        YO, yk = YOr.nxt()
                        act(SQJ[:, :], HMt[:, :], AF.Square, (hk,), (sqk,))
                        S.op("dve", lambda e, STs=STs, SQJ=SQJ: e.reduce_sum(out=STs[:, 0:1], in_=SQJ[:, :],
                                                                              axis=AX.X), (sqk,), (stk,))
                        ts(STs[:, 1:2], STs[:, 0:1], 1.0 / D, EPS, ALU.mult, ALU.add, (stk,), (stk,))
                        tt(STs[:, 2:3], STs[:, 1:2], CM[:, :], ALU.pow, (stk, "CM"), (stk,), eng="pool")
                        stt(YO[:, :], HMt[:, :], STs[:, 2:3], GFB[:, :], ALU.mult, ALU.mult,
                            (hk, stk, "GFB"), (yk,))
                        dma("sp", out[t * 128:(t + 1) * 128, :], YO[:, :], reads=(yk,))
        gh.close()
    return nc


def _tables(core):
    b, p = divmod(core, 4)
    n_real_pre = 2048 * p
    pad = NPRE - n_real_pre
    pos = np.arange(L, dtype=np.int64) - pad
    pos = np.maximum(pos, 0).astype(np.float32)
    half = 128
    inv_freq = (1.0 / (10000.0 ** np.linspace(0.0, 1.0, half, dtype=np.float32))).astype(np.float32)
    ang = (pos[:, None] * inv_freq[None, :]).astype(np.float32)
    cos = np.cos(ang).astype(np.float32)
    sin = np.sin(ang).astype(np.float32)
    g = (1.0 - 2.0 ** (-5.0 - np.arange(H, dtype=np.float64)))
    t = np.arange(NPRE, dtype=np.float64)
    dec = g[None, :] ** (NPRE - 1 - t)[:, None]
    dec = dec * (DK ** -0.5)
    decpre = dec.reshape(NPRE // 128, 128, H).transpose(1, 0, 2).astype(np.float32)
    i = np.arange(128, dtype=np.float64)
    ctab = np.zeros((128, 16), np.float32)
    ctab[:, 0:4] = g[None, :] ** (i + 1.0)[:, None]
    ctab[:, 4:8] = (g[None, :] ** (127.0 - i)[:, None]) * (DK ** -0.5)
    ctab[:, 8:12] = (g[None, :] ** np.maximum(15.0 - i, 0.0)[:, None]) * (DK ** -0.5)
    jj = i[:, None]
    ii = i[None, :]
    maskT = np.zeros((128, H, 128), np.float32)
    for h in range(H):
        maskT[:, h, :] = (g[h] ** (-(jj + 1.0))) * (ii >= jj) * (DK ** -0.5)
    mgrp = np.zeros((128, NPG), np.float32)
    for gi in range(NPG):
        mgrp[:, gi] = 1.0 if gi * 512 >= pad else 0.0
    return cos, sin, decpre, ctab, maskT, mgrp


def kernel(x, meta_tokens, norm_mix_g, w_in, rnn_conv_w, rnn_conv_b, rg_a_w, rg_a_b,
           rg_x_w, rg_x_b, lru_lambda, w_branch_rnn, w_branch_ret, w_out, norm_ffn_g,
           w_up, ffn_conv_w, ffn_conv_b, w_down, norm_final_g, _ret_maps=False):
    f = lambda a: np.ascontiguousarray(np.asarray(a, dtype=np.float32))
    x = f(x)
    meta = f(meta_tokens)
    pvec = np.zeros((128, 80), np.float32)
    cw = f(rnn_conv_w)[0].reshape(4, NB, 128)
    for j in range(4):
        pvec[:, j * 10:(j + 1) * 10] = cw[j].T
    pvec[:, 40:50] = f(rnn_conv_b)[0].reshape(NB, 128).T
    pvec[:, 50:60] = f(rg_a_b)[0].reshape(NB, 128).T
    pvec[:, 60:70] = f(rg_x_b)[0].reshape(NB, 128).T
    pvec[:, 70:80] = f(lru_lambda)[0].reshape(NB, 128).T
    fvec = np.zeros((128, 4, 48), np.float32)
    fw = f(ffn_conv_w)[0].reshape(3, 48, 128)
    for j in range(3):
        fvec[:, j, :] = fw[j].T
    fvec[:, 3, :] = f(ffn_conv_b)[0].reshape(48, 128).T
    rgw = np.ascontiguousarray(np.stack([f(rg_a_w)[0], f(rg_x_w)[0]], 0).transpose(2, 0, 1, 3))
    g1b = np.ascontiguousarray(np.broadcast_to(f(norm_mix_g)[0].reshape(KD, 128).T[:, :, None], (128, KD, 128)))
    g2b = np.ascontiguousarray(np.broadcast_to(f(norm_ffn_g)[0].reshape(KD, 128).T[:, :, None], (128, KD, 128)))
    gfb = np.ascontiguousarray(np.broadcast_to(f(norm_final_g)[None, :], (128, D)))
    iden = np.eye(128, dtype=np.float32)
    shared = {
        "w_in": f(w_in)[0], "w_brnn": f(w_branch_rnn)[0], "w_bret": f(w_branch_ret)[0], "w_out": f(w_out)[0],
        "w_up": f(w_up)[0], "w_down": f(w_down)[0], "rgw": rgw, "pvec": pvec, "fvec": fvec,
        "g1b": g1b, "g2b": g2b, "gfb": gfb, "iden": iden,
    }
    in_maps = []
    for core in range(8):
        b, p = divmod(core, 4)
        seq = np.concatenate([meta, x[b]], 0)
        end = NMETA + 2048 * (p + 1)
        stream = np.zeros((L, D), np.float32)
        stream[L - end:] = seq[:end]
        cos, sin, decpre, ctab, maskT, mgrp = _tables(core)
        m = dict(shared)
        m.update({"xs": stream, "cosT": cos, "sinT": sin, "decpre": decpre, "ctab": ctab,
                  "maskT": maskT, "mgrp": mgrp})
        in_maps.append(m)
    if _ret_maps:
        return in_maps
    nc = build()
    res = run_bass_kernel_spmd(nc, in_maps, core_ids=list(range(8)))
    outp = np.zeros((2, SEQ, D), np.float32)
    for core in range(8):
        b, p = divmod(core, 4)
        outp[b, p * 2048:(p + 1) * 2048] = res.results[core]["out"]
    return outp
```

```python
from contextlib import ExitStack
import numpy as np
import concourse.bass as bass
import concourse.mybir as mybir
from concourse.bass_utils import run_bass_kernel_spmd

F32 = mybir.dt.float32
BF16 = mybir.dt.bfloat16
AF = mybir.ActivationFunctionType
ALU = mybir.AluOpType
AX = mybir.AxisListType

D = 1024
KD = 8
SEQ = 8192
NMETA = 16
DRNN = 1280
NB = 10
H = 4
DK = 256
DV = 512
DFF = 3072
DIN = 10752
EPS = 1e-6
NPRE = 6144
NPG = NPRE // 512
HALO = 16
NOWN = 2048
NM = HALO + NOWN
L = NPRE + NM
NT_MAIN = 17
O_U, O_UG, O_Q, O_K, O_V, O_G, O_GA, O_GB = 0, 1280, 2560, 3584, 4608, 6656, 8704, 9728

MG = [(0, 16)] + [(16 + 512 * i, 512) for i in range(4)]


def tile_rows(t):
    if t == 0:
        return 0, 16
    return 16 + 128 * (t - 1), 128


class Sched:
    ENG = ("pe", "act", "dve", "pool", "sp")

    def __init__(self, nc, es):
        self.nc = nc
        self.q = {e: [] for e in self.ENG}
        self.sems = {}
        self.cnt = {}
        for e in self.ENG:
            self.sems[e] = es.enter_context(nc.semaphore("s_" + e))
            self.cnt[e] = 0
        self.ndma = 12
        self.dma_rr = 0
        for i in range(self.ndma):
            k = "d%d" % i
            self.sems[k] = es.enter_context(nc.semaphore("s_" + k))
            self.cnt[k] = 0
        self.waited = {e: {} for e in self.ENG}
        self.res = {}
        self.nops = 0

    def op(self, eng, fn, reads=(), writes=(), dma=False):
        deps = {}

        def add(ev):
            if ev is None:
                return
            s, v = ev
            if deps.get(s, 0) < v:
                deps[s] = v

        for r in reads:
            st = self.res.get(r)
            if st:
                add(st["w"])
        for w in writes:
            st = self.res.get(w)
            if st:
                add(st["w"])
                for s, v in st["r"].items():
                    add((s, v))
        if dma:
            k = "d%d" % self.dma_rr
            self.dma_rr = (self.dma_rr + 1) % self.ndma
            if self.cnt[k] > 0:
                add((k, 16 * self.cnt[k]))
            self.cnt[k] += 1
            ev = (k, 16 * self.cnt[k])
            inc = 16
        else:
            self.cnt[eng] += 1
            ev = (eng, self.cnt[eng])
            inc = 1
        waits = []
        for s, v in deps.items():
            if s == eng and eng == "pe":
                continue
            if self.waited[eng].get(s, 0) >= v:
                continue
            self.waited[eng][s] = v
            waits.append((s, v))
        sem = self.sems[ev[0]]
        sems = self.sems

        def emit(e, fn=fn, waits=waits, sem=sem, inc=inc):
            for s, v in waits:
                e.wait_ge(sems[s], v)
            fn(e).then_inc(sem, inc)

        self.q[eng].append(emit)
        for r in reads:
            st = self.res.setdefault(r, {"w": None, "r": {}})
            if st["r"].get(ev[0], 0) < ev[1]:
                st["r"][ev[0]] = ev[1]
        for w in writes:
            self.res[w] = {"w": ev, "r": {}}
        self.nops += 1

    def finish(self, eng="sp"):
        waits = [(k, 16 * self.cnt[k]) for k in self.sems if k[1:].isdigit() and self.cnt[k] > 0]
        sems = self.sems

        def emit(e):
            for s, v in waits:
                e.wait_ge(sems[s], v)

        self.q[eng].append(emit)


class Stage:
    _n = [0]

    def __init__(self, nc, S):
        self.nc, self.S = nc, S
        self.es = ExitStack()
        Stage._n[0] += 1
        self.pfx = "g%d_" % Stage._n[0]

    def __enter__(self):
        self.es.__enter__()
        return self

    def sb(self, name, shape, dt=F32):
        return self.es.enter_context(self.nc.sbuf_tensor(self.pfx + name, list(shape), dt))

    def ps(self, name, shape, dt=F32):
        return self.es.enter_context(self.nc.psum_tensor(self.pfx + name, list(shape), dt))

    def __exit__(self, *a):
        S = self.S
        S.barrier()
        q = S.q
        with self.nc.Block() as block:
            @block.tensor
            def _(e):
                for f in q["pe"]:
                    f(e)

            @block.scalar
            def _(e):
                for f in q["act"]:
                    f(e)

            @block.vector
            def _(e):
                for f in q["dve"]:
                    f(e)

            @block.gpsimd
            def _(e):
                for f in q["pool"]:
                    f(e)

            @block.sync
            def _(e):
                for f in q["sp"]:
                    f(e)
        S.q = {e: [] for e in S.ENG}
        S.res = {}
        return self.es.__exit__(*a)


def _barrier(self):
    targets = []
    for k in self.sems:
        v = self.cnt[k] * (16 if k[1:].isdigit() else 1)
        if v > 0:
            targets.append((k, v))
    sems = self.sems
    for eng in self.ENG:
        waits = []
        for s, v in targets:
            if self.waited[eng].get(s, 0) >= v:
                continue
            self.waited[eng][s] = v
            waits.append((s, v))

        def emit(e, waits=waits):
            for s, v in waits:
                e.wait_ge(sems[s], v)

        self.q[eng].append(emit)


Sched.barrier = _barrier


def build(stop=99, dbg=False):
    nc = bass.Bass("TRN2", target_bir_lowering=False)
    skind = "ExternalOutput" if dbg else "Internal"

    def din(name, shape, dt=F32):
        return nc.dram_tensor(name, list(shape), dt, kind="ExternalInput").ap()

    xs = din("xs", [L, D])
    w_in = din("w_in", [D, DIN])
    w_brnn = din("w_brnn", [DRNN, D])
    w_bret = din("w_bret", [H * DV, D])
    w_out = din("w_out", [D, D])
    w_up = din("w_up", [D, 2 * DFF])
    w_down = din("w_down", [DFF, D])
    rgw = din("rgw", [128, 2, NB, 128])
    pvec = din("pvec", [128, 80])
    fvec = din("fvec", [128, 4, 48])
    g1b = din("g1b", [128, KD, 128])
    g2b = din("g2b", [128, KD, 128])
    gfb = din("gfb", [128, D])
    cosT = din("cosT", [L, 128])
    sinT = din("sinT", [L, 128])
    decpre = din("decpre", [128, NPRE // 128, H])
    ctab = din("ctab", [128, 16])
    maskT = din("maskT", [128, H, 128])
    mgrp = din("mgrp", [128, NPG])
    iden_in = din("iden", [128, 128])
    out = nc.dram_tensor("out", [NOWN, D], F32, kind="ExternalOutput").ap()
    XNT_d = nc.dram_tensor("xnt_d", [128, KD, L], BF16, kind=skind).ap()
    OA_d = nc.dram_tensor("oa_d", [128, NB, NM], BF16, kind=skind).ap()
    OB_d = nc.dram_tensor("ob_d", [128, 16, NM], BF16, kind=skind).ap()
    MIX_d = nc.dram_tensor("mix_d", [128, KD, NM], BF16, kind=skind).ap()
    HM_d = nc.dram_tensor("hm_d", [NT_MAIN * 128, D], F32, kind=skind).ap()
    XN2_d = nc.dram_tensor("xn2_d", [128, KD, NM], BF16, kind=skind).ap()
    ACT_d = nc.dram_tensor("act_d", [128, 24, NOWN], BF16, kind=skind).ap()
    SF_d = nc.dram_tensor("sf_d", [128, 2 * H, DV], F32, kind=skind).ap()

    ges = ExitStack()
    with ges:
        S = Sched(nc, ges)

        def dma(eng, out_ap, in_ap, reads=(), writes=()):
            S.op(eng, lambda e: e.dma_start(out=out_ap, in_=in_ap), reads, writes, dma=True)

        def act(out_ap, in_ap, func, reads, writes, bias=None, scale=None):
            kw = {}
            if bias is not None:
                kw["bias"] = bias
            if scale is not None:
                kw["scale"] = scale
            S.op("act", lambda e: e.activation(out=out_ap, in_=in_ap, func=func, **kw), reads, writes)

        def tt(out_ap, a, b, op, reads, writes, eng="dve"):
            S.op(eng, lambda e: e.tensor_tensor(out=out_ap, in0=a, in1=b, op=op), reads, writes)

        def ts(out_ap, a, s1, s2, op0, op1, reads, writes, eng="dve"):
            if op1 is None:
                S.op(eng, lambda e: e.tensor_scalar(out=out_ap, in0=a, scalar1=s1, scalar2=None, op0=op0),
                     reads, writes)
            else:
                S.op(eng, lambda e: e.tensor_scalar(out=out_ap, in0=a, scalar1=s1, scalar2=s2, op0=op0, op1=op1),
                     reads, writes)

        def stt(out_ap, a, s, b, op0, op1, reads, writes):
            S.op("dve", lambda e: e.scalar_tensor_tensor(out=out_ap, in0=a, scalar=s, in1=b, op0=op0, op1=op1),
                 reads, writes)

        def cp(eng, out_ap, in_ap, reads, writes):
            if eng == "act":
                S.op(eng, lambda e: e.activation(out=out_ap, in_=in_ap, func=AF.Copy), reads, writes)
            else:
                S.op(eng, lambda e: e.tensor_copy(out=out_ap, in_=in_ap), reads, writes)

        def mm(out_ap, lhsT, rhs, start, stop, reads, writes):
            S.op("pe", lambda e: e.matmul(out_ap, lhsT, rhs, start=start, stop=stop), reads, writes)

        STG = [ges.enter_context(nc.sbuf_tensor("STG%d" % i, [128, 640], F32)) for i in range(4)]
        stg_i = [0]

        cast_engs = [("pool",)]

        def wcast(dst_view, src_ap, key):
            kc = dst_view.shape[1]
            ncols = dst_view.shape[2]
            for k in range(kc):
                i = stg_i[0] % 4
                stg_i[0] += 1
                engs = cast_engs[0]
                dma("sp", STG[i][:, 0:ncols], src_ap[k * 128:(k + 1) * 128, :], writes=("STG%d" % i,))
                cp(engs[k % len(engs)], dst_view[:, k, :], STG[i][:, 0:ncols], ("STG%d" % i,), (key,))

        class Rot:
            def __init__(self, st, name, shape, dt, n, psum=False):
                self.items = []
                for i in range(n):
                    t = (st.ps if psum else st.sb)("%s%d" % (name, i), shape, dt)
                    self.items.append((t, "%s%d" % (name, i)))
                self.i = 0

            def nxt(self):
                it = self.items[self.i % len(self.items)]
                self.i += 1
                return it

        def pipeline(n, phases):
            for step in range(n + len(phases) - 1):
                for pi in reversed(range(len(phases))):
                    t = step - pi
                    if 0 <= t < n:
                        phases[pi](t)

        def norm_bufs(st, T, n=3, npt=2):
            T["SQJ"] = Rot(st, "SQJ", [128, D], F32, 3)
            T["ST"] = Rot(st, "ST", [128, 4], F32, n + 2)
            T["XB"] = Rot(st, "XB", [128, D], BF16, n)
            T["PT"] = Rot(st, "PT", [128, KD, 128], BF16, npt, psum=True)
            T["CM"] = st.sb("CM", [128, 1])
            S.op("pool", lambda e: e.memset(T["CM"][:], -0.5), (), ("CM",))

        def norm_phases(T, items, gb, gbkey):
            ctx = {}

            def p0(t):
                ctx[t] = dict(zip(("x_ap", "xk"), items[t]["x_fn"]()))

            def p1(t):
                c = ctx[t]
                rows = items[t]["rows"]
                SQJ, sqk = T["SQJ"].nxt()
                act(SQJ[0:rows, :], c["x_ap"], AF.Square, (c["xk"],), (sqk,))
                c.update(SQJ=SQJ, sqk=sqk)

            def p2(t):
                c = ctx[t]
                rows = items[t]["rows"]
                STt, stk = T["ST"].nxt()
                SQJ = c["SQJ"]
                S.op("dve", lambda e: e.reduce_sum(out=STt[0:rows, 0:1], in_=SQJ[0:rows, :], axis=AX.X),
                     (c["sqk"],), (stk,))
                ts(STt[0:rows, 1:2], STt[0:rows, 0:1], 1.0 / D, EPS, ALU.mult, ALU.add, (stk,), (stk,))
                tt(STt[0:rows, 2:3], STt[0:rows, 1:2], T["CM"][0:rows, :], ALU.pow, (stk, "CM"), (stk,), eng="pool")
                c.update(STt=STt, stk=stk)

            def p3(t):
                c = ctx[t]
                rows = items[t]["rows"]
                XBt, xbk = T["XB"].nxt()
                act(XBt[0:rows, :], c["x_ap"], AF.Identity, (c["xk"], c["stk"]), (xbk,), scale=c["STt"][0:rows, 2:3])
                c.update(XBt=XBt, xbk=xbk)

            def p4(t):
                c = ctx[t]
                rows = items[t]["rows"]
                PTt, ptk = T["PT"].nxt()
                XBt = c["XBt"]
                for kc in range(KD):
                    S.op("pe", lambda e, kc=kc: e.transpose(PTt[:, kc, 0:rows],
                                                            XBt[0:rows, kc * 128:(kc + 1) * 128],
                                                            T["IDB"][0:rows, 0:rows]),
                         (c["xbk"], "IDB"), (ptk,))
                c.update(PTt=PTt, ptk=ptk)

            def p5(t):
                c = ctx.pop(t)
                it = items[t]
                rows = it["rows"]
                tt(it["dst"], c["PTt"][:, :, 0:rows], gb[:, :, 0:rows], ALU.mult, (c["ptk"], gbkey), (it["dkey"],))
                if it.get("post"):
                    it["post"]()

            return [p0, p1, p2, p3, p4, p5]

        def load_iden(st, T):
            T["IDF"] = st.sb("IDF", [128, 128])
            T["IDB"] = st.sb("IDB", [128, 128], BF16)
            dma("sp", T["IDF"][:], iden_in[:, :], writes=("IDF",))
            cp("dve", T["IDB"][:], T["IDF"][:], ("IDF",), ("IDB",))

        stream_tiles = [(i * 128, 128) for i in range(NPRE // 128)] + [(NPRE, 16)] + \
                       [(NPRE + 16 + i * 128, 128) for i in range(16)]
        main_tiles = stream_tiles[NPRE // 128:]
        pre_groups = [(i * 512, 512) for i in range(NPG)]
        main_groups = [(NPRE, 16)] + [(NPRE + 16 + i * 512, 512) for i in range(4)]

        def rotary(T, src, skey, dst, dkey, rows, nh, c, s, ckey, sgn_eng="dve"):
            for h in range(nh):
                x1 = src[0:rows, h * DK:h * DK + 128]
                x2 = src[0:rows, h * DK + 128:(h + 1) * DK]
                R0, k0 = T["RT"].nxt()
                R1, k1 = T["RT"].nxt()
                R2, k2 = T["RT"].nxt()
                R3, k3 = T["RT"].nxt()
                tt(R0[0:rows, :], x1, c, ALU.mult, (skey, ckey), (k0,))
                tt(R1[0:rows, :], x2, s, ALU.mult, (skey, ckey), (k1,), eng="pool")
                tt(R2[0:rows, :], x2, c, ALU.mult, (skey, ckey), (k2,))
                tt(R3[0:rows, :], x1, s, ALU.mult, (skey, ckey), (k3,), eng="pool")
                tt(dst[0:rows, h * DK:h * DK + 128], R0[0:rows, :], R1[0:rows, :], ALU.subtract,
                   (k0, k1), (dkey,))
                tt(dst[0:rows, h * DK + 128:(h + 1) * DK], R2[0:rows, :], R3[0:rows, :], ALU.add,
                   (k2, k3), (dkey,))

        with Stage(nc, S) as st:
            T = {}
            load_iden(st, T)
            G1B = st.sb("G1B", [128, KD, 128])
            dma("sp", G1B[:], g1b[:, :, :], writes=("G1B",))
            WK = st.sb("WK", [128, KD, H * DK], BF16)
            WV = st.sb("WV", [128, KD, H * DV], BF16)
            cast_engs[0] = ("pool", "act")
            for c0 in range(0, H * DK, 512):
                wcast(WK[:, :, c0:c0 + 512], w_in[:, O_K + c0:O_K + c0 + 512], "WK")
            for c0 in range(0, H * DV, 512):
                wcast(WV[:, :, c0:c0 + 512], w_in[:, O_V + c0:O_V + c0 + 512], "WV")
            cast_engs[0] = ("pool",)
            DPRE = st.sb("DPRE", [128, NPRE // 128, H])
            dma("sp", DPRE[:], decpre[:, :, :], writes=("DPRE",))
            XSr = Rot(st, "XS", [128, D], F32, 6)
            norm_bufs(st, T, 3, npt=1)
            XOr = Rot(st, "XO", [128, KD, 512], BF16, 2)
            CSr = Rot(st, "CS", [128, 2, 128], F32, 3)
            T["RT"] = Rot(st, "RT", [128, 128], F32, 8)
            KFr = Rot(st, "KF", [128, H * DK], F32, 2)
            KRr = Rot(st, "KR", [128, 4, H * DK], BF16, 2)
            VVr = Rot(st, "VV", [128, 4, H * DV], BF16, 2)
            SF = st.sb("SF", [128, 2 * H, DV])
            S.op("dve", lambda e: e.memset(SF[:], 0.0), (), tuple("SF%d" % i for i in range(2 * H)))
            PK = [st.ps("PK%d" % i, [128, 512]) for i in range(2)]
            PVv = [st.ps("PV%d" % i, [128, 512]) for i in range(2)]
            PSr = Rot(st, "PS", [128, 512], F32, 2, psum=True)
            items = []
            tinfo = []
            for gidx, (c0, ncols) in enumerate(pre_groups + main_groups):
                XOt, ok = XOr.nxt()
                nt = max(1, ncols // 128)
                is_pre = gidx < NPG
                for j in range(nt):
                    rows = min(128, ncols)
                    r0 = c0 + j * 128

                    def x_fn(r0=r0, rows=rows):
                        XSt, xk = XSr.nxt()
                        dma("sp", XSt[0:rows, :], xs[r0:r0 + rows, :], writes=(xk,))
                        return XSt[0:rows, :], xk

                    post = None
                    if j == nt - 1:
                        def post(XOt=XOt, ok=ok, c0=c0, ncols=ncols, nt=nt):
                            dma("sp", XNT_d[:, :, c0:c0 + ncols], XOt[:, :, 0:ncols],
                                reads=tuple("%s_%d" % (ok, jj) for jj in range(nt)))
                    items.append(dict(x_fn=x_fn, rows=rows, dst=XOt[:, :, j * 128:j * 128 + rows],
                                      dkey="%s_%d" % (ok, j), post=post))
                    tinfo.append((gidx, j, is_pre, XOt, "%s_%d" % (ok, j), r0))
            gst = {}
            ctx = {}

            def c1(i):
                g, t4, is_pre, XGt, xk, r0 = tinfo[i]
                if not is_pre:
                    return
                if t4 == 0:
                    gst[g] = (KRr.nxt(), VVr.nxt())
                (KR, krk), (VV, vvk) = gst[g]
                tile = g * 4 + t4
                CS, csk = CSr.nxt()
                dma("sp", CS[:, 0, :], cosT[r0:r0 + 128, :], writes=(csk,))
                dma("sp", CS[:, 1, :], sinT[r0:r0 + 128, :], writes=(csk,))
                KF, kfk = KFr.nxt()
                xt = lambda kc: XGt[:, kc, t4 * 128:(t4 + 1) * 128]
                for hf in range(2):
                    for kc in range(KD):
                        mm(PK[hf][:, :], xt(kc), WK[:, kc, hf * 512:(hf + 1) * 512], kc == 0, kc == KD - 1,
                           (xk, "WK"), ("PK%d" % hf,))
                for h in range(H):
                    hf, o = divmod(h, 2)
                    act(KF[:, h * DK:(h + 1) * DK], PK[hf][:, o * DK:(o + 1) * DK], AF.Identity,
                        ("PK%d" % hf, "DPRE"), (kfk,), scale=DPRE[:, tile, h:h + 1])
                for q4 in range(4):
                    pv, pvk = PVv[q4 % 2], "PV%d" % (q4 % 2)
                    for kc in range(KD):
                        mm(pv[:, :], xt(kc), WV[:, kc, q4 * 512:(q4 + 1) * 512], kc == 0, kc == KD - 1,
                           (xk, "WV"), (pvk,))
                    cp("act", VV[:, t4, q4 * 512:(q4 + 1) * 512], pv[:, :], (pvk,), (vvk,))
                ctx[i] = (KF, kfk, CS, csk)

            def c2(i):
                g, t4, is_pre, XGt, xk, r0 = tinfo[i]
                if not is_pre:
                    return
                (KR, krk), (VV, vvk) = gst[g]
                KF, kfk, CS, csk = ctx.pop(i)
                rotary(T, KF, kfk, KR[:, t4, :], krk, 128, H, CS[:, 0, :], CS[:, 1, :], csk)
                if t4 == 3:
                    for h in range(H):
                        for dc in range(2):
                            Pt, pk = PSr.nxt()
                            for t in range(4):
                                mm(Pt[:, :], KR[:, t, h * DK + dc * 128:h * DK + (dc + 1) * 128],
                                   VV[:, t, h * DV:(h + 1) * DV], t == 0, t == 3, (krk, vvk), (pk,))
                            tt(SF[:, h * 2 + dc, :], Pt[:, :], SF[:, h * 2 + dc, :], ALU.add,
                               (pk, "SF%d" % (h * 2 + dc)), ("SF%d" % (h * 2 + dc),))

            pipeline(len(items), norm_phases(T, items, G1B, "G1B") + [c1, c2])
            dma("sp", SF_d[:, :, :], SF[:], reads=tuple("SF%d" % i for i in range(2 * H)))

        if stop <= 1:
            return nc
        with Stage(nc, S) as st:
            WU = st.sb("WU", [128, KD, DRNN], BF16)
            WG = st.sb("WG", [128, KD, DRNN], BF16)
            RGW = st.sb("RGW", [128, 2, NB, 128], BF16)
            PV = st.sb("PV", [128, 80])
            MGR = st.sb("MGR", [128, NPG])
            BM = st.sb("BM", [128, NPG, NB])
            SC = st.sb("SC", [128, 3, NB])
            HB = st.sb("HB", [128, 2 * NB])
            dma("sp", PV[:], pvec[:, :], writes=("PV",))
            dma("sp", MGR[:], mgrp[:, :], writes=("MGR",))
            cast_engs[0] = ("pool", "act")
            for c0 in range(0, DRNN, 640):
                wcast(WU[:, :, c0:c0 + 640], w_in[:, O_U + c0:O_U + c0 + 640], "WU")
            for ax in range(2):
                for n0 in range(0, NB, 5):
                    i = stg_i[0] % 4
                    stg_i[0] += 1
                    dma("sp", STG[i][:, 0:640].rearrange("p (n j) -> p n j", n=5), rgw[:, ax, n0:n0 + 5, :],
                        writes=("STG%d" % i,))
                    cp("pool", RGW[:, ax, n0:n0 + 5, :], STG[i][:, 0:640].rearrange("p (n j) -> p n j", n=5),
                       ("STG%d" % i,), ("RGW",))
            for c0 in range(0, DRNN, 640):
                wcast(WG[:, :, c0:c0 + 640], w_in[:, O_UG + c0:O_UG + c0 + 640], "WG")
            cast_engs[0] = ("pool",)
            act(SC[:, 0, :], PV[:, 70:80], AF.Exp, ("PV",), ("SC0",), scale=-1.0)
            act(SC[:, 0, :], SC[:, 0, :], AF.Ln, ("SC0",), ("SC0",), bias=1.0)
            ts(SC[:, 1, :], SC[:, 0, :], -8.0, None, ALU.mult, None, ("SC0",), ("SC",))
            ts(SC[:, 2, :], SC[:, 0, :], -4.0, None, ALU.mult, None, ("SC0",), ("SC",))
            ts(HB[:, :], PV[:, 50:70], 0.5, None, ALU.mult, None, ("PV",), ("HB",))
            for g in range(NPG):
                ts(BM[:, g, :], PV[:, 40:50], MGR[:, g:g + 1], None, ALU.mult, None, ("PV", "MGR"), ("BM",))
            XG = [st.sb("XG%d" % i, [128, KD, 512], BF16) for i in range(2)]
            IDFb = st.sb("IDFb", [128, 128])
            dma("sp", IDFb[:], iden_in[:, :], writes=("IDFb",))
            DG = st.sb("DG", [128, 4, NB, 128], BF16)
            for j in range(4):
                for n in range(NB):
                    ts(DG[:, j, n, :], IDFb[:, :], PV[:, j * 10 + n:j * 10 + n + 1], None, ALU.mult, None,
                       ("IDFb", "PV"), ("DG",))
            UB = st.sb("UB", [128, NB, 516], BF16)
            HST = st.sb("HST", [128, NB])
            S.op("dve", lambda e: e.memset(UB[:], 0.0), (), tuple("U%d" % n for n in range(NB)))
            S.op("dve", lambda e: e.memset(HST[:], 0.0), (), tuple("HST%d" % n for n in range(NB)))
            BS = 4
            QB = st.sb("QB", [128, 1])
            S.op("pool", lambda e: e.memset(QB[:], 0.25), (), ("QB",))
            XRr = Rot(st, "XR", [128, 512], BF16, 5)
            TRr = Rot(st, "TR", [128, 512], F32, 3)
            TIr = Rot(st, "TI", [128, 512], F32, 3)
            BBr = Rot(st, "BB", [128, 512], F32, 5)
            HHr = Rot(st, "HH", [128, 512], F32, 3)
            OATr = Rot(st, "OAT", [128, 512], BF16, 3)
            GGb = [st.sb("GG%d" % i, [128, NB, 512], BF16) for i in range(3)]
            AAg = [st.sb("AAg%d" % i, [128, BS, 512]) for i in range(2)]
            OMg = [st.sb("OMg%d" % i, [128, BS, 512]) for i in range(2)]
            T1g = [st.sb("T1g%d" % i, [128, BS, 512]) for i in range(2)]
            PU = Rot(st, "PU", [128, 512], F32, 2, psum=True)
            PC = Rot(st, "PC", [128, 512], F32, 2, psum=True)
            PRA = Rot(st, "PRA", [128, 512], F32, 2, psum=True)
            PRX = Rot(st, "PRX", [128, 512], F32, 2, psum=True)
            allg = [(g, True) for g in pre_groups] + [(g, False) for g in main_groups]
            chains = []
            for gi, ((c0, ncols), is_pre) in enumerate(allg):
                for n in range(NB):
                    chains.append((gi, c0, ncols, is_pre, n))
            NCH = len(chains)
            ctx = {}

            def load_xg(gi):
                (c0, ncols), _ = allg[gi]
                dma("sp", XG[gi % 2][:, :, 0:ncols], XNT_d[:, :, c0:c0 + ncols], writes=("XG%d" % (gi % 2),))

            load_xg(0)

            def b1(ci):
                gi, c0, ncols, is_pre, n = chains[ci]
                XGt, xk = XG[gi % 2], "XG%d" % (gi % 2)
                if n == 0:
                    if gi + 1 < len(allg):
                        load_xg(gi + 1)
                    if not is_pre:
                        mgi = gi - NPG
                        GG, ggk = GGb[mgi % 3], "GG%d" % (mgi % 3)
                        for n2 in range(NB):
                            g_ps, gk = PU.nxt()
                            for kc in range(KD):
                                mm(g_ps[:, 0:ncols], WG[:, kc, n2 * 128:(n2 + 1) * 128], XGt[:, kc, 0:ncols],
                                   kc == 0, kc == KD - 1, ("WG", xk), (gk,))
                            act(GG[:, n2, 0:ncols], g_ps[:, 0:ncols], AF.Gelu_apprx_tanh, (gk,), (ggk,))
                u_ps, puk = PU.nxt()
                for kc in range(KD):
                    mm(u_ps[:, 0:ncols], WU[:, kc, n * 128:(n + 1) * 128], XGt[:, kc, 0:ncols], kc == 0,
                       kc == KD - 1, ("WU", xk), (puk,))
                ctx[ci] = dict(u_ps=u_ps, puk=puk)

            def b2(ci):
                gi, c0, ncols, is_pre, n = chains[ci]
                c = ctx[ci]
                cp("dve", UB[:, n, 3:3 + ncols], c["u_ps"][:, 0:ncols], (c["puk"],), ("U%d" % n,))

            def b3(ci):
                gi, c0, ncols, is_pre, n = chains[ci]
                c = ctx[ci]
                uk = "U%d" % n
                pc, pck = PC.nxt()
                for j in range(4):
                    mm(pc[:, 0:ncols], DG[:, j, n, :], UB[:, n, j:j + ncols], j == 0, j == 3, ("DG", uk), (pck,))
                c.update(pc=pc, pck=pck)

            def b4(ci):
                gi, c0, ncols, is_pre, n = chains[ci]
                c = ctx[ci]
                uk = "U%d" % n
                XR, xrk = XRr.nxt()
                if is_pre:
                    g = c0 // 512
                    act(XR[:, 0:ncols], c["pc"][:, 0:ncols], AF.Identity, (c["pck"], "BM"), (xrk,),
                        bias=BM[:, g, n:n + 1])
                else:
                    act(XR[:, 0:ncols], c["pc"][:, 0:ncols], AF.Identity, (c["pck"], "PV"), (xrk,),
                        bias=PV[:, 40 + n:41 + n])
                cp("dve", UB[:, n, 0:3], UB[:, n, ncols:ncols + 3], (uk,), (uk,))
                import os as _os
                if _os.environ.get("DBGXR") and not is_pre:
                    dma("sp", OA_d[:, n, c0 - NPRE:c0 - NPRE + ncols], XR[:, 0:ncols], reads=(xrk,))
                c.update(XR=XR, xrk=xrk)

            def b5(ci):
                gi, c0, ncols, is_pre, n = chains[ci]
                c = ctx[ci]
                XR, xrk = c["XR"], c["xrk"]
                ra, rak = PRA.nxt()
                rx, rxk = PRX.nxt()
                mm(ra[:, 0:ncols], RGW[:, 0, n, :], XR[:, 0:ncols], True, True, ("RGW", xrk), (rak,))
                mm(rx[:, 0:ncols], RGW[:, 1, n, :], XR[:, 0:ncols], True, True, ("RGW", xrk), (rxk,))
                c.update(ra=ra, rak=rak, rx=rx, rxk=rxk)

            def b6(ci):
                gi, c0, ncols, is_pre, n = chains[ci]
                c = ctx[ci]
                TR, trk = TRr.nxt()
                TI, tik = TIr.nxt()
                act(TR[:, 0:ncols], c["ra"][:, 0:ncols], AF.Tanh, (c["rak"], "HB"), (trk,), bias=HB[:, n:n + 1],
                    scale=0.5)
                act(TI[:, 0:ncols], c["rx"][:, 0:ncols], AF.Tanh, (c["rxk"], "HB"), (tik,),
                    bias=HB[:, NB + n:NB + n + 1], scale=0.5)
                c.update(TR=TR, trk=trk, TI=TI, tik=tik)

            def b7(ci):
                gi, c0, ncols, is_pre, n = chains[ci]
                c = ctx[ci]
                bid, slot = divmod(ci, BS)
                bb = bid % 2
                TR, trk, TI, tik, XR, xrk = c["TR"], c["trk"], c["TI"], c["tik"], c["XR"], c["xrk"]
                act(AAg[bb][:, slot, 0:ncols], TR[:, 0:ncols], AF.Exp, (trk, "SC"), ("AAg%d" % bb,),
                    scale=SC[:, 2, n:n + 1], bias=SC[:, 2, n:n + 1])
                stt(T1g[bb][:, slot, 0:ncols], TI[:, 0:ncols], 1.0, XR[:, 0:ncols], ALU.add, ALU.mult,
                    (tik, xrk), ("T1g%d" % bb,))

            def b7b(ci):
                gi, c0, ncols, is_pre, n = chains[ci]
                bid, slot = divmod(ci, BS)
                bb = bid % 2
                if ncols < 512:
                    S.op("pool", lambda e, bb=bb, slot=slot: e.memset(OMg[bb][:, slot, :], 0.0),
                         ("OMg%d" % bb,), ("OMg%d" % bb,))
                tt(OMg[bb][:, slot, 0:ncols], AAg[bb][:, slot, 0:ncols], AAg[bb][:, slot, 0:ncols], ALU.mult,
                   ("AAg%d" % bb,), ("OMg%d" % bb,), eng="pool")

            def batch_of(ci):
                if ci % BS == BS - 1 or ci == NCH - 1:
                    bid = ci // BS
                    return bid % 2, list(range(bid * BS, ci + 1))
                return None

            def b8(ci):
                b = batch_of(ci)
                if b is None:
                    return
                bb, ids = b
                nb = len(ids)
                act(OMg[bb][:, 0:nb, :], OMg[bb][:, 0:nb, :], AF.Sqrt, ("OMg%d" % bb, "QB"), ("OMg%d" % bb,),
                    scale=-0.25, bias=QB[:, 0:1])

            def b9(ci):
                b = batch_of(ci)
                if b is None:
                    return
                bb, ids = b
                for slot, cj in enumerate(ids):
                    gi, c0, ncols, is_pre, n = chains[cj]
                    BB, bbk = BBr.nxt()
                    tt(BB[:, 0:ncols], T1g[bb][:, slot, 0:ncols], OMg[bb][:, slot, 0:ncols], ALU.mult,
                       ("T1g%d" % bb, "OMg%d" % bb), (bbk,), eng="pool")
                    ctx[cj].update(BB=BB, bbk=bbk)

            def b10(ci):
                b = batch_of(ci)
                if b is None:
                    return
                bb, ids = b
                for slot, cj in enumerate(ids):
                    gi, c0, ncols, is_pre, n = chains[cj]
                    c = ctx.pop(cj)
                    BB, bbk = c["BB"], c["bbk"]
                    HH, hhk = HHr.nxt()
                    hk = "HST%d" % n
                    S.op("dve", lambda e, n=n, ncols=ncols, HH=HH, BB=BB, bb=bb, slot=slot: e.tensor_tensor_scan(
                        out=HH[:, 0:ncols], data0=AAg[bb][:, slot, 0:ncols], data1=BB[:, 0:ncols],
                        initial=HST[:, n:n + 1], op0=ALU.mult, op1=ALU.add), ("AAg%d" % bb, bbk, hk), (hhk,))
                    cp("dve", HST[:, n:n + 1], HH[:, ncols - 1:ncols], (hhk,), (hk,))
                    if not is_pre:
                        mgi = gi - NPG
                        GG, ggk = GGb[mgi % 3], "GG%d" % (mgi % 3)
                        OAT, ok = OATr.nxt()
                        tt(OAT[:, 0:ncols], HH[:, 0:ncols], GG[:, n, 0:ncols], ALU.mult, (hhk, ggk), (ok,))
                        m0 = c0 - NPRE
                        import os as _os
                        if not _os.environ.get("DBGXR"):
                            dma("sp", OA_d[:, n, m0:m0 + ncols], OAT[:, 0:ncols], reads=(ok,))

            pipeline(NCH, [b1, b2, b3, b4, b5, b6, b7, b7b, b8, b9, b10])
        if stop <= 2:
            return nc

        GAM = [1.0 - 2.0 ** (-5.0 - h) for h in range(H)]
        with Stage(nc, S) as st:
            T = {}
            load_iden(st, T)
            XM = st.sb("XM", [128, KD, NM], BF16)
            for kc in range(KD):
                dma("sp", XM[:, kc, :], XNT_d[:, kc, NPRE:NPRE + NM], writes=("XM",))
            CT = st.sb("CT", [128, 16])
            MT = st.sb("MT", [128, H, 128])
            CM = st.sb("CM", [128, 1])
            S.op("pool", lambda e: e.memset(CM[:], -0.5), (), ("CM",))
            dma("sp", CT[:], ctab[:, :], writes=("CT",))
            dma("sp", MT[:], maskT[:, :, :], writes=("MT",))
            WH = [st.sb("WH%d" % i, [128, KD, 1536], BF16) for i in range(2)]
            CSr = Rot(st, "CS", [128, 2, 128], F32, 4)
            T["RT"] = Rot(st, "RT", [128, 128], F32, 8)
            QKFr = Rot(st, "QKF", [128, 2 * DK], F32, 3)
            QKRr = Rot(st, "QKR", [128, 2 * DK], BF16, 3)
            KDr = Rot(st, "KDc", [128, DK], BF16, 5)
            VVr = Rot(st, "VV", [128, DV], BF16, 8)
            SGr = Rot(st, "SG", [128, DV], F32, 11)
            QKTr = Rot(st, "QKT", [128, 4, 128], BF16, 5)
            SCTr = Rot(st, "SCT", [128, 128], BF16, 3)
            SF = st.sb("SF", [128, 2 * H, DV])
            SBFring = [st.sb("SBFr%d" % j, [128, 2, DV], BF16) for j in range(5)]
            OFr = Rot(st, "OF", [128, DV], F32, 4)
            OBTr = Rot(st, "OBT", [128, DV], BF16, 3)
            OBOr = Rot(st, "OBO", [128, 4, 128], BF16, 2)
            SQJr = Rot(st, "SQJ", [128, DV], F32, 3)
            STr = Rot(st, "STs", [128, 4], F32, 4)
            dma("sp", SF[:], SF_d[:, :, :], writes=tuple("SF%d" % i for i in range(2 * H)))
            PQ = st.ps("PQ", [128, 512])
            PV_ = st.ps("PVm", [128, 512])
            PG = st.ps("PG", [128, 512])
            PSC = st.ps("PSC", [128, 512])
            PO = st.ps("PO", [128, 512])
            PStb = [st.ps("PSt%d" % i, [128, 512]) for i in range(2)]
            PT8 = st.ps("PT8", [128, 8, 128], BF16)
            PT = PT8[:, 0:4, :]
            PT2 = PT8[:, 4:8, :]
            its = [(h, ti) for h in range(H) for ti in range(len(main_tiles))]
            NT = len(main_tiles)
            ctx = {}
            sbf = {}

            def load_w(h):
                W = WH[h % 2]
                wk = "WH%d" % (h % 2)
                wcast(W[:, :, 0:256], w_in[:, O_Q + h * DK:O_Q + (h + 1) * DK], wk)
                wcast(W[:, :, 256:512], w_in[:, O_K + h * DK:O_K + (h + 1) * DK], wk)
                wcast(W[:, :, 512:1024], w_in[:, O_V + h * DV:O_V + (h + 1) * DV], wk)
                wcast(W[:, :, 1024:1536], w_in[:, O_G + h * DV:O_G + (h + 1) * DV], wk)

            cast_engs[0] = ("pool", "act")
            load_w(0)
            cast_engs[0] = ("pool",)

            def geo(i):
                h, ti = its[i]
                r0, rows = main_tiles[ti]
                return h, ti, r0, rows, r0 - NPRE

            def q1(i):
                h, ti, r0, rows, m0 = geo(i)
                if ti == 0 and h + 1 < H:
                    load_w(h + 1)
                W = WH[h % 2]
                wk = "WH%d" % (h % 2)
                CS, csk = CSr.nxt()
                dma("sp", CS[0:rows, 0, :], cosT[r0:r0 + rows, :], writes=(csk,))
                dma("sp", CS[0:rows, 1, :], sinT[r0:r0 + rows, :], writes=(csk,))
                xt = lambda kc: XM[:, kc, m0:m0 + rows]
                for kc in range(KD):
                    mm(PQ[0:rows, :], xt(kc), W[:, kc, 0:512], kc == 0, kc == KD - 1, ("XM", wk), ("PQ",))
                for kc in range(KD):
                    mm(PV_[0:rows, :], xt(kc), W[:, kc, 512:1024], kc == 0, kc == KD - 1, ("XM", wk), ("PVm",))
                for kc in range(KD):
                    mm(PG[0:rows, :], xt(kc), W[:, kc, 1024:1536], kc == 0, kc == KD - 1, ("XM", wk), ("PG",))
                ctx[i] = dict(CS=CS, csk=csk)

            def q2(i):
                h, ti, r0, rows, m0 = geo(i)
                c = ctx[i]
                QKF, qfk = QKFr.nxt()
                VV, vvk = VVr.nxt()
                SG, sgk = SGr.nxt()
                act(QKF[0:rows, 0:DK], PQ[0:rows, 0:DK], AF.Identity, ("PQ", "CT"), (qfk,),
                    scale=CT[0:rows, h:h + 1])
                cp("act", QKF[0:rows, DK:2 * DK], PQ[0:rows, DK:2 * DK], ("PQ",), (qfk,))
                cp("act", VV[0:rows, :], PV_[0:rows, :], ("PVm",), (vvk,))
                act(SG[0:rows, :], PG[0:rows, :], AF.Silu, ("PG",), (sgk,))
                c.update(QKF=QKF, qfk=qfk, VV=VV, vvk=vvk, SG=SG, sgk=sgk)

            def q3(i):
                h, ti, r0, rows, m0 = geo(i)
                c = ctx[i]
                QKR, qrk = QKRr.nxt()
                rotary(T, c["QKF"], c["qfk"], QKR, qrk, rows, 2, c["CS"][0:rows, 0, :], c["CS"][0:rows, 1, :],
                       c["csk"])
                KDc, kdk = KDr.nxt()
                kcol = (4 + h) if rows == 128 else (8 + h)
                ts(KDc[0:rows, :], QKR[0:rows, DK:2 * DK], CT[0:rows, kcol:kcol + 1], None, ALU.mult, None,
                   (qrk, "CT"), (kdk,))
                c.update(QKR=QKR, qrk=qrk, KDc=KDc, kdk=kdk)

            def state_mm(i, dc):
                h, ti, r0, rows, m0 = geo(i)
                c = ctx[i]
                mm(PStb[dc][:, :], c["KDc"][0:rows, dc * 128:(dc + 1) * 128], c["VV"][0:rows, :], True, True,
                   (c["kdk"], c["vvk"]), ("PSt%d" % dc,))

            def state_stt(i, dc):
                h, ti, r0, rows, m0 = geo(i)
                sfk = "SF%d" % (h * 2 + dc)
                gC = GAM[h] ** rows
                stt(SF[:, h * 2 + dc, :], SF[:, h * 2 + dc, :], gC, PStb[dc][:, :], ALU.mult, ALU.add,
                    (sfk, "PSt%d" % dc), (sfk,))

            def q4(i):
                h, ti, r0, rows, m0 = geo(i)
                c = ctx[i]
                QKR, qrk = c["QKR"], c["qrk"]
                for j in range(4):
                    S.op("pe", lambda e, j=j, rows=rows, QKR=QKR: e.transpose(
                        PT[:, j, 0:rows], QKR[0:rows, j * 128:(j + 1) * 128], T["IDB"][0:rows, 0:rows]),
                        (qrk, "IDB"), ("PT8",))
                if ti + 1 < NT:
                    state_mm(i, 0)
                    state_mm(i, 1)

            def q5(i):
                h, ti, r0, rows, m0 = geo(i)
                c = ctx[i]
                QKT, qtk = QKTr.nxt()
                cp("dve", QKT[:, :, 0:rows], PT[:, 0:4, 0:rows], ("PT8",), (qtk,))
                if ti + 1 < NT:
                    state_stt(i, 0)
                    state_stt(i, 1)
                c.update(QKT=QKT, qtk=qtk)

            def q6(i):
                h, ti, r0, rows, m0 = geo(i)
                c = ctx[i]
                QKT, qtk = c["QKT"], c["qtk"]
                for dc in range(2):
                    mm(PSC[0:rows, 0:rows], QKT[:, 2 + dc, 0:rows], QKT[:, dc, 0:rows], dc == 0, dc == 1,
                       (qtk,), ("PSC",))
                if ti + 1 < NT:
                    j = (i + 1) % 5
                    for dc in range(2):
                        cp("act", SBFring[j][:, dc, :], SF[:, h * 2 + dc, :], ("SF%d" % (h * 2 + dc),),
                           ("SBFr%d_%d" % (j, dc),))

            def q7(i):
                h, ti, r0, rows, m0 = geo(i)
                c = ctx[i]
                SCT, sck = SCTr.nxt()
                tt(SCT[0:rows, 0:rows], PSC[0:rows, 0:rows], MT[0:rows, h, 0:rows], ALU.mult,
                   ("PSC", "MT"), (sck,))
                c.update(SCT=SCT, sck=sck)

            def q8(i):
                h, ti, r0, rows, m0 = geo(i)
                c = ctx[i]
                j = i % 5
                VV, vvk, QKT, qtk, SCT, sck = c["VV"], c["vvk"], c["QKT"], c["qtk"], c["SCT"], c["sck"]
                mm(PO[0:rows, :], SCT[0:rows, 0:rows], VV[0:rows, :], True, False, (sck, vvk), ("PO",))
                for dc in range(2):
                    if ti == 0:
                        mm(PO[0:rows, :], QKT[:, dc, 0:rows], SB0[:, h * 2 + dc, :], False, dc == 1,
                           (qtk, "SB0"), ("PO",))
                    else:
                        mm(PO[0:rows, :], QKT[:, dc, 0:rows], SBFring[j][:, dc, :], False, dc == 1,
                           (qtk, "SBFr%d_%d" % (j, dc)), ("PO",))

            def q9(i):
                h, ti, r0, rows, m0 = geo(i)
                c = ctx[i]
                OF, ofk = OFr.nxt()
                SQJ, sqk = SQJr.nxt()
                cp("act", OF[0:rows, :], PO[0:rows, :], ("PO",), (ofk,))
                act(SQJ[0:rows, :], PO[0:rows, :], AF.Square, ("PO",), (sqk,))
                c.update(OF=OF, ofk=ofk, SQJ=SQJ, sqk=sqk)

            def q10(i):
                h, ti, r0, rows, m0 = geo(i)
                c = ctx[i]
                STs, stk = STr.nxt()
                SQJ, sqk = c["SQJ"], c["sqk"]
                S.op("dve", lambda e, rows=rows, STs=STs, SQJ=SQJ: e.reduce_sum(
                    out=STs[0:rows, 0:1], in_=SQJ[0:rows, :], axis=AX.X), (sqk,), (stk,))
                ts(STs[0:rows, 1:2], STs[0:rows, 0:1], 1.0 / DV, EPS, ALU.mult, ALU.add, (stk,), (stk,))
                tt(STs[0:rows, 2:3], STs[0:rows, 1:2], CM[0:rows, :], ALU.pow, (stk, "CM"), (stk,), eng="pool")
                c.update(STs=STs, stk=stk)

            def q11(i):
                h, ti, r0, rows, m0 = geo(i)
                c = ctx[i]
                OBT, obk = OBTr.nxt()
                stt(OBT[0:rows, :], c["OF"][0:rows, :], c["STs"][0:rows, 2:3], c["SG"][0:rows, :], ALU.mult,
                    ALU.mult, (c["ofk"], c["stk"], c["sgk"]), (obk,))
                c.update(OBT=OBT, obk=obk)

            def q12(i):
                h, ti, r0, rows, m0 = geo(i)
                c = ctx[i]
                OBT, obk = c["OBT"], c["obk"]
                for ec in range(4):
                    S.op("pe", lambda e, ec=ec, rows=rows, OBT=OBT: e.transpose(
                        PT2[:, ec, 0:rows], OBT[0:rows, ec * 128:(ec + 1) * 128], T["IDB"][0:rows, 0:rows]),
                        (obk, "IDB"), ("PT8",))

            def q13(i):
                h, ti, r0, rows, m0 = geo(i)
                ctx.pop(i)
                OBO, ook = OBOr.nxt()
                cp("act", OBO[:, :, 0:rows], PT2[:, :, 0:rows], ("PT8",), (ook,))
                dma("sp", OB_d[:, h * 4:(h + 1) * 4, m0:m0 + rows], OBO[:, :, 0:rows], reads=(ook,))

            SB0 = st.sb("SB0", [128, 2 * H, DV], BF16)
            cp("act", SB0[:, :, :], SF[:, :, :], tuple("SF%d" % i for i in range(2 * H)), ("SB0",))
            pipeline(len(its), [q1, q2, q3, q4, q5, q6, q7, q8, q9, q10, q11, q12, q13])

        if stop <= 4:
            return nc
        with Stage(nc, S) as st:
            XM = st.sb("XM", [128, KD, NM], BF16)
            OA = st.sb("OA", [128, NB, NM], BF16)
            OB = st.sb("OB", [128, 16, NM], BF16)
            WA = [st.sb("WA%d" % i, [128, NB, 128], BF16) for i in range(2)]
            WR = [st.sb("WR%d" % i, [128, 16, 128], BF16) for i in range(2)]
            WGa = [st.sb("WGa%d" % i, [128, KD, 128], BF16) for i in range(2)]
            WGb = [st.sb("WGb%d" % i, [128, KD, 128], BF16) for i in range(2)]
            Rt = {k: Rot(st, k, [128, 512], F32, 2) for k in ["GA", "GB", "M1", "M2"]}
            MO = [st.sb("MO%d" % i, [128, NM], BF16) for i in range(2)]
            PEr = [Rot(st, "PE%d_" % i, [128, 512], F32, 2, psum=True) for i in range(4)]

            def load_e(c):
                wi = c % 2
                cs = slice(c * 128, (c + 1) * 128)
                wcast(WA[wi][:], w_brnn[:, cs], "WA%d" % wi)
                wcast(WGa[wi][:], w_in[:, O_GA + c * 128:O_GA + (c + 1) * 128], "WGa%d" % wi)
                wcast(WR[wi][:], w_bret[:, cs], "WR%d" % wi)
                wcast(WGb[wi][:], w_in[:, O_GB + c * 128:O_GB + (c + 1) * 128], "WGb%d" % wi)

            cast_engs[0] = ("pool", "act")
            load_e(0)
            cast_engs[0] = ("pool",)
            for kc in range(NB):
                dma("sp", OA[:, kc, :], OA_d[:, kc, :], writes=("OA",))
            for kc in range(KD):
                dma("sp", XM[:, kc, :], XNT_d[:, kc, NPRE:NPRE + NM], writes=("XM",))
            for kc in range(16):
                dma("sp", OB[:, kc, :], OB_d[:, kc, :], writes=("OB",))
            eits = [(c, gi) for c in range(8) for gi in range(len(MG))]
            ectx = {}

            def e1(i):
                c, gi = eits[i]
                m0, ncols = MG[gi]
                wi = c % 2
                if gi == 0 and c + 1 < 8:
                    load_e(c + 1)
                ps_ = [r.nxt() for r in PEr]
                for k in range(NB):
                    mm(ps_[0][0][:, 0:ncols], WA[wi][:, k, :], OA[:, k, m0:m0 + ncols], k == 0, k == NB - 1,
                       ("WA%d" % wi, "OA"), (ps_[0][1],))
                for k in range(KD):
                    mm(ps_[2][0][:, 0:ncols], WGa[wi][:, k, :], XM[:, k, m0:m0 + ncols], k == 0, k == KD - 1,
                       ("WGa%d" % wi, "XM"), (ps_[2][1],))
                for k in range(16):
                    mm(ps_[1][0][:, 0:ncols], WR[wi][:, k, :], OB[:, k, m0:m0 + ncols], k == 0, k == 15,
                       ("WR%d" % wi, "OB"), (ps_[1][1],))
                for k in range(KD):
                    mm(ps_[3][0][:, 0:ncols], WGb[wi][:, k, :], XM[:, k, m0:m0 + ncols], k == 0, k == KD - 1,
                       ("WGb%d" % wi, "XM"), (ps_[3][1],))
                ectx[i] = ps_

            def e2(i):
                c, gi = eits[i]
                m0, ncols = MG[gi]
                wi = c % 2
                ps_ = ectx.pop(i)
                GA, gak = Rt["GA"].nxt()
                GB, gbk = Rt["GB"].nxt()
                M1, m1k = Rt["M1"].nxt()
                M2, m2k = Rt["M2"].nxt()
                act(GA[:, 0:ncols], ps_[2][0][:, 0:ncols], AF.Sigmoid, (ps_[2][1],), (gak,))
                act(GB[:, 0:ncols], ps_[3][0][:, 0:ncols], AF.Sigmoid, (ps_[3][1],), (gbk,))
                tt(M1[:, 0:ncols], ps_[0][0][:, 0:ncols], GA[:, 0:ncols], ALU.mult, (ps_[0][1], gak), (m1k,))
                tt(M2[:, 0:ncols], ps_[1][0][:, 0:ncols], GB[:, 0:ncols], ALU.mult, (ps_[1][1], gbk), (m2k,))
                tt(MO[wi][:, m0:m0 + ncols], M1[:, 0:ncols], M2[:, 0:ncols], ALU.add, (m1k, m2k),
                   ("MO%d" % wi,), eng="pool")
                if gi == len(MG) - 1:
                    dma("sp", MIX_d[:, c, :], MO[wi][:, :], reads=("MO%d" % wi,))

            pipeline(len(eits), [e1, e2])

        if stop <= 5:
            return nc
        with Stage(nc, S) as st:
            T = {}
            load_iden(st, T)
            MX = st.sb("MX", [128, KD, NM], BF16)
            for kc in range(KD):
                dma("sp", MX[:, kc, :], MIX_d[:, kc, :], writes=("MX",))
            WO = st.sb("WO", [128, KD, D], BF16)
            cast_engs[0] = ("pool", "act")
            for c0 in range(0, D, 512):
                wcast(WO[:, :, c0:c0 + 512], w_out[:, c0:c0 + 512], "WO")
            cast_engs[0] = ("pool",)
            G2B = st.sb("G2B", [128, KD, 128])
            dma("sp", G2B[:], g2b[:, :, :], writes=("G2B",))
            XSr = Rot(st, "XS", [128, D], F32, 3)
            HMr = Rot(st, "HM", [128, D], F32, 6)
            norm_bufs(st, T, 3)
            XOr = Rot(st, "XO", [128, KD, 128], BF16, 8)
            PF = Rot(st, "PF", [128, 512], F32, 4, psum=True)
            items = []
            for ti, (r0, rows) in enumerate(main_tiles):
                m0 = r0 - NPRE
                XOt, ok = XOr.nxt()

                def x_fn(ti=ti, r0=r0, rows=rows, m0=m0):
                    XSt, xk = XSr.nxt()
                    HMt, hk = HMr.nxt()
                    dma("sp", XSt[0:rows, :], xs[r0:r0 + rows, :], writes=(xk,))
                    for hf in range(2):
                        Pt, pk = PF.nxt()
                        for kc in range(KD):
                            mm(Pt[0:rows, :], MX[:, kc, m0:m0 + rows], WO[:, kc, hf * 512:(hf + 1) * 512],
                               kc == 0, kc == KD - 1, ("MX", "WO"), (pk,))
                        tt(HMt[0:rows, hf * 512:(hf + 1) * 512], Pt[0:rows, :],
                           XSt[0:rows, hf * 512:(hf + 1) * 512], ALU.add, (pk, xk), (hk,))
                    dma("sp", HM_d[ti * 128:ti * 128 + rows, :], HMt[0:rows, :], reads=(hk,))
                    return HMt[0:rows, :], hk

                def post(XOt=XOt, ok=ok, m0=m0, rows=rows):
                    dma("sp", XN2_d[:, :, m0:m0 + rows], XOt[:, :, 0:rows], reads=(ok,))
                items.append(dict(x_fn=x_fn, rows=rows, dst=XOt[:, :, 0:rows], dkey=ok, post=post))
            pipeline(len(items), norm_phases(T, items, G2B, "G2B"))

        if stop <= 6:
            return nc
        gh = ExitStack()
        gh.__enter__()
        AC = gh.enter_context(nc.sbuf_tensor("ACgh", [128, 24, NOWN], BF16))
        with Stage(nc, S) as st:
            X2 = st.sb("X2", [128, KD, NM], BF16)
            for kc in range(KD):
                dma("sp", X2[:, kc, :], XN2_d[:, kc, :], writes=("X2",))
            FV = st.sb("FV", [128, 4, 48])
            dma("sp", FV[:], fvec[:, :, :], writes=("FV",))
            WUg = [st.sb("WUg%d" % i, [128, KD, 128], BF16) for i in range(3)]
            WUv = [st.sb("WUv%d" % i, [128, KD, 128], BF16) for i in range(3)]
            FU = [st.sb("FU%d" % i, [128, 2, NM]) for i in range(2)]
            FAgr = Rot(st, "FAg", [128, NOWN], F32, 2)
            FAv = st.sb("FAv", [128, NOWN])
            P = [st.ps("PG%d" % i, [128, 512]) for i in range(8)]

            def load_g(c):
                w3 = c % 3
                wcast(WUg[w3][:], w_up[:, c * 128:(c + 1) * 128], "WUg%d" % w3)
                wcast(WUv[w3][:], w_up[:, DFF + c * 128:DFF + (c + 1) * 128], "WUv%d" % w3)

            load_g(0)
            load_g(1)
            pcount = [0]
            for c in range(24):
                wi = c % 2
                if c + 2 < 24:
                    load_g(c + 2)
                w3 = c % 3
                fk = "FU%d" % wi
                for gi, (m0, ncols) in enumerate(MG):
                    b0 = (pcount[0] % 4) * 2
                    pcount[0] += 1
                    pg, pv = P[b0], P[b0 + 1]
                    kg, kv = "PG%d" % b0, "PG%d" % (b0 + 1)
                    for k in range(KD):
                        mm(pg[:, 0:ncols], WUg[w3][:, k, :], X2[:, k, m0:m0 + ncols], k == 0, k == KD - 1,
                           ("WUg%d" % w3, "X2"), (kg,))
                    for k in range(KD):
                        mm(pv[:, 0:ncols], WUv[w3][:, k, :], X2[:, k, m0:m0 + ncols], k == 0, k == KD - 1,
                           ("WUv%d" % w3, "X2"), (kv,))
                    cp("act", FU[wi][:, 0, m0:m0 + ncols], pg[:, 0:ncols], (kg,), (fk,))
                    cp("act", FU[wi][:, 1, m0:m0 + ncols], pv[:, 0:ncols], (kv,), (fk,))
                FG, fgk = FAgr.nxt()
                for j, cc in ((0, c), (1, 24 + c)):
                    FAj, fj = (FG, fgk) if j == 0 else (FAv, "FAv")
                    ts(FAj[:, :], FU[wi][:, j, 14:14 + NOWN], FV[:, 0, cc:cc + 1], FV[:, 3, cc:cc + 1],
                       ALU.mult, ALU.add, (fk, "FV"), (fj,))
                    stt(FAj[:, :], FU[wi][:, j, 15:15 + NOWN], FV[:, 1, cc:cc + 1], FAj[:, :],
                        ALU.mult, ALU.add, (fk, "FV", fj), (fj,))
                    stt(FAj[:, :], FU[wi][:, j, 16:16 + NOWN], FV[:, 2, cc:cc + 1], FAj[:, :],
                        ALU.mult, ALU.add, (fk, "FV", fj), (fj,))
                act(FG[:, :], FG[:, :], AF.Gelu_apprx_tanh, (fgk,), (fgk,))
                tt(AC[:, c, :], FG[:, :], FAv[:, :], ALU.mult, (fgk, "FAv"), ("AC%d" % c,))
                if dbg:
                    dma("sp", ACT_d[:, c, :], AC[:, c, :], reads=("AC%d" % c,))

        if stop <= 7:
            gh.close()
            return nc
        with Stage(nc, S) as st:
            WD = [st.sb("WD%d" % i, [128, 24, 512], BF16) for i in range(2)]
            cast_engs[0] = ("pool", "act")
            for hf in range(2):
                for c0 in range(0, 24, 8):
                    wcast(WD[hf][:, c0:c0 + 8, :], w_down[c0 * 128:(c0 + 8) * 128, hf * 512:(hf + 1) * 512],
                          "WD%d" % hf)
            cast_engs[0] = ("pool",)
            GFB = st.sb("GFB", [128, D])
            dma("sp", GFB[:], gfb[:, :], writes=("GFB",))
            CM = st.sb("CM", [128, 1])
            S.op("pool", lambda e: e.memset(CM[:], -0.5), (), ("CM",))
            HMr = Rot(st, "HM", [128, D], F32, 4)
            YOr = Rot(st, "YO", [128, D], F32, 3)
            SQr = Rot(st, "SQJ", [128, D], F32, 2)
            STr = Rot(st, "STs", [128, 4], F32, 3)
            Pr = Rot(st, "PH", [128, 512], F32, 4, psum=True)
            for t in range(16):
                rr = (t + 1) * 128
                HMt, hk = HMr.nxt()
                dma("sp", HMt[:, :], HM_d[rr:rr + 128, :], writes=(hk,))
                for hf in range(2):
                    Pt, pk = Pr.nxt()
                    for c in range(24):
                        mm(Pt[:, :], AC[:, c, t * 128:(t + 1) * 128], WD[hf][:, c, :], c == 0, c == 23,
                           ("WD%d" % hf,), (pk,))
                    tt(HMt[:, hf * 512:(hf + 1) * 512], Pt[:, :], HMt[:, hf * 512:(hf + 1) * 512],
                       ALU.add, (pk, hk), (hk,))
                SQJ, sqk = SQr.nxt()
                STs, stk = STr.nxt()
                YO, yk = YOr.nxt()
                act(SQJ[:, :], HMt[:, :], AF.Square, (hk,), (sqk,))
                S.op("dve", lambda e, STs=STs, SQJ=SQJ: e.reduce_sum(out=STs[:, 0:1], in_=SQJ[:, :],
                                                                      axis=AX.X), (sqk,), (stk,))
                ts(STs[:, 1:2], STs[:, 0:1], 1.0 / D, EPS, ALU.mult, ALU.add, (stk,), (stk,))
                tt(STs[:, 2:3], STs[:, 1:2], CM[:, :], ALU.pow, (stk, "CM"), (stk,), eng="pool")
                stt(YO[:, :], HMt[:, :], STs[:, 2:3], GFB[:, :], ALU.mult, ALU.mult,
                    (hk, stk, "GFB"), (yk,))
                dma("sp", out[t * 128:(t + 1) * 128, :], YO[:, :], reads=(yk,))
        gh.close()
    return nc


def _tables(core):
    b, p = divmod(core, 4)
    n_real_pre = 2048 * p
    pad = NPRE - n_real_pre
    pos = np.arange(L, dtype=np.int64) - pad
    pos = np.maximum(pos, 0).astype(np.float32)
    half = 128
    inv_freq = (1.0 / (10000.0 ** np.linspace(0.0, 1.0, half, dtype=np.float32))).astype(np.float32)
    ang = (pos[:, None] * inv_freq[None, :]).astype(np.float32)
    cos = np.cos(ang).astype(np.float32)
    sin = np.sin(ang).astype(np.float32)
    g = (1.0 - 2.0 ** (-5.0 - np.arange(H, dtype=np.float64)))
    t = np.arange(NPRE, dtype=np.float64)
    dec = g[None, :] ** (NPRE - 1 - t)[:, None]
    dec = dec * (DK ** -0.5)
    decpre = dec.reshape(NPRE // 128, 128, H).transpose(1, 0, 2).astype(np.float32)
    i = np.arange(128, dtype=np.float64)
    ctab = np.zeros((128, 16), np.float32)
    ctab[:, 0:4] = g[None, :] ** (i + 1.0)[:, None]
    ctab[:, 4:8] = (g[None, :] ** (127.0 - i)[:, None]) * (DK ** -0.5)
    ctab[:, 8:12] = (g[None, :] ** np.maximum(15.0 - i, 0.0)[:, None]) * (DK ** -0.5)
    jj = i[:, None]
    ii = i[None, :]
    maskT = np.zeros((128, H, 128), np.float32)
    for h in range(H):
        maskT[:, h, :] = (g[h] ** (-(jj + 1.0))) * (ii >= jj) * (DK ** -0.5)
    mgrp = np.zeros((128, NPG), np.float32)
    for gi in range(NPG):
        mgrp[:, gi] = 1.0 if gi * 512 >= pad else 0.0
    return cos, sin, decpre, ctab, maskT, mgrp


def kernel(x, meta_tokens, norm_mix_g, w_in, rnn_conv_w, rnn_conv_b, rg_a_w, rg_a_b,
           rg_x_w, rg_x_b, lru_lambda, w_branch_rnn, w_branch_ret, w_out, norm_ffn_g,
           w_up, ffn_conv_w, ffn_conv_b, w_down, norm_final_g, _ret_maps=False):
    f = lambda a: np.ascontiguousarray(np.asarray(a, dtype=np.float32))
    x = f(x)
    meta = f(meta_tokens)
    pvec = np.zeros((128, 80), np.float32)
    cw = f(rnn_conv_w)[0].reshape(4, NB, 128)
    for j in range(4):
        pvec[:, j * 10:(j + 1) * 10] = cw[j].T
    pvec[:, 40:50] = f(rnn_conv_b)[0].reshape(NB, 128).T
    pvec[:, 50:60] = f(rg_a_b)[0].reshape(NB, 128).T
    pvec[:, 60:70] = f(rg_x_b)[0].reshape(NB, 128).T
    pvec[:, 70:80] = f(lru_lambda)[0].reshape(NB, 128).T
    fvec = np.zeros((128, 4, 48), np.float32)
    fw = f(ffn_conv_w)[0].reshape(3, 48, 128)
    for j in range(3):
        fvec[:, j, :] = fw[j].T
    fvec[:, 3, :] = f(ffn_conv_b)[0].reshape(48, 128).T
    rgw = np.ascontiguousarray(np.stack([f(rg_a_w)[0], f(rg_x_w)[0]], 0).transpose(2, 0, 1, 3))
    g1b = np.ascontiguousarray(np.broadcast_to(f(norm_mix_g)[0].reshape(KD, 128).T[:, :, None], (128, KD, 128)))
    g2b = np.ascontiguousarray(np.broadcast_to(f(norm_ffn_g)[0].reshape(KD, 128).T[:, :, None], (128, KD, 128)))
    gfb = np.ascontiguousarray(np.broadcast_to(f(norm_final_g)[None, :], (128, D)))
    iden = np.eye(128, dtype=np.float32)
    shared = {
        "w_in": f(w_in)[0], "w_brnn": f(w_branch_rnn)[0], "w_bret": f(w_branch_ret)[0], "w_out": f(w_out)[0],
        "w_up": f(w_up)[0], "w_down": f(w_down)[0], "rgw": rgw, "pvec": pvec, "fvec": fvec,
        "g1b": g1b, "g2b": g2b, "gfb": gfb, "iden": iden,
    }
    in_maps = []
    for core in range(8):
        b, p = divmod(core, 4)
        seq = np.concatenate([meta, x[b]], 0)
        end = NMETA + 2048 * (p + 1)
        stream = np.zeros((L, D), np.float32)
        stream[L - end:] = seq[:end]
        cos, sin, decpre, ctab, maskT, mgrp = _tables(core)
        m = dict(shared)
        m.update({"xs": stream, "cosT": cos, "sinT": sin, "decpre": decpre, "ctab": ctab,
                  "maskT": maskT, "mgrp": mgrp})
        in_maps.append(m)
    if _ret_maps:
        return in_maps
    nc = build()
    res = run_bass_kernel_spmd(nc, in_maps, core_ids=list(range(8)))
    outp = np.zeros((2, SEQ, D), np.float32)
    for core in range(8):
        b, p = divmod(core, 4)
        outp[b, p * 2048:(p + 1) * 2048] = res.results[core]["out"]
    return outp
```

```python
from contextlib import ExitStack
import numpy as np
import concourse.bass as bass
import concourse.mybir as mybir
from concourse.bass_utils import run_bass_kernel_spmd

F32 = mybir.dt.float32
BF16 = mybir.dt.bfloat16
AF = mybir.ActivationFunctionType
ALU = mybir.AluOpType
AX = mybir.AxisListType

D = 1024
KD = 8
SEQ = 8192
NMETA = 16
DRNN = 1280
NB = 10
H = 4
DK = 256
DV = 512
DFF = 3072
DIN = 10752
EPS = 1e-6
NPRE = 6144
NPG = NPRE // 512
HALO = 16
NOWN = 2048
NM = HALO + NOWN
L = NPRE + NM
NT_MAIN = 17
O_U, O_UG, O_Q, O_K, O_V, O_G, O_GA, O_GB = 0, 1280, 2560, 3584, 4608, 6656, 8704, 9728

MG = [(0, 16)] + [(16 + 512 * i, 512) for i in range(4)]


def tile_rows(t):
    if t == 0:
        return 0, 16
    return 16 + 128 * (t - 1), 128


class Sched:
    ENG = ("pe", "act", "dve", "pool", "sp")

    def __init__(self, nc, es):
        self.nc = nc
        self.q = {e: [] for e in self.ENG}
        self.sems = {}
        self.cnt = {}
        for e in self.ENG:
            self.sems[e] = es.enter_context(nc.semaphore("s_" + e))
            self.cnt[e] = 0
        self.ndma = 12
        self.dma_rr = 0
        for i in range(self.ndma):
            k = "d%d" % i
            self.sems[k] = es.enter_context(nc.semaphore("s_" + k))
            self.cnt[k] = 0
        self.waited = {e: {} for e in self.ENG}
        self.res = {}
        self.nops = 0

    def op(self, eng, fn, reads=(), writes=(), dma=False):
        deps = {}

        def add(ev):
            if ev is None:
                return
            s, v = ev
            if deps.get(s, 0) < v:
                deps[s] = v

        for r in reads:
            st = self.res.get(r)
            if st:
                add(st["w"])
        for w in writes:
            st = self.res.get(w)
            if st:
                add(st["w"])
                for s, v in st["r"].items():
                    add((s, v))
        if dma:
            k = "d%d" % self.dma_rr
            self.dma_rr = (self.dma_rr + 1) % self.ndma
            if self.cnt[k] > 0:
                add((k, 16 * self.cnt[k]))
            self.cnt[k] += 1
            ev = (k, 16 * self.cnt[k])
            inc = 16
        else:
            self.cnt[eng] += 1
            ev = (eng, self.cnt[eng])
            inc = 1
        waits = []
        for s, v in deps.items():
            if s == eng and eng == "pe":
                continue
            if self.waited[eng].get(s, 0) >= v:
                continue
            self.waited[eng][s] = v
            waits.append((s, v))
        sem = self.sems[ev[0]]
        sems = self.sems

        def emit(e, fn=fn, waits=waits, sem=sem, inc=inc):
            for s, v in waits:
                e.wait_ge(sems[s], v)
            fn(e).then_inc(sem, inc)

        self.q[eng].append(emit)
        for r in reads:
            st = self.res.setdefault(r, {"w": None, "r": {}})
            if st["r"].get(ev[0], 0) < ev[1]:
                st["r"][ev[0]] = ev[1]
        for w in writes:
            self.res[w] = {"w": ev, "r": {}}
        self.nops += 1

    def finish(self, eng="sp"):
        waits = [(k, 16 * self.cnt[k]) for k in self.sems if k[1:].isdigit() and self.cnt[k] > 0]
        sems = self.sems

        def emit(e):
            for s, v in waits:
                e.wait_ge(sems[s], v)

        self.q[eng].append(emit)


class Stage:
    _n = [0]

    def __init__(self, nc, S):
        self.nc, self.S = nc, S
        self.es = ExitStack()
        Stage._n[0] += 1
        self.pfx = "g%d_" % Stage._n[0]

    def __enter__(self):
        self.es.__enter__()
        return self

    def sb(self, name, shape, dt=F32):
        return self.es.enter_context(self.nc.sbuf_tensor(self.pfx + name, list(shape), dt))

    def ps(self, name, shape, dt=F32):
        return self.es.enter_context(self.nc.psum_tensor(self.pfx + name, list(shape), dt))

    def __exit__(self, *a):
        S = self.S
        S.barrier()
        q = S.q
        with self.nc.Block() as block:
            @block.tensor
            def _(e):
                for f in q["pe"]:
                    f(e)

            @block.scalar
            def _(e):
                for f in q["act"]:
                    f(e)

            @block.vector
            def _(e):
                for f in q["dve"]:
                    f(e)

            @block.gpsimd
            def _(e):
                for f in q["pool"]:
                    f(e)

            @block.sync
            def _(e):
                for f in q["sp"]:
                    f(e)
        S.q = {e: [] for e in S.ENG}
        S.res = {}
        return self.es.__exit__(*a)


def _barrier(self):
    targets = []
    for k in self.sems:
        v = self.cnt[k] * (16 if k[1:].isdigit() else 1)
        if v > 0:
            targets.append((k, v))
    sems = self.sems
    for eng in self.ENG:
        waits = []
        for s, v in targets:
            if self.waited[eng].get(s, 0) >= v:
                continue
            self.waited[eng][s] = v
            waits.append((s, v))

        def emit(e, waits=waits):
            for s, v in waits:
                e.wait_ge(sems[s], v)

        self.q[eng].append(emit)


Sched.barrier = _barrier


def build(stop=99, dbg=False):
    nc = bass.Bass("TRN2", target_bir_lowering=False)
    skind = "ExternalOutput" if dbg else "Internal"

    def din(name, shape, dt=F32):
        return nc.dram_tensor(name, list(shape), dt, kind="ExternalInput").ap()

    xs = din("xs", [L, D])
    w_in = din("w_in", [D, DIN])
    w_brnn = din("w_brnn", [DRNN, D])
    w_bret = din("w_bret", [H * DV, D])
    w_out = din("w_out", [D, D])
    w_up = din("w_up", [D, 2 * DFF])
    w_down = din("w_down", [DFF, D])
    rgw = din("rgw", [128, 2, NB, 128])
    pvec = din("pvec", [128, 80])
    fvec = din("fvec", [128, 4, 48])
    g1b = din("g1b", [128, KD, 128])
    g2b = din("g2b", [128, KD, 128])
    gfb = din("gfb", [128, D])
    cosT = din("cosT", [L, 128])
    sinT = din("sinT", [L, 128])
    decpre = din("decpre", [128, NPRE // 128, H])
    ctab = din("ctab", [128, 16])
    maskT = din("maskT", [128, H, 128])
    mgrp = din("mgrp", [128, NPG])
    iden_in = din("iden", [128, 128])
    out = nc.dram_tensor("out", [NOWN, D], F32, kind="ExternalOutput").ap()
    XNT_d = nc.dram_tensor("xnt_d", [128, KD, L], BF16, kind=skind).ap()
    OA_d = nc.dram_tensor("oa_d", [128, NB, NM], BF16, kind=skind).ap()
    OB_d = nc.dram_tensor("ob_d", [128, 16, NM], BF16, kind=skind).ap()
    MIX_d = nc.dram_tensor("mix_d", [128, KD, NM], BF16, kind=skind).ap()
    HM_d = nc.dram_tensor("hm_d", [NT_MAIN * 128, D], F32, kind=skind).ap()
    XN2_d = nc.dram_tensor("xn2_d", [128, KD, NM], BF16, kind=skind).ap()
    ACT_d = nc.dram_tensor("act_d", [128, 24, NOWN], BF16, kind=skind).ap()
    SF_d = nc.dram_tensor("sf_d", [128, 2 * H, DV], F32, kind=skind).ap()

    ges = ExitStack()
    with ges:
        S = Sched(nc, ges)

        def dma(eng, out_ap, in_ap, reads=(), writes=()):
            S.op(eng, lambda e: e.dma_start(out=out_ap, in_=in_ap), reads, writes, dma=True)

        def act(out_ap, in_ap, func, reads, writes, bias=None, scale=None):
            kw = {}
            if bias is not None:
                kw["bias"] = bias
            if scale is not None:
                kw["scale"] = scale
            S.op("act", lambda e: e.activation(out=out_ap, in_=in_ap, func=func, **kw), reads, writes)

        def tt(out_ap, a, b, op, reads, writes, eng="dve"):
            S.op(eng, lambda e: e.tensor_tensor(out=out_ap, in0=a, in1=b, op=op), reads, writes)

        def ts(out_ap, a, s1, s2, op0, op1, reads, writes, eng="dve"):
            if op1 is None:
                S.op(eng, lambda e: e.tensor_scalar(out=out_ap, in0=a, scalar1=s1, scalar2=None, op0=op0),
                     reads, writes)
            else:
                S.op(eng, lambda e: e.tensor_scalar(out=out_ap, in0=a, scalar1=s1, scalar2=s2, op0=op0, op1=op1),
                     reads, writes)

        def stt(out_ap, a, s, b, op0, op1, reads, writes):
            S.op("dve", lambda e: e.scalar_tensor_tensor(out=out_ap, in0=a, scalar=s, in1=b, op0=op0, op1=op1),
                 reads, writes)

        def cp(eng, out_ap, in_ap, reads, writes):
            if eng == "act":
                S.op(eng, lambda e: e.activation(out=out_ap, in_=in_ap, func=AF.Copy), reads, writes)
            else:
                S.op(eng, lambda e: e.tensor_copy(out=out_ap, in_=in_ap), reads, writes)

        def mm(out_ap, lhsT, rhs, start, stop, reads, writes):
            S.op("pe", lambda e: e.matmul(out_ap, lhsT, rhs, start=start, stop=stop), reads, writes)

        STG = [ges.enter_context(nc.sbuf_tensor("STG%d" % i, [128, 640], F32)) for i in range(4)]
        stg_i = [0]

        cast_engs = [("pool",)]

        def wcast(dst_view, src_ap, key):
            kc = dst_view.shape[1]
            ncols = dst_view.shape[2]
            for k in range(kc):
                i = stg_i[0] % 4
                stg_i[0] += 1
                engs = cast_engs[0]
                dma("sp", STG[i][:, 0:ncols], src_ap[k * 128:(k + 1) * 128, :], writes=("STG%d" % i,))
                cp(engs[k % len(engs)], dst_view[:, k, :], STG[i][:, 0:ncols], ("STG%d" % i,), (key,))

        class Rot:
            def __init__(self, st, name, shape, dt, n, psum=False):
                self.items = []
                for i in range(n):
                    t = (st.ps if psum else st.sb)("%s%d" % (name, i), shape, dt)
                    self.items.append((t, "%s%d" % (name, i)))
                self.i = 0

            def nxt(self):
                it = self.items[self.i % len(self.items)]
                self.i += 1
                return it

        def pipeline(n, phases):
            for step in range(n + len(phases) - 1):
                for pi in reversed(range(len(phases))):
                    t = step - pi
                    if 0 <= t < n:
                        phases[pi](t)

        def norm_bufs(st, T, n=3, npt=2):
            T["SQJ"] = Rot(st, "SQJ", [128, D], F32, 3)
            T["ST"] = Rot(st, "ST", [128, 4], F32, n + 2)
            T["XB"] = Rot(st, "XB", [128, D], BF16, n)
            T["PT"] = Rot(st, "PT", [128, KD, 128], BF16, npt, psum=True)
            T["CM"] = st.sb("CM", [128, 1])
            S.op("pool", lambda e: e.memset(T["CM"][:], -0.5), (), ("CM",))

        def norm_phases(T, items, gb, gbkey):
            ctx = {}

            def p0(t):
                ctx[t] = dict(zip(("x_ap", "xk"), items[t]["x_fn"]()))

            def p1(t):
                c = ctx[t]
                rows = items[t]["rows"]
                SQJ, sqk = T["SQJ"].nxt()
                act(SQJ[0:rows, :], c["x_ap"], AF.Square, (c["xk"],), (sqk,))
                c.update(SQJ=SQJ, sqk=sqk)

            def p2(t):
                c = ctx[t]
                rows = items[t]["rows"]
                STt, stk = T["ST"].nxt()
                SQJ = c["SQJ"]
                S.op("dve", lambda e: e.reduce_sum(out=STt[0:rows, 0:1], in_=SQJ[0:rows, :], axis=AX.X),
                     (c["sqk"],), (stk,))
                ts(STt[0:rows, 1:2], STt[0:rows, 0:1], 1.0 / D, EPS, ALU.mult, ALU.add, (stk,), (stk,))
                tt(STt[0:rows, 2:3], STt[0:rows, 1:2], T["CM"][0:rows, :], ALU.pow, (stk, "CM"), (stk,), eng="pool")
                c.update(STt=STt, stk=stk)

            def p3(t):
                c = ctx[t]
                rows = items[t]["rows"]
                XBt, xbk = T["XB"].nxt()
                act(XBt[0:rows, :], c["x_ap"], AF.Identity, (c["xk"], c["stk"]), (xbk,), scale=c["STt"][0:rows, 2:3])
                c.update(XBt=XBt, xbk=xbk)

            def p4(t):
                c = ctx[t]
                rows = items[t]["rows"]
                PTt, ptk = T["PT"].nxt()
                XBt = c["XBt"]
                for kc in range(KD):
                    S.op("pe", lambda e, kc=kc: e.transpose(PTt[:, kc, 0:rows],
                                                            XBt[0:rows, kc * 128:(kc + 1) * 128],
                                                            T["IDB"][0:rows, 0:rows]),
                         (c["xbk"], "IDB"), (ptk,))
                c.update(PTt=PTt, ptk=ptk)

            def p5(t):
                c = ctx.pop(t)
                it = items[t]
                rows = it["rows"]
                tt(it["dst"], c["PTt"][:, :, 0:rows], gb[:, :, 0:rows], ALU.mult, (c["ptk"], gbkey), (it["dkey"],))
                if it.get("post"):
                    it["post"]()

            return [p0, p1, p2, p3, p4, p5]

        def load_iden(st, T):
            T["IDF"] = st.sb("IDF", [128, 128])
            T["IDB"] = st.sb("IDB", [128, 128], BF16)
            dma("sp", T["IDF"][:], iden_in[:, :], writes=("IDF",))
            cp("dve", T["IDB"][:], T["IDF"][:], ("IDF",), ("IDB",))

        stream_tiles = [(i * 128, 128) for i in range(NPRE // 128)] + [(NPRE, 16)] + \
                       [(NPRE + 16 + i * 128, 128) for i in range(16)]
        main_tiles = stream_tiles[NPRE // 128:]
        pre_groups = [(i * 512, 512) for i in range(NPG)]
        main_groups = [(NPRE, 16)] + [(NPRE + 16 + i * 512, 512) for i in range(4)]

        def rotary(T, src, skey, dst, dkey, rows, nh, c, s, ckey, sgn_eng="dve"):
            for h in range(nh):
                x1 = src[0:rows, h * DK:h * DK + 128]
                x2 = src[0:rows, h * DK + 128:(h + 1) * DK]
                R0, k0 = T["RT"].nxt()
                R1, k1 = T["RT"].nxt()
                R2, k2 = T["RT"].nxt()
                R3, k3 = T["RT"].nxt()
                tt(R0[0:rows, :], x1, c, ALU.mult, (skey, ckey), (k0,))
                tt(R1[0:rows, :], x2, s, ALU.mult, (skey, ckey), (k1,), eng="pool")
                tt(R2[0:rows, :], x2, c, ALU.mult, (skey, ckey), (k2,))
                tt(R3[0:rows, :], x1, s, ALU.mult, (skey, ckey), (k3,), eng="pool")
                tt(dst[0:rows, h * DK:h * DK + 128], R0[0:rows, :], R1[0:rows, :], ALU.subtract,
                   (k0, k1), (dkey,))
                tt(dst[0:rows, h * DK + 128:(h + 1) * DK], R2[0:rows, :], R3[0:rows, :], ALU.add,
                   (k2, k3), (dkey,))

        with Stage(nc, S) as st:
            T = {}
            load_iden(st, T)
            G1B = st.sb("G1B", [128, KD, 128])
            dma("sp", G1B[:], g1b[:, :, :], writes=("G1B",))
            WK = st.sb("WK", [128, KD, H * DK], BF16)
            WV = st.sb("WV", [128, KD, H * DV], BF16)
            cast_engs[0] = ("pool", "act")
            for c0 in range(0, H * DK, 512):
                wcast(WK[:, :, c0:c0 + 512], w_in[:, O_K + c0:O_K + c0 + 512], "WK")
            for c0 in range(0, H * DV, 512):
                wcast(WV[:, :, c0:c0 + 512], w_in[:, O_V + c0:O_V + c0 + 512], "WV")
            cast_engs[0] = ("pool",)
            DPRE = st.sb("DPRE", [128, NPRE // 128, H])
            dma("sp", DPRE[:], decpre[:, :, :], writes=("DPRE",))
            XSr = Rot(st, "XS", [128, D], F32, 6)
            norm_bufs(st, T, 3, npt=1)
            XOr = Rot(st, "XO", [128, KD, 512], BF16, 2)
            CSr = Rot(st, "CS", [128, 2, 128], F32, 3)
            T["RT"] = Rot(st, "RT", [128, 128], F32, 8)
            KFr = Rot(st, "KF", [128, H * DK], F32, 2)
            KRr = Rot(st, "KR", [128, 4, H * DK], BF16, 2)
            VVr = Rot(st, "VV", [128, 4, H * DV], BF16, 2)
            SF = st.sb("SF", [128, 2 * H, DV])
            S.op("dve", lambda e: e.memset(SF[:], 0.0), (), tuple("SF%d" % i for i in range(2 * H)))
            PK = [st.ps("PK%d" % i, [128, 512]) for i in range(2)]
            PVv = [st.ps("PV%d" % i, [128, 512]) for i in range(2)]
            PSr = Rot(st, "PS", [128, 512], F32, 2, psum=True)
            items = []
            tinfo = []
            for gidx, (c0, ncols) in enumerate(pre_groups + main_groups):
                XOt, ok = XOr.nxt()
                nt = max(1, ncols // 128)
                is_pre = gidx < NPG
                for j in range(nt):
                    rows = min(128, ncols)
                    r0 = c0 + j * 128

                    def x_fn(r0=r0, rows=rows):
                        XSt, xk = XSr.nxt()
                        dma("sp", XSt[0:rows, :], xs[r0:r0 + rows, :], writes=(xk,))
                        return XSt[0:rows, :], xk

                    post = None
                    if j == nt - 1:
                        def post(XOt=XOt, ok=ok, c0=c0, ncols=ncols, nt=nt):
                            dma("sp", XNT_d[:, :, c0:c0 + ncols], XOt[:, :, 0:ncols],
                                reads=tuple("%s_%d" % (ok, jj) for jj in range(nt)))
                    items.append(dict(x_fn=x_fn, rows=rows, dst=XOt[:, :, j * 128:j * 128 + rows],
                                      dkey="%s_%d" % (ok, j), post=post))
                    tinfo.append((gidx, j, is_pre, XOt, "%s_%d" % (ok, j), r0))
            gst = {}
            ctx = {}

            def c1(i):
                g, t4, is_pre, XGt, xk, r0 = tinfo[i]
                if not is_pre:
                    return
                if t4 == 0:
                    gst[g] = (KRr.nxt(), VVr.nxt())
                (KR, krk), (VV, vvk) = gst[g]
                tile = g * 4 + t4
                CS, csk = CSr.nxt()
                dma("sp", CS[:, 0, :], cosT[r0:r0 + 128, :], writes=(csk,))
                dma("sp", CS[:, 1, :], sinT[r0:r0 + 128, :], writes=(csk,))
                KF, kfk = KFr.nxt()
                xt = lambda kc: XGt[:, kc, t4 * 128:(t4 + 1) * 128]
                for hf in range(2):
                    for kc in range(KD):
                        mm(PK[hf][:, :], xt(kc), WK[:, kc, hf * 512:(hf + 1) * 512], kc == 0, kc == KD - 1,
                           (xk, "WK"), ("PK%d" % hf,))
                for h in range(H):
                    hf, o = divmod(h, 2)
                    act(KF[:, h * DK:(h + 1) * DK], PK[hf][:, o * DK:(o + 1) * DK], AF.Identity,
                        ("PK%d" % hf, "DPRE"), (kfk,), scale=DPRE[:, tile, h:h + 1])
                for q4 in range(4):
                    pv, pvk = PVv[q4 % 2], "PV%d" % (q4 % 2)
                    for kc in range(KD):
                        mm(pv[:, :], xt(kc), WV[:, kc, q4 * 512:(q4 + 1) * 512], kc == 0, kc == KD - 1,
                           (xk, "WV"), (pvk,))
                    cp("act", VV[:, t4, q4 * 512:(q4 + 1) * 512], pv[:, :], (pvk,), (vvk,))
                ctx[i] = (KF, kfk, CS, csk)

            def c2(i):
                g, t4, is_pre, XGt, xk, r0 = tinfo[i]
                if not is_pre:
                    return
                (KR, krk), (VV, vvk) = gst[g]
                KF, kfk, CS, csk = ctx.pop(i)
                rotary(T, KF, kfk, KR[:, t4, :], krk, 128, H, CS[:, 0, :], CS[:, 1, :], csk)
                if t4 == 3:
                    for h in range(H):
                        for dc in range(2):
                            Pt, pk = PSr.nxt()
                            for t in range(4):
                                mm(Pt[:, :], KR[:, t, h * DK + dc * 128:h * DK + (dc + 1) * 128],
                                   VV[:, t, h * DV:(h + 1) * DV], t == 0, t == 3, (krk, vvk), (pk,))
                            tt(SF[:, h * 2 + dc, :], Pt[:, :], SF[:, h * 2 + dc, :], ALU.add,
                               (pk, "SF%d" % (h * 2 + dc)), ("SF%d" % (h * 2 + dc),))

            pipeline(len(items), norm_phases(T, items, G1B, "G1B") + [c1, c2])
            dma("sp", SF_d[:, :, :], SF[:], reads=tuple("SF%d" % i for i in range(2 * H)))

        if stop <= 1:
            return nc
        with Stage(nc, S) as st:
            WU = st.sb("WU", [128, KD, DRNN], BF16)
            WG = st.sb("WG", [128, KD, DRNN], BF16)
            RGW = st.sb("RGW", [128, 2, NB, 128], BF16)
            PV = st.sb("PV", [128, 80])
            MGR = st.sb("MGR", [128, NPG])
            BM = st.sb("BM", [128, NPG, NB])
            SC = st.sb("SC", [128, 3, NB])
            HB = st.sb("HB", [128, 2 * NB])
            dma("sp", PV[:], pvec[:, :], writes=("PV",))
            dma("sp", MGR[:], mgrp[:, :], writes=("MGR",))
            cast_engs[0] = ("pool", "act")
            for c0 in range(0, DRNN, 640):
                wcast(WU[:, :, c0:c0 + 640], w_in[:, O_U + c0:O_U + c0 + 640], "WU")
            for ax in range(2):
                for n0 in range(0, NB, 5):
                    i = stg_i[0] % 4
                    stg_i[0] += 1
                    dma("sp", STG[i][:, 0:640].rearrange("p (n j) -> p n j", n=5), rgw[:, ax, n0:n0 + 5, :],
                        writes=("STG%d" % i,))
                    cp("pool", RGW[:, ax, n0:n0 + 5, :], STG[i][:, 0:640].rearrange("p (n j) -> p n j", n=5),
                       ("STG%d" % i,), ("RGW",))
            for c0 in range(0, DRNN, 640):
                wcast(WG[:, :, c0:c0 + 640], w_in[:, O_UG + c0:O_UG + c0 + 640], "WG")
            cast_engs[0] = ("pool",)
            act(SC[:, 0, :], PV[:, 70:80], AF.Exp, ("PV",), ("SC0",), scale=-1.0)
            act(SC[:, 0, :], SC[:, 0, :], AF.Ln, ("SC0",), ("SC0",), bias=1.0)
            ts(SC[:, 1, :], SC[:, 0, :], -8.0, None, ALU.mult, None, ("SC0",), ("SC",))
            ts(SC[:, 2, :], SC[:, 0, :], -4.0, None, ALU.mult, None, ("SC0",), ("SC",))
            ts(HB[:, :], PV[:, 50:70], 0.5, None, ALU.mult, None, ("PV",), ("HB",))
            for g in range(NPG):
                ts(BM[:, g, :], PV[:, 40:50], MGR[:, g:g + 1], None, ALU.mult, None, ("PV", "MGR"), ("BM",))
            XG = [st.sb("XG%d" % i, [128, KD, 512], BF16) for i in range(2)]
            IDFb = st.sb("IDFb", [128, 128])
            dma("sp", IDFb[:], iden_in[:, :], writes=("IDFb",))
            DG = st.sb("DG", [128, 4, NB, 128], BF16)
            for j in range(4):
                for n in range(NB):
                    ts(DG[:, j, n, :], IDFb[:, :], PV[:, j * 10 + n:j * 10 + n + 1], None, ALU.mult, None,
                       ("IDFb", "PV"), ("DG",))
            UB = st.sb("UB", [128, NB, 516], BF16)
            HST = st.sb("HST", [128, NB])
            S.op("dve", lambda e: e.memset(UB[:], 0.0), (), tuple("U%d" % n for n in range(NB)))
            S.op("dve", lambda e: e.memset(HST[:], 0.0), (), tuple("HST%d" % n for n in range(NB)))
            BS = 4
            QB = st.sb("QB", [128, 1])
            S.op("pool", lambda e: e.memset(QB[:], 0.25), (), ("QB",))
            XRr = Rot(st, "XR", [128, 512], BF16, 5)
            TRr = Rot(st, "TR", [128, 512], F32, 3)
            TIr = Rot(st, "TI", [128, 512], F32, 3)
            BBr = Rot(st, "BB", [128, 512], F32, 5)
            HHr = Rot(st, "HH", [128, 512], F32, 3)
            OATr = Rot(st, "OAT", [128, 512], BF16, 3)
            GGb = [st.sb("GG%d" % i, [128, NB, 512], BF16) for i in range(3)]
            AAg = [st.sb("AAg%d" % i, [128, BS, 512]) for i in range(2)]
            OMg = [st.sb("OMg%d" % i, [128, BS, 512]) for i in range(2)]
            T1g = [st.sb("T1g%d" % i, [128, BS, 512]) for i in range(2)]
            PU = Rot(st, "PU", [128, 512], F32, 2, psum=True)
            PC = Rot(st, "PC", [128, 512], F32, 2, psum=True)
            PRA = Rot(st, "PRA", [128, 512], F32, 2, psum=True)
            PRX = Rot(st, "PRX", [128, 512], F32, 2, psum=True)
            allg = [(g, True) for g in pre_groups] + [(g, False) for g in main_groups]
            chains = []
            for gi, ((c0, ncols), is_pre) in enumerate(allg):
                for n in range(NB):
                    chains.append((gi, c0, ncols, is_pre, n))
            NCH = len(chains)
            ctx = {}

            def load_xg(gi):
                (c0, ncols), _ = allg[gi]
                dma("sp", XG[gi % 2][:, :, 0:ncols], XNT_d[:, :, c0:c0 + ncols], writes=("XG%d" % (gi % 2),))

            load_xg(0)

            def b1(ci):
                gi, c0, ncols, is_pre, n = chains[ci]
                XGt, xk = XG[gi % 2], "XG%d" % (gi % 2)
                if n == 0:
                    if gi + 1 < len(allg):
                        load_xg(gi + 1)
                    if not is_pre:
                        mgi = gi - NPG
                        GG, ggk = GGb[mgi % 3], "GG%d" % (mgi % 3)
                        for n2 in range(NB):
                            g_ps, gk = PU.nxt()
                            for kc in range(KD):
                                mm(g_ps[:, 0:ncols], WG[:, kc, n2 * 128:(n2 + 1) * 128], XGt[:, kc, 0:ncols],
                                   kc == 0, kc == KD - 1, ("WG", xk), (gk,))
                            act(GG[:, n2, 0:ncols], g_ps[:, 0:ncols], AF.Gelu_apprx_tanh, (gk,), (ggk,))
                u_ps, puk = PU.nxt()
                for kc in range(KD):
                    mm(u_ps[:, 0:ncols], WU[:, kc, n * 128:(n + 1) * 128], XGt[:, kc, 0:ncols], kc == 0,
                       kc == KD - 1, ("WU", xk), (puk,))
                ctx[ci] = dict(u_ps=u_ps, puk=puk)

            def b2(ci):
                gi, c0, ncols, is_pre, n = chains[ci]
                c = ctx[ci]
                cp("dve", UB[:, n, 3:3 + ncols], c["u_ps"][:, 0:ncols], (c["puk"],), ("U%d" % n,))

            def b3(ci):
                gi, c0, ncols, is_pre, n = chains[ci]
                c = ctx[ci]
                uk = "U%d" % n
                pc, pck = PC.nxt()
                for j in range(4):
                    mm(pc[:, 0:ncols], DG[:, j, n, :], UB[:, n, j:j + ncols], j == 0, j == 3, ("DG", uk), (pck,))
                c.update(pc=pc, pck=pck)

            def b4(ci):
                gi, c0, ncols, is_pre, n = chains[ci]
                c = ctx[ci]
                uk = "U%d" % n
                XR, xrk = XRr.nxt()
                if is_pre:
                    g = c0 // 512
                    act(XR[:, 0:ncols], c["pc"][:, 0:ncols], AF.Identity, (c["pck"], "BM"), (xrk,),
                        bias=BM[:, g, n:n + 1])
                else:
                    act(XR[:, 0:ncols], c["pc"][:, 0:ncols], AF.Identity, (c["pck"], "PV"), (xrk,),
                        bias=PV[:, 40 + n:41 + n])
                cp("dve", UB[:, n, 0:3], UB[:, n, ncols:ncols + 3], (uk,), (uk,))
                import os as _os
                if _os.environ.get("DBGXR") and not is_pre:
                    dma("sp", OA_d[:, n, c0 - NPRE:c0 - NPRE + ncols], XR[:, 0:ncols], reads=(xrk,))
                c.update(XR=XR, xrk=xrk)

            def b5(ci):
                gi, c0, ncols, is_pre, n = chains[ci]
                c = ctx[ci]
                XR, xrk = c["XR"], c["xrk"]
                ra, rak = PRA.nxt()
                rx, rxk = PRX.nxt()
                mm(ra[:, 0:ncols], RGW[:, 0, n, :], XR[:, 0:ncols], True, True, ("RGW", xrk), (rak,))
                mm(rx[:, 0:ncols], RGW[:, 1, n, :], XR[:, 0:ncols], True, True, ("RGW", xrk), (rxk,))
                c.update(ra=ra, rak=rak, rx=rx, rxk=rxk)

            def b6(ci):
                gi, c0, ncols, is_pre, n = chains[ci]
                c = ctx[ci]
                TR, trk = TRr.nxt()
                TI, tik = TIr.nxt()
                act(TR[:, 0:ncols], c["ra"][:, 0:ncols], AF.Tanh, (c["rak"], "HB"), (trk,), bias=HB[:, n:n + 1],
                    scale=0.5)
                act(TI[:, 0:ncols], c["rx"][:, 0:ncols], AF.Tanh, (c["rxk"], "HB"), (tik,),
                    bias=HB[:, NB + n:NB + n + 1], scale=0.5)
                c.update(TR=TR, trk=trk, TI=TI, tik=tik)

            def b7(ci):
                gi, c0, ncols, is_pre, n = chains[ci]
                c = ctx[ci]
                bid, slot = divmod(ci, BS)
                bb = bid % 2
                TR, trk, TI, tik, XR, xrk = c["TR"], c["trk"], c["TI"], c["tik"], c["XR"], c["xrk"]
                act(AAg[bb][:, slot, 0:ncols], TR[:, 0:ncols], AF.Exp, (trk, "SC"), ("AAg%d" % bb,),
                    scale=SC[:, 2, n:n + 1], bias=SC[:, 2, n:n + 1])
                stt(T1g[bb][:, slot, 0:ncols], TI[:, 0:ncols], 1.0, XR[:, 0:ncols], ALU.add, ALU.mult,
                    (tik, xrk), ("T1g%d" % bb,))

            def b7b(ci):
                gi, c0, ncols, is_pre, n = chains[ci]
                bid, slot = divmod(ci, BS)
                bb = bid % 2
                if ncols < 512:
                    S.op("pool", lambda e, bb=bb, slot=slot: e.memset(OMg[bb][:, slot, :], 0.0),
                         ("OMg%d" % bb,), ("OMg%d" % bb,))
                tt(OMg[bb][:, slot, 0:ncols], AAg[bb][:, slot, 0:ncols], AAg[bb][:, slot, 0:ncols], ALU.mult,
                   ("AAg%d" % bb,), ("OMg%d" % bb,), eng="pool")

            def batch_of(ci):
                if ci % BS == BS - 1 or ci == NCH - 1:
                    bid = ci // BS
                    return bid % 2, list(range(bid * BS, ci + 1))
                return None

            def b8(ci):
                b = batch_of(ci)
                if b is None:
                    return
                bb, ids = b
                nb = len(ids)
                act(OMg[bb][:, 0:nb, :], OMg[bb][:, 0:nb, :], AF.Sqrt, ("OMg%d" % bb, "QB"), ("OMg%d" % bb,),
                    scale=-0.25, bias=QB[:, 0:1])

            def b9(ci):
                b = batch_of(ci)
                if b is None:
                    return
                bb, ids = b
                for slot, cj in enumerate(ids):
                    gi, c0, ncols, is_pre, n = chains[cj]
                    BB, bbk = BBr.nxt()
                    tt(BB[:, 0:ncols], T1g[bb][:, slot, 0:ncols], OMg[bb][:, slot, 0:ncols], ALU.mult,
                       ("T1g%d" % bb, "OMg%d" % bb), (bbk,), eng="pool")
                    ctx[cj].update(BB=BB, bbk=bbk)

            def b10(ci):
                b = batch_of(ci)
                if b is None:
                    return
                bb, ids = b
                for slot, cj in enumerate(ids):
                    gi, c0, ncols, is_pre, n = chains[cj]
                    c = ctx.pop(cj)
                    BB, bbk = c["BB"], c["bbk"]
                    HH, hhk = HHr.nxt()
                    hk = "HST%d" % n
                    S.op("dve", lambda e, n=n, ncols=ncols, HH=HH, BB=BB, bb=bb, slot=slot: e.tensor_tensor_scan(
                        out=HH[:, 0:ncols], data0=AAg[bb][:, slot, 0:ncols], data1=BB[:, 0:ncols],
                        initial=HST[:, n:n + 1], op0=ALU.mult, op1=ALU.add), ("AAg%d" % bb, bbk, hk), (hhk,))
                    cp("dve", HST[:, n:n + 1], HH[:, ncols - 1:ncols], (hhk,), (hk,))
                    if not is_pre:
                        mgi = gi - NPG
                        GG, ggk = GGb[mgi % 3], "GG%d" % (mgi % 3)
                        OAT, ok = OATr.nxt()
                        tt(OAT[:, 0:ncols], HH[:, 0:ncols], GG[:, n, 0:ncols], ALU.mult, (hhk, ggk), (ok,))
                        m0 = c0 - NPRE
                        import os as _os
                        if not _os.environ.get("DBGXR"):
                            dma("sp", OA_d[:, n, m0:m0 + ncols], OAT[:, 0:ncols], reads=(ok,))

            pipeline(NCH, [b1, b2, b3, b4, b5, b6, b7, b7b, b8, b9, b10])
        if stop <= 2:
            return nc

        de = ExitStack()
        de.__enter__()
        XM = de.enter_context(nc.sbuf_tensor("XMde", [128, KD, NM], BF16))
        GAM = [1.0 - 2.0 ** (-5.0 - h) for h in range(H)]
        with Stage(nc, S) as st:
            T = {}
            load_iden(st, T)
            for kc in range(KD):
                dma("sp", XM[:, kc, :], XNT_d[:, kc, NPRE:NPRE + NM], writes=("XM",))
            CT = st.sb("CT", [128, 16])
            MT = st.sb("MT", [128, H, 128])
            CM = st.sb("CM", [128, 1])
            S.op("pool", lambda e: e.memset(CM[:], -0.5), (), ("CM",))
            dma("sp", CT[:], ctab[:, :], writes=("CT",))
            dma("sp", MT[:], maskT[:, :, :], writes=("MT",))
            WH = [st.sb("WH%d" % i, [128, KD, 1536], BF16) for i in range(2)]
            CSr = Rot(st, "CS", [128, 2, 128], F32, 4)
            T["RT"] = Rot(st, "RT", [128, 128], F32, 8)
            QKFr = Rot(st, "QKF", [128, 2 * DK], F32, 3)
            QKRr = Rot(st, "QKR", [128, 2 * DK], BF16, 3)
            KDr = Rot(st, "KDc", [128, DK], BF16, 5)
            VVr = Rot(st, "VV", [128, DV], BF16, 8)
            SGr = Rot(st, "SG", [128, DV], F32, 11)
            QKTr = Rot(st, "QKT", [128, 4, 128], BF16, 5)
            SCTr = Rot(st, "SCT", [128, 128], BF16, 3)
            SF = st.sb("SF", [128, 2 * H, DV])
            SBFring = [st.sb("SBFr%d" % j, [128, 2, DV], BF16) for j in range(5)]
            OFr = Rot(st, "OF", [128, DV], F32, 4)
            OBTr = Rot(st, "OBT", [128, DV], BF16, 3)
            OBOr = Rot(st, "OBO", [128, 4, 128], BF16, 2)
            SQJr = Rot(st, "SQJ", [128, DV], F32, 3)
            STr = Rot(st, "STs", [128, 4], F32, 4)
            dma("sp", SF[:], SF_d[:, :, :], writes=tuple("SF%d" % i for i in range(2 * H)))
            PQ = st.ps("PQ", [128, 512])
            PV_ = st.ps("PVm", [128, 512])
            PG = st.ps("PG", [128, 512])
            PSC = st.ps("PSC", [128, 512])
            PO = st.ps("PO", [128, 512])
            PStb = [st.ps("PSt%d" % i, [128, 512]) for i in range(2)]
            PT8 = st.ps("PT8", [128, 8, 128], BF16)
            PT = PT8[:, 0:4, :]
            PT2 = PT8[:, 4:8, :]
            its = [(h, ti) for h in range(H) for ti in range(len(main_tiles))]
            NT = len(main_tiles)
            ctx = {}
            sbf = {}

            def load_w(h):
                W = WH[h % 2]
                wk = "WH%d" % (h % 2)
                wcast(W[:, :, 0:256], w_in[:, O_Q + h * DK:O_Q + (h + 1) * DK], wk)
                wcast(W[:, :, 256:512], w_in[:, O_K + h * DK:O_K + (h + 1) * DK], wk)
                wcast(W[:, :, 512:1024], w_in[:, O_V + h * DV:O_V + (h + 1) * DV], wk)
                wcast(W[:, :, 1024:1536], w_in[:, O_G + h * DV:O_G + (h + 1) * DV], wk)

            cast_engs[0] = ("pool", "act")
            load_w(0)
            cast_engs[0] = ("pool",)

            def geo(i):
                h, ti = its[i]
                r0, rows = main_tiles[ti]
                return h, ti, r0, rows, r0 - NPRE

            def q1(i):
                h, ti, r0, rows, m0 = geo(i)
                if ti == 0 and h + 1 < H:
                    load_w(h + 1)
                W = WH[h % 2]
                wk = "WH%d" % (h % 2)
                CS, csk = CSr.nxt()
                dma("sp", CS[0:rows, 0, :], cosT[r0:r0 + rows, :], writes=(csk,))
                dma("sp", CS[0:rows, 1, :], sinT[r0:r0 + rows, :], writes=(csk,))
                xt = lambda kc: XM[:, kc, m0:m0 + rows]
                for kc in range(KD):
                    mm(PQ[0:rows, :], xt(kc), W[:, kc, 0:512], kc == 0, kc == KD - 1, ("XM", wk), ("PQ",))
                for kc in range(KD):
                    mm(PV_[0:rows, :], xt(kc), W[:, kc, 512:1024], kc == 0, kc == KD - 1, ("XM", wk), ("PVm",))
                for kc in range(KD):
                    mm(PG[0:rows, :], xt(kc), W[:, kc, 1024:1536], kc == 0, kc == KD - 1, ("XM", wk), ("PG",))
                ctx[i] = dict(CS=CS, csk=csk)

            def q2(i):
                h, ti, r0, rows, m0 = geo(i)
                c = ctx[i]
                QKF, qfk = QKFr.nxt()
                VV, vvk = VVr.nxt()
                SG, sgk = SGr.nxt()
                act(QKF[0:rows, 0:DK], PQ[0:rows, 0:DK], AF.Identity, ("PQ", "CT"), (qfk,),
                    scale=CT[0:rows, h:h + 1])
                cp("act", QKF[0:rows, DK:2 * DK], PQ[0:rows, DK:2 * DK], ("PQ",), (qfk,))
                cp("act", VV[0:rows, :], PV_[0:rows, :], ("PVm",), (vvk,))
                act(SG[0:rows, :], PG[0:rows, :], AF.Silu, ("PG",), (sgk,))
                c.update(QKF=QKF, qfk=qfk, VV=VV, vvk=vvk, SG=SG, sgk=sgk)

            def q3(i):
                h, ti, r0, rows, m0 = geo(i)
                c = ctx[i]
                QKR, qrk = QKRr.nxt()
                rotary(T, c["QKF"], c["qfk"], QKR, qrk, rows, 2, c["CS"][0:rows, 0, :], c["CS"][0:rows, 1, :],
                       c["csk"])
                KDc, kdk = KDr.nxt()
                kcol = (4 + h) if rows == 128 else (8 + h)
                ts(KDc[0:rows, :], QKR[0:rows, DK:2 * DK], CT[0:rows, kcol:kcol + 1], None, ALU.mult, None,
                   (qrk, "CT"), (kdk,))
                c.update(QKR=QKR, qrk=qrk, KDc=KDc, kdk=kdk)

            def state_mm(i, dc):
                h, ti, r0, rows, m0 = geo(i)
                c = ctx[i]
                mm(PStb[dc][:, :], c["KDc"][0:rows, dc * 128:(dc + 1) * 128], c["VV"][0:rows, :], True, True,
                   (c["kdk"], c["vvk"]), ("PSt%d" % dc,))

            def state_stt(i, dc):
                h, ti, r0, rows, m0 = geo(i)
                sfk = "SF%d" % (h * 2 + dc)
                gC = GAM[h] ** rows
                stt(SF[:, h * 2 + dc, :], SF[:, h * 2 + dc, :], gC, PStb[dc][:, :], ALU.mult, ALU.add,
                    (sfk, "PSt%d" % dc), (sfk,))

            def q4(i):
                h, ti, r0, rows, m0 = geo(i)
                c = ctx[i]
                QKR, qrk = c["QKR"], c["qrk"]
                for j in range(4):
                    S.op("pe", lambda e, j=j, rows=rows, QKR=QKR: e.transpose(
                        PT[:, j, 0:rows], QKR[0:rows, j * 128:(j + 1) * 128], T["IDB"][0:rows, 0:rows]),
                        (qrk, "IDB"), ("PT8",))
                if ti + 1 < NT:
                    state_mm(i, 0)
                    state_mm(i, 1)

            def q5(i):
                h, ti, r0, rows, m0 = geo(i)
                c = ctx[i]
                QKT, qtk = QKTr.nxt()
                cp("dve", QKT[:, :, 0:rows], PT[:, 0:4, 0:rows], ("PT8",), (qtk,))
                if ti + 1 < NT:
                    state_stt(i, 0)
                    state_stt(i, 1)
                c.update(QKT=QKT, qtk=qtk)

            def q6(i):
                h, ti, r0, rows, m0 = geo(i)
                c = ctx[i]
                QKT, qtk = c["QKT"], c["qtk"]
                for dc in range(2):
                    mm(PSC[0:rows, 0:rows], QKT[:, 2 + dc, 0:rows], QKT[:, dc, 0:rows], dc == 0, dc == 1,
                       (qtk,), ("PSC",))
                if ti + 1 < NT:
                    j = (i + 1) % 5
                    for dc in range(2):
                        cp("act", SBFring[j][:, dc, :], SF[:, h * 2 + dc, :], ("SF%d" % (h * 2 + dc),),
                           ("SBFr%d_%d" % (j, dc),))

            def q7(i):
                h, ti, r0, rows, m0 = geo(i)
                c = ctx[i]
                SCT, sck = SCTr.nxt()
                tt(SCT[0:rows, 0:rows], PSC[0:rows, 0:rows], MT[0:rows, h, 0:rows], ALU.mult,
                   ("PSC", "MT"), (sck,))
                c.update(SCT=SCT, sck=sck)

            def q8(i):
                h, ti, r0, rows, m0 = geo(i)
                c = ctx[i]
                j = i % 5
                VV, vvk, QKT, qtk, SCT, sck = c["VV"], c["vvk"], c["QKT"], c["qtk"], c["SCT"], c["sck"]
                mm(PO[0:rows, :], SCT[0:rows, 0:rows], VV[0:rows, :], True, False, (sck, vvk), ("PO",))
                for dc in range(2):
                    if ti == 0:
                        mm(PO[0:rows, :], QKT[:, dc, 0:rows], SB0[:, h * 2 + dc, :], False, dc == 1,
                           (qtk, "SB0"), ("PO",))
                    else:
                        mm(PO[0:rows, :], QKT[:, dc, 0:rows], SBFring[j][:, dc, :], False, dc == 1,
                           (qtk, "SBFr%d_%d" % (j, dc)), ("PO",))

            def q9(i):
                h, ti, r0, rows, m0 = geo(i)
                c = ctx[i]
                OF, ofk = OFr.nxt()
                SQJ, sqk = SQJr.nxt()
                cp("act", OF[0:rows, :], PO[0:rows, :], ("PO",), (ofk,))
                act(SQJ[0:rows, :], PO[0:rows, :], AF.Square, ("PO",), (sqk,))
                c.update(OF=OF, ofk=ofk, SQJ=SQJ, sqk=sqk)

            def q10(i):
                h, ti, r0, rows, m0 = geo(i)
                c = ctx[i]
                STs, stk = STr.nxt()
                SQJ, sqk = c["SQJ"], c["sqk"]
                S.op("dve", lambda e, rows=rows, STs=STs, SQJ=SQJ: e.reduce_sum(
                    out=STs[0:rows, 0:1], in_=SQJ[0:rows, :], axis=AX.X), (sqk,), (stk,))
                ts(STs[0:rows, 1:2], STs[0:rows, 0:1], 1.0 / DV, EPS, ALU.mult, ALU.add, (stk,), (stk,))
                tt(STs[0:rows, 2:3], STs[0:rows, 1:2], CM[0:rows, :], ALU.pow, (stk, "CM"), (stk,), eng="pool")
                c.update(STs=STs, stk=stk)

            def q11(i):
                h, ti, r0, rows, m0 = geo(i)
                c = ctx[i]
                OBT, obk = OBTr.nxt()
                stt(OBT[0:rows, :], c["OF"][0:rows, :], c["STs"][0:rows, 2:3], c["SG"][0:rows, :], ALU.mult,
                    ALU.mult, (c["ofk"], c["stk"], c["sgk"]), (obk,))
                c.update(OBT=OBT, obk=obk)

            def q12(i):
                h, ti, r0, rows, m0 = geo(i)
                c = ctx[i]
                OBT, obk = c["OBT"], c["obk"]
                for ec in range(4):
                    S.op("pe", lambda e, ec=ec, rows=rows, OBT=OBT: e.transpose(
                        PT2[:, ec, 0:rows], OBT[0:rows, ec * 128:(ec + 1) * 128], T["IDB"][0:rows, 0:rows]),
                        (obk, "IDB"), ("PT8",))

            def q13(i):
                h, ti, r0, rows, m0 = geo(i)
                ctx.pop(i)
                OBO, ook = OBOr.nxt()
                cp("act", OBO[:, :, 0:rows], PT2[:, :, 0:rows], ("PT8",), (ook,))
                dma("sp", OB_d[:, h * 4:(h + 1) * 4, m0:m0 + rows], OBO[:, :, 0:rows], reads=(ook,))

            SB0 = st.sb("SB0", [128, 2 * H, DV], BF16)
            cp("act", SB0[:, :, :], SF[:, :, :], tuple("SF%d" % i for i in range(2 * H)), ("SB0",))
            pipeline(len(its), [q1, q2, q3, q4, q5, q6, q7, q8, q9, q10, q11, q12, q13])

        if stop <= 4:
            de.close()
            return nc
        with Stage(nc, S) as st:
            OA = st.sb("OA", [128, NB, NM], BF16)
            OB = st.sb("OB", [128, 16, NM], BF16)
            WA = [st.sb("WA%d" % i, [128, NB, 128], BF16) for i in range(2)]
            WR = [st.sb("WR%d" % i, [128, 16, 128], BF16) for i in range(2)]
            WGa = [st.sb("WGa%d" % i, [128, KD, 128], BF16) for i in range(2)]
            WGb = [st.sb("WGb%d" % i, [128, KD, 128], BF16) for i in range(2)]
            Rt = {k: Rot(st, k, [128, 512], F32, 2) for k in ["GA", "GB", "M1", "M2"]}
            MO = [st.sb("MO%d" % i, [128, NM], BF16) for i in range(2)]
            PEr = [Rot(st, "PE%d_" % i, [128, 512], F32, 2, psum=True) for i in range(4)]

            def load_e(c):
                wi = c % 2
                cs = slice(c * 128, (c + 1) * 128)
                wcast(WA[wi][:], w_brnn[:, cs], "WA%d" % wi)
                wcast(WGa[wi][:], w_in[:, O_GA + c * 128:O_GA + (c + 1) * 128], "WGa%d" % wi)
                wcast(WR[wi][:], w_bret[:, cs], "WR%d" % wi)
                wcast(WGb[wi][:], w_in[:, O_GB + c * 128:O_GB + (c + 1) * 128], "WGb%d" % wi)

            cast_engs[0] = ("pool", "act")
            load_e(0)
            cast_engs[0] = ("pool",)
            for kc in range(NB):
                dma("sp", OA[:, kc, :], OA_d[:, kc, :], writes=("OA",))
            for kc in range(16):
                dma("sp", OB[:, kc, :], OB_d[:, kc, :], writes=("OB",))
            eits = [(c, gi) for c in range(8) for gi in range(len(MG))]
            ectx = {}

            def e1(i):
                c, gi = eits[i]
                m0, ncols = MG[gi]
                wi = c % 2
                if gi == 0 and c + 1 < 8:
                    load_e(c + 1)
                ps_ = [r.nxt() for r in PEr]
                for k in range(NB):
                    mm(ps_[0][0][:, 0:ncols], WA[wi][:, k, :], OA[:, k, m0:m0 + ncols], k == 0, k == NB - 1,
                       ("WA%d" % wi, "OA"), (ps_[0][1],))
                for k in range(KD):
                    mm(ps_[2][0][:, 0:ncols], WGa[wi][:, k, :], XM[:, k, m0:m0 + ncols], k == 0, k == KD - 1,
                       ("WGa%d" % wi, "XM"), (ps_[2][1],))
                for k in range(16):
                    mm(ps_[1][0][:, 0:ncols], WR[wi][:, k, :], OB[:, k, m0:m0 + ncols], k == 0, k == 15,
                       ("WR%d" % wi, "OB"), (ps_[1][1],))
                for k in range(KD):
                    mm(ps_[3][0][:, 0:ncols], WGb[wi][:, k, :], XM[:, k, m0:m0 + ncols], k == 0, k == KD - 1,
                       ("WGb%d" % wi, "XM"), (ps_[3][1],))
                ectx[i] = ps_

            def e2(i):
                c, gi = eits[i]
                m0, ncols = MG[gi]
                wi = c % 2
                ps_ = ectx.pop(i)
                GA, gak = Rt["GA"].nxt()
                GB, gbk = Rt["GB"].nxt()
                M1, m1k = Rt["M1"].nxt()
                M2, m2k = Rt["M2"].nxt()
                act(GA[:, 0:ncols], ps_[2][0][:, 0:ncols], AF.Sigmoid, (ps_[2][1],), (gak,))
                act(GB[:, 0:ncols], ps_[3][0][:, 0:ncols], AF.Sigmoid, (ps_[3][1],), (gbk,))
                tt(M1[:, 0:ncols], ps_[0][0][:, 0:ncols], GA[:, 0:ncols], ALU.mult, (ps_[0][1], gak), (m1k,))
                tt(M2[:, 0:ncols], ps_[1][0][:, 0:ncols], GB[:, 0:ncols], ALU.mult, (ps_[1][1], gbk), (m2k,))
                tt(MO[wi][:, m0:m0 + ncols], M1[:, 0:ncols], M2[:, 0:ncols], ALU.add, (m1k, m2k),
                   ("MO%d" % wi,), eng="pool")
                if gi == len(MG) - 1:
                    dma("sp", MIX_d[:, c, :], MO[wi][:, :], reads=("MO%d" % wi,))

            pipeline(len(eits), [e1, e2])

        de.close()
        if stop <= 5:
            return nc
        with Stage(nc, S) as st:
            T = {}
            load_iden(st, T)
            MX = st.sb("MX", [128, KD, NM], BF16)
            for kc in range(KD):
                dma("sp", MX[:, kc, :], MIX_d[:, kc, :], writes=("MX",))
            WO = st.sb("WO", [128, KD, D], BF16)
            cast_engs[0] = ("pool", "act")
            for c0 in range(0, D, 512):
                wcast(WO[:, :, c0:c0 + 512], w_out[:, c0:c0 + 512], "WO")
            cast_engs[0] = ("pool",)
            G2B = st.sb("G2B", [128, KD, 128])
            dma("sp", G2B[:], g2b[:, :, :], writes=("G2B",))
            XSr = Rot(st, "XS", [128, D], F32, 3)
            HMr = Rot(st, "HM", [128, D], F32, 6)
            norm_bufs(st, T, 3)
            XOr = Rot(st, "XO", [128, KD, 128], BF16, 8)
            PF = Rot(st, "PF", [128, 512], F32, 4, psum=True)
            items = []
            for ti, (r0, rows) in enumerate(main_tiles):
                m0 = r0 - NPRE
                XOt, ok = XOr.nxt()

                def x_fn(ti=ti, r0=r0, rows=rows, m0=m0):
                    XSt, xk = XSr.nxt()
                    HMt, hk = HMr.nxt()
                    dma("sp", XSt[0:rows, :], xs[r0:r0 + rows, :], writes=(xk,))
                    for hf in range(2):
                        Pt, pk = PF.nxt()
                        for kc in range(KD):
                            mm(Pt[0:rows, :], MX[:, kc, m0:m0 + rows], WO[:, kc, hf * 512:(hf + 1) * 512],
                               kc == 0, kc == KD - 1, ("MX", "WO"), (pk,))
                        tt(HMt[0:rows, hf * 512:(hf + 1) * 512], Pt[0:rows, :],
                           XSt[0:rows, hf * 512:(hf + 1) * 512], ALU.add, (pk, xk), (hk,))
                    dma("sp", HM_d[ti * 128:ti * 128 + rows, :], HMt[0:rows, :], reads=(hk,))
                    return HMt[0:rows, :], hk

                def post(XOt=XOt, ok=ok, m0=m0, rows=rows):
                    dma("sp", XN2_d[:, :, m0:m0 + rows], XOt[:, :, 0:rows], reads=(ok,))
                items.append(dict(x_fn=x_fn, rows=rows, dst=XOt[:, :, 0:rows], dkey=ok, post=post))
            pipeline(len(items), norm_phases(T, items, G2B, "G2B"))

        if stop <= 6:
            return nc
        gh = ExitStack()
        gh.__enter__()
        AC = gh.enter_context(nc.sbuf_tensor("ACgh", [128, 24, NOWN], BF16))
        with Stage(nc, S) as st:
            X2 = st.sb("X2", [128, KD, NM], BF16)
            for kc in range(KD):
                dma("sp", X2[:, kc, :], XN2_d[:, kc, :], writes=("X2",))
            FV = st.sb("FV", [128, 4, 48])
            dma("sp", FV[:], fvec[:, :, :], writes=("FV",))
            WUg = [st.sb("WUg%d" % i, [128, KD, 256], BF16) for i in range(2)]
            WUv = [st.sb("WUv%d" % i, [128, KD, 256], BF16) for i in range(2)]
            FU = [st.sb("FU%d" % i, [128, 2, NM]) for i in range(2)]
            FAgr = Rot(st, "FAg", [128, NOWN], F32, 1)
            FAv = st.sb("FAv", [128, NOWN])
            P = [st.ps("PG%d" % i, [128, 512]) for i in range(8)]

            def load_g(cp_):
                w2 = cp_ % 2
                wcast(WUg[w2][:], w_up[:, cp_ * 256:(cp_ + 1) * 256], "WUg%d" % w2)
                wcast(WUv[w2][:], w_up[:, DFF + cp_ * 256:DFF + (cp_ + 1) * 256], "WUv%d" % w2)

            cast_engs[0] = ("pool", "act")
            load_g(0)
            cast_engs[0] = ("pool",)
            pcount = [0]
            for c in range(24):
                wi = c % 2
                cp_, sub = divmod(c, 2)
                if sub == 0 and cp_ + 1 < 12:
                    load_g(cp_ + 1)
                w2 = cp_ % 2
                ws = slice(sub * 128, (sub + 1) * 128)
                fk = "FU%d" % wi
                for gi, (m0, ncols) in enumerate(MG):
                    b0 = (pcount[0] % 4) * 2
                    pcount[0] += 1
                    pg, pv = P[b0], P[b0 + 1]
                    kg, kv = "PG%d" % b0, "PG%d" % (b0 + 1)
                    for k in range(KD):
                        mm(pg[:, 0:ncols], WUg[w2][:, k, ws], X2[:, k, m0:m0 + ncols], k == 0, k == KD - 1,
                           ("WUg%d" % w2, "X2"), (kg,))
                    for k in range(KD):
                        mm(pv[:, 0:ncols], WUv[w2][:, k, ws], X2[:, k, m0:m0 + ncols], k == 0, k == KD - 1,
                           ("WUv%d" % w2, "X2"), (kv,))
                    cp("act", FU[wi][:, 0, m0:m0 + ncols], pg[:, 0:ncols], (kg,), (fk,))
                    cp("act", FU[wi][:, 1, m0:m0 + ncols], pv[:, 0:ncols], (kv,), (fk,))
                FG, fgk = FAgr.nxt()
                for j, cc in ((0, c), (1, 24 + c)):
                    FAj, fj = (FG, fgk) if j == 0 else (FAv, "FAv")
                    ts(FAj[:, :], FU[wi][:, j, 14:14 + NOWN], FV[:, 0, cc:cc + 1], FV[:, 3, cc:cc + 1],
                       ALU.mult, ALU.add, (fk, "FV"), (fj,))
                    stt(FAj[:, :], FU[wi][:, j, 15:15 + NOWN], FV[:, 1, cc:cc + 1], FAj[:, :],
                        ALU.mult, ALU.add, (fk, "FV", fj), (fj,))
                    stt(FAj[:, :], FU[wi][:, j, 16:16 + NOWN], FV[:, 2, cc:cc + 1], FAj[:, :],
                        ALU.mult, ALU.add, (fk, "FV", fj), (fj,))
                act(FG[:, :], FG[:, :], AF.Gelu_apprx_tanh, (fgk,), (fgk,))
                tt(AC[:, c, :], FG[:, :], FAv[:, :], ALU.mult, (fgk, "FAv"), ("AC%d" % c,))
                if dbg:
                    dma("sp", ACT_d[:, c, :], AC[:, c, :], reads=("AC%d" % c,))

        if stop <= 7:
            gh.close()
            return nc
        with Stage(nc, S) as st:
            WD = [st.sb("WD%d" % i, [128, 24, 512], BF16) for i in range(2)]
            cast_engs[0] = ("pool", "act")
            for hf in range(2):
                for c0 in range(0, 24, 8):
                    wcast(WD[hf][:, c0:c0 + 8, :], w_down[c0 * 128:(c0 + 8) * 128, hf * 512:(hf + 1) * 512],
                          "WD%d" % hf)
            cast_engs[0] = ("pool",)
            GFB = st.sb("GFB", [128, D])
            dma("sp", GFB[:], gfb[:, :], writes=("GFB",))
            CM = st.sb("CM", [128, 1])
            S.op("pool", lambda e: e.memset(CM[:], -0.5), (), ("CM",))
            HMr = Rot(st, "HM", [128, D], F32, 4)
            YOr = Rot(st, "YO", [128, D], F32, 3)
            SQr = Rot(st, "SQJ", [128, D], F32, 2)
            STr = Rot(st, "STs", [128, 4], F32, 3)
            Pr = Rot(st, "PH", [128, 512], F32, 4, psum=True)
            for t in range(16):
                rr = (t + 1) * 128
                HMt, hk = HMr.nxt()
                dma("sp", HMt[:, :], HM_d[rr:rr + 128, :], writes=(hk,))
                for hf in range(2):
                    Pt, pk = Pr.nxt()
                    for c in range(24):
                        mm(Pt[:, :], AC[:, c, t * 128:(t + 1) * 128], WD[hf][:, c, :], c == 0, c == 23,
                           ("WD%d" % hf,), (pk,))
                    tt(HMt[:, hf * 512:(hf + 1) * 512], Pt[:, :], HMt[:, hf * 512:(hf + 1) * 512],
                       ALU.add, (pk, hk), (hk,))
                SQJ, sqk = SQr.nxt()
                STs, stk = STr.nxt()
                YO, yk = YOr.nxt()
                act(SQJ[:, :], HMt[:, :], AF.Square, (hk,), (sqk,))
                S.op("dve", lambda e, STs=STs, SQJ=SQJ: e.reduce_sum(out=STs[:, 0:1], in_=SQJ[:, :],
                                                                      axis=AX.X), (sqk,), (stk,))
                ts(STs[:, 1:2], STs[:, 0:1], 1.0 / D, EPS, ALU.mult, ALU.add, (stk,), (stk,))
                tt(STs[:, 2:3], STs[:, 1:2], CM[:, :], ALU.pow, (stk, "CM"), (stk,), eng="pool")
                stt(YO[:, :], HMt[:, :], STs[:, 2:3], GFB[:, :], ALU.mult, ALU.mult,
                    (hk, stk, "GFB"), (yk,))
                dma("sp", out[t * 128:(t + 1) * 128, :], YO[:, :], reads=(yk,))
        gh.close()
    return nc


def _tables(core):
    b, p = divmod(core, 4)
    n_real_pre = 2048 * p
    pad = NPRE - n_real_pre
    pos = np.arange(L, dtype=np.int64) - pad
    pos = np.maximum(pos, 0).astype(np.float32)
    half = 128
    inv_freq = (1.0 / (10000.0 ** np.linspace(0.0, 1.0, half, dtype=np.float32))).astype(np.float32)
    ang = (pos[:, None] * inv_freq[None, :]).astype(np.float32)
    cos = np.cos(ang).astype(np.float32)
    sin = np.sin(ang).astype(np.float32)
    g = (1.0 - 2.0 ** (-5.0 - np.arange(H, dtype=np.float64)))
    t = np.arange(NPRE, dtype=np.float64)
    dec = g[None, :] ** (NPRE - 1 - t)[:, None]
    dec = dec * (DK ** -0.5)
    decpre = dec.reshape(NPRE // 128, 128, H).transpose(1, 0, 2).astype(np.float32)
    i = np.arange(128, dtype=np.float64)
    ctab = np.zeros((128, 16), np.float32)
    ctab[:, 0:4] = g[None, :] ** (i + 1.0)[:, None]
    ctab[:, 4:8] = (g[None, :] ** (127.0 - i)[:, None]) * (DK ** -0.5)
    ctab[:, 8:12] = (g[None, :] ** np.maximum(15.0 - i, 0.0)[:, None]) * (DK ** -0.5)
    jj = i[:, None]
    ii = i[None, :]
    maskT = np.zeros((128, H, 128), np.float32)
    for h in range(H):
        maskT[:, h, :] = (g[h] ** (-(jj + 1.0))) * (ii >= jj) * (DK ** -0.5)
    mgrp = np.zeros((128, NPG), np.float32)
    for gi in range(NPG):
        mgrp[:, gi] = 1.0 if gi * 512 >= pad else 0.0
    return cos, sin, decpre, ctab, maskT, mgrp


def kernel(x, meta_tokens, norm_mix_g, w_in, rnn_conv_w, rnn_conv_b, rg_a_w, rg_a_b,
           rg_x_w, rg_x_b, lru_lambda, w_branch_rnn, w_branch_ret, w_out, norm_ffn_g,
           w_up, ffn_conv_w, ffn_conv_b, w_down, norm_final_g, _ret_maps=False):
    f = lambda a: np.ascontiguousarray(np.asarray(a, dtype=np.float32))
    x = f(x)
    meta = f(meta_tokens)
    pvec = np.zeros((128, 80), np.float32)
    cw = f(rnn_conv_w)[0].reshape(4, NB, 128)
    for j in range(4):
        pvec[:, j * 10:(j + 1) * 10] = cw[j].T
    pvec[:, 40:50] = f(rnn_conv_b)[0].reshape(NB, 128).T
    pvec[:, 50:60] = f(rg_a_b)[0].reshape(NB, 128).T
    pvec[:, 60:70] = f(rg_x_b)[0].reshape(NB, 128).T
    pvec[:, 70:80] = f(lru_lambda)[0].reshape(NB, 128).T
    fvec = np.zeros((128, 4, 48), np.float32)
    fw = f(ffn_conv_w)[0].reshape(3, 48, 128)
    for j in range(3):
        fvec[:, j, :] = fw[j].T
    fvec[:, 3, :] = f(ffn_conv_b)[0].reshape(48, 128).T
    rgw = np.ascontiguousarray(np.stack([f(rg_a_w)[0], f(rg_x_w)[0]], 0).transpose(2, 0, 1, 3))
    g1b = np.ascontiguousarray(np.broadcast_to(f(norm_mix_g)[0].reshape(KD, 128).T[:, :, None], (128, KD, 128)))
    g2b = np.ascontiguousarray(np.broadcast_to(f(norm_ffn_g)[0].reshape(KD, 128).T[:, :, None], (128, KD, 128)))
    gfb = np.ascontiguousarray(np.broadcast_to(f(norm_final_g)[None, :], (128, D)))
    iden = np.eye(128, dtype=np.float32)
    shared = {
        "w_in": f(w_in)[0], "w_brnn": f(w_branch_rnn)[0], "w_bret": f(w_branch_ret)[0], "w_out": f(w_out)[0],
        "w_up": f(w_up)[0], "w_down": f(w_down)[0], "rgw": rgw, "pvec": pvec, "fvec": fvec,
        "g1b": g1b, "g2b": g2b, "gfb": gfb, "iden": iden,
    }
    in_maps = []
    for core in range(8):
        b, p = divmod(core, 4)
        seq = np.concatenate([meta, x[b]], 0)
        end = NMETA + 2048 * (p + 1)
        stream = np.zeros((L, D), np.float32)
        stream[L - end:] = seq[:end]
        cos, sin, decpre, ctab, maskT, mgrp = _tables(core)
        m = dict(shared)
        m.update({"xs": stream, "cosT": cos, "sinT": sin, "decpre": decpre, "ctab": ctab,
                  "maskT": maskT, "mgrp": mgrp})
        in_maps.append(m)
    if _ret_maps:
        return in_maps
    nc = build()
    res = run_bass_kernel_spmd(nc, in_maps, core_ids=list(range(8)))
    outp = np.zeros((2, SEQ, D), np.float32)
    for core in range(8):
        b, p = divmod(core, 4)
        outp[b, p * 2048:(p + 1) * 2048] = res.results[core]["out"]
    return outp
```

```python
from contextlib import ExitStack
import numpy as np
import concourse.bass as bass
import concourse.mybir as mybir
from concourse.bass_utils import run_bass_kernel_spmd

F32 = mybir.dt.float32
BF16 = mybir.dt.bfloat16
AF = mybir.ActivationFunctionType
ALU = mybir.AluOpType
AX = mybir.AxisListType

D = 1024
KD = 8
SEQ = 8192
NMETA = 16
DRNN = 1280
NB = 10
H = 4
DK = 256
DV = 512
DFF = 3072
DIN = 10752
EPS = 1e-6
NPRE = 6144
NPG = NPRE // 512
HALO = 16
NOWN = 2048
NM = HALO + NOWN
L = NPRE + NM
NT_MAIN = 17
O_U, O_UG, O_Q, O_K, O_V, O_G, O_GA, O_GB = 0, 1280, 2560, 3584, 4608, 6656, 8704, 9728

MG = [(0, 16)] + [(16 + 512 * i, 512) for i in range(4)]


def tile_rows(t):
    if t == 0:
        return 0, 16
    return 16 + 128 * (t - 1), 128


class Sched:
    ENG = ("pe", "act", "dve", "pool", "sp")

    def __init__(self, nc, es):
        self.nc = nc
        self.q = {e: [] for e in self.ENG}
        self.sems = {}
        self.cnt = {}
        for e in self.ENG:
            self.sems[e] = es.enter_context(nc.semaphore("s_" + e))
            self.cnt[e] = 0
        self.ndma = 12
        self.dma_rr = 0
        for i in range(self.ndma):
            k = "d%d" % i
            self.sems[k] = es.enter_context(nc.semaphore("s_" + k))
            self.cnt[k] = 0
        self.waited = {e: {} for e in self.ENG}
        self.res = {}
        self.nops = 0

    def op(self, eng, fn, reads=(), writes=(), dma=False):
        deps = {}

        def add(ev):
            if ev is None:
                return
            s, v = ev
            if deps.get(s, 0) < v:
                deps[s] = v

        for r in reads:
            st = self.res.get(r)
            if st:
                add(st["w"])
        for w in writes:
            st = self.res.get(w)
            if st:
                add(st["w"])
                for s, v in st["r"].items():
                    add((s, v))
        if dma:
            k = "d%d" % self.dma_rr
            self.dma_rr = (self.dma_rr + 1) % self.ndma
            if self.cnt[k] > 0:
                add((k, 16 * self.cnt[k]))
            self.cnt[k] += 1
            ev = (k, 16 * self.cnt[k])
            inc = 16
        else:
            self.cnt[eng] += 1
            ev = (eng, self.cnt[eng])
            inc = 1
        waits = []
        for s, v in deps.items():
            if s == eng and eng == "pe":
                continue
            if self.waited[eng].get(s, 0) >= v:
                continue
            self.waited[eng][s] = v
            waits.append((s, v))
        sem = self.sems[ev[0]]
        sems = self.sems

        def emit(e, fn=fn, waits=waits, sem=sem, inc=inc):
            for s, v in waits:
                e.wait_ge(sems[s], v)
            fn(e).then_inc(sem, inc)

        self.q[eng].append(emit)
        for r in reads:
            st = self.res.setdefault(r, {"w": None, "r": {}})
            if st["r"].get(ev[0], 0) < ev[1]:
                st["r"][ev[0]] = ev[1]
        for w in writes:
            self.res[w] = {"w": ev, "r": {}}
        self.nops += 1

    def finish(self, eng="sp"):
        waits = [(k, 16 * self.cnt[k]) for k in self.sems if k[1:].isdigit() and self.cnt[k] > 0]
        sems = self.sems

        def emit(e):
            for s, v in waits:
                e.wait_ge(sems[s], v)

        self.q[eng].append(emit)


class Stage:
    _n = [0]

    def __init__(self, nc, S):
        self.nc, self.S = nc, S
        self.es = ExitStack()
        Stage._n[0] += 1
        self.pfx = "g%d_" % Stage._n[0]

    def __enter__(self):
        self.es.__enter__()
        return self

    def sb(self, name, shape, dt=F32):
        return self.es.enter_context(self.nc.sbuf_tensor(self.pfx + name, list(shape), dt))

    def ps(self, name, shape, dt=F32):
        return self.es.enter_context(self.nc.psum_tensor(self.pfx + name, list(shape), dt))

    def __exit__(self, *a):
        S = self.S
        S.barrier()
        q = S.q
        with self.nc.Block() as block:
            @block.tensor
            def _(e):
                for f in q["pe"]:
                    f(e)

            @block.scalar
            def _(e):
                for f in q["act"]:
                    f(e)

            @block.vector
            def _(e):
                for f in q["dve"]:
                    f(e)

            @block.gpsimd
            def _(e):
                for f in q["pool"]:
                    f(e)

            @block.sync
            def _(e):
                for f in q["sp"]:
                    f(e)
        S.q = {e: [] for e in S.ENG}
        S.res = {}
        return self.es.__exit__(*a)


def _barrier(self):
    targets = []
    for k in self.sems:
        v = self.cnt[k] * (16 if k[1:].isdigit() else 1)
        if v > 0:
            targets.append((k, v))
    sems = self.sems
    for eng in self.ENG:
        waits = []
        for s, v in targets:
            if self.waited[eng].get(s, 0) >= v:
                continue
            self.waited[eng][s] = v
            waits.append((s, v))

        def emit(e, waits=waits):
            for s, v in waits:
                e.wait_ge(sems[s], v)

        self.q[eng].append(emit)


Sched.barrier = _barrier


def build(stop=99, dbg=False):
    nc = bass.Bass("TRN2", target_bir_lowering=False)
    skind = "ExternalOutput" if dbg else "Internal"

    def din(name, shape, dt=F32):
        return nc.dram_tensor(name, list(shape), dt, kind="ExternalInput").ap()

    xs = din("xs", [L, D])
    w_in = din("w_in", [D, DIN])
    w_brnn = din("w_brnn", [DRNN, D])
    w_bret = din("w_bret", [H * DV, D])
    w_out = din("w_out", [D, D])
    w_up = din("w_up", [D, 2 * DFF])
    w_down = din("w_down", [DFF, D])
    rgw = din("rgw", [128, 2, NB, 128])
    pvec = din("pvec", [128, 80])
    fvec = din("fvec", [128, 4, 48])
    g1b = din("g1b", [128, KD, 128])
    g2b = din("g2b", [128, KD, 128])
    gfb = din("gfb", [128, D])
    cosT = din("cosT", [L, 128])
    sinT = din("sinT", [L, 128])
    decpre = din("decpre", [128, NPRE // 128, H])
    ctab = din("ctab", [128, 16])
    maskT = din("maskT", [128, H, 128])
    mgrp = din("mgrp", [128, NPG])
    iden_in = din("iden", [128, 128])
    out = nc.dram_tensor("out", [NOWN, D], F32, kind="ExternalOutput").ap()
    XNT_d = nc.dram_tensor("xnt_d", [128, KD, L], BF16, kind=skind).ap()
    OA_d = nc.dram_tensor("oa_d", [128, NB, NM], BF16, kind=skind).ap()
    OB_d = nc.dram_tensor("ob_d", [128, 16, NM], BF16, kind=skind).ap()
    MIX_d = nc.dram_tensor("mix_d", [128, KD, NM], BF16, kind=skind).ap()
    HM_d = nc.dram_tensor("hm_d", [NT_MAIN * 128, D], F32, kind=skind).ap()
    XN2_d = nc.dram_tensor("xn2_d", [128, KD, NM], BF16, kind=skind).ap()
    ACT_d = nc.dram_tensor("act_d", [128, 24, NOWN], BF16, kind=skind).ap()
    SF_d = nc.dram_tensor("sf_d", [128, 2 * H, DV], F32, kind=skind).ap()

    ges = ExitStack()
    with ges:
        S = Sched(nc, ges)

        def dma(eng, out_ap, in_ap, reads=(), writes=()):
            S.op(eng, lambda e: e.dma_start(out=out_ap, in_=in_ap), reads, writes, dma=True)

        def act(out_ap, in_ap, func, reads, writes, bias=None, scale=None):
            kw = {}
            if bias is not None:
                kw["bias"] = bias
            if scale is not None:
                kw["scale"] = scale
            S.op("act", lambda e: e.activation(out=out_ap, in_=in_ap, func=func, **kw), reads, writes)

        def tt(out_ap, a, b, op, reads, writes, eng="dve"):
            S.op(eng, lambda e: e.tensor_tensor(out=out_ap, in0=a, in1=b, op=op), reads, writes)

        def ts(out_ap, a, s1, s2, op0, op1, reads, writes, eng="dve"):
            if op1 is None:
                S.op(eng, lambda e: e.tensor_scalar(out=out_ap, in0=a, scalar1=s1, scalar2=None, op0=op0),
                     reads, writes)
            else:
                S.op(eng, lambda e: e.tensor_scalar(out=out_ap, in0=a, scalar1=s1, scalar2=s2, op0=op0, op1=op1),
                     reads, writes)

        def stt(out_ap, a, s, b, op0, op1, reads, writes):
            S.op("dve", lambda e: e.scalar_tensor_tensor(out=out_ap, in0=a, scalar=s, in1=b, op0=op0, op1=op1),
                 reads, writes)

        def cp(eng, out_ap, in_ap, reads, writes):
            if eng == "act":
                S.op(eng, lambda e: e.activation(out=out_ap, in_=in_ap, func=AF.Copy), reads, writes)
            else:
                S.op(eng, lambda e: e.tensor_copy(out=out_ap, in_=in_ap), reads, writes)

        def mm(out_ap, lhsT, rhs, start, stop, reads, writes):
            S.op("pe", lambda e: e.matmul(out_ap, lhsT, rhs, start=start, stop=stop), reads, writes)

        STG = [ges.enter_context(nc.sbuf_tensor("STG%d" % i, [128, 640], F32)) for i in range(4)]
        stg_i = [0]

        cast_engs = [("pool",)]

        def wcast(dst_view, src_ap, key):
            kc = dst_view.shape[1]
            ncols = dst_view.shape[2]
            for k in range(kc):
                i = stg_i[0] % 4
                stg_i[0] += 1
                engs = cast_engs[0]
                dma("sp", STG[i][:, 0:ncols], src_ap[k * 128:(k + 1) * 128, :], writes=("STG%d" % i,))
                cp(engs[k % len(engs)], dst_view[:, k, :], STG[i][:, 0:ncols], ("STG%d" % i,), (key,))

        class Rot:
            def __init__(self, st, name, shape, dt, n, psum=False):
                self.items = []
                for i in range(n):
                    t = (st.ps if psum else st.sb)("%s%d" % (name, i), shape, dt)
                    self.items.append((t, "%s%d" % (name, i)))
                self.i = 0

            def nxt(self):
                it = self.items[self.i % len(self.items)]
                self.i += 1
                return it

        def pipeline(n, phases):
            for step in range(n + len(phases) - 1):
                for pi in reversed(range(len(phases))):
                    t = step - pi
                    if 0 <= t < n:
                        phases[pi](t)

        def norm_bufs(st, T, n=3, npt=2):
            T["SQJ"] = Rot(st, "SQJ", [128, D], F32, 3)
            T["ST"] = Rot(st, "ST", [128, 4], F32, n + 2)
            T["XB"] = Rot(st, "XB", [128, D], BF16, n)
            T["PT"] = Rot(st, "PT", [128, KD, 128], BF16, npt, psum=True)
            T["CM"] = st.sb("CM", [128, 1])
            S.op("pool", lambda e: e.memset(T["CM"][:], -0.5), (), ("CM",))

        def norm_phases(T, items, gb, gbkey):
            ctx = {}

            def p0(t):
                ctx[t] = dict(zip(("x_ap", "xk"), items[t]["x_fn"]()))

            def p1(t):
                c = ctx[t]
                rows = items[t]["rows"]
                SQJ, sqk = T["SQJ"].nxt()
                act(SQJ[0:rows, :], c["x_ap"], AF.Square, (c["xk"],), (sqk,))
                c.update(SQJ=SQJ, sqk=sqk)

            def p2(t):
                c = ctx[t]
                rows = items[t]["rows"]
                STt, stk = T["ST"].nxt()
                SQJ = c["SQJ"]
                S.op("dve", lambda e: e.reduce_sum(out=STt[0:rows, 0:1], in_=SQJ[0:rows, :], axis=AX.X),
                     (c["sqk"],), (stk,))
                ts(STt[0:rows, 1:2], STt[0:rows, 0:1], 1.0 / D, EPS, ALU.mult, ALU.add, (stk,), (stk,))
                tt(STt[0:rows, 2:3], STt[0:rows, 1:2], T["CM"][0:rows, :], ALU.pow, (stk, "CM"), (stk,), eng="pool")
                c.update(STt=STt, stk=stk)

            def p3(t):
                c = ctx[t]
                rows = items[t]["rows"]
                XBt, xbk = T["XB"].nxt()
                act(XBt[0:rows, :], c["x_ap"], AF.Identity, (c["xk"], c["stk"]), (xbk,), scale=c["STt"][0:rows, 2:3])
                c.update(XBt=XBt, xbk=xbk)

            def p4(t):
                c = ctx[t]
                rows = items[t]["rows"]
                PTt, ptk = T["PT"].nxt()
                XBt = c["XBt"]
                for kc in range(KD):
                    S.op("pe", lambda e, kc=kc: e.transpose(PTt[:, kc, 0:rows],
                                                            XBt[0:rows, kc * 128:(kc + 1) * 128],
                                                            T["IDB"][0:rows, 0:rows]),
                         (c["xbk"], "IDB"), (ptk,))
                c.update(PTt=PTt, ptk=ptk)

            def p5(t):
                c = ctx.pop(t)
                it = items[t]
                rows = it["rows"]
                tt(it["dst"], c["PTt"][:, :, 0:rows], gb[:, :, 0:rows], ALU.mult, (c["ptk"], gbkey), (it["dkey"],))
                if it.get("post"):
                    it["post"]()

            return [p0, p1, p2, p3, p4, p5]

        def load_iden(st, T):
            T["IDF"] = st.sb("IDF", [128, 128])
            T["IDB"] = st.sb("IDB", [128, 128], BF16)
            dma("sp", T["IDF"][:], iden_in[:, :], writes=("IDF",))
            cp("dve", T["IDB"][:], T["IDF"][:], ("IDF",), ("IDB",))

        stream_tiles = [(i * 128, 128) for i in range(NPRE // 128)] + [(NPRE, 16)] + \
                       [(NPRE + 16 + i * 128, 128) for i in range(16)]
        main_tiles = stream_tiles[NPRE // 128:]
        pre_groups = [(i * 512, 512) for i in range(NPG)]
        main_groups = [(NPRE, 16)] + [(NPRE + 16 + i * 512, 512) for i in range(4)]

        def rotary(T, src, skey, dst, dkey, rows, nh, c, s, ckey, sgn_eng="dve"):
            for h in range(nh):
                x1 = src[0:rows, h * DK:h * DK + 128]
                x2 = src[0:rows, h * DK + 128:(h + 1) * DK]
                R0, k0 = T["RT"].nxt()
                R1, k1 = T["RT"].nxt()
                R2, k2 = T["RT"].nxt()
                R3, k3 = T["RT"].nxt()
                tt(R0[0:rows, :], x1, c, ALU.mult, (skey, ckey), (k0,))
                tt(R1[0:rows, :], x2, s, ALU.mult, (skey, ckey), (k1,), eng="pool")
                tt(R2[0:rows, :], x2, c, ALU.mult, (skey, ckey), (k2,))
                tt(R3[0:rows, :], x1, s, ALU.mult, (skey, ckey), (k3,), eng="pool")
                tt(dst[0:rows, h * DK:h * DK + 128], R0[0:rows, :], R1[0:rows, :], ALU.subtract,
                   (k0, k1), (dkey,))
                tt(dst[0:rows, h * DK + 128:(h + 1) * DK], R2[0:rows, :], R3[0:rows, :], ALU.add,
                   (k2, k3), (dkey,))

        with Stage(nc, S) as st:
            T = {}
            load_iden(st, T)
            G1B = st.sb("G1B", [128, KD, 128])
            dma("sp", G1B[:], g1b[:, :, :], writes=("G1B",))
            WK = st.sb("WK", [128, KD, H * DK], BF16)
            WV = st.sb("WV", [128, KD, H * DV], BF16)
            cast_engs[0] = ("pool", "act")
            for c0 in range(0, H * DK, 512):
                wcast(WK[:, :, c0:c0 + 512], w_in[:, O_K + c0:O_K + c0 + 512], "WK")
            for c0 in range(0, H * DV, 512):
                wcast(WV[:, :, c0:c0 + 512], w_in[:, O_V + c0:O_V + c0 + 512], "WV")
            cast_engs[0] = ("pool",)
            DPRE = st.sb("DPRE", [128, NPRE // 128, H])
            dma("sp", DPRE[:], decpre[:, :, :], writes=("DPRE",))
            XSr = Rot(st, "XS", [128, D], F32, 6)
            norm_bufs(st, T, 3, npt=1)
            XOr = Rot(st, "XO", [128, KD, 512], BF16, 2)
            CSr = Rot(st, "CS", [128, 2, 128], F32, 3)
            T["RT"] = Rot(st, "RT", [128, 128], F32, 8)
            KFr = Rot(st, "KF", [128, H * DK], F32, 2)
            KRr = Rot(st, "KR", [128, 4, H * DK], BF16, 2)
            VVr = Rot(st, "VV", [128, 4, H * DV], BF16, 2)
            SF = st.sb("SF", [128, 2 * H, DV])
            S.op("dve", lambda e: e.memset(SF[:], 0.0), (), tuple("SF%d" % i for i in range(2 * H)))
            PK = [st.ps("PK%d" % i, [128, 512]) for i in range(2)]
            PVv = [st.ps("PV%d" % i, [128, 512]) for i in range(2)]
            PSr = Rot(st, "PS", [128, 512], F32, 2, psum=True)
            items = []
            tinfo = []
            for gidx, (c0, ncols) in enumerate(pre_groups + main_groups):
                XOt, ok = XOr.nxt()
                nt = max(1, ncols // 128)
                is_pre = gidx < NPG
                for j in range(nt):
                    rows = min(128, ncols)
                    r0 = c0 + j * 128

                    def x_fn(r0=r0, rows=rows):
                        XSt, xk = XSr.nxt()
                        dma("sp", XSt[0:rows, :], xs[r0:r0 + rows, :], writes=(xk,))
                        return XSt[0:rows, :], xk

                    post = None
                    if j == nt - 1:
                        def post(XOt=XOt, ok=ok, c0=c0, ncols=ncols, nt=nt):
                            dma("sp", XNT_d[:, :, c0:c0 + ncols], XOt[:, :, 0:ncols],
                                reads=tuple("%s_%d" % (ok, jj) for jj in range(nt)))
                    items.append(dict(x_fn=x_fn, rows=rows, dst=XOt[:, :, j * 128:j * 128 + rows],
                                      dkey="%s_%d" % (ok, j), post=post))
                    tinfo.append((gidx, j, is_pre, XOt, "%s_%d" % (ok, j), r0))
            gst = {}
            ctx = {}

            def c1(i):
                g, t4, is_pre, XGt, xk, r0 = tinfo[i]
                if not is_pre:
                    return
                if t4 == 0:
                    gst[g] = (KRr.nxt(), VVr.nxt())
                (KR, krk), (VV, vvk) = gst[g]
                tile = g * 4 + t4
                CS, csk = CSr.nxt()
                dma("sp", CS[:, 0, :], cosT[r0:r0 + 128, :], writes=(csk,))
                dma("sp", CS[:, 1, :], sinT[r0:r0 + 128, :], writes=(csk,))
                KF, kfk = KFr.nxt()
                xt = lambda kc: XGt[:, kc, t4 * 128:(t4 + 1) * 128]
                for hf in range(2):
                    for kc in range(KD):
                        mm(PK[hf][:, :], xt(kc), WK[:, kc, hf * 512:(hf + 1) * 512], kc == 0, kc == KD - 1,
                           (xk, "WK"), ("PK%d" % hf,))
                for h in range(H):
                    hf, o = divmod(h, 2)
                    act(KF[:, h * DK:(h + 1) * DK], PK[hf][:, o * DK:(o + 1) * DK], AF.Identity,
                        ("PK%d" % hf, "DPRE"), (kfk,), scale=DPRE[:, tile, h:h + 1])
                for q4 in range(4):
                    pv, pvk = PVv[q4 % 2], "PV%d" % (q4 % 2)
                    for kc in range(KD):
                        mm(pv[:, :], xt(kc), WV[:, kc, q4 * 512:(q4 + 1) * 512], kc == 0, kc == KD - 1,
                           (xk, "WV"), (pvk,))
                    cp("act", VV[:, t4, q4 * 512:(q4 + 1) * 512], pv[:, :], (pvk,), (vvk,))
                ctx[i] = (KF, kfk, CS, csk)

            def c2(i):
                g, t4, is_pre, XGt, xk, r0 = tinfo[i]
                if not is_pre:
                    return
                (KR, krk), (VV, vvk) = gst[g]
                KF, kfk, CS, csk = ctx.pop(i)
                rotary(T, KF, kfk, KR[:, t4, :], krk, 128, H, CS[:, 0, :], CS[:, 1, :], csk)
                if t4 == 3:
                    for h in range(H):
                        for dc in range(2):
                            Pt, pk = PSr.nxt()
                            for t in range(4):
                                mm(Pt[:, :], KR[:, t, h * DK + dc * 128:h * DK + (dc + 1) * 128],
                                   VV[:, t, h * DV:(h + 1) * DV], t == 0, t == 3, (krk, vvk), (pk,))
                            tt(SF[:, h * 2 + dc, :], Pt[:, :], SF[:, h * 2 + dc, :], ALU.add,
                               (pk, "SF%d" % (h * 2 + dc)), ("SF%d" % (h * 2 + dc),))

            pipeline(len(items), norm_phases(T, items, G1B, "G1B") + [c1, c2])
            dma("sp", SF_d[:, :, :], SF[:], reads=tuple("SF%d" % i for i in range(2 * H)))

        if stop <= 1:
            return nc
        with Stage(nc, S) as st:
            WU = st.sb("WU", [128, KD, DRNN], BF16)
            WG = st.sb("WG", [128, KD, DRNN], BF16)
            RGW = st.sb("RGW", [128, 2, NB, 128], BF16)
            PV = st.sb("PV", [128, 80])
            MGR = st.sb("MGR", [128, NPG])
            BM = st.sb("BM", [128, NPG, NB])
            SC = st.sb("SC", [128, 3, NB])
            HB = st.sb("HB", [128, 2 * NB])
            dma("sp", PV[:], pvec[:, :], writes=("PV",))
            dma("sp", MGR[:], mgrp[:, :], writes=("MGR",))
            cast_engs[0] = ("pool", "act")
            for c0 in range(0, DRNN, 640):
                wcast(WU[:, :, c0:c0 + 640], w_in[:, O_U + c0:O_U + c0 + 640], "WU")
            for ax in range(2):
                for n0 in range(0, NB, 5):
                    i = stg_i[0] % 4
                    stg_i[0] += 1
                    dma("sp", STG[i][:, 0:640].rearrange("p (n j) -> p n j", n=5), rgw[:, ax, n0:n0 + 5, :],
                        writes=("STG%d" % i,))
                    cp("pool", RGW[:, ax, n0:n0 + 5, :], STG[i][:, 0:640].rearrange("p (n j) -> p n j", n=5),
                       ("STG%d" % i,), ("RGW",))
            for c0 in range(0, DRNN, 640):
                wcast(WG[:, :, c0:c0 + 640], w_in[:, O_UG + c0:O_UG + c0 + 640], "WG")
            cast_engs[0] = ("pool",)
            act(SC[:, 0, :], PV[:, 70:80], AF.Exp, ("PV",), ("SC0",), scale=-1.0)
            act(SC[:, 0, :], SC[:, 0, :], AF.Ln, ("SC0",), ("SC0",), bias=1.0)
            ts(SC[:, 1, :], SC[:, 0, :], -8.0, None, ALU.mult, None, ("SC0",), ("SC",))
            ts(SC[:, 2, :], SC[:, 0, :], -4.0, None, ALU.mult, None, ("SC0",), ("SC",))
            ts(HB[:, :], PV[:, 50:70], 0.5, None, ALU.mult, None, ("PV",), ("HB",))
            for g in range(NPG):
                ts(BM[:, g, :], PV[:, 40:50], MGR[:, g:g + 1], None, ALU.mult, None, ("PV", "MGR"), ("BM",))
            XG = [st.sb("XG%d" % i, [128, KD, 512], BF16) for i in range(2)]
            IDFb = st.sb("IDFb", [128, 128])
            dma("sp", IDFb[:], iden_in[:, :], writes=("IDFb",))
            DG = st.sb("DG", [128, 4, NB, 128], BF16)
            for j in range(4):
                for n in range(NB):
                    ts(DG[:, j, n, :], IDFb[:, :], PV[:, j * 10 + n:j * 10 + n + 1], None, ALU.mult, None,
                       ("IDFb", "PV"), ("DG",))
            UB = st.sb("UB", [128, NB, 516], BF16)
            HST = st.sb("HST", [128, NB])
            S.op("dve", lambda e: e.memset(UB[:], 0.0), (), tuple("U%d" % n for n in range(NB)))
            S.op("dve", lambda e: e.memset(HST[:], 0.0), (), tuple("HST%d" % n for n in range(NB)))
            BS = 4
            QB = st.sb("QB", [128, 1])
            S.op("pool", lambda e: e.memset(QB[:], 0.25), (), ("QB",))
            XRr = Rot(st, "XR", [128, 512], BF16, 5)
            TRr = Rot(st, "TR", [128, 512], F32, 3)
            TIr = Rot(st, "TI", [128, 512], F32, 3)
            BBr = Rot(st, "BB", [128, 512], F32, 5)
            HHr = Rot(st, "HH", [128, 512], F32, 3)
            OATr = Rot(st, "OAT", [128, 512], BF16, 3)
            GGb = [st.sb("GG%d" % i, [128, NB, 512], BF16) for i in range(3)]
            AAg = [st.sb("AAg%d" % i, [128, BS, 512]) for i in range(2)]
            OMg = [st.sb("OMg%d" % i, [128, BS, 512]) for i in range(2)]
            T1g = [st.sb("T1g%d" % i, [128, BS, 512]) for i in range(2)]
            PU = Rot(st, "PU", [128, 512], F32, 2, psum=True)
            PC = Rot(st, "PC", [128, 512], F32, 2, psum=True)
            PRA = Rot(st, "PRA", [128, 512], F32, 2, psum=True)
            PRX = Rot(st, "PRX", [128, 512], F32, 2, psum=True)
            allg = [(g, True) for g in pre_groups] + [(g, False) for g in main_groups]
            chains = []
            for gi, ((c0, ncols), is_pre) in enumerate(allg):
                for n in range(NB):
                    chains.append((gi, c0, ncols, is_pre, n))
            NCH = len(chains)
            ctx = {}

            def load_xg(gi):
                (c0, ncols), _ = allg[gi]
                dma("sp", XG[gi % 2][:, :, 0:ncols], XNT_d[:, :, c0:c0 + ncols], writes=("XG%d" % (gi % 2),))

            load_xg(0)

            def b1(ci):
                gi, c0, ncols, is_pre, n = chains[ci]
                XGt, xk = XG[gi % 2], "XG%d" % (gi % 2)
                if n == 0:
                    if gi + 1 < len(allg):
                        load_xg(gi + 1)
                    if not is_pre:
                        mgi = gi - NPG
                        GG, ggk = GGb[mgi % 3], "GG%d" % (mgi % 3)
                        for n2 in range(NB):
                            g_ps, gk = PU.nxt()
                            for kc in range(KD):
                                mm(g_ps[:, 0:ncols], WG[:, kc, n2 * 128:(n2 + 1) * 128], XGt[:, kc, 0:ncols],
                                   kc == 0, kc == KD - 1, ("WG", xk), (gk,))
                            act(GG[:, n2, 0:ncols], g_ps[:, 0:ncols], AF.Gelu_apprx_tanh, (gk,), (ggk,))
                u_ps, puk = PU.nxt()
                for kc in range(KD):
                    mm(u_ps[:, 0:ncols], WU[:, kc, n * 128:(n + 1) * 128], XGt[:, kc, 0:ncols], kc == 0,
                       kc == KD - 1, ("WU", xk), (puk,))
                ctx[ci] = dict(u_ps=u_ps, puk=puk)

            def b2(ci):
                gi, c0, ncols, is_pre, n = chains[ci]
                c = ctx[ci]
                cp("dve", UB[:, n, 3:3 + ncols], c["u_ps"][:, 0:ncols], (c["puk"],), ("U%d" % n,))

            def b3(ci):
                gi, c0, ncols, is_pre, n = chains[ci]
                c = ctx[ci]
                uk = "U%d" % n
                pc, pck = PC.nxt()
                for j in range(4):
                    mm(pc[:, 0:ncols], DG[:, j, n, :], UB[:, n, j:j + ncols], j == 0, j == 3, ("DG", uk), (pck,))
                c.update(pc=pc, pck=pck)

            def b4(ci):
                gi, c0, ncols, is_pre, n = chains[ci]
                c = ctx[ci]
                uk = "U%d" % n
                XR, xrk = XRr.nxt()
                if is_pre:
                    g = c0 // 512
                    act(XR[:, 0:ncols], c["pc"][:, 0:ncols], AF.Identity, (c["pck"], "BM"), (xrk,),
                        bias=BM[:, g, n:n + 1])
                else:
                    act(XR[:, 0:ncols], c["pc"][:, 0:ncols], AF.Identity, (c["pck"], "PV"), (xrk,),
                        bias=PV[:, 40 + n:41 + n])
                cp("pool", UB[:, n, 0:3], UB[:, n, ncols:ncols + 3], (uk,), (uk,))
                import os as _os
                if _os.environ.get("DBGXR") and not is_pre:
                    dma("sp", OA_d[:, n, c0 - NPRE:c0 - NPRE + ncols], XR[:, 0:ncols], reads=(xrk,))
                c.update(XR=XR, xrk=xrk)

            def b5(ci):
                gi, c0, ncols, is_pre, n = chains[ci]
                c = ctx[ci]
                XR, xrk = c["XR"], c["xrk"]
                ra, rak = PRA.nxt()
                rx, rxk = PRX.nxt()
                mm(ra[:, 0:ncols], RGW[:, 0, n, :], XR[:, 0:ncols], True, True, ("RGW", xrk), (rak,))
                mm(rx[:, 0:ncols], RGW[:, 1, n, :], XR[:, 0:ncols], True, True, ("RGW", xrk), (rxk,))
                c.update(ra=ra, rak=rak, rx=rx, rxk=rxk)

            def b6(ci):
                gi, c0, ncols, is_pre, n = chains[ci]
                c = ctx[ci]
                TR, trk = TRr.nxt()
                TI, tik = TIr.nxt()
                act(TR[:, 0:ncols], c["ra"][:, 0:ncols], AF.Tanh, (c["rak"], "HB"), (trk,), bias=HB[:, n:n + 1],
                    scale=0.5)
                act(TI[:, 0:ncols], c["rx"][:, 0:ncols], AF.Tanh, (c["rxk"], "HB"), (tik,),
                    bias=HB[:, NB + n:NB + n + 1], scale=0.5)
                c.update(TR=TR, trk=trk, TI=TI, tik=tik)

            def b7(ci):
                gi, c0, ncols, is_pre, n = chains[ci]
                c = ctx[ci]
                bid, slot = divmod(ci, BS)
                bb = bid % 2
                TR, trk, TI, tik, XR, xrk = c["TR"], c["trk"], c["TI"], c["tik"], c["XR"], c["xrk"]
                act(AAg[bb][:, slot, 0:ncols], TR[:, 0:ncols], AF.Exp, (trk, "SC"), ("AAg%d" % bb,),
                    scale=SC[:, 2, n:n + 1], bias=SC[:, 2, n:n + 1])
                stt(T1g[bb][:, slot, 0:ncols], TI[:, 0:ncols], 1.0, XR[:, 0:ncols], ALU.add, ALU.mult,
                    (tik, xrk), ("T1g%d" % bb,))

            def b7b(ci):
                gi, c0, ncols, is_pre, n = chains[ci]
                bid, slot = divmod(ci, BS)
                bb = bid % 2
                if ncols < 512:
                    S.op("pool", lambda e, bb=bb, slot=slot: e.memset(OMg[bb][:, slot, :], 0.0),
                         ("OMg%d" % bb,), ("OMg%d" % bb,))
                tt(OMg[bb][:, slot, 0:ncols], AAg[bb][:, slot, 0:ncols], AAg[bb][:, slot, 0:ncols], ALU.mult,
                   ("AAg%d" % bb,), ("OMg%d" % bb,), eng="pool")

            def batch_of(ci):
                if ci % BS == BS - 1 or ci == NCH - 1:
                    bid = ci // BS
                    return bid % 2, list(range(bid * BS, ci + 1))
                return None

            def b8(ci):
                b = batch_of(ci)
                if b is None:
                    return
                bb, ids = b
                nb = len(ids)
                act(OMg[bb][:, 0:nb, :], OMg[bb][:, 0:nb, :], AF.Sqrt, ("OMg%d" % bb, "QB"), ("OMg%d" % bb,),
                    scale=-0.25, bias=QB[:, 0:1])

            def b9(ci):
                b = batch_of(ci)
                if b is None:
                    return
                bb, ids = b
                for slot, cj in enumerate(ids):
                    gi, c0, ncols, is_pre, n = chains[cj]
                    BB, bbk = BBr.nxt()
                    tt(BB[:, 0:ncols], T1g[bb][:, slot, 0:ncols], OMg[bb][:, slot, 0:ncols], ALU.mult,
                       ("T1g%d" % bb, "OMg%d" % bb), (bbk,), eng="pool")
                    ctx[cj].update(BB=BB, bbk=bbk)

            def b10(ci):
                b = batch_of(ci)
                if b is None:
                    return
                bb, ids = b
                for slot, cj in enumerate(ids):
                    gi, c0, ncols, is_pre, n = chains[cj]
                    c = ctx.pop(cj)
                    BB, bbk = c["BB"], c["bbk"]
                    HH, hhk = HHr.nxt()
                    hk = "HST%d" % n
                    S.op("dve", lambda e, n=n, ncols=ncols, HH=HH, BB=BB, bb=bb, slot=slot: e.tensor_tensor_scan(
                        out=HH[:, 0:ncols], data0=AAg[bb][:, slot, 0:ncols], data1=BB[:, 0:ncols],
                        initial=HST[:, n:n + 1], op0=ALU.mult, op1=ALU.add), ("AAg%d" % bb, bbk, hk), (hhk,))
                    cp("dve", HST[:, n:n + 1], HH[:, ncols - 1:ncols], (hhk,), (hk,))
                    if not is_pre:
                        mgi = gi - NPG
                        GG, ggk = GGb[mgi % 3], "GG%d" % (mgi % 3)
                        OAT, ok = OATr.nxt()
                        tt(OAT[:, 0:ncols], HH[:, 0:ncols], GG[:, n, 0:ncols], ALU.mult, (hhk, ggk), (ok,))
                        m0 = c0 - NPRE
                        import os as _os
                        if not _os.environ.get("DBGXR"):
                            dma("sp", OA_d[:, n, m0:m0 + ncols], OAT[:, 0:ncols], reads=(ok,))

            pipeline(NCH, [b1, b2, b3, b4, b5, b6, b7, b7b, b8, b9, b10])
        if stop <= 2:
            return nc

        de = ExitStack()
        de.__enter__()
        XM = de.enter_context(nc.sbuf_tensor("XMde", [128, KD, NM], BF16))
        GAM = [1.0 - 2.0 ** (-5.0 - h) for h in range(H)]
        with Stage(nc, S) as st:
            T = {}
            load_iden(st, T)
            for kc in range(KD):
                dma("sp", XM[:, kc, :], XNT_d[:, kc, NPRE:NPRE + NM], writes=("XM",))
            CT = st.sb("CT", [128, 16])
            MT = st.sb("MT", [128, H, 128])
            CM = st.sb("CM", [128, 1])
            S.op("pool", lambda e: e.memset(CM[:], -0.5), (), ("CM",))
            dma("sp", CT[:], ctab[:, :], writes=("CT",))
            dma("sp", MT[:], maskT[:, :, :], writes=("MT",))
            WH = [st.sb("WH%d" % i, [128, KD, 1536], BF16) for i in range(2)]
            CSr = Rot(st, "CS", [128, 2, 128], F32, 4)
            T["RT"] = Rot(st, "RT", [128, 128], F32, 8)
            QKFr = Rot(st, "QKF", [128, 2 * DK], F32, 3)
            QKRr = Rot(st, "QKR", [128, 2 * DK], BF16, 3)
            KDr = Rot(st, "KDc", [128, DK], BF16, 5)
            VVr = Rot(st, "VV", [128, DV], BF16, 8)
            SGr = Rot(st, "SG", [128, DV], F32, 11)
            QKTr = Rot(st, "QKT", [128, 4, 128], BF16, 5)
            SCTr = Rot(st, "SCT", [128, 128], BF16, 3)
            SF = st.sb("SF", [128, 2 * H, DV])
            SBFring = [st.sb("SBFr%d" % j, [128, 2, DV], BF16) for j in range(5)]
            OFr = Rot(st, "OF", [128, DV], F32, 4)
            OBTr = Rot(st, "OBT", [128, DV], BF16, 3)
            OBOr = Rot(st, "OBO", [128, 4, 128], BF16, 2)
            SQJr = Rot(st, "SQJ", [128, DV], F32, 3)
            STr = Rot(st, "STs", [128, 4], F32, 4)
            dma("sp", SF[:], SF_d[:, :, :], writes=tuple("SF%d" % i for i in range(2 * H)))
            PQ = st.ps("PQ", [128, 512])
            PV_ = st.ps("PVm", [128, 512])
            PG = st.ps("PG", [128, 512])
            PSC = st.ps("PSC", [128, 512])
            PO = st.ps("PO", [128, 512])
            PStb = [st.ps("PSt%d" % i, [128, 512]) for i in range(2)]
            PT8 = st.ps("PT8", [128, 8, 128], BF16)
            PT = PT8[:, 0:4, :]
            PT2 = PT8[:, 4:8, :]
            its = [(h, ti) for h in range(H) for ti in range(len(main_tiles))]
            NT = len(main_tiles)
            ctx = {}
            sbf = {}

            def load_w(h):
                W = WH[h % 2]
                wk = "WH%d" % (h % 2)
                wcast(W[:, :, 0:256], w_in[:, O_Q + h * DK:O_Q + (h + 1) * DK], wk)
                wcast(W[:, :, 256:512], w_in[:, O_K + h * DK:O_K + (h + 1) * DK], wk)
                wcast(W[:, :, 512:1024], w_in[:, O_V + h * DV:O_V + (h + 1) * DV], wk)
                wcast(W[:, :, 1024:1536], w_in[:, O_G + h * DV:O_G + (h + 1) * DV], wk)

            cast_engs[0] = ("pool", "act")
            load_w(0)
            cast_engs[0] = ("pool",)

            def geo(i):
                h, ti = its[i]
                r0, rows = main_tiles[ti]
                return h, ti, r0, rows, r0 - NPRE

            def q1(i):
                h, ti, r0, rows, m0 = geo(i)
                if ti == 0 and h + 1 < H:
                    load_w(h + 1)
                W = WH[h % 2]
                wk = "WH%d" % (h % 2)
                CS, csk = CSr.nxt()
                dma("sp", CS[0:rows, 0, :], cosT[r0:r0 + rows, :], writes=(csk,))
                dma("sp", CS[0:rows, 1, :], sinT[r0:r0 + rows, :], writes=(csk,))
                xt = lambda kc: XM[:, kc, m0:m0 + rows]
                for kc in range(KD):
                    mm(PQ[0:rows, :], xt(kc), W[:, kc, 0:512], kc == 0, kc == KD - 1, ("XM", wk), ("PQ",))
                for kc in range(KD):
                    mm(PV_[0:rows, :], xt(kc), W[:, kc, 512:1024], kc == 0, kc == KD - 1, ("XM", wk), ("PVm",))
                for kc in range(KD):
                    mm(PG[0:rows, :], xt(kc), W[:, kc, 1024:1536], kc == 0, kc == KD - 1, ("XM", wk), ("PG",))
                ctx[i] = dict(CS=CS, csk=csk)

            def q2(i):
                h, ti, r0, rows, m0 = geo(i)
                c = ctx[i]
                QKF, qfk = QKFr.nxt()
                VV, vvk = VVr.nxt()
                SG, sgk = SGr.nxt()
                act(QKF[0:rows, 0:DK], PQ[0:rows, 0:DK], AF.Identity, ("PQ", "CT"), (qfk,),
                    scale=CT[0:rows, h:h + 1])
                cp("act", QKF[0:rows, DK:2 * DK], PQ[0:rows, DK:2 * DK], ("PQ",), (qfk,))
                cp("act", VV[0:rows, :], PV_[0:rows, :], ("PVm",), (vvk,))
                act(SG[0:rows, :], PG[0:rows, :], AF.Silu, ("PG",), (sgk,))
                c.update(QKF=QKF, qfk=qfk, VV=VV, vvk=vvk, SG=SG, sgk=sgk)

            def q3(i):
                h, ti, r0, rows, m0 = geo(i)
                c = ctx[i]
                QKR, qrk = QKRr.nxt()
                rotary(T, c["QKF"], c["qfk"], QKR, qrk, rows, 2, c["CS"][0:rows, 0, :], c["CS"][0:rows, 1, :],
                       c["csk"])
                KDc, kdk = KDr.nxt()
                kcol = (4 + h) if rows == 128 else (8 + h)
                ts(KDc[0:rows, :], QKR[0:rows, DK:2 * DK], CT[0:rows, kcol:kcol + 1], None, ALU.mult, None,
                   (qrk, "CT"), (kdk,))
                c.update(QKR=QKR, qrk=qrk, KDc=KDc, kdk=kdk)

            def state_mm(i, dc):
                h, ti, r0, rows, m0 = geo(i)
                c = ctx[i]
                mm(PStb[dc][:, :], c["KDc"][0:rows, dc * 128:(dc + 1) * 128], c["VV"][0:rows, :], True, True,
                   (c["kdk"], c["vvk"]), ("PSt%d" % dc,))

            def state_stt(i, dc):
                h, ti, r0, rows, m0 = geo(i)
                sfk = "SF%d" % (h * 2 + dc)
                gC = GAM[h] ** rows
                stt(SF[:, h * 2 + dc, :], SF[:, h * 2 + dc, :], gC, PStb[dc][:, :], ALU.mult, ALU.add,
                    (sfk, "PSt%d" % dc), (sfk,))

            def q4(i):
                h, ti, r0, rows, m0 = geo(i)
                c = ctx[i]
                QKR, qrk = c["QKR"], c["qrk"]
                for j in range(4):
                    S.op("pe", lambda e, j=j, rows=rows, QKR=QKR: e.transpose(
                        PT[:, j, 0:rows], QKR[0:rows, j * 128:(j + 1) * 128], T["IDB"][0:rows, 0:rows]),
                        (qrk, "IDB"), ("PT8",))
                if ti + 1 < NT:
                    state_mm(i, 0)
                    state_mm(i, 1)

            def q5(i):
                h, ti, r0, rows, m0 = geo(i)
                c = ctx[i]
                QKT, qtk = QKTr.nxt()
                cp("act", QKT[:, :, 0:rows], PT[:, 0:4, 0:rows], ("PT8",), (qtk,))
                if ti + 1 < NT:
                    state_stt(i, 0)
                    state_stt(i, 1)
                c.update(QKT=QKT, qtk=qtk)

            def q6(i):
                h, ti, r0, rows, m0 = geo(i)
                c = ctx[i]
                QKT, qtk = c["QKT"], c["qtk"]
                for dc in range(2):
                    mm(PSC[0:rows, 0:rows], QKT[:, 2 + dc, 0:rows], QKT[:, dc, 0:rows], dc == 0, dc == 1,
                       (qtk,), ("PSC",))
                if ti + 1 < NT:
                    j = (i + 1) % 5
                    for dc in range(2):
                        cp("act", SBFring[j][:, dc, :], SF[:, h * 2 + dc, :], ("SF%d" % (h * 2 + dc),),
                           ("SBFr%d_%d" % (j, dc),))

            def q7(i):
                h, ti, r0, rows, m0 = geo(i)
                c = ctx[i]
                SCT, sck = SCTr.nxt()
                tt(SCT[0:rows, 0:rows], PSC[0:rows, 0:rows], MT[0:rows, h, 0:rows], ALU.mult,
                   ("PSC", "MT"), (sck,))
                c.update(SCT=SCT, sck=sck)

            def q8(i):
                h, ti, r0, rows, m0 = geo(i)
                c = ctx[i]
                j = i % 5
                VV, vvk, QKT, qtk, SCT, sck = c["VV"], c["vvk"], c["QKT"], c["qtk"], c["SCT"], c["sck"]
                mm(PO[0:rows, :], SCT[0:rows, 0:rows], VV[0:rows, :], True, False, (sck, vvk), ("PO",))
                for dc in range(2):
                    if ti == 0:
                        mm(PO[0:rows, :], QKT[:, dc, 0:rows], SB0[:, h * 2 + dc, :], False, dc == 1,
                           (qtk, "SB0"), ("PO",))
                    else:
                        mm(PO[0:rows, :], QKT[:, dc, 0:rows], SBFring[j][:, dc, :], False, dc == 1,
                           (qtk, "SBFr%d_%d" % (j, dc)), ("PO",))

            def q9(i):
                h, ti, r0, rows, m0 = geo(i)
                c = ctx[i]
                OF, ofk = OFr.nxt()
                SQJ, sqk = SQJr.nxt()
                cp("act", OF[0:rows, :], PO[0:rows, :], ("PO",), (ofk,))
                act(SQJ[0:rows, :], PO[0:rows, :], AF.Square, ("PO",), (sqk,))
                c.update(OF=OF, ofk=ofk, SQJ=SQJ, sqk=sqk)

            def q10(i):
                h, ti, r0, rows, m0 = geo(i)
                c = ctx[i]
                STs, stk = STr.nxt()
                SQJ, sqk = c["SQJ"], c["sqk"]
                S.op("dve", lambda e, rows=rows, STs=STs, SQJ=SQJ: e.reduce_sum(
                    out=STs[0:rows, 0:1], in_=SQJ[0:rows, :], axis=AX.X), (sqk,), (stk,))
                ts(STs[0:rows, 1:2], STs[0:rows, 0:1], 1.0 / DV, EPS, ALU.mult, ALU.add, (stk,), (stk,))
                tt(STs[0:rows, 2:3], STs[0:rows, 1:2], CM[0:rows, :], ALU.pow, (stk, "CM"), (stk,), eng="pool")
                c.update(STs=STs, stk=stk)

            def q11(i):
                h, ti, r0, rows, m0 = geo(i)
                c = ctx[i]
                OBT, obk = OBTr.nxt()
                stt(OBT[0:rows, :], c["OF"][0:rows, :], c["STs"][0:rows, 2:3], c["SG"][0:rows, :], ALU.mult,
                    ALU.mult, (c["ofk"], c["stk"], c["sgk"]), (obk,))
                c.update(OBT=OBT, obk=obk)

            def q12(i):
                h, ti, r0, rows, m0 = geo(i)
                c = ctx[i]
                OBT, obk = c["OBT"], c["obk"]
                for ec in range(4):
                    S.op("pe", lambda e, ec=ec, rows=rows, OBT=OBT: e.transpose(
                        PT2[:, ec, 0:rows], OBT[0:rows, ec * 128:(ec + 1) * 128], T["IDB"][0:rows, 0:rows]),
                        (obk, "IDB"), ("PT8",))

            def q13(i):
                h, ti, r0, rows, m0 = geo(i)
                ctx.pop(i)
                OBO, ook = OBOr.nxt()
                cp("act", OBO[:, :, 0:rows], PT2[:, :, 0:rows], ("PT8",), (ook,))
                dma("sp", OB_d[:, h * 4:(h + 1) * 4, m0:m0 + rows], OBO[:, :, 0:rows], reads=(ook,))

            SB0 = st.sb("SB0", [128, 2 * H, DV], BF16)
            cp("act", SB0[:, :, :], SF[:, :, :], tuple("SF%d" % i for i in range(2 * H)), ("SB0",))
            pipeline(len(its), [q1, q2, q3, q4, q5, q6, q7, q8, q9, q10, q11, q12, q13])

        if stop <= 4:
            de.close()
            return nc
        with Stage(nc, S) as st:
            OA = st.sb("OA", [128, NB, NM], BF16)
            OB = st.sb("OB", [128, 16, NM], BF16)
            WA = [st.sb("WA%d" % i, [128, NB, 128], BF16) for i in range(2)]
            WR = [st.sb("WR%d" % i, [128, 16, 128], BF16) for i in range(2)]
            WGa = [st.sb("WGa%d" % i, [128, KD, 128], BF16) for i in range(2)]
            WGb = [st.sb("WGb%d" % i, [128, KD, 128], BF16) for i in range(2)]
            Rt = {k: Rot(st, k, [128, 512], F32, 2) for k in ["GA", "GB", "M1", "M2"]}
            MO = [st.sb("MO%d" % i, [128, NM], BF16) for i in range(2)]
            PEr = [Rot(st, "PE%d_" % i, [128, 512], F32, 2, psum=True) for i in range(4)]

            def load_e(c):
                wi = c % 2
                cs = slice(c * 128, (c + 1) * 128)
                wcast(WA[wi][:], w_brnn[:, cs], "WA%d" % wi)
                wcast(WGa[wi][:], w_in[:, O_GA + c * 128:O_GA + (c + 1) * 128], "WGa%d" % wi)
                wcast(WR[wi][:], w_bret[:, cs], "WR%d" % wi)
                wcast(WGb[wi][:], w_in[:, O_GB + c * 128:O_GB + (c + 1) * 128], "WGb%d" % wi)

            cast_engs[0] = ("pool", "act")
            load_e(0)
            cast_engs[0] = ("pool",)
            for kc in range(NB):
                dma("sp", OA[:, kc, :], OA_d[:, kc, :], writes=("OA",))
            for kc in range(16):
                dma("sp", OB[:, kc, :], OB_d[:, kc, :], writes=("OB",))
            eits = [(c, gi) for c in range(8) for gi in range(len(MG))]
            ectx = {}

            def e1(i):
                c, gi = eits[i]
                m0, ncols = MG[gi]
                wi = c % 2
                if gi == 0 and c + 1 < 8:
                    load_e(c + 1)
                ps_ = [r.nxt() for r in PEr]
                for k in range(NB):
                    mm(ps_[0][0][:, 0:ncols], WA[wi][:, k, :], OA[:, k, m0:m0 + ncols], k == 0, k == NB - 1,
                       ("WA%d" % wi, "OA"), (ps_[0][1],))
                for k in range(KD):
                    mm(ps_[2][0][:, 0:ncols], WGa[wi][:, k, :], XM[:, k, m0:m0 + ncols], k == 0, k == KD - 1,
                       ("WGa%d" % wi, "XM"), (ps_[2][1],))
                for k in range(16):
                    mm(ps_[1][0][:, 0:ncols], WR[wi][:, k, :], OB[:, k, m0:m0 + ncols], k == 0, k == 15,
                       ("WR%d" % wi, "OB"), (ps_[1][1],))
                for k in range(KD):
                    mm(ps_[3][0][:, 0:ncols], WGb[wi][:, k, :], XM[:, k, m0:m0 + ncols], k == 0, k == KD - 1,
                       ("WGb%d" % wi, "XM"), (ps_[3][1],))
                ectx[i] = ps_

            def e2(i):
                c, gi = eits[i]
                m0, ncols = MG[gi]
                wi = c % 2
                ps_ = ectx.pop(i)
                GA, gak = Rt["GA"].nxt()
                GB, gbk = Rt["GB"].nxt()
                M1, m1k = Rt["M1"].nxt()
                M2, m2k = Rt["M2"].nxt()
                act(GA[:, 0:ncols], ps_[2][0][:, 0:ncols], AF.Sigmoid, (ps_[2][1],), (gak,))
                act(GB[:, 0:ncols], ps_[3][0][:, 0:ncols], AF.Sigmoid, (ps_[3][1],), (gbk,))
                tt(M1[:, 0:ncols], ps_[0][0][:, 0:ncols], GA[:, 0:ncols], ALU.mult, (ps_[0][1], gak), (m1k,))
                tt(M2[:, 0:ncols], ps_[1][0][:, 0:ncols], GB[:, 0:ncols], ALU.mult, (ps_[1][1], gbk), (m2k,))
                tt(MO[wi][:, m0:m0 + ncols], M1[:, 0:ncols], M2[:, 0:ncols], ALU.add, (m1k, m2k),
                   ("MO%d" % wi,), eng="pool")
                if gi == len(MG) - 1:
                    dma("sp", MIX_d[:, c, :], MO[wi][:, :], reads=("MO%d" % wi,))

            pipeline(len(eits), [e1, e2])

        de.close()
        if stop <= 5:
            return nc
        with Stage(nc, S) as st:
            T = {}
            load_iden(st, T)
            MX = st.sb("MX", [128, KD, NM], BF16)
            for kc in range(KD):
                dma("sp", MX[:, kc, :], MIX_d[:, kc, :], writes=("MX",))
            WO = st.sb("WO", [128, KD, D], BF16)
            cast_engs[0] = ("pool", "act")
            for c0 in range(0, D, 512):
                wcast(WO[:, :, c0:c0 + 512], w_out[:, c0:c0 + 512], "WO")
            cast_engs[0] = ("pool",)
            G2B = st.sb("G2B", [128, KD, 128])
            dma("sp", G2B[:], g2b[:, :, :], writes=("G2B",))
            XSr = Rot(st, "XS", [128, D], F32, 3)
            HMr = Rot(st, "HM", [128, D], F32, 6)
            norm_bufs(st, T, 3)
            XOr = Rot(st, "XO", [128, KD, 128], BF16, 8)
            PF = Rot(st, "PF", [128, 512], F32, 4, psum=True)
            items = []
            for ti, (r0, rows) in enumerate(main_tiles):
                m0 = r0 - NPRE
                XOt, ok = XOr.nxt()

                def x_fn(ti=ti, r0=r0, rows=rows, m0=m0):
                    XSt, xk = XSr.nxt()
                    HMt, hk = HMr.nxt()
                    dma("sp", XSt[0:rows, :], xs[r0:r0 + rows, :], writes=(xk,))
                    for hf in range(2):
                        Pt, pk = PF.nxt()
                        for kc in range(KD):
                            mm(Pt[0:rows, :], MX[:, kc, m0:m0 + rows], WO[:, kc, hf * 512:(hf + 1) * 512],
                               kc == 0, kc == KD - 1, ("MX", "WO"), (pk,))
                        tt(HMt[0:rows, hf * 512:(hf + 1) * 512], Pt[0:rows, :],
                           XSt[0:rows, hf * 512:(hf + 1) * 512], ALU.add, (pk, xk), (hk,))
                    dma("sp", HM_d[ti * 128:ti * 128 + rows, :], HMt[0:rows, :], reads=(hk,))
                    return HMt[0:rows, :], hk

                def post(XOt=XOt, ok=ok, m0=m0, rows=rows):
                    dma("sp", XN2_d[:, :, m0:m0 + rows], XOt[:, :, 0:rows], reads=(ok,))
                items.append(dict(x_fn=x_fn, rows=rows, dst=XOt[:, :, 0:rows], dkey=ok, post=post))
            pipeline(len(items), norm_phases(T, items, G2B, "G2B"))

        if stop <= 6:
            return nc
        gh = ExitStack()
        gh.__enter__()
        AC = gh.enter_context(nc.sbuf_tensor("ACgh", [128, 24, NOWN], BF16))
        with Stage(nc, S) as st:
            X2 = st.sb("X2", [128, KD, NM], BF16)
            for kc in range(KD):
                dma("sp", X2[:, kc, :], XN2_d[:, kc, :], writes=("X2",))
            FV = st.sb("FV", [128, 4, 48])
            dma("sp", FV[:], fvec[:, :, :], writes=("FV",))
            WUg = [st.sb("WUg%d" % i, [128, KD, 256], BF16) for i in range(2)]
            WUv = [st.sb("WUv%d" % i, [128, KD, 256], BF16) for i in range(2)]
            FU = [st.sb("FU%d" % i, [128, 2, NM]) for i in range(2)]
            FAgr = Rot(st, "FAg", [128, NOWN], F32, 1)
            FAv = st.sb("FAv", [128, NOWN])
            P = [st.ps("PG%d" % i, [128, 512]) for i in range(8)]

            def load_g(cp_):
                w2 = cp_ % 2
                wcast(WUg[w2][:], w_up[:, cp_ * 256:(cp_ + 1) * 256], "WUg%d" % w2)
                wcast(WUv[w2][:], w_up[:, DFF + cp_ * 256:DFF + (cp_ + 1) * 256], "WUv%d" % w2)

            cast_engs[0] = ("pool", "act")
            load_g(0)
            cast_engs[0] = ("pool",)
            pcount = [0]
            for c in range(24):
                wi = c % 2
                cp_, sub = divmod(c, 2)
                if sub == 0 and cp_ + 1 < 12:
                    load_g(cp_ + 1)
                w2 = cp_ % 2
                ws = slice(sub * 128, (sub + 1) * 128)
                fk = "FU%d" % wi
                for gi, (m0, ncols) in enumerate(MG):
                    b0 = (pcount[0] % 4) * 2
                    pcount[0] += 1
                    pg, pv = P[b0], P[b0 + 1]
                    kg, kv = "PG%d" % b0, "PG%d" % (b0 + 1)
                    for k in range(KD):
                        mm(pg[:, 0:ncols], WUg[w2][:, k, ws], X2[:, k, m0:m0 + ncols], k == 0, k == KD - 1,
                           ("WUg%d" % w2, "X2"), (kg,))
                    for k in range(KD):
                        mm(pv[:, 0:ncols], WUv[w2][:, k, ws], X2[:, k, m0:m0 + ncols], k == 0, k == KD - 1,
                           ("WUv%d" % w2, "X2"), (kv,))
                    cp("act", FU[wi][:, 0, m0:m0 + ncols], pg[:, 0:ncols], (kg,), (fk,))
                    cp("act", FU[wi][:, 1, m0:m0 + ncols], pv[:, 0:ncols], (kv,), (fk,))
                FG, fgk = FAgr.nxt()
                for j, cc in ((0, c), (1, 24 + c)):
                    FAj, fj = (FG, fgk) if j == 0 else (FAv, "FAv")
                    ts(FAj[:, :], FU[wi][:, j, 14:14 + NOWN], FV[:, 0, cc:cc + 1], FV[:, 3, cc:cc + 1],
                       ALU.mult, ALU.add, (fk, "FV"), (fj,))
                    stt(FAj[:, :], FU[wi][:, j, 15:15 + NOWN], FV[:, 1, cc:cc + 1], FAj[:, :],
                        ALU.mult, ALU.add, (fk, "FV", fj), (fj,))
                    stt(FAj[:, :], FU[wi][:, j, 16:16 + NOWN], FV[:, 2, cc:cc + 1], FAj[:, :],
                        ALU.mult, ALU.add, (fk, "FV", fj), (fj,))
                act(FG[:, :], FG[:, :], AF.Gelu_apprx_tanh, (fgk,), (fgk,))
                tt(AC[:, c, :], FG[:, :], FAv[:, :], ALU.mult, (fgk, "FAv"), ("AC%d" % c,))
                if dbg:
                    dma("sp", ACT_d[:, c, :], AC[:, c, :], reads=("AC%d" % c,))

        if stop <= 7:
            gh.close()
            return nc
        with Stage(nc, S) as st:
            WD = [st.sb("WD%d" % i, [128, 24, 512], BF16) for i in range(2)]
            cast_engs[0] = ("pool", "act")
            for hf in range(2):
                for c0 in range(0, 24, 8):
                    wcast(WD[hf][:, c0:c0 + 8, :], w_down[c0 * 128:(c0 + 8) * 128, hf * 512:(hf + 1) * 512],
                          "WD%d" % hf)
            cast_engs[0] = ("pool",)
            GFB = st.sb("GFB", [128, D])
            dma("sp", GFB[:], gfb[:, :], writes=("GFB",))
            CM = st.sb("CM", [128, 1])
            S.op("pool", lambda e: e.memset(CM[:], -0.5), (), ("CM",))
            HMr = Rot(st, "HM", [128, D], F32, 4)
            YOr = Rot(st, "YO", [128, D], F32, 3)
            SQr = Rot(st, "SQJ", [128, D], F32, 2)
            STr = Rot(st, "STs", [128, 4], F32, 3)
            Pr = Rot(st, "PH", [128, 512], F32, 4, psum=True)
            for t in range(16):
                rr = (t + 1) * 128
                HMt, hk = HMr.nxt()
                dma("sp", HMt[:, :], HM_d[rr:rr + 128, :], writes=(hk,))
                for hf in range(2):
                    Pt, pk = Pr.nxt()
                    for c in range(24):
                        mm(Pt[:, :], AC[:, c, t * 128:(t + 1) * 128], WD[hf][:, c, :], c == 0, c == 23,
                           ("WD%d" % hf,), (pk,))
                    tt(HMt[:, hf * 512:(hf + 1) * 512], Pt[:, :], HMt[:, hf * 512:(hf + 1) * 512],
                       ALU.add, (pk, hk), (hk,))
                SQJ, sqk = SQr.nxt()
                STs, stk = STr.nxt()
                YO, yk = YOr.nxt()
                act(SQJ[:, :], HMt[:, :], AF.Square, (hk,), (sqk,))
                S.op("dve", lambda e, STs=STs, SQJ=SQJ: e.reduce_sum(out=STs[:, 0:1], in_=SQJ[:, :],
                                                                      axis=AX.X), (sqk,), (stk,))
                ts(STs[:, 1:2], STs[:, 0:1], 1.0 / D, EPS, ALU.mult, ALU.add, (stk,), (stk,))
                tt(STs[:, 2:3], STs[:, 1:2], CM[:, :], ALU.pow, (stk, "CM"), (stk,), eng="pool")
                stt(YO[:, :], HMt[:, :], STs[:, 2:3], GFB[:, :], ALU.mult, ALU.mult,
                    (hk, stk, "GFB"), (yk,))
                dma("sp", out[t * 128:(t + 1) * 128, :], YO[:, :], reads=(yk,))
        gh.close()
    return nc


def _tables(core):
    b, p = divmod(core, 4)
    n_real_pre = 2048 * p
    pad = NPRE - n_real_pre
    pos = np.arange(L, dtype=np.int64) - pad
    pos = np.maximum(pos, 0).astype(np.float32)
    half = 128
    inv_freq = (1.0 / (10000.0 ** np.linspace(0.0, 1.0, half, dtype=np.float32))).astype(np.float32)
    ang = (pos[:, None] * inv_freq[None, :]).astype(np.float32)
    cos = np.cos(ang).astype(np.float32)
    sin = np.sin(ang).astype(np.float32)
    g = (1.0 - 2.0 ** (-5.0 - np.arange(H, dtype=np.float64)))
    t = np.arange(NPRE, dtype=np.float64)
    dec = g[None, :] ** (NPRE - 1 - t)[:, None]
    dec = dec * (DK ** -0.5)
    decpre = dec.reshape(NPRE // 128, 128, H).transpose(1, 0, 2).astype(np.float32)
    i = np.arange(128, dtype=np.float64)
    ctab = np.zeros((128, 16), np.float32)
    ctab[:, 0:4] = g[None, :] ** (i + 1.0)[:, None]
    ctab[:, 4:8] = (g[None, :] ** (127.0 - i)[:, None]) * (DK ** -0.5)
    ctab[:, 8:12] = (g[None, :] ** np.maximum(15.0 - i, 0.0)[:, None]) * (DK ** -0.5)
    jj = i[:, None]
    ii = i[None, :]
    maskT = np.zeros((128, H, 128), np.float32)
    for h in range(H):
        maskT[:, h, :] = (g[h] ** (-(jj + 1.0))) * (ii >= jj) * (DK ** -0.5)
    mgrp = np.zeros((128, NPG), np.float32)
    for gi in range(NPG):
        mgrp[:, gi] = 1.0 if gi * 512 >= pad else 0.0
    return cos, sin, decpre, ctab, maskT, mgrp


def kernel(x, meta_tokens, norm_mix_g, w_in, rnn_conv_w, rnn_conv_b, rg_a_w, rg_a_b,
           rg_x_w, rg_x_b, lru_lambda, w_branch_rnn, w_branch_ret, w_out, norm_ffn_g,
           w_up, ffn_conv_w, ffn_conv_b, w_down, norm_final_g, _ret_maps=False):
    f = lambda a: np.ascontiguousarray(np.asarray(a, dtype=np.float32))
    x = f(x)
    meta = f(meta_tokens)
    pvec = np.zeros((128, 80), np.float32)
    cw = f(rnn_conv_w)[0].reshape(4, NB, 128)
    for j in range(4):
        pvec[:, j * 10:(j + 1) * 10] = cw[j].T
    pvec[:, 40:50] = f(rnn_conv_b)[0].reshape(NB, 128).T
    pvec[:, 50:60] = f(rg_a_b)[0].reshape(NB, 128).T
    pvec[:, 60:70] = f(rg_x_b)[0].reshape(NB, 128).T
    pvec[:, 70:80] = f(lru_lambda)[0].reshape(NB, 128).T
    fvec = np.zeros((128, 4, 48), np.float32)
    fw = f(ffn_conv_w)[0].reshape(3, 48, 128)
    for j in range(3):
        fvec[:, j, :] = fw[j].T
    fvec[:, 3, :] = f(ffn_conv_b)[0].reshape(48, 128).T
    rgw = np.ascontiguousarray(np.stack([f(rg_a_w)[0], f(rg_x_w)[0]], 0).transpose(2, 0, 1, 3))
    g1b = np.ascontiguousarray(np.broadcast_to(f(norm_mix_g)[0].reshape(KD, 128).T[:, :, None], (128, KD, 128)))
    g2b = np.ascontiguousarray(np.broadcast_to(f(norm_ffn_g)[0].reshape(KD, 128).T[:, :, None], (128, KD, 128)))
    gfb = np.ascontiguousarray(np.broadcast_to(f(norm_final_g)[None, :], (128, D)))
    iden = np.eye(128, dtype=np.float32)
    shared = {
        "w_in": f(w_in)[0], "w_brnn": f(w_branch_rnn)[0], "w_bret": f(w_branch_ret)[0], "w_out": f(w_out)[0],
        "w_up": f(w_up)[0], "w_down": f(w_down)[0], "rgw": rgw, "pvec": pvec, "fvec": fvec,
        "g1b": g1b, "g2b": g2b, "gfb": gfb, "iden": iden,
    }
    in_maps = []
    for core in range(8):
        b, p = divmod(core, 4)
        seq = np.concatenate([meta, x[b]], 0)
        end = NMETA + 2048 * (p + 1)
        stream = np.zeros((L, D), np.float32)
        stream[L - end:] = seq[:end]
        cos, sin, decpre, ctab, maskT, mgrp = _tables(core)
        m = dict(shared)
        m.update({"xs": stream, "cosT": cos, "sinT": sin, "decpre": decpre, "ctab": ctab,
                  "maskT": maskT, "mgrp": mgrp})
        in_maps.append(m)
    if _ret_maps:
        return in_maps
    nc = build()
    res = run_bass_kernel_spmd(nc, in_maps, core_ids=list(range(8)))
    outp = np.zeros((2, SEQ, D), np.float32)
    for core in range(8):
        b, p = divmod(core, 4)
        outp[b, p * 2048:(p + 1) * 2048] = res.results[core]["out"]
    return outp
```

```python
from contextlib import ExitStack
import numpy as np
import concourse.bass as bass
import concourse.mybir as mybir
from concourse.bass_utils import run_bass_kernel_spmd

F32 = mybir.dt.float32
BF16 = mybir.dt.bfloat16
AF = mybir.ActivationFunctionType
ALU = mybir.AluOpType
AX = mybir.AxisListType

D = 1024
KD = 8
SEQ = 8192
NMETA = 16
DRNN = 1280
NB = 10
H = 4
DK = 256
DV = 512
DFF = 3072
DIN = 10752
EPS = 1e-6
NPRE = 6144
NPG = NPRE // 512
HALO = 16
NOWN = 2048
NM = HALO + NOWN
L = NPRE + NM
NT_MAIN = 17
O_U, O_UG, O_Q, O_K, O_V, O_G, O_GA, O_GB = 0, 1280, 2560, 3584, 4608, 6656, 8704, 9728

MG = [(0, 16)] + [(16 + 512 * i, 512) for i in range(4)]


def tile_rows(t):
    if t == 0:
        return 0, 16
    return 16 + 128 * (t - 1), 128


class Sched:
    ENG = ("pe", "act", "dve", "pool", "sp")

    def __init__(self, nc, es):
        self.nc = nc
        self.q = {e: [] for e in self.ENG}
        self.sems = {}
        self.cnt = {}
        for e in self.ENG:
            self.sems[e] = es.enter_context(nc.semaphore("s_" + e))
            self.cnt[e] = 0
        self.ndma = 12
        self.dma_rr = 0
        for i in range(self.ndma):
            k = "d%d" % i
            self.sems[k] = es.enter_context(nc.semaphore("s_" + k))
            self.cnt[k] = 0
        self.waited = {e: {} for e in self.ENG}
        self.res = {}
        self.nops = 0

    def op(self, eng, fn, reads=(), writes=(), dma=False):
        deps = {}

        def add(ev):
            if ev is None:
                return
            s, v = ev
            if deps.get(s, 0) < v:
                deps[s] = v

        for r in reads:
            st = self.res.get(r)
            if st:
                add(st["w"])
        for w in writes:
            st = self.res.get(w)
            if st:
                add(st["w"])
                for s, v in st["r"].items():
                    add((s, v))
        if dma:
            k = "d%d" % self.dma_rr
            self.dma_rr = (self.dma_rr + 1) % self.ndma
            if self.cnt[k] > 0:
                add((k, 16 * self.cnt[k]))
            self.cnt[k] += 1
            ev = (k, 16 * self.cnt[k])
            inc = 16
        else:
            self.cnt[eng] += 1
            ev = (eng, self.cnt[eng])
            inc = 1
        waits = []
        for s, v in deps.items():
            if s == eng and eng == "pe":
                continue
            if self.waited[eng].get(s, 0) >= v:
                continue
            self.waited[eng][s] = v
            waits.append((s, v))
        sem = self.sems[ev[0]]
        sems = self.sems

        def emit(e, fn=fn, waits=waits, sem=sem, inc=inc):
            for s, v in waits:
                e.wait_ge(sems[s], v)
            fn(e).then_inc(sem, inc)

        self.q[eng].append(emit)
        for r in reads:
            st = self.res.setdefault(r, {"w": None, "r": {}})
            if st["r"].get(ev[0], 0) < ev[1]:
                st["r"][ev[0]] = ev[1]
        for w in writes:
            self.res[w] = {"w": ev, "r": {}}
        self.nops += 1

    def finish(self, eng="sp"):
        waits = [(k, 16 * self.cnt[k]) for k in self.sems if k[1:].isdigit() and self.cnt[k] > 0]
        sems = self.sems

        def emit(e):
            for s, v in waits:
                e.wait_ge(sems[s], v)

        self.q[eng].append(emit)


class Stage:
    _n = [0]

    def __init__(self, nc, S):
        self.nc, self.S = nc, S
        self.es = ExitStack()
        Stage._n[0] += 1
        self.pfx = "g%d_" % Stage._n[0]

    def __enter__(self):
        self.es.__enter__()
        return self

    def sb(self, name, shape, dt=F32):
        return self.es.enter_context(self.nc.sbuf_tensor(self.pfx + name, list(shape), dt))

    def ps(self, name, shape, dt=F32):
        return self.es.enter_context(self.nc.psum_tensor(self.pfx + name, list(shape), dt))

    def __exit__(self, *a):
        S = self.S
        S.barrier()
        q = S.q
        with self.nc.Block() as block:
            @block.tensor
            def _(e):
                for f in q["pe"]:
                    f(e)

            @block.scalar
            def _(e):
                for f in q["act"]:
                    f(e)

            @block.vector
            def _(e):
                for f in q["dve"]:
                    f(e)

            @block.gpsimd
            def _(e):
                for f in q["pool"]:
                    f(e)

            @block.sync
            def _(e):
                for f in q["sp"]:
                    f(e)
        S.q = {e: [] for e in S.ENG}
        S.res = {}
        return self.es.__exit__(*a)


def _barrier(self):
    targets = []
    for k in self.sems:
        v = self.cnt[k] * (16 if k[1:].isdigit() else 1)
        if v > 0:
            targets.append((k, v))
    sems = self.sems
    for eng in self.ENG:
        waits = []
        for s, v in targets:
            if self.waited[eng].get(s, 0) >= v:
                continue
            self.waited[eng][s] = v
            waits.append((s, v))

        def emit(e, waits=waits):
            for s, v in waits:
                e.wait_ge(sems[s], v)

        self.q[eng].append(emit)


Sched.barrier = _barrier


def build(stop=99, dbg=False):
    nc = bass.Bass("TRN2", target_bir_lowering=False)
    skind = "ExternalOutput" if dbg else "Internal"

    def din(name, shape, dt=F32):
        return nc.dram_tensor(name, list(shape), dt, kind="ExternalInput").ap()

    xs = din("xs", [L, D])
    w_in = din("w_in", [D, DIN])
    w_brnn = din("w_brnn", [DRNN, D])
    w_bret = din("w_bret", [H * DV, D])
    w_out = din("w_out", [D, D])
    w_up = din("w_up", [D, 2 * DFF])
    w_down = din("w_down", [DFF, D])
    rgw = din("rgw", [128, 2, NB, 128])
    pvec = din("pvec", [128, 80])
    fvec = din("fvec", [128, 4, 48])
    g1b = din("g1b", [128, KD, 128])
    g2b = din("g2b", [128, KD, 128])
    gfb = din("gfb", [128, D])
    cosT = din("cosT", [L, 128])
    sinT = din("sinT", [L, 128])
    decpre = din("decpre", [128, NPRE // 128, H])
    ctab = din("ctab", [128, 16])
    maskT = din("maskT", [128, H, 128])
    mgrp = din("mgrp", [128, NPG])
    iden_in = din("iden", [128, 128])
    out = nc.dram_tensor("out", [NOWN, D], F32, kind="ExternalOutput").ap()
    XNT_d = nc.dram_tensor("xnt_d", [128, KD, L], BF16, kind=skind).ap()
    OA_d = nc.dram_tensor("oa_d", [128, NB, NM], BF16, kind=skind).ap()
    OB_d = nc.dram_tensor("ob_d", [128, 16, NM], BF16, kind=skind).ap()
    MIX_d = nc.dram_tensor("mix_d", [128, KD, NM], BF16, kind=skind).ap()
    HM_d = nc.dram_tensor("hm_d", [NT_MAIN * 128, D], F32, kind=skind).ap()
    XN2_d = nc.dram_tensor("xn2_d", [128, KD, NM], BF16, kind=skind).ap()
    ACT_d = nc.dram_tensor("act_d", [128, 24, NOWN], BF16, kind=skind).ap()
    SF_d = nc.dram_tensor("sf_d", [128, 2 * H, DV], F32, kind=skind).ap()

    ges = ExitStack()
    with ges:
        S = Sched(nc, ges)

        def dma(eng, out_ap, in_ap, reads=(), writes=()):
            S.op(eng, lambda e: e.dma_start(out=out_ap, in_=in_ap), reads, writes, dma=True)

        def act(out_ap, in_ap, func, reads, writes, bias=None, scale=None):
            kw = {}
            if bias is not None:
                kw["bias"] = bias
            if scale is not None:
                kw["scale"] = scale
            S.op("act", lambda e: e.activation(out=out_ap, in_=in_ap, func=func, **kw), reads, writes)

        def tt(out_ap, a, b, op, reads, writes, eng="dve"):
            S.op(eng, lambda e: e.tensor_tensor(out=out_ap, in0=a, in1=b, op=op), reads, writes)

        def ts(out_ap, a, s1, s2, op0, op1, reads, writes, eng="dve"):
            if op1 is None:
                S.op(eng, lambda e: e.tensor_scalar(out=out_ap, in0=a, scalar1=s1, scalar2=None, op0=op0),
                     reads, writes)
            else:
                S.op(eng, lambda e: e.tensor_scalar(out=out_ap, in0=a, scalar1=s1, scalar2=s2, op0=op0, op1=op1),
                     reads, writes)

        def stt(out_ap, a, s, b, op0, op1, reads, writes):
            S.op("dve", lambda e: e.scalar_tensor_tensor(out=out_ap, in0=a, scalar=s, in1=b, op0=op0, op1=op1),
                 reads, writes)

        def cp(eng, out_ap, in_ap, reads, writes):
            if eng == "act":
                S.op(eng, lambda e: e.activation(out=out_ap, in_=in_ap, func=AF.Copy), reads, writes)
            else:
                S.op(eng, lambda e: e.tensor_copy(out=out_ap, in_=in_ap), reads, writes)

        def mm(out_ap, lhsT, rhs, start, stop, reads, writes):
            S.op("pe", lambda e: e.matmul(out_ap, lhsT, rhs, start=start, stop=stop), reads, writes)

        STG = [ges.enter_context(nc.sbuf_tensor("STG%d" % i, [128, 640], F32)) for i in range(4)]
        stg_i = [0]

        cast_engs = [("pool",)]

        def wcast(dst_view, src_ap, key):
            kc = dst_view.shape[1]
            ncols = dst_view.shape[2]
            for k in range(kc):
                i = stg_i[0] % 4
                stg_i[0] += 1
                engs = cast_engs[0]
                dma("sp", STG[i][:, 0:ncols], src_ap[k * 128:(k + 1) * 128, :], writes=("STG%d" % i,))
                cp(engs[k % len(engs)], dst_view[:, k, :], STG[i][:, 0:ncols], ("STG%d" % i,), (key,))

        class Rot:
            def __init__(self, st, name, shape, dt, n, psum=False):
                self.items = []
                for i in range(n):
                    t = (st.ps if psum else st.sb)("%s%d" % (name, i), shape, dt)
                    self.items.append((t, "%s%d" % (name, i)))
                self.i = 0

            def nxt(self):
                it = self.items[self.i % len(self.items)]
                self.i += 1
                return it

        def pipeline(n, phases):
            for step in range(n + len(phases) - 1):
                for pi in reversed(range(len(phases))):
                    t = step - pi
                    if 0 <= t < n:
                        phases[pi](t)

        def norm_bufs(st, T, n=3, npt=2):
            T["SQJ"] = Rot(st, "SQJ", [128, D], F32, 3)
            T["ST"] = Rot(st, "ST", [128, 4], F32, n + 2)
            T["XB"] = Rot(st, "XB", [128, D], BF16, n)
            T["PT"] = Rot(st, "PT", [128, KD, 128], BF16, npt, psum=True)
            T["CM"] = st.sb("CM", [128, 1])
            S.op("pool", lambda e: e.memset(T["CM"][:], -0.5), (), ("CM",))

        def norm_phases(T, items, gb, gbkey):
            ctx = {}

            def p0(t):
                ctx[t] = dict(zip(("x_ap", "xk"), items[t]["x_fn"]()))

            def p1(t):
                c = ctx[t]
                rows = items[t]["rows"]
                SQJ, sqk = T["SQJ"].nxt()
                act(SQJ[0:rows, :], c["x_ap"], AF.Square, (c["xk"],), (sqk,))
                c.update(SQJ=SQJ, sqk=sqk)

            def p2(t):
                c = ctx[t]
                rows = items[t]["rows"]
                STt, stk = T["ST"].nxt()
                SQJ = c["SQJ"]
                S.op("dve", lambda e: e.reduce_sum(out=STt[0:rows, 0:1], in_=SQJ[0:rows, :], axis=AX.X),
                     (c["sqk"],), (stk,))
                ts(STt[0:rows, 1:2], STt[0:rows, 0:1], 1.0 / D, EPS, ALU.mult, ALU.add, (stk,), (stk,))
                tt(STt[0:rows, 2:3], STt[0:rows, 1:2], T["CM"][0:rows, :], ALU.pow, (stk, "CM"), (stk,), eng="pool")
                c.update(STt=STt, stk=stk)

            def p3(t):
                c = ctx[t]
                rows = items[t]["rows"]
                XBt, xbk = T["XB"].nxt()
                act(XBt[0:rows, :], c["x_ap"], AF.Identity, (c["xk"], c["stk"]), (xbk,), scale=c["STt"][0:rows, 2:3])
                c.update(XBt=XBt, xbk=xbk)

            def p4(t):
                c = ctx[t]
                rows = items[t]["rows"]
                PTt, ptk = T["PT"].nxt()
                XBt = c["XBt"]
                for kc in range(KD):
                    S.op("pe", lambda e, kc=kc: e.transpose(PTt[:, kc, 0:rows],
                                                            XBt[0:rows, kc * 128:(kc + 1) * 128],
                                                            T["IDB"][0:rows, 0:rows]),
                         (c["xbk"], "IDB"), (ptk,))
                c.update(PTt=PTt, ptk=ptk)

            def p5(t):
                c = ctx.pop(t)
                it = items[t]
                rows = it["rows"]
                tt(it["dst"], c["PTt"][:, :, 0:rows], gb[:, :, 0:rows], ALU.mult, (c["ptk"], gbkey), (it["dkey"],))
                if it.get("post"):
                    it["post"]()

            return [p0, p1, p2, p3, p4, p5]

        def load_iden(st, T):
            T["IDF"] = st.sb("IDF", [128, 128])
            T["IDB"] = st.sb("IDB", [128, 128], BF16)
            dma("sp", T["IDF"][:], iden_in[:, :], writes=("IDF",))
            cp("dve", T["IDB"][:], T["IDF"][:], ("IDF",), ("IDB",))

        stream_tiles = [(i * 128, 128) for i in range(NPRE // 128)] + [(NPRE, 16)] + \
                       [(NPRE + 16 + i * 128, 128) for i in range(16)]
        main_tiles = stream_tiles[NPRE // 128:]
        pre_groups = [(i * 512, 512) for i in range(NPG)]
        main_groups = [(NPRE, 16)] + [(NPRE + 16 + i * 512, 512) for i in range(4)]

        def rotary(T, src, skey, dst, dkey, rows, nh, c, s, ckey, sgn_eng="dve"):
            for h in range(nh):
                x1 = src[0:rows, h * DK:h * DK + 128]
                x2 = src[0:rows, h * DK + 128:(h + 1) * DK]
                R0, k0 = T["RT"].nxt()
                R1, k1 = T["RT"].nxt()
                R2, k2 = T["RT"].nxt()
                R3, k3 = T["RT"].nxt()
                tt(R0[0:rows, :], x1, c, ALU.mult, (skey, ckey), (k0,))
                tt(R1[0:rows, :], x2, s, ALU.mult, (skey, ckey), (k1,), eng="pool")
                tt(R2[0:rows, :], x2, c, ALU.mult, (skey, ckey), (k2,))
                tt(R3[0:rows, :], x1, s, ALU.mult, (skey, ckey), (k3,), eng="pool")
                tt(dst[0:rows, h * DK:h * DK + 128], R0[0:rows, :], R1[0:rows, :], ALU.subtract,
                   (k0, k1), (dkey,))
                tt(dst[0:rows, h * DK + 128:(h + 1) * DK], R2[0:rows, :], R3[0:rows, :], ALU.add,
                   (k2, k3), (dkey,))

        with Stage(nc, S) as st:
            T = {}
            load_iden(st, T)
            G1B = st.sb("G1B", [128, KD, 128])
            dma("sp", G1B[:], g1b[:, :, :], writes=("G1B",))
            WK = st.sb("WK", [128, KD, H * DK], BF16)
            WV = st.sb("WV", [128, KD, H * DV], BF16)
            cast_engs[0] = ("pool", "act")
            for c0 in range(0, H * DK, 512):
                wcast(WK[:, :, c0:c0 + 512], w_in[:, O_K + c0:O_K + c0 + 512], "WK")
            for c0 in range(0, H * DV, 512):
                wcast(WV[:, :, c0:c0 + 512], w_in[:, O_V + c0:O_V + c0 + 512], "WV")
            cast_engs[0] = ("pool",)
            DPRE = st.sb("DPRE", [128, NPRE // 128, H])
            dma("sp", DPRE[:], decpre[:, :, :], writes=("DPRE",))
            XSr = Rot(st, "XS", [128, D], F32, 6)
            norm_bufs(st, T, 3, npt=1)
            XOr = Rot(st, "XO", [128, KD, 512], BF16, 2)
            CSr = Rot(st, "CS", [128, 2, 128], F32, 3)
            T["RT"] = Rot(st, "RT", [128, 128], F32, 8)
            KFr = Rot(st, "KF", [128, H * DK], F32, 2)
            KRr = Rot(st, "KR", [128, 4, H * DK], BF16, 2)
            VVr = Rot(st, "VV", [128, 4, H * DV], BF16, 2)
            SF = st.sb("SF", [128, 2 * H, DV])
            S.op("dve", lambda e: e.memset(SF[:], 0.0), (), tuple("SF%d" % i for i in range(2 * H)))
            PK = [st.ps("PK%d" % i, [128, 512]) for i in range(2)]
            PVv = [st.ps("PV%d" % i, [128, 512]) for i in range(2)]
            PSr = Rot(st, "PS", [128, 512], F32, 2, psum=True)
            items = []
            tinfo = []
            for gidx, (c0, ncols) in enumerate(pre_groups + main_groups):
                XOt, ok = XOr.nxt()
                nt = max(1, ncols // 128)
                is_pre = gidx < NPG
                for j in range(nt):
                    rows = min(128, ncols)
                    r0 = c0 + j * 128

                    def x_fn(r0=r0, rows=rows):
                        XSt, xk = XSr.nxt()
                        dma("sp", XSt[0:rows, :], xs[r0:r0 + rows, :], writes=(xk,))
                        return XSt[0:rows, :], xk

                    post = None
                    if j == nt - 1:
                        def post(XOt=XOt, ok=ok, c0=c0, ncols=ncols, nt=nt):
                            dma("sp", XNT_d[:, :, c0:c0 + ncols], XOt[:, :, 0:ncols],
                                reads=tuple("%s_%d" % (ok, jj) for jj in range(nt)))
                    items.append(dict(x_fn=x_fn, rows=rows, dst=XOt[:, :, j * 128:j * 128 + rows],
                                      dkey="%s_%d" % (ok, j), post=post))
                    tinfo.append((gidx, j, is_pre, XOt, "%s_%d" % (ok, j), r0))
            gst = {}
            ctx = {}

            def c1(i):
                g, t4, is_pre, XGt, xk, r0 = tinfo[i]
                if not is_pre:
                    return
                if t4 == 0:
                    gst[g] = (KRr.nxt(), VVr.nxt())
                (KR, krk), (VV, vvk) = gst[g]
                tile = g * 4 + t4
                CS, csk = CSr.nxt()
                dma("sp", CS[:, 0, :], cosT[r0:r0 + 128, :], writes=(csk,))
                dma("sp", CS[:, 1, :], sinT[r0:r0 + 128, :], writes=(csk,))
                KF, kfk = KFr.nxt()
                xt = lambda kc: XGt[:, kc, t4 * 128:(t4 + 1) * 128]
                for hf in range(2):
                    for kc in range(KD):
                        mm(PK[hf][:, :], xt(kc), WK[:, kc, hf * 512:(hf + 1) * 512], kc == 0, kc == KD - 1,
                           (xk, "WK"), ("PK%d" % hf,))
                for h in range(H):
                    hf, o = divmod(h, 2)
                    act(KF[:, h * DK:(h + 1) * DK], PK[hf][:, o * DK:(o + 1) * DK], AF.Identity,
                        ("PK%d" % hf, "DPRE"), (kfk,), scale=DPRE[:, tile, h:h + 1])
                for q4 in range(4):
                    pv, pvk = PVv[q4 % 2], "PV%d" % (q4 % 2)
                    for kc in range(KD):
                        mm(pv[:, :], xt(kc), WV[:, kc, q4 * 512:(q4 + 1) * 512], kc == 0, kc == KD - 1,
                           (xk, "WV"), (pvk,))
                    cp("act", VV[:, t4, q4 * 512:(q4 + 1) * 512], pv[:, :], (pvk,), (vvk,))
                ctx[i] = (KF, kfk, CS, csk)

            def c2(i):
                g, t4, is_pre, XGt, xk, r0 = tinfo[i]
                if not is_pre:
                    return
                (KR, krk), (VV, vvk) = gst[g]
                KF, kfk, CS, csk = ctx.pop(i)
                rotary(T, KF, kfk, KR[:, t4, :], krk, 128, H, CS[:, 0, :], CS[:, 1, :], csk)
                if t4 == 3:
                    for h in range(H):
                        for dc in range(2):
                            Pt, pk = PSr.nxt()
                            for t in range(4):
                                mm(Pt[:, :], KR[:, t, h * DK + dc * 128:h * DK + (dc + 1) * 128],
                                   VV[:, t, h * DV:(h + 1) * DV], t == 0, t == 3, (krk, vvk), (pk,))
                            tt(SF[:, h * 2 + dc, :], Pt[:, :], SF[:, h * 2 + dc, :], ALU.add,
                               (pk, "SF%d" % (h * 2 + dc)), ("SF%d" % (h * 2 + dc),))

            pipeline(len(items), norm_phases(T, items, G1B, "G1B") + [c1, c2])
            dma("sp", SF_d[:, :, :], SF[:], reads=tuple("SF%d" % i for i in range(2 * H)))

        if stop <= 1:
            return nc
        with Stage(nc, S) as st:
            WU = st.sb("WU", [128, KD, DRNN], BF16)
            WG = st.sb("WG", [128, KD, DRNN], BF16)
            RGW = st.sb("RGW", [128, 2, NB, 128], BF16)
            PV = st.sb("PV", [128, 80])
            MGR = st.sb("MGR", [128, NPG])
            BM = st.sb("BM", [128, NPG, NB])
            SC = st.sb("SC", [128, 3, NB])
            HB = st.sb("HB", [128, 2 * NB])
            dma("sp", PV[:], pvec[:, :], writes=("PV",))
            dma("sp", MGR[:], mgrp[:, :], writes=("MGR",))
            cast_engs[0] = ("pool", "act")
            for c0 in range(0, DRNN, 640):
                wcast(WU[:, :, c0:c0 + 640], w_in[:, O_U + c0:O_U + c0 + 640], "WU")
            for ax in range(2):
                for n0 in range(0, NB, 5):
                    i = stg_i[0] % 4
                    stg_i[0] += 1
                    dma("sp", STG[i][:, 0:640].rearrange("p (n j) -> p n j", n=5), rgw[:, ax, n0:n0 + 5, :],
                        writes=("STG%d" % i,))
                    cp("pool", RGW[:, ax, n0:n0 + 5, :], STG[i][:, 0:640].rearrange("p (n j) -> p n j", n=5),
                       ("STG%d" % i,), ("RGW",))
            for c0 in range(0, DRNN, 640):
                wcast(WG[:, :, c0:c0 + 640], w_in[:, O_UG + c0:O_UG + c0 + 640], "WG")
            cast_engs[0] = ("pool",)
            act(SC[:, 0, :], PV[:, 70:80], AF.Exp, ("PV",), ("SC0",), scale=-1.0)
            act(SC[:, 0, :], SC[:, 0, :], AF.Ln, ("SC0",), ("SC0",), bias=1.0)
            ts(SC[:, 1, :], SC[:, 0, :], -8.0, None, ALU.mult, None, ("SC0",), ("SC",))
            ts(SC[:, 2, :], SC[:, 0, :], -4.0, None, ALU.mult, None, ("SC0",), ("SC",))
            ts(HB[:, :], PV[:, 50:70], 0.5, None, ALU.mult, None, ("PV",), ("HB",))
            for g in range(NPG):
                ts(BM[:, g, :], PV[:, 40:50], MGR[:, g:g + 1], None, ALU.mult, None, ("PV", "MGR"), ("BM",))
            XG = [st.sb("XG%d" % i, [128, KD, 512], BF16) for i in range(2)]
            IDFb = st.sb("IDFb", [128, 128])
            dma("sp", IDFb[:], iden_in[:, :], writes=("IDFb",))
            DG = st.sb("DG", [128, 4, NB, 128], BF16)
            for j in range(4):
                for n in range(NB):
                    ts(DG[:, j, n, :], IDFb[:, :], PV[:, j * 10 + n:j * 10 + n + 1], None, ALU.mult, None,
                       ("IDFb", "PV"), ("DG",))
            UB = st.sb("UB", [128, NB, 516], BF16)
            HST = st.sb("HST", [128, NB])
            S.op("dve", lambda e: e.memset(UB[:], 0.0), (), tuple("U%d" % n for n in range(NB)))
            S.op("dve", lambda e: e.memset(HST[:], 0.0), (), tuple("HST%d" % n for n in range(NB)))
            BS = 4
            QB = st.sb("QB", [128, 1])
            S.op("pool", lambda e: e.memset(QB[:], 0.25), (), ("QB",))
            XRr = Rot(st, "XR", [128, 512], BF16, 5)
            TRr = Rot(st, "TR", [128, 512], F32, 3)
            TIr = Rot(st, "TI", [128, 512], F32, 3)
            BBr = Rot(st, "BB", [128, 512], F32, 5)
            HHr = Rot(st, "HH", [128, 512], F32, 3)
            OATr = Rot(st, "OAT", [128, 512], BF16, 3)
            GGb = [st.sb("GG%d" % i, [128, NB, 512], BF16) for i in range(3)]
            AAg = [st.sb("AAg%d" % i, [128, BS, 512]) for i in range(2)]
            OMg = [st.sb("OMg%d" % i, [128, BS, 512]) for i in range(2)]
            T1g = [st.sb("T1g%d" % i, [128, BS, 512]) for i in range(2)]
            PU = Rot(st, "PU", [128, 512], F32, 2, psum=True)
            PC = Rot(st, "PC", [128, 512], F32, 2, psum=True)
            PRA = Rot(st, "PRA", [128, 512], F32, 2, psum=True)
            PRX = Rot(st, "PRX", [128, 512], F32, 2, psum=True)
            allg = [(g, True) for g in pre_groups] + [(g, False) for g in main_groups]
            chains = []
            for gi, ((c0, ncols), is_pre) in enumerate(allg):
                for n in range(NB):
                    chains.append((gi, c0, ncols, is_pre, n))
            NCH = len(chains)
            ctx = {}

            def load_xg(gi):
                (c0, ncols), _ = allg[gi]
                dma("sp", XG[gi % 2][:, :, 0:ncols], XNT_d[:, :, c0:c0 + ncols], writes=("XG%d" % (gi % 2),))

            load_xg(0)

            def b1(ci):
                gi, c0, ncols, is_pre, n = chains[ci]
                XGt, xk = XG[gi % 2], "XG%d" % (gi % 2)
                if n == 0:
                    if gi + 1 < len(allg):
                        load_xg(gi + 1)
                    if not is_pre:
                        mgi = gi - NPG
                        GG, ggk = GGb[mgi % 3], "GG%d" % (mgi % 3)
                        for n2 in range(NB):
                            g_ps, gk = PU.nxt()
                            for kc in range(KD):
                                mm(g_ps[:, 0:ncols], WG[:, kc, n2 * 128:(n2 + 1) * 128], XGt[:, kc, 0:ncols],
                                   kc == 0, kc == KD - 1, ("WG", xk), (gk,))
                            act(GG[:, n2, 0:ncols], g_ps[:, 0:ncols], AF.Gelu_apprx_tanh, (gk,), (ggk,))
                u_ps, puk = PU.nxt()
                for kc in range(KD):
                    mm(u_ps[:, 0:ncols], WU[:, kc, n * 128:(n + 1) * 128], XGt[:, kc, 0:ncols], kc == 0,
                       kc == KD - 1, ("WU", xk), (puk,))
                ctx[ci] = dict(u_ps=u_ps, puk=puk)

            def b2(ci):
                gi, c0, ncols, is_pre, n = chains[ci]
                c = ctx[ci]
                cp("dve", UB[:, n, 3:3 + ncols], c["u_ps"][:, 0:ncols], (c["puk"],), ("U%d" % n,))

            def b3(ci):
                gi, c0, ncols, is_pre, n = chains[ci]
                c = ctx[ci]
                uk = "U%d" % n
                pc, pck = PC.nxt()
                for j in range(4):
                    mm(pc[:, 0:ncols], DG[:, j, n, :], UB[:, n, j:j + ncols], j == 0, j == 3, ("DG", uk), (pck,))
                c.update(pc=pc, pck=pck)

            def b4(ci):
                gi, c0, ncols, is_pre, n = chains[ci]
                c = ctx[ci]
                uk = "U%d" % n
                XR, xrk = XRr.nxt()
                if is_pre:
                    g = c0 // 512
                    act(XR[:, 0:ncols], c["pc"][:, 0:ncols], AF.Identity, (c["pck"], "BM"), (xrk,),
                        bias=BM[:, g, n:n + 1])
                else:
                    act(XR[:, 0:ncols], c["pc"][:, 0:ncols], AF.Identity, (c["pck"], "PV"), (xrk,),
                        bias=PV[:, 40 + n:41 + n])
                cp("pool", UB[:, n, 0:3], UB[:, n, ncols:ncols + 3], (uk,), (uk,))
                import os as _os
                if _os.environ.get("DBGXR") and not is_pre:
                    dma("sp", OA_d[:, n, c0 - NPRE:c0 - NPRE + ncols], XR[:, 0:ncols], reads=(xrk,))
                c.update(XR=XR, xrk=xrk)

            def b5(ci):
                gi, c0, ncols, is_pre, n = chains[ci]
                c = ctx[ci]
                XR, xrk = c["XR"], c["xrk"]
                ra, rak = PRA.nxt()
                rx, rxk = PRX.nxt()
                mm(ra[:, 0:ncols], RGW[:, 0, n, :], XR[:, 0:ncols], True, True, ("RGW", xrk), (rak,))
                mm(rx[:, 0:ncols], RGW[:, 1, n, :], XR[:, 0:ncols], True, True, ("RGW", xrk), (rxk,))
                c.update(ra=ra, rak=rak, rx=rx, rxk=rxk)

            def b6(ci):
                gi, c0, ncols, is_pre, n = chains[ci]
                c = ctx[ci]
                TR, trk = TRr.nxt()
                TI, tik = TIr.nxt()
                act(TR[:, 0:ncols], c["ra"][:, 0:ncols], AF.Tanh, (c["rak"], "HB"), (trk,), bias=HB[:, n:n + 1],
                    scale=0.5)
                act(TI[:, 0:ncols], c["rx"][:, 0:ncols], AF.Tanh, (c["rxk"], "HB"), (tik,),
                    bias=HB[:, NB + n:NB + n + 1], scale=0.5)
                c.update(TR=TR, trk=trk, TI=TI, tik=tik)

            def b7(ci):
                gi, c0, ncols, is_pre, n = chains[ci]
                c = ctx[ci]
                bid, slot = divmod(ci, BS)
                bb = bid % 2
                TR, trk, TI, tik, XR, xrk = c["TR"], c["trk"], c["TI"], c["tik"], c["XR"], c["xrk"]
                act(AAg[bb][:, slot, 0:ncols], TR[:, 0:ncols], AF.Exp, (trk, "SC"), ("AAg%d" % bb,),
                    scale=SC[:, 2, n:n + 1], bias=SC[:, 2, n:n + 1])
                stt(T1g[bb][:, slot, 0:ncols], TI[:, 0:ncols], 1.0, XR[:, 0:ncols], ALU.add, ALU.mult,
                    (tik, xrk), ("T1g%d" % bb,))

            def b7b(ci):
                gi, c0, ncols, is_pre, n = chains[ci]
                bid, slot = divmod(ci, BS)
                bb = bid % 2
                if ncols < 512:
                    S.op("pool", lambda e, bb=bb, slot=slot: e.memset(OMg[bb][:, slot, :], 0.0),
                         ("OMg%d" % bb,), ("OMg%d" % bb,))
                tt(OMg[bb][:, slot, 0:ncols], AAg[bb][:, slot, 0:ncols], AAg[bb][:, slot, 0:ncols], ALU.mult,
                   ("AAg%d" % bb,), ("OMg%d" % bb,), eng="pool")

            def batch_of(ci):
                if ci % BS == BS - 1 or ci == NCH - 1:
                    bid = ci // BS
                    return bid % 2, list(range(bid * BS, ci + 1))
                return None

            def b8(ci):
                b = batch_of(ci)
                if b is None:
                    return
                bb, ids = b
                nb = len(ids)
                act(OMg[bb][:, 0:nb, :], OMg[bb][:, 0:nb, :], AF.Sqrt, ("OMg%d" % bb, "QB"), ("OMg%d" % bb,),
                    scale=-0.25, bias=QB[:, 0:1])

            def b9(ci):
                b = batch_of(ci)
                if b is None:
                    return
                bb, ids = b
                for slot, cj in enumerate(ids):
                    gi, c0, ncols, is_pre, n = chains[cj]
                    BB, bbk = BBr.nxt()
                    tt(BB[:, 0:ncols], T1g[bb][:, slot, 0:ncols], OMg[bb][:, slot, 0:ncols], ALU.mult,
                       ("T1g%d" % bb, "OMg%d" % bb), (bbk,), eng="pool")
                    ctx[cj].update(BB=BB, bbk=bbk)

            def b10(ci):
                b = batch_of(ci)
                if b is None:
                    return
                bb, ids = b
                for slot, cj in enumerate(ids):
                    gi, c0, ncols, is_pre, n = chains[cj]
                    c = ctx.pop(cj)
                    BB, bbk = c["BB"], c["bbk"]
                    HH, hhk = HHr.nxt()
                    hk = "HST%d" % n
                    S.op("dve", lambda e, n=n, ncols=ncols, HH=HH, BB=BB, bb=bb, slot=slot: e.tensor_tensor_scan(
                        out=HH[:, 0:ncols], data0=AAg[bb][:, slot, 0:ncols], data1=BB[:, 0:ncols],
                        initial=HST[:, n:n + 1], op0=ALU.mult, op1=ALU.add), ("AAg%d" % bb, bbk, hk), (hhk,))
                    cp("dve", HST[:, n:n + 1], HH[:, ncols - 1:ncols], (hhk,), (hk,))
                    if not is_pre:
                        mgi = gi - NPG
                        GG, ggk = GGb[mgi % 3], "GG%d" % (mgi % 3)
                        OAT, ok = OATr.nxt()
                        tt(OAT[:, 0:ncols], HH[:, 0:ncols], GG[:, n, 0:ncols], ALU.mult, (hhk, ggk), (ok,))
                        m0 = c0 - NPRE
                        import os as _os
                        if not _os.environ.get("DBGXR"):
                            dma("sp", OA_d[:, n, m0:m0 + ncols], OAT[:, 0:ncols], reads=(ok,))

            pipeline(NCH, [b1, b2, b3, b4, b5, b6, b7, b7b, b8, b9, b10])
        if stop <= 2:
            return nc

        de = ExitStack()
        de.__enter__()
        XM = de.enter_context(nc.sbuf_tensor("XMde", [128, KD, NM], BF16))
        GAM = [1.0 - 2.0 ** (-5.0 - h) for h in range(H)]
        with Stage(nc, S) as st:
            T = {}
            load_iden(st, T)
            for kc in range(KD):
                dma("sp", XM[:, kc, :], XNT_d[:, kc, NPRE:NPRE + NM], writes=("XM",))
            CT = st.sb("CT", [128, 16])
            MT = st.sb("MT", [128, H, 128])
            CM = st.sb("CM", [128, 1])
            S.op("pool", lambda e: e.memset(CM[:], -0.5), (), ("CM",))
            dma("sp", CT[:], ctab[:, :], writes=("CT",))
            dma("sp", MT[:], maskT[:, :, :], writes=("MT",))
            WH = [st.sb("WH%d" % i, [128, KD, 1536], BF16) for i in range(2)]
            CSr = Rot(st, "CS", [128, 2, 128], F32, 4)
            T["RT"] = Rot(st, "RT", [128, 128], F32, 8)
            QKFr = Rot(st, "QKF", [128, 2 * DK], F32, 3)
            QKRr = Rot(st, "QKR", [128, 2 * DK], BF16, 3)
            KDr = Rot(st, "KDc", [128, DK], BF16, 5)
            VVr = Rot(st, "VV", [128, DV], BF16, 8)
            SGr = Rot(st, "SG", [128, DV], F32, 11)
            QKTr = Rot(st, "QKT", [128, 4, 128], BF16, 5)
            SCTr = Rot(st, "SCT", [128, 128], BF16, 3)
            SF = st.sb("SF", [128, 2 * H, DV])
            SBFring = [st.sb("SBFr%d" % j, [128, 2, DV], BF16) for j in range(5)]
            OFr = Rot(st, "OF", [128, DV], F32, 4)
            OBTr = Rot(st, "OBT", [128, DV], BF16, 3)
            OBOr = Rot(st, "OBO", [128, 4, 128], BF16, 2)
            SQJr = Rot(st, "SQJ", [128, DV], F32, 3)
            STr = Rot(st, "STs", [128, 4], F32, 4)
            dma("sp", SF[:], SF_d[:, :, :], writes=tuple("SF%d" % i for i in range(2 * H)))
            PQ = st.ps("PQ", [128, 512])
            PV_ = st.ps("PVm", [128, 512])
            PG = st.ps("PG", [128, 512])
            PSC = st.ps("PSC", [128, 512])
            PO = st.ps("PO", [128, 512])
            PStb = [st.ps("PSt%d" % i, [128, 512]) for i in range(2)]
            PT8 = st.ps("PT8", [128, 8, 128], BF16)
            PT = PT8[:, 0:4, :]
            PT2 = PT8[:, 4:8, :]
            its = [(h, ti) for h in range(H) for ti in range(len(main_tiles))]
            NT = len(main_tiles)
            ctx = {}
            sbf = {}

            def load_w(h):
                W = WH[h % 2]
                wk = "WH%d" % (h % 2)
                wcast(W[:, :, 0:256], w_in[:, O_Q + h * DK:O_Q + (h + 1) * DK], wk)
                wcast(W[:, :, 256:512], w_in[:, O_K + h * DK:O_K + (h + 1) * DK], wk)
                wcast(W[:, :, 512:1024], w_in[:, O_V + h * DV:O_V + (h + 1) * DV], wk)
                wcast(W[:, :, 1024:1536], w_in[:, O_G + h * DV:O_G + (h + 1) * DV], wk)

            cast_engs[0] = ("pool", "act")
            load_w(0)
            cast_engs[0] = ("pool",)

            def geo(i):
                h, ti = its[i]
                r0, rows = main_tiles[ti]
                return h, ti, r0, rows, r0 - NPRE

            def q1(i):
                h, ti, r0, rows, m0 = geo(i)
                if ti == 0 and h + 1 < H:
                    load_w(h + 1)
                W = WH[h % 2]
                wk = "WH%d" % (h % 2)
                CS, csk = CSr.nxt()
                dma("sp", CS[0:rows, 0, :], cosT[r0:r0 + rows, :], writes=(csk,))
                dma("sp", CS[0:rows, 1, :], sinT[r0:r0 + rows, :], writes=(csk,))
                xt = lambda kc: XM[:, kc, m0:m0 + rows]
                for kc in range(KD):
                    mm(PQ[0:rows, :], xt(kc), W[:, kc, 0:512], kc == 0, kc == KD - 1, ("XM", wk), ("PQ",))
                for kc in range(KD):
                    mm(PV_[0:rows, :], xt(kc), W[:, kc, 512:1024], kc == 0, kc == KD - 1, ("XM", wk), ("PVm",))
                for kc in range(KD):
                    mm(PG[0:rows, :], xt(kc), W[:, kc, 1024:1536], kc == 0, kc == KD - 1, ("XM", wk), ("PG",))
                ctx[i] = dict(CS=CS, csk=csk)

            def q2(i):
                h, ti, r0, rows, m0 = geo(i)
                c = ctx[i]
                QKF, qfk = QKFr.nxt()
                VV, vvk = VVr.nxt()
                SG, sgk = SGr.nxt()
                act(QKF[0:rows, 0:DK], PQ[0:rows, 0:DK], AF.Identity, ("PQ", "CT"), (qfk,),
                    scale=CT[0:rows, h:h + 1])
                cp("act", QKF[0:rows, DK:2 * DK], PQ[0:rows, DK:2 * DK], ("PQ",), (qfk,))
                cp("act", VV[0:rows, :], PV_[0:rows, :], ("PVm",), (vvk,))
                act(SG[0:rows, :], PG[0:rows, :], AF.Silu, ("PG",), (sgk,))
                c.update(QKF=QKF, qfk=qfk, VV=VV, vvk=vvk, SG=SG, sgk=sgk)

            def q3(i):
                h, ti, r0, rows, m0 = geo(i)
                c = ctx[i]
                QKR, qrk = QKRr.nxt()
                rotary(T, c["QKF"], c["qfk"], QKR, qrk, rows, 2, c["CS"][0:rows, 0, :], c["CS"][0:rows, 1, :],
                       c["csk"])
                KDc, kdk = KDr.nxt()
                kcol = (4 + h) if rows == 128 else (8 + h)
                ts(KDc[0:rows, :], QKR[0:rows, DK:2 * DK], CT[0:rows, kcol:kcol + 1], None, ALU.mult, None,
                   (qrk, "CT"), (kdk,))
                c.update(QKR=QKR, qrk=qrk, KDc=KDc, kdk=kdk)

            def state_mm(i, dc):
                h, ti, r0, rows, m0 = geo(i)
                c = ctx[i]
                mm(PStb[dc][:, :], c["KDc"][0:rows, dc * 128:(dc + 1) * 128], c["VV"][0:rows, :], True, True,
                   (c["kdk"], c["vvk"]), ("PSt%d" % dc,))

            def state_stt(i, dc):
                h, ti, r0, rows, m0 = geo(i)
                sfk = "SF%d" % (h * 2 + dc)
                gC = GAM[h] ** rows
                stt(SF[:, h * 2 + dc, :], SF[:, h * 2 + dc, :], gC, PStb[dc][:, :], ALU.mult, ALU.add,
                    (sfk, "PSt%d" % dc), (sfk,))

            def q4(i):
                h, ti, r0, rows, m0 = geo(i)
                c = ctx[i]
                QKR, qrk = c["QKR"], c["qrk"]
                for j in range(4):
                    S.op("pe", lambda e, j=j, rows=rows, QKR=QKR: e.transpose(
                        PT[:, j, 0:rows], QKR[0:rows, j * 128:(j + 1) * 128], T["IDB"][0:rows, 0:rows]),
                        (qrk, "IDB"), ("PT8",))
                if ti + 1 < NT:
                    state_mm(i, 0)
                    state_mm(i, 1)

            def q5(i):
                h, ti, r0, rows, m0 = geo(i)
                c = ctx[i]
                QKT, qtk = QKTr.nxt()
                cp("act", QKT[:, :, 0:rows], PT[:, 0:4, 0:rows], ("PT8",), (qtk,))
                if ti + 1 < NT:
                    state_stt(i, 0)
                    state_stt(i, 1)
                c.update(QKT=QKT, qtk=qtk)

            def q6(i):
                h, ti, r0, rows, m0 = geo(i)
                c = ctx[i]
                QKT, qtk = c["QKT"], c["qtk"]
                for dc in range(2):
                    mm(PSC[0:rows, 0:rows], QKT[:, 2 + dc, 0:rows], QKT[:, dc, 0:rows], dc == 0, dc == 1,
                       (qtk,), ("PSC",))
                if ti + 1 < NT:
                    j = (i + 1) % 5
                    for dc in range(2):
                        cp("act", SBFring[j][:, dc, :], SF[:, h * 2 + dc, :], ("SF%d" % (h * 2 + dc),),
                           ("SBFr%d_%d" % (j, dc),))

            def q7(i):
                h, ti, r0, rows, m0 = geo(i)
                c = ctx[i]
                SCT, sck = SCTr.nxt()
                tt(SCT[0:rows, 0:rows], PSC[0:rows, 0:rows], MT[0:rows, h, 0:rows], ALU.mult,
                   ("PSC", "MT"), (sck,))
                c.update(SCT=SCT, sck=sck)

            def q8(i):
                h, ti, r0, rows, m0 = geo(i)
                c = ctx[i]
                j = i % 5
                VV, vvk, QKT, qtk, SCT, sck = c["VV"], c["vvk"], c["QKT"], c["qtk"], c["SCT"], c["sck"]
                mm(PO[0:rows, :], SCT[0:rows, 0:rows], VV[0:rows, :], True, False, (sck, vvk), ("PO",))
                for dc in range(2):
                    if ti == 0:
                        mm(PO[0:rows, :], QKT[:, dc, 0:rows], SB0[:, h * 2 + dc, :], False, dc == 1,
                           (qtk, "SB0"), ("PO",))
                    else:
                        mm(PO[0:rows, :], QKT[:, dc, 0:rows], SBFring[j][:, dc, :], False, dc == 1,
                           (qtk, "SBFr%d_%d" % (j, dc)), ("PO",))

            def q9(i):
                h, ti, r0, rows, m0 = geo(i)
                c = ctx[i]
                OF, ofk = OFr.nxt()
                SQJ, sqk = SQJr.nxt()
                cp("act", OF[0:rows, :], PO[0:rows, :], ("PO",), (ofk,))
                act(SQJ[0:rows, :], PO[0:rows, :], AF.Square, ("PO",), (sqk,))
                c.update(OF=OF, ofk=ofk, SQJ=SQJ, sqk=sqk)

            def q10(i):
                h, ti, r0, rows, m0 = geo(i)
                c = ctx[i]
                STs, stk = STr.nxt()
                SQJ, sqk = c["SQJ"], c["sqk"]
                S.op("dve", lambda e, rows=rows, STs=STs, SQJ=SQJ: e.reduce_sum(
                    out=STs[0:rows, 0:1], in_=SQJ[0:rows, :], axis=AX.X), (sqk,), (stk,))
                ts(STs[0:rows, 1:2], STs[0:rows, 0:1], 1.0 / DV, EPS, ALU.mult, ALU.add, (stk,), (stk,))
                tt(STs[0:rows, 2:3], STs[0:rows, 1:2], CM[0:rows, :], ALU.pow, (stk, "CM"), (stk,), eng="pool")
                c.update(STs=STs, stk=stk)

            def q11(i):
                h, ti, r0, rows, m0 = geo(i)
                c = ctx[i]
                OBT, obk = OBTr.nxt()
                stt(OBT[0:rows, :], c["OF"][0:rows, :], c["STs"][0:rows, 2:3], c["SG"][0:rows, :], ALU.mult,
                    ALU.mult, (c["ofk"], c["stk"], c["sgk"]), (obk,))
                c.update(OBT=OBT, obk=obk)

            def q12(i):
                h, ti, r0, rows, m0 = geo(i)
                c = ctx[i]
                OBT, obk = c["OBT"], c["obk"]
                for ec in range(4):
                    S.op("pe", lambda e, ec=ec, rows=rows, OBT=OBT: e.transpose(
                        PT2[:, ec, 0:rows], OBT[0:rows, ec * 128:(ec + 1) * 128], T["IDB"][0:rows, 0:rows]),
                        (obk, "IDB"), ("PT8",))

            def q13(i):
                h, ti, r0, rows, m0 = geo(i)
                ctx.pop(i)
                OBO, ook = OBOr.nxt()
                cp("act", OBO[:, :, 0:rows], PT2[:, :, 0:rows], ("PT8",), (ook,))
                dma("sp", OB_d[:, h * 4:(h + 1) * 4, m0:m0 + rows], OBO[:, :, 0:rows], reads=(ook,))

            SB0 = st.sb("SB0", [128, 2 * H, DV], BF16)
            cp("act", SB0[:, :, :], SF[:, :, :], tuple("SF%d" % i for i in range(2 * H)), ("SB0",))
            pipeline(len(its), [q1, q2, q3, q4, q5, q6, q7, q8, q9, q10, q11, q12, q13])

        if stop <= 4:
            de.close()
            return nc
        with Stage(nc, S) as st:
            OA = st.sb("OA", [128, NB, NM], BF16)
            OB = st.sb("OB", [128, 16, NM], BF16)
            WA = [st.sb("WA%d" % i, [128, NB, 128], BF16) for i in range(2)]
            WR = [st.sb("WR%d" % i, [128, 16, 128], BF16) for i in range(2)]
            WGa = [st.sb("WGa%d" % i, [128, KD, 128], BF16) for i in range(2)]
            WGb = [st.sb("WGb%d" % i, [128, KD, 128], BF16) for i in range(2)]
            Rt = {k: Rot(st, k, [128, 512], F32, 2) for k in ["GA", "GB", "M1", "M2"]}
            MO = [st.sb("MO%d" % i, [128, NM], BF16) for i in range(2)]
            PEr = [Rot(st, "PE%d_" % i, [128, 512], F32, 2, psum=True) for i in range(4)]

            def load_e(c):
                wi = c % 2
                cs = slice(c * 128, (c + 1) * 128)
                wcast(WA[wi][:], w_brnn[:, cs], "WA%d" % wi)
                wcast(WGa[wi][:], w_in[:, O_GA + c * 128:O_GA + (c + 1) * 128], "WGa%d" % wi)
                wcast(WR[wi][:], w_bret[:, cs], "WR%d" % wi)
                wcast(WGb[wi][:], w_in[:, O_GB + c * 128:O_GB + (c + 1) * 128], "WGb%d" % wi)

            cast_engs[0] = ("pool", "act")
            load_e(0)
            cast_engs[0] = ("pool",)
            for kc in range(NB):
                dma("sp", OA[:, kc, :], OA_d[:, kc, :], writes=("OA",))
            for kc in range(16):
                dma("sp", OB[:, kc, :], OB_d[:, kc, :], writes=("OB",))
            eits = [(c, gi) for c in range(8) for gi in range(len(MG))]
            ectx = {}

            def e1(i):
                c, gi = eits[i]
                m0, ncols = MG[gi]
                wi = c % 2
                if gi == 0 and c + 1 < 8:
                    load_e(c + 1)
                ps_ = [r.nxt() for r in PEr]
                for k in range(NB):
                    mm(ps_[0][0][:, 0:ncols], WA[wi][:, k, :], OA[:, k, m0:m0 + ncols], k == 0, k == NB - 1,
                       ("WA%d" % wi, "OA"), (ps_[0][1],))
                for k in range(KD):
                    mm(ps_[2][0][:, 0:ncols], WGa[wi][:, k, :], XM[:, k, m0:m0 + ncols], k == 0, k == KD - 1,
                       ("WGa%d" % wi, "XM"), (ps_[2][1],))
                for k in range(16):
                    mm(ps_[1][0][:, 0:ncols], WR[wi][:, k, :], OB[:, k, m0:m0 + ncols], k == 0, k == 15,
                       ("WR%d" % wi, "OB"), (ps_[1][1],))
                for k in range(KD):
                    mm(ps_[3][0][:, 0:ncols], WGb[wi][:, k, :], XM[:, k, m0:m0 + ncols], k == 0, k == KD - 1,
                       ("WGb%d" % wi, "XM"), (ps_[3][1],))
                ectx[i] = ps_

            def e2(i):
                c, gi = eits[i]
                m0, ncols = MG[gi]
                wi = c % 2
                ps_ = ectx.pop(i)
                GA, gak = Rt["GA"].nxt()
                GB, gbk = Rt["GB"].nxt()
                M1, m1k = Rt["M1"].nxt()
                M2, m2k = Rt["M2"].nxt()
                act(GA[:, 0:ncols], ps_[2][0][:, 0:ncols], AF.Sigmoid, (ps_[2][1],), (gak,))
                act(GB[:, 0:ncols], ps_[3][0][:, 0:ncols], AF.Sigmoid, (ps_[3][1],), (gbk,))
                tt(M1[:, 0:ncols], ps_[0][0][:, 0:ncols], GA[:, 0:ncols], ALU.mult, (ps_[0][1], gak), (m1k,))
                tt(M2[:, 0:ncols], ps_[1][0][:, 0:ncols], GB[:, 0:ncols], ALU.mult, (ps_[1][1], gbk), (m2k,))
                tt(MO[wi][:, m0:m0 + ncols], M1[:, 0:ncols], M2[:, 0:ncols], ALU.add, (m1k, m2k),
                   ("MO%d" % wi,), eng="pool")
                if gi == len(MG) - 1:
                    dma("sp", MIX_d[:, c, :], MO[wi][:, :], reads=("MO%d" % wi,))

            pipeline(len(eits), [e1, e2])

        de.close()
        if stop <= 5:
            return nc
        with Stage(nc, S) as st:
            T = {}
            load_iden(st, T)
            MX = st.sb("MX", [128, KD, NM], BF16)
            for kc in range(KD):
                dma("sp", MX[:, kc, :], MIX_d[:, kc, :], writes=("MX",))
            WO = st.sb("WO", [128, KD, D], BF16)
            cast_engs[0] = ("pool", "act")
            for c0 in range(0, D, 512):
                wcast(WO[:, :, c0:c0 + 512], w_out[:, c0:c0 + 512], "WO")
            cast_engs[0] = ("pool",)
            G2B = st.sb("G2B", [128, KD, 128])
            dma("sp", G2B[:], g2b[:, :, :], writes=("G2B",))
            XSr = Rot(st, "XS", [128, D], F32, 3)
            HMr = Rot(st, "HM", [128, D], F32, 6)
            norm_bufs(st, T, 3)
            XOr = Rot(st, "XO", [128, KD, 128], BF16, 8)
            PF = Rot(st, "PF", [128, 512], F32, 4, psum=True)
            items = []
            for ti, (r0, rows) in enumerate(main_tiles):
                m0 = r0 - NPRE
                XOt, ok = XOr.nxt()

                def x_fn(ti=ti, r0=r0, rows=rows, m0=m0):
                    XSt, xk = XSr.nxt()
                    HMt, hk = HMr.nxt()
                    dma("sp", XSt[0:rows, :], xs[r0:r0 + rows, :], writes=(xk,))
                    for hf in range(2):
                        Pt, pk = PF.nxt()
                        for kc in range(KD):
                            mm(Pt[0:rows, :], MX[:, kc, m0:m0 + rows], WO[:, kc, hf * 512:(hf + 1) * 512],
                               kc == 0, kc == KD - 1, ("MX", "WO"), (pk,))
                        tt(HMt[0:rows, hf * 512:(hf + 1) * 512], Pt[0:rows, :],
                           XSt[0:rows, hf * 512:(hf + 1) * 512], ALU.add, (pk, xk), (hk,))
                    dma("sp", HM_d[ti * 128:ti * 128 + rows, :], HMt[0:rows, :], reads=(hk,))
                    return HMt[0:rows, :], hk

                def post(XOt=XOt, ok=ok, m0=m0, rows=rows):
                    dma("sp", XN2_d[:, :, m0:m0 + rows], XOt[:, :, 0:rows], reads=(ok,))
                items.append(dict(x_fn=x_fn, rows=rows, dst=XOt[:, :, 0:rows], dkey=ok, post=post))
            pipeline(len(items), norm_phases(T, items, G2B, "G2B"))

        if stop <= 6:
            return nc
        gh = ExitStack()
        gh.__enter__()
        AC = gh.enter_context(nc.sbuf_tensor("ACgh", [128, 24, NOWN], BF16))
        with Stage(nc, S) as st:
            X2 = st.sb("X2", [128, KD, NM], BF16)
            for kc in range(KD):
                dma("sp", X2[:, kc, :], XN2_d[:, kc, :], writes=("X2",))
            FV = st.sb("FV", [128, 4, 48])
            dma("sp", FV[:], fvec[:, :, :], writes=("FV",))
            WUg = [st.sb("WUg%d" % i, [128, KD, 256], BF16) for i in range(2)]
            WUv = [st.sb("WUv%d" % i, [128, KD, 256], BF16) for i in range(2)]
            FU = [st.sb("FU%d" % i, [128, 2, NM]) for i in range(2)]
            FAgr = Rot(st, "FAg", [128, NOWN], F32, 1)
            FAv = st.sb("FAv", [128, NOWN])
            P = [st.ps("PG%d" % i, [128, 512]) for i in range(8)]

            def load_g(cp_):
                w2 = cp_ % 2
                wcast(WUg[w2][:], w_up[:, cp_ * 256:(cp_ + 1) * 256], "WUg%d" % w2)
                wcast(WUv[w2][:], w_up[:, DFF + cp_ * 256:DFF + (cp_ + 1) * 256], "WUv%d" % w2)

            cast_engs[0] = ("pool", "act")
            load_g(0)
            cast_engs[0] = ("pool",)
            pcount = [0]
            for c in range(24):
                wi = c % 2
                cp_, sub = divmod(c, 2)
                if sub == 0 and cp_ + 1 < 12:
                    load_g(cp_ + 1)
                w2 = cp_ % 2
                ws = slice(sub * 128, (sub + 1) * 128)
                fk = "FU%d" % wi
                for gi, (m0, ncols) in enumerate(MG):
                    b0 = (pcount[0] % 4) * 2
                    pcount[0] += 1
                    pg, pv = P[b0], P[b0 + 1]
                    kg, kv = "PG%d" % b0, "PG%d" % (b0 + 1)
                    for k in range(KD):
                        mm(pg[:, 0:ncols], WUg[w2][:, k, ws], X2[:, k, m0:m0 + ncols], k == 0, k == KD - 1,
                           ("WUg%d" % w2, "X2"), (kg,))
                    for k in range(KD):
                        mm(pv[:, 0:ncols], WUv[w2][:, k, ws], X2[:, k, m0:m0 + ncols], k == 0, k == KD - 1,
                           ("WUv%d" % w2, "X2"), (kv,))
                    cp("act", FU[wi][:, 0, m0:m0 + ncols], pg[:, 0:ncols], (kg,), (fk,))
                    cp("act", FU[wi][:, 1, m0:m0 + ncols], pv[:, 0:ncols], (kv,), (fk,))
                FG, fgk = FAgr.nxt()
                for j, cc in ((0, c), (1, 24 + c)):
                    FAj, fj = (FG, fgk) if j == 0 else (FAv, "FAv")
                    act(FAj[:, :], FU[wi][:, j, 14:14 + NOWN], AF.Identity, (fk, "FV"), (fj,),
                        scale=FV[:, 0, cc:cc + 1], bias=FV[:, 3, cc:cc + 1])
                    stt(FAj[:, :], FU[wi][:, j, 15:15 + NOWN], FV[:, 1, cc:cc + 1], FAj[:, :],
                        ALU.mult, ALU.add, (fk, "FV", fj), (fj,))
                    stt(FAj[:, :], FU[wi][:, j, 16:16 + NOWN], FV[:, 2, cc:cc + 1], FAj[:, :],
                        ALU.mult, ALU.add, (fk, "FV", fj), (fj,))
                act(FG[:, :], FG[:, :], AF.Gelu_apprx_tanh, (fgk,), (fgk,))
                tt(AC[:, c, :], FG[:, :], FAv[:, :], ALU.mult, (fgk, "FAv"), ("AC%d" % c,))
                if dbg:
                    dma("sp", ACT_d[:, c, :], AC[:, c, :], reads=("AC%d" % c,))

        if stop <= 7:
            gh.close()
            return nc
        with Stage(nc, S) as st:
            WD = [st.sb("WD%d" % i, [128, 24, 512], BF16) for i in range(2)]
            cast_engs[0] = ("pool", "act")
            for hf in range(2):
                for c0 in range(0, 24, 8):
                    wcast(WD[hf][:, c0:c0 + 8, :], w_down[c0 * 128:(c0 + 8) * 128, hf * 512:(hf + 1) * 512],
                          "WD%d" % hf)
            cast_engs[0] = ("pool",)
            GFB = st.sb("GFB", [128, D])
            dma("sp", GFB[:], gfb[:, :], writes=("GFB",))
            CM = st.sb("CM", [128, 1])
            S.op("pool", lambda e: e.memset(CM[:], -0.5), (), ("CM",))
            HMr = Rot(st, "HM", [128, D], F32, 4)
            YOr = Rot(st, "YO", [128, D], F32, 3)
            SQr = Rot(st, "SQJ", [128, D], F32, 2)
            STr = Rot(st, "STs", [128, 4], F32, 3)
            Pr = Rot(st, "PH", [128, 512], F32, 4, psum=True)
            for t in range(16):
                rr = (t + 1) * 128
                HMt, hk = HMr.nxt()
                dma("sp", HMt[:, :], HM_d[rr:rr + 128, :], writes=(hk,))
                for hf in range(2):
                    Pt, pk = Pr.nxt()
                    for c in range(24):
                        mm(Pt[:, :], AC[:, c, t * 128:(t + 1) * 128], WD[hf][:, c, :], c == 0, c == 23,
                           ("WD%d" % hf,), (pk,))
                    tt(HMt[:, hf * 512:(hf + 1) * 512], Pt[:, :], HMt[:, hf * 512:(hf + 1) * 512],
                       ALU.add, (pk, hk), (hk,))
                SQJ, sqk = SQr.nxt()
                STs, stk = STr.nxt()
                YO, yk = YOr.nxt()
                act(SQJ[:, :], HMt[:, :], AF.Square, (hk,), (sqk,))
                S.op("dve", lambda e, STs=STs, SQJ=SQJ: e.reduce_sum(out=STs[:, 0:1], in_=SQJ[:, :],
                                                                      axis=AX.X), (sqk,), (stk,))
                ts(STs[:, 1:2], STs[:, 0:1], 1.0 / D, EPS, ALU.mult, ALU.add, (stk,), (stk,))
                tt(STs[:, 2:3], STs[:, 1:2], CM[:, :], ALU.pow, (stk, "CM"), (stk,), eng="pool")
                stt(YO[:, :], HMt[:, :], STs[:, 2:3], GFB[:, :], ALU.mult, ALU.mult,
                    (hk, stk, "GFB"), (yk,))
                dma("sp", out[t * 128:(t + 1) * 128, :], YO[:, :], reads=(yk,))
        gh.close()
    return nc


def _tables(core):
    b, p = divmod(core, 4)
    n_real_pre = 2048 * p
    pad = NPRE - n_real_pre
    pos = np.arange(L, dtype=np.int64) - pad
    pos = np.maximum(pos, 0).astype(np.float32)
    half = 128
    inv_freq = (1.0 / (10000.0 ** np.linspace(0.0, 1.0, half, dtype=np.float32))).astype(np.float32)
    ang = (pos[:, None] * inv_freq[None, :]).astype(np.float32)
    cos = np.cos(ang).astype(np.float32)
    sin = np.sin(ang).astype(np.float32)
    g = (1.0 - 2.0 ** (-5.0 - np.arange(H, dtype=np.float64)))
    t = np.arange(NPRE, dtype=np.float64)
    dec = g[None, :] ** (NPRE - 1 - t)[:, None]
    dec = dec * (DK ** -0.5)
    decpre = dec.reshape(NPRE // 128, 128, H).transpose(1, 0, 2).astype(np.float32)
    i = np.arange(128, dtype=np.float64)
    ctab = np.zeros((128, 16), np.float32)
    ctab[:, 0:4] = g[None, :] ** (i + 1.0)[:, None]
    ctab[:, 4:8] = (g[None, :] ** (127.0 - i)[:, None]) * (DK ** -0.5)
    ctab[:, 8:12] = (g[None, :] ** np.maximum(15.0 - i, 0.0)[:, None]) * (DK ** -0.5)
    jj = i[:, None]
    ii = i[None, :]
    maskT = np.zeros((128, H, 128), np.float32)
    for h in range(H):
        maskT[:, h, :] = (g[h] ** (-(jj + 1.0))) * (ii >= jj) * (DK ** -0.5)
    mgrp = np.zeros((128, NPG), np.float32)
    for gi in range(NPG):
        mgrp[:, gi] = 1.0 if gi * 512 >= pad else 0.0
    return cos, sin, decpre, ctab, maskT, mgrp


def kernel(x, meta_tokens, norm_mix_g, w_in, rnn_conv_w, rnn_conv_b, rg_a_w, rg_a_b,
           rg_x_w, rg_x_b, lru_lambda, w_branch_rnn, w_branch_ret, w_out, norm_ffn_g,
           w_up, ffn_conv_w, ffn_conv_b, w_down, norm_final_g, _ret_maps=False):
    f = lambda a: np.ascontiguousarray(np.asarray(a, dtype=np.float32))
    x = f(x)
    meta = f(meta_tokens)
    pvec = np.zeros((128, 80), np.float32)
    cw = f(rnn_conv_w)[0].reshape(4, NB, 128)
    for j in range(4):
        pvec[:, j * 10:(j + 1) * 10] = cw[j].T
    pvec[:, 40:50] = f(rnn_conv_b)[0].reshape(NB, 128).T
    pvec[:, 50:60] = f(rg_a_b)[0].reshape(NB, 128).T
    pvec[:, 60:70] = f(rg_x_b)[0].reshape(NB, 128).T
    pvec[:, 70:80] = f(lru_lambda)[0].reshape(NB, 128).T
    fvec = np.zeros((128, 4, 48), np.float32)
    fw = f(ffn_conv_w)[0].reshape(3, 48, 128)
    for j in range(3):
        fvec[:, j, :] = fw[j].T
    fvec[:, 3, :] = f(ffn_conv_b)[0].reshape(48, 128).T
    rgw = np.ascontiguousarray(np.stack([f(rg_a_w)[0], f(rg_x_w)[0]], 0).transpose(2, 0, 1, 3))
    g1b = np.ascontiguousarray(np.broadcast_to(f(norm_mix_g)[0].reshape(KD, 128).T[:, :, None], (128, KD, 128)))
    g2b = np.ascontiguousarray(np.broadcast_to(f(norm_ffn_g)[0].reshape(KD, 128).T[:, :, None], (128, KD, 128)))
    gfb = np.ascontiguousarray(np.broadcast_to(f(norm_final_g)[None, :], (128, D)))
    iden = np.eye(128, dtype=np.float32)
    shared = {
        "w_in": f(w_in)[0], "w_brnn": f(w_branch_rnn)[0], "w_bret": f(w_branch_ret)[0], "w_out": f(w_out)[0],
        "w_up": f(w_up)[0], "w_down": f(w_down)[0], "rgw": rgw, "pvec": pvec, "fvec": fvec,
        "g1b": g1b, "g2b": g2b, "gfb": gfb, "iden": iden,
    }
    in_maps = []
    for core in range(8):
        b, p = divmod(core, 4)
        seq = np.concatenate([meta, x[b]], 0)
        end = NMETA + 2048 * (p + 1)
        stream = np.zeros((L, D), np.float32)
        stream[L - end:] = seq[:end]
        cos, sin, decpre, ctab, maskT, mgrp = _tables(core)
        m = dict(shared)
        m.update({"xs": stream, "cosT": cos, "sinT": sin, "decpre": decpre, "ctab": ctab,
                  "maskT": maskT, "mgrp": mgrp})
        in_maps.append(m)
    if _ret_maps:
        return in_maps
    nc = build()
    res = run_bass_kernel_spmd(nc, in_maps, core_ids=list(range(8)))
    outp = np.zeros((2, SEQ, D), np.float32)
    for core in range(8):
        b, p = divmod(core, 4)
        outp[b, p * 2048:(p + 1) * 2048] = res.results[core]["out"]
    return outp
```
